# Optimizing a Trainium2 kernel written in Bass

```python
import math
import jax
import jax.numpy as jnp
from jax import lax
import numpy as np

D_MODEL = 1024
BATCH = 8
SEQ = 2048
DEPTH = 2

N_META = 16
Q_BLOCK = 128
N_BRANCH = 4
BRANCH_WIDTH = 512
A_HEADS = 4
A_HEAD_DIM = 64
A_V_DIM = 2 * A_HEAD_DIM
B_HEADS = 4
B_KEY_DIM = 64
B_VAL_DIM = 128
B_GATE_RANK = 16
B_GATE_TAU = 16.0
B_CHUNK = 64
C_HEADS = 4
C_Q_RANK = 256
C_KV_RANK = 128
C_NOPE_DIM = 128
C_ROPE_DIM = 64
C_V_DIM = 128
ROPE_THETA = 10000.0
D_Q_HEADS = 8
D_KV_HEADS = 2
D_HEAD_DIM = 64
WINDOW = 128
N_BUCKETS = 32
MAX_DISTANCE = 128
N_BIAS_COLS = 2 * A_HEADS + D_Q_HEADS
D_FF = 2816
CONV_WIDTH = 3
NORM_EPS = 1e-6
NEG_INF = -1e30

IN_WIDTHS = (
    A_HEADS * 2 * A_HEAD_DIM,
    A_HEADS * 2 * A_HEAD_DIM,
    A_HEADS * A_V_DIM,
    B_HEADS * B_KEY_DIM,
    B_HEADS * B_KEY_DIM,
    B_HEADS * B_VAL_DIM,
    B_HEADS * B_VAL_DIM,
    2 * B_GATE_RANK,
    C_Q_RANK,
    C_KV_RANK,
    C_ROPE_DIM,
    D_Q_HEADS * D_HEAD_DIM,
    D_KV_HEADS * D_HEAD_DIM,
    D_KV_HEADS * D_HEAD_DIM,
    N_BRANCH * D_MODEL,
)
N_IN = sum(IN_WIDTHS)

kernel_name = 'hybrid_parallel_gated_encoder'


def _rms_norm(x, g):
    xf = x.astype(jnp.float32)
    y = xf * lax.rsqrt(jnp.mean(xf * xf, axis=-1, keepdims=True) + NORM_EPS)
    return (y * g.astype(jnp.float32)).astype(x.dtype)


def _rel_bucket(rel):
    half = N_BUCKETS // 2
    max_exact = half // 2
    ret = jnp.where(rel > 0, half, 0)
    n = jnp.abs(rel)
    nf = jnp.maximum(n, 1).astype(jnp.float32)
    large = max_exact + (jnp.log(nf / max_exact) / math.log(MAX_DISTANCE / max_exact)
                         * (half - max_exact)).astype(jnp.int32)
    large = jnp.minimum(large, half - 1)
    return ret + jnp.where(n < max_exact, n, large)


def _rope(x, pos):
    half = x.shape[-1] // 2
    inv = ROPE_THETA ** (-jnp.arange(half, dtype=jnp.float32) / half)
    ang = pos.astype(jnp.float32)[:, None] * inv[None, :]
    ang = ang.reshape((1, ang.shape[0]) + (1,) * (x.ndim - 3) + (half,))
    cos, sin = jnp.cos(ang), jnp.sin(ang)
    x1 = x[..., :half].astype(jnp.float32)
    x2 = x[..., half:].astype(jnp.float32)
    return jnp.concatenate([x1 * cos - x2 * sin, x1 * sin + x2 * cos], axis=-1).astype(x.dtype)


def _query_blocks(t):
    bsz, T = t.shape[:2]
    nblk = -(-T // Q_BLOCK)
    pad = [(0, 0), (0, nblk * Q_BLOCK - T)] + [(0, 0)] * (t.ndim - 2)
    t = jnp.pad(t, pad).reshape((bsz, nblk, Q_BLOCK) + t.shape[2:])
    return jnp.moveaxis(t, 1, 0)


def _merge_query_blocks(o, T):
    o = jnp.moveaxis(o, 0, 1)
    return o.reshape((o.shape[0], -1) + o.shape[3:])[:, :T]


def _diff_attention(q, k, v, lam, subln_g, bias_table, lam_init):
    bsz, T = q.shape[:2]
    nblk = -(-T // Q_BLOCK)
    qpos = jnp.arange(nblk * Q_BLOCK, dtype=jnp.int32).reshape(nblk, Q_BLOCK)
    kpos = jnp.arange(T, dtype=jnp.int32)
    tab = bias_table[:, :2 * A_HEADS].astype(jnp.float32).reshape(N_BUCKETS, 2, A_HEADS)
    lam = lam.astype(jnp.float32)
    lam_val = jnp.exp(jnp.sum(lam[0] * lam[1])) - jnp.exp(jnp.sum(lam[2] * lam[3])) + lam_init
    scale = A_HEAD_DIM ** -0.5

    def block(args):
        qb, qp = args
        s = jnp.einsum('bqhmd,bkhmd->bmhqk', qb, k).astype(jnp.float32) * scale
        bias = jnp.transpose(tab[_rel_bucket(kpos[None, :] - qp[:, None])], (2, 3, 0, 1))
        p = jax.nn.softmax(s + bias[None], axis=-1)
        a = (p[:, 0] - lam_val * p[:, 1]).astype(v.dtype)
        return jnp.einsum('bhqk,bkhe->bqhe', a, v)

    o = _merge_query_blocks(lax.map(block, (_query_blocks(q), qpos)), T)
    o = _rms_norm(o, subln_g) * (1.0 - lam_init)
    return o.reshape(bsz, T, A_HEADS * A_V_DIM)


def _gla_direction(q, k, v, log_a):
    bsz, L, H, dk = q.shape
    dv = v.shape[-1]
    n = L // B_CHUNK
    q, k, log_a = [t.reshape(bsz, n, B_CHUNK, H, dk) for t in (q, k, log_a)]
    v = v.reshape(bsz, n, B_CHUNK, H, dv)
    b = jnp.cumsum(log_a, axis=2)
    b_end = b[:, :, -1]
    q_dec = q * jnp.exp(b)
    k_in = k * jnp.exp(-b)
    k_out = k * jnp.exp(b_end[:, :, None] - b)
    causal = jnp.tril(jnp.ones((B_CHUNK, B_CHUNK), q.dtype))
    att = jnp.einsum('bnthd,bnshd->bnhts', q_dec, k_in) * causal
    o = jnp.einsum('bnhts,bnshe->bnthe', att, v)
    d_state = jnp.einsum('bnshd,bnshe->bnhde', k_out, v)

    def step(S, inp):
        dS, dec = inp
        return dec[..., None] * S + dS, S

    S0 = jnp.zeros((bsz, H, dk, dv), q.dtype)
    _, S_start = lax.scan(step, S0, (jnp.moveaxis(d_state, 1, 0), jnp.moveaxis(jnp.exp(b_end), 1, 0)))
    o = o + jnp.einsum('bnthd,nbhde->bnthe', q_dec, S_start)
    return o.reshape(bsz, L, H, dv)


def _gla(q, k, v, gate_lat, w_gate_up, b_gate_up, r, norm_g):
    bsz, T = q.shape[:2]
    lead = B_CHUNK - N_META
    tail = (-(T + lead)) % B_CHUNK
    pre = jnp.einsum('btgr,grc->btgc', gate_lat, w_gate_up) + b_gate_up
    log_a = jax.nn.log_sigmoid(pre.astype(jnp.float32)) / B_GATE_TAU
    log_a = log_a.reshape(bsz, T, 2, B_HEADS, B_KEY_DIM)

    def pad(t):
        return jnp.pad(t.astype(jnp.float32), [(0, 0), (lead, tail)] + [(0, 0)] * (t.ndim - 2))

    qf, kf, vf, la = pad(q) * B_KEY_DIM ** -0.5, pad(k), pad(v), pad(log_a)
    o_fwd = _gla_direction(qf, kf, vf, la[:, :, 0])
    o_bwd = _gla_direction(qf[:, ::-1], kf[:, ::-1], vf[:, ::-1], la[:, ::-1, 1])[:, ::-1]
    o = (o_fwd + o_bwd)[:, lead:lead + T]
    o = _rms_norm(o, norm_g).astype(v.dtype)
    return o.reshape(bsz, T, B_HEADS * B_VAL_DIM) * jax.nn.silu(r)


def _mla(q_lat, kv_lat, k_rope, q_norm_g, w_q_up, kv_norm_g, w_kv_up):
    bsz, T = q_lat.shape[:2]
    pos = jnp.arange(T, dtype=jnp.int32)
    q = (_rms_norm(q_lat, q_norm_g) @ w_q_up).reshape(bsz, T, C_HEADS, C_NOPE_DIM + C_ROPE_DIM)
    q_nope, q_pe = q[..., :C_NOPE_DIM], _rope(q[..., C_NOPE_DIM:], pos)
    kv = (_rms_norm(kv_lat, kv_norm_g) @ w_kv_up).reshape(bsz, T, C_HEADS, C_NOPE_DIM + C_V_DIM)
    k_nope, v = kv[..., :C_NOPE_DIM], kv[..., C_NOPE_DIM:]
    k_pe = _rope(k_rope, pos)
    scale = (C_NOPE_DIM + C_ROPE_DIM) ** -0.5

    def block(args):
        qn, qr = args
        s = jnp.einsum('bqhd,bkhd->bhqk', qn, k_nope) + jnp.einsum('bqhr,bkr->bhqk', qr, k_pe)
        p = jax.nn.softmax(s.astype(jnp.float32) * scale, axis=-1).astype(v.dtype)
        return jnp.einsum('bhqk,bkhe->bqhe', p, v)

    o = lax.map(block, (_query_blocks(q_nope), _query_blocks(q_pe)))
    return _merge_query_blocks(o, T).reshape(bsz, T, C_HEADS * C_V_DIM)


def _window_gqa(q, k, v, sinks, bias_table):
    bsz, T = q.shape[:2]
    G = D_Q_HEADS // D_KV_HEADS
    nblk = -(-T // Q_BLOCK)
    Tp = nblk * Q_BLOCK
    qb = jnp.pad(q, ((0, 0), (0, Tp - T), (0, 0), (0, 0))).reshape(
        bsz, nblk, Q_BLOCK, D_KV_HEADS, G, D_HEAD_DIM)

    def band(t):
        tp = jnp.pad(t, ((0, 0), (Q_BLOCK, Tp - T + Q_BLOCK), (0, 0), (0, 0))).reshape(
            bsz, nblk + 2, Q_BLOCK, D_KV_HEADS, D_HEAD_DIM)
        return jnp.concatenate([tp[:, :-2], tp[:, 1:-1], tp[:, 2:]], axis=2)

    kw, vw = band(k), band(v)
    km, vm = k[:, :N_META], v[:, :N_META]
    qpos = jnp.arange(Tp, dtype=jnp.int32).reshape(nblk, Q_BLOCK)
    kpos = (jnp.arange(nblk, dtype=jnp.int32)[:, None] - 1) * Q_BLOCK \
        + jnp.arange(3 * Q_BLOCK, dtype=jnp.int32)[None, :]
    rel = kpos[:, None, :] - qpos[:, :, None]
    visible = (jnp.abs(rel) <= WINDOW) & (kpos[:, None, :] >= N_META) & (kpos[:, None, :] < T)
    rel_meta = jnp.arange(N_META, dtype=jnp.int32)[None, None, :] - qpos[:, :, None]
    tab = bias_table[:, 2 * A_HEADS:].astype(jnp.float32).reshape(N_BUCKETS, D_KV_HEADS, G)
    bias_band = jnp.transpose(tab[_rel_bucket(rel)], (0, 3, 4, 1, 2))
    bias_meta = jnp.transpose(tab[_rel_bucket(rel_meta)], (0, 3, 4, 1, 2))
    scale = D_HEAD_DIM ** -0.5
    s_band = jnp.einsum('bnqkgd,bnskd->bnkgqs', qb, kw).astype(jnp.float32) * scale + bias_band[None]
    s_band = jnp.where(visible[None, :, None, None], s_band, NEG_INF)
    s_meta = jnp.einsum('bnqkgd,bmkd->bnkgqm', qb, km).astype(jnp.float32) * scale + bias_meta[None]
    s_sink = jnp.broadcast_to(sinks.astype(jnp.float32).reshape(1, 1, D_KV_HEADS, G, 1, 1),
                              s_meta.shape[:-1] + (1,))
    p = jax.nn.softmax(jnp.concatenate([s_band, s_meta, s_sink], axis=-1), axis=-1).astype(v.dtype)
    nb = 3 * Q_BLOCK
    o = jnp.einsum('bnkgqs,bnskd->bnqkgd', p[..., :nb], vw) \
        + jnp.einsum('bnkgqm,bmkd->bnqkgd', p[..., nb:nb + N_META], vm)
    return o.reshape(bsz, Tp, D_Q_HEADS * D_HEAD_DIM)[:, :T]


def _conv_ffn(h, w_up, conv_w, conv_b, w_down):
    u = h @ w_up
    up = jnp.pad(u, ((0, 0), (1, 1), (0, 0)))
    u = conv_w[0] * up[:, :-2] + conv_w[1] * up[:, 1:-1] + conv_w[2] * up[:, 2:] + conv_b
    gate, val = jnp.split(u, 2, axis=-1)
    return (jax.nn.gelu(gate, approximate=True) * val) @ w_down


def setup_inputs(seed: int = 0) -> dict:
    key = jax.random.key(seed)
    ks = jax.random.split(key, 24)
    f32 = jnp.float32

    def nrm(k, shape, scale):
        return jax.random.normal(k, shape, f32) * scale

    def gain(k, shape):
        return 1.0 + 0.05 * jax.random.normal(k, shape, f32)

    return {
        'x': nrm(ks[0], (BATCH, SEQ, D_MODEL), 1.0),
        'meta_tokens': nrm(ks[1], (N_META, D_MODEL), 1.0),
        'rel_bias_table': nrm(ks[2], (N_BUCKETS, N_BIAS_COLS), 0.5),
        'norm_mix_pre': gain(ks[3], (DEPTH, D_MODEL)),
        'norm_mix_post': gain(ks[4], (DEPTH, D_MODEL)),
        'norm_ffn_pre': gain(ks[5], (DEPTH, D_MODEL)),
        'norm_ffn_post': gain(ks[6], (DEPTH, D_MODEL)),
        'w_in': nrm(ks[7], (DEPTH, D_MODEL, N_IN), D_MODEL ** -0.5),
        'diff_lambda': nrm(ks[8], (DEPTH, 4, A_HEAD_DIM), 0.1),
        'diff_subln': gain(ks[9], (DEPTH, A_V_DIM)),
        'gla_gate_up': nrm(ks[10], (DEPTH, 2, B_GATE_RANK, B_HEADS * B_KEY_DIM), B_GATE_RANK ** -0.5),
        'gla_gate_bias': nrm(ks[11], (DEPTH, 2, B_HEADS * B_KEY_DIM), 0.1),
        'gla_norm': gain(ks[12], (DEPTH, B_VAL_DIM)),
        'mla_q_norm': gain(ks[13], (DEPTH, C_Q_RANK)),
        'mla_w_q_up': nrm(ks[14], (DEPTH, C_Q_RANK, C_HEADS * (C_NOPE_DIM + C_ROPE_DIM)), C_Q_RANK ** -0.5),
        'mla_kv_norm': gain(ks[15], (DEPTH, C_KV_RANK)),
        'mla_w_kv_up': nrm(ks[16], (DEPTH, C_KV_RANK, C_HEADS * (C_NOPE_DIM + C_V_DIM)), C_KV_RANK ** -0.5),
        'swa_sinks': nrm(ks[17], (DEPTH, D_Q_HEADS), 0.5),
        'w_branch': nrm(ks[18], (DEPTH, N_BRANCH, BRANCH_WIDTH, D_MODEL), BRANCH_WIDTH ** -0.5),
        'w_out': nrm(ks[19], (DEPTH, D_MODEL, D_MODEL), D_MODEL ** -0.5),
        'ffn_w_up': nrm(ks[20], (DEPTH, D_MODEL, 2 * D_FF), D_MODEL ** -0.5),
        'ffn_conv_w': nrm(ks[21], (DEPTH, CONV_WIDTH, 2 * D_FF), CONV_WIDTH ** -0.5),
        'ffn_conv_b': nrm(ks[22], (DEPTH, 2 * D_FF), 0.02),
        'ffn_w_down': nrm(ks[23], (DEPTH, D_FF, D_MODEL), D_FF ** -0.5),
    }


def reference(x, meta_tokens, rel_bias_table, norm_mix_pre, norm_mix_post, norm_ffn_pre,
              norm_ffn_post, w_in, diff_lambda, diff_subln, gla_gate_up, gla_gate_bias, gla_norm,
              mla_q_norm, mla_w_q_up, mla_kv_norm, mla_w_kv_up, swa_sinks, w_branch, w_out,
              ffn_w_up, ffn_conv_w, ffn_conv_b, ffn_w_down):
    bsz = x.shape[0]
    meta = jnp.broadcast_to(meta_tokens.astype(x.dtype)[None], (bsz, N_META, D_MODEL))
    h = jnp.concatenate([meta, x], axis=1)
    T = h.shape[1]
    offsets = np.cumsum(IN_WIDTHS)[:-1].tolist()
    for l in range(DEPTH):
        lam_init = 0.8 - 0.6 * math.exp(-0.3 * l)
        u = _rms_norm(h, norm_mix_pre[l])
        (a_q, a_k, a_v, b_q, b_k, b_v, b_r, b_g, c_qa, c_kva, c_kr,
         d_q, d_k, d_v, gate_logits) = jnp.split(u @ w_in[l], offsets, axis=-1)
        o_a = _diff_attention(a_q.reshape(bsz, T, A_HEADS, 2, A_HEAD_DIM),
                              a_k.reshape(bsz, T, A_HEADS, 2, A_HEAD_DIM),
                              a_v.reshape(bsz, T, A_HEADS, A_V_DIM),
                              diff_lambda[l], diff_subln[l], rel_bias_table, lam_init)
        o_b = _gla(b_q.reshape(bsz, T, B_HEADS, B_KEY_DIM), b_k.reshape(bsz, T, B_HEADS, B_KEY_DIM),
                   b_v.reshape(bsz, T, B_HEADS, B_VAL_DIM), b_g.reshape(bsz, T, 2, B_GATE_RANK),
                   gla_gate_up[l], gla_gate_bias[l], b_r, gla_norm[l])
        o_c = _mla(c_qa, c_kva, c_kr, mla_q_norm[l], mla_w_q_up[l], mla_kv_norm[l], mla_w_kv_up[l])
        o_d = _window_gqa(d_q.reshape(bsz, T, D_Q_HEADS, D_HEAD_DIM),
                          d_k.reshape(bsz, T, D_KV_HEADS, D_HEAD_DIM),
                          d_v.reshape(bsz, T, D_KV_HEADS, D_HEAD_DIM), swa_sinks[l], rel_bias_table)
        branches = jnp.stack([o_a, o_b, o_c, o_d], axis=2)
        gates = jax.nn.sigmoid(gate_logits.reshape(bsz, T, N_BRANCH, D_MODEL))
        merged = jnp.sum(jnp.einsum('btnw,nwd->btnd', branches, w_branch[l]) * gates, axis=2)
        h = h + _rms_norm(merged @ w_out[l], norm_mix_post[l])
        f = _conv_ffn(_rms_norm(h, norm_ffn_pre[l]), ffn_w_up[l], ffn_conv_w[l], ffn_conv_b[l], ffn_w_down[l])
        h = h + _rms_norm(f, norm_ffn_post[l])
    return h[:, N_META:]
```

```python
import math
import os
import numpy as np
from contextlib import ExitStack
import concourse.bass as bass
import concourse.mybir as mybir
from concourse.bass_utils import run_bass_kernel_spmd

F32 = mybir.dt.float32
BF16 = mybir.dt.bfloat16
AF = mybir.ActivationFunctionType
ALU = mybir.AluOpType

DEPTH = 2
D = 1024
SEQ = 2048
NMETA = 16
T = SEQ + NMETA
TB = 344
NB = 6
NT = 17
EPS = 1e-6
DFF = 2816
NIN = 8416
OA_Q, OA_K, OA_V = 0, 512, 1024
OB_Q, OB_K, OB_V, OB_R, OB_G = 1536, 1792, 2048, 2560, 3072
OC_QA, OC_KVA, OC_KR = 3104, 3360, 3488
OD_Q, OD_K, OD_V = 3552, 4064, 4192
O_GATE = 4320
NEG = -30000.0


def trows(j):
    return 128 if j < 16 else 16


class Dep:
    __slots__ = ("w", "r")

    def __init__(self):
        self.w = None
        self.r = []


class Op:
    __slots__ = ("eng", "fn", "deps", "ms", "val", "sem", "is_dma")

    def __init__(self, eng, fn, is_dma):
        self.eng = eng
        self.fn = fn
        self.deps = []
        self.ms = False
        self.val = 0
        self.sem = None
        self.is_dma = is_dma


ENGS = ("pe", "act", "dve", "pool", "sp")
NDMASEM = 8


class Prog:
    def __init__(self, nc, es):
        self.nc = nc
        self.es = es
        self.ops = {e: [] for e in ENGS}
        self.dma_hist = {e: [] for e in ENGS}
        self.dd = {}

    def D(self, key):
        d = self.dd.get(key)
        if d is None:
            d = self.dd[key] = Dep()
        return d

    def op(self, eng, fn, r=(), w=(), dma=False, extra=()):
        o = Op(eng, fn, dma)
        need = list(extra)
        for k in r:
            d = self.D(k)
            if d.w is not None:
                need.append(d.w)
        for k in w:
            d = self.D(k)
            if d.w is not None:
                need.append(d.w)
            for q in d.r:
                need.append(q)
        if dma:
            h = self.dma_hist[eng]
            if len(h) >= NDMASEM:
                need.append(h[-NDMASEM])
            h.append(o)
        seen = set()
        for p in need:
            if p is o or id(p) in seen:
                continue
            seen.add(id(p))
            if (not dma) and eng == "pe" and p.eng == "pe" and not p.is_dma:
                continue
            o.deps.append(p)
        for k in r:
            self.D(k).r.append(o)
        for k in w:
            d = self.D(k)
            d.w = o
            d.r = []
        self.ops[eng].append(o)
        return o

    def barrier(self):
        lasts = []
        for e in ENGS:
            cl = [o for o in self.ops[e] if not o.is_dma]
            if cl:
                lasts.append(cl[-1])
            lasts.extend(self.dma_hist[e][-NDMASEM:])
        for e in ENGS:
            if self.ops[e]:
                self.op(e, lambda eng: eng.nop(), extra=lasts)

    def finalize(self, final_ops=()):
        nc, es = self.nc, self.es
        for e in ENGS:
            for o in self.ops[e]:
                for p in o.deps:
                    p.ms = True
        esem = {e: es.enter_context(nc.semaphore("s_" + e)) for e in ENGS}
        dsem = {e: [es.enter_context(nc.semaphore("d_%s%d" % (e, i))) for i in range(NDMASEM)]
                for e in ENGS if self.dma_hist[e]}
        for e in ENGS:
            cnt = 0
            dcnt = [0] * NDMASEM
            k = 0
            for o in self.ops[e]:
                if o.is_dma:
                    s = k % NDMASEM
                    k += 1
                    dcnt[s] += 16
                    o.sem = dsem[e][s]
                    o.val = dcnt[s]
                elif o.ms:
                    cnt += 1
                    o.sem = esem[e]
                    o.val = cnt
        engobj = {"pe": "tensor", "act": "scalar", "dve": "vector", "pool": "gpsimd", "sp": "sync"}
        block = es.enter_context(nc.Block())

        def emit(e):
            def body(eng):
                known = {}
                for o in self.ops[e]:
                    wl = {}
                    for p in o.deps:
                        key = id(p.sem)
                        if known.get(key, 0) >= p.val:
                            continue
                        if key not in wl or wl[key][1] < p.val:
                            wl[key] = (p.sem, p.val)
                    for key, (s, v) in wl.items():
                        eng.wait_ge(s, v)
                        known[key] = v
                    ins = o.fn(eng)
                    if o.is_dma:
                        ins.then_inc(o.sem, 16)
                    elif o.ms:
                        ins.then_inc(o.sem, 1)
                if e == "sp":
                    for o in final_ops:
                        eng.wait_ge(o.sem, o.val)
            return body

        for e in ENGS:
            if self.ops[e] or e == "sp":
                getattr(block, engobj[e])(emit(e))


def rel_bucket(rel):
    rel = np.asarray(rel, dtype=np.int64)
    half, max_exact = 16, 8
    ret = np.where(rel > 0, half, 0)
    n = np.abs(rel)
    nf = np.maximum(n, 1).astype(np.float32)
    large = max_exact + (np.log(nf / np.float32(max_exact)) / np.float32(math.log(128 / max_exact))
                         * (half - max_exact)).astype(np.int32)
    large = np.minimum(large, half - 1)
    return ret + np.where(n < max_exact, n, large)


def rel_bucket_jax(rel):
    import jax
    import jax.numpy as jnp
    with jax.default_device(jax.devices("cpu")[0]):
        rel = jnp.asarray(np.asarray(rel, dtype=np.int32))
        half, max_exact = 16, 8
        ret = jnp.where(rel > 0, half, 0)
        n = jnp.abs(rel)
        nf = jnp.maximum(n, 1).astype(jnp.float32)
        large = max_exact + (jnp.log(nf / max_exact) / math.log(128 / max_exact) * (half - max_exact)).astype(jnp.int32)
        large = jnp.minimum(large, half - 1)
        return np.asarray(ret + jnp.where(n < max_exact, n, large))


A_OS = sorted(set(TB * b - 128 * j for b in range(NB) for j in range(NT)))
A_NEAR = [o for o in A_OS if not (127 - o <= -91 or -o - (TB - 1) >= 91)]
A_C = -min(A_NEAR)
A_W = TB + max(A_NEAR) + A_C
D_QB = [(256 * i, 256) for i in range(8)] + [(2048, 16)]
D_C = 256
D_W = 256 + 384
D_SLABW = D_W + 256 + 256


class KB:
    def __init__(self, nc, dbg=None):
        self.nc = nc
        self.dbg = dbg or {}

    def mm(self, out, lhsT, rhs, start, stop, r, w):
        return self.P.op("pe", lambda e: e.matmul(out, lhsT=lhsT, rhs=rhs, start=start, stop=stop), r, w)

    def act(self, out, in_, func, r, w, bias=None, scale=1.0, accum=None):
        def f(e):
            kw = {}
            if bias is not None:
                kw["bias"] = bias
            if accum is not None:
                kw["accum_out"] = accum
            return e.activation(out=out, in_=in_, func=func, scale=scale, **kw)
        return self.P.op("act", f, r, w)

    def stt(self, eng, out, in0, scalar, in1, op0, op1, r, w):
        nm = {"dve": "vector", "pool": "gpsimd"}[eng]
        return self.P.op(eng, lambda e: e.scalar_tensor_tensor(out=out, in0=in0, scalar=scalar, in1=in1, op0=op0, op1=op1), r, w)

    def ts(self, eng, out, in0, s1, s2, op0, op1, r, w):
        if s2 is None:
            return self.P.op(eng, lambda e: e.tensor_scalar(out=out, in0=in0, scalar1=s1, scalar2=None, op0=op0), r, w)
        return self.P.op(eng, lambda e: e.tensor_scalar(out=out, in0=in0, scalar1=s1, scalar2=s2, op0=op0, op1=op1), r, w)

    def tt(self, eng, out, in0, in1, op, r, w):
        return self.P.op(eng, lambda e: e.tensor_tensor(out=out, in0=in0, in1=in1, op=op), r, w)

    def cp(self, eng, out, in_, r, w):
        if eng == "act":
            return self.P.op("act", lambda e: e.copy(out=out, in_=in_), r, w)
        return self.P.op(eng, lambda e: e.tensor_copy(out=out, in_=in_), r, w)

    def recip(self, out, in_, r, w):
        return self.P.op("dve", lambda e: e.reciprocal(out=out, in_=in_), r, w)

    def memset(self, eng, ap, val, w):
        return self.P.op(eng, lambda e: e.memset(ap, val), (), w)

    def dma(self, q, out, in_, r, w):
        return self.P.op(q, lambda e: e.dma_start(out=out, in_=in_), r, w, dma=True)

    def sb(self, es, name, shape, dt):
        self.sbcnt = getattr(self, "sbcnt", 0) + 1
        return es.enter_context(self.nc.sbuf_tensor("sb%d_%s" % (self.sbcnt, name), shape, dt))

    def U(self, kc, t0, t1):
        return self.uT[:, kc, t0 + 1:t1 + 1]

    def psb(self, group):
        lst = self.psgroups[group]
        i = self.psidx.get(group, 0)
        self.psidx[group] = i + 1
        b = lst[i % len(lst)]
        return self.ps[b], ("ps", b)

    def rstd_from(self, srcs, n, Dn, rkeys, out_ap, out_key, sq_ap, sq_key, sq_eng="act"):
        pst, pk = self.ps[7], ("ps", 7)
        for i, s in enumerate(srcs):
            self.act(sq_ap[:, i % 2, :n], s, AF.Square, rkeys, [(sq_key, i % 2)])
            self.mm(pst[:, :n], self.onesf[:], sq_ap[:, i % 2, :n], i == 0, i == len(srcs) - 1, [(sq_key, i % 2), "onesf"], [pk])
        self.ts("dve", out_ap, pst[:, :n], 1.0 / Dn, EPS, ALU.mult, ALU.add, [pk], [out_key])
        self.recip(out_ap, out_ap, [out_key], [out_key])
        self.act(out_ap, out_ap, AF.Sqrt, [out_key], [out_key])

    def loadw(self, dst, src, wkey):
        return self.dma("pool", dst, src, (), [wkey])

    def build(self, layers=(0, 1)):
        nc = self.nc
        I = {}

        def din(name, shape):
            I[name] = nc.dram_tensor(name, list(shape), F32, kind="ExternalInput").ap()
            return I[name]

        din("h0T", [D, T])
        din("w_in", [DEPTH, D, NIN])
        din("w_branch", [DEPTH, 4, 512, D])
        din("w_out", [DEPTH, D, D])
        din("ffn_w_up", [DEPTH, D, 2 * DFF])
        din("ffn_w_down", [DEPTH, DFF, D])
        din("mla_w_q_up", [DEPTH, 256, 768])
        din("mla_w_q_up_sw", [DEPTH, 256, 256])
        din("mla_w_kv_up", [DEPTH, 128, 1024])
        din("w_in_kr_sw", [DEPTH, D, 64])
        din("gla_gate_up", [DEPTH, 2, 16, 256])
        din("gains", [128, DEPTH * 4 * 8])
        din("convw", [128, DEPTH * 4 * 44])
        din("smallc", [128, DEPTH * 8])
        din("lamrep", [128, DEPTH * 256])
        din("gbias", [128, DEPTH * 512])
        din("sinkrep", [128, DEPTH * 8])
        din("aconst", [128, 16])
        din("dconst", [128, 8])
        din("slabA", [8, 128, A_W])
        din("slabD", [8, 128, D_SLABW])
        din("rope", [64, 2 * T])
        din("glam", [128, 4 * 128])
        outT = nc.dram_tensor("outT", [D, T], F32, kind="ExternalOutput").ap()
        hT = nc.dram_tensor("hT_scr", [D, T], F32, kind="Internal").ap()
        dbg_out = {}
        for k, shp in self.dbg.items():
            dbg_out[k] = nc.dram_tensor("dbg_" + k, list(shp), F32, kind="ExternalOutput").ap()
        self.dbg_out = dbg_out
        self.I = I

        with ExitStack() as es:
            P = self.P = Prog(nc, es)
            self.ps = [es.enter_context(nc.psum_tensor("ps%d" % i, [128, 512], F32)) for i in range(8)]
            self.psgroups = {"s": [0, 1, 2], "o": [3, 4], "d": [5, 6], "x": [0, 1, 2, 3, 4, 5, 6]}
            self.psidx = {}
            self.uT = self.sb(es, "uT", [128, 8, T + 2], BF16)
            self.onesf = self.sb(es, "onesf", [128, 128], F32)
            self.onesb = self.sb(es, "onesb", [128, 128], BF16)
            self.gains = self.sb(es, "gains", [128, DEPTH * 32], F32)
            self.convw = self.sb(es, "convw", [128, DEPTH * 4 * 44], F32)
            self.smallc = self.sb(es, "smallc", [128, DEPTH * 8], F32)
            self.aconst = self.sb(es, "aconst", [128, 16], F32)
            self.dconst = self.sb(es, "dconst", [128, 8], F32)
            self.memset("dve", self.onesf[:], 1.0, ["onesf"])
            self.memset("dve", self.onesb[:], 1.0, ["onesb"])
            self.memset("dve", self.uT[:, :, 0:1], 0.0, ["uT"])
            self.memset("dve", self.uT[:, :, T + 1:T + 2], 0.0, ["uT"])
            self.dma("sp", self.gains[:], I["gains"], (), ["gains"])
            self.dma("sp", self.convw[:], I["convw"], (), ["convw"])
            self.dma("sp", self.smallc[:], I["smallc"], (), ["smallc"])
            self.dma("sp", self.aconst[:], I["aconst"], (), ["aconst"])
            self.dma("sp", self.dconst[:], I["dconst"], (), ["dconst"])

            finals = []
            nl = len(layers)
            for li, l in enumerate(layers):
                hsrc = I["h0T"] if li == 0 else hT
                last = (li == nl - 1)
                self.norm1(l, hsrc)
                with ExitStack() as les:
                    self.oall = self.sb(les, "oall", [128, 16, T], BF16)
                    self.mixers(l)
                    if not os.environ.get("ONLYMIX"):
                        self.merge_out(l, hsrc, hT, les)
                P.barrier()
                if not os.environ.get("ONLYMIX"):
                    finals += self.ffn(l, hT, outT if last else hT)
                P.barrier()
            P.finalize(finals)
        return nc

    def gcol(self, l, which, kc):
        i = (l * 4 + which) * 8 + kc
        return self.gains[:, i:i + 1]

    def norm_block(self, hb, hkey, b, gl, gw, tag):
        sq, rs = self.nsq, self.nrs
        self.rstd_from([hb[:, kc, :] for kc in range(8)], TB, D, [hkey], rs[:, :], "nrs", sq, "nsq")
        for kc in range(8):
            self.stt("dve", self.U(kc, b * TB, (b + 1) * TB), hb[:, kc, :], self.gcol(gl, gw, kc), rs[:, :],
                     ALU.mult, ALU.mult, [hkey, "nrs", "gains"], [("uT", kc)])

    def norm1(self, l, hsrc):
        with ExitStack() as es:
            hb2 = [self.sb(es, "n1h%d" % i, [128, 8, TB], F32) for i in range(2)]
            self.nsq = self.sb(es, "n1sq", [128, 2, TB], F32)
            self.nrs = self.sb(es, "n1rs", [128, TB], F32)
            hv = hsrc.rearrange("(c p) t -> p c t", p=128)
            for b in range(NB):
                hb = hb2[b % 2]
                hk = ("n1h", b % 2)
                self.dma("sp", hb[:], hv[:, :, b * TB:(b + 1) * TB], (), [hk])
                self.norm_block(hb, hk, b, l, 0, "n1")
            self.P.barrier()

    def mixers(self, l):
        import os
        sel = os.environ.get("MIX", "cadb")
        for nm, fn, c0 in (("c", self.mix_c, 8), ("a", self.mix_a, 0), ("d", self.mix_d, 12), ("b", self.mix_b, 4)):
            if nm in sel:
                fn(l)
            else:
                self.memset("dve", self.oall[:, c0:c0 + 4, :], 0.0, ["oall"])
            self.P.barrier()
        if "oall" in self.dbg_out and l == 0:
            self.dbgdump_oall()

    def dbgdump_oall(self):
        with ExitStack() as es:
            tmp = self.sb(es, "dbgtmp", [128, T], F32)
            for c in range(16):
                self.cp("dve", tmp[:], self.oall[:, c, :], ["oall"], ["dbgtmp"])
                self.dma("sp", self.dbg_out["oall"][c * 128:(c + 1) * 128, :], tmp[:], ["dbgtmp"], ())
            self.P.barrier()

    def proj_fm(self, w, wkey, ncontr, col0, M, rhs_fn, rkeys, evac):
        for b in range(NB):
            pst, pk = self.psb("x")
            for kc in range(ncontr):
                self.mm(pst[:M, :TB], w[:, kc, col0:col0 + M], rhs_fn(kc, b), kc == 0, kc == ncontr - 1,
                        [wkey] + rkeys, [pk])
            evac(b, pst[:M, :TB], pk)

    def proj_tm(self, w, wkey, ncontr, col0, N, lhs_fn, rkeys, tiles, evac):
        for j, (t0, n) in enumerate(tiles):
            pst, pk = self.psb("x")
            for kc in range(ncontr):
                self.mm(pst[:n, :N], lhs_fn(kc, t0, n), w[:, kc, col0:col0 + N], kc == 0, kc == ncontr - 1,
                        [wkey] + rkeys, [pk])
            evac(j, n, pst[:n, :N], pk)

    def attn_inner(self, tag, b_q0, b_qn, tiles, ptbuf, o_M, v_fn, ones_fn, rkeys, scale):
        ops_, ok = self.psb("o")
        dps, dk = self.psb("d")
        n = len(tiles)
        for i, tl in enumerate(tiles):
            kr = tl["rows"]
            pst, pk = self.psb("s")
            nm = len(tl["mms"])
            for mi, (lt, rh) in enumerate(tl["mms"]):
                self.mm(pst[:kr, :b_qn], lt, rh, mi == 0, mi == nm - 1, rkeys, [pk])
            pi = self.ptidx
            self.ptidx += 1
            pt = ptbuf[pi % len(ptbuf)]
            ptk = (tag + "pt", pi % len(ptbuf))
            bias = tl["bias"]
            if bias is None:
                self.act(pt[:kr, :b_qn], pst[:kr, :b_qn], AF.Exp, [pk], [ptk], scale=scale)
            elif bias[0] == "c":
                self.act(pt[:kr, :b_qn], pst[:kr, :b_qn], AF.Exp, [pk] + bias[2], [ptk], bias=bias[1][:kr, :], scale=scale)
            else:
                tmp = self.sbias[pi % 2]
                tk = (tag + "sb", pi % 2)
                self.stt("dve", tmp[:kr, :b_qn], pst[:kr, :b_qn], scale, bias[1], ALU.mult, ALU.add, [pk] + bias[2], [tk])
                self.act(pt[:kr, :b_qn], tmp[:kr, :b_qn], AF.Exp, [tk], [ptk])
            vl, vkeys = v_fn(tl)
            self.mm(ops_[:o_M, :b_qn], vl, pt[:kr, :b_qn], i == 0, i == n - 1, [ptk] + vkeys, [ok])
            ol = ones_fn(tl)
            self.mm(dps[:o_M, :b_qn], ol, pt[:kr, :b_qn], i == 0, i == n - 1, [ptk, "onesb"], [dk])
        return ops_, ok, dps, dk

    def mix_c(self, l):
        I = self.I
        with ExitStack() as es:
            wc = self.sb(es, "c_w", [128, 8, 512], BF16)
            wq = self.sb(es, "c_wq", [128, 2, 768 + 256], BF16)
            wkv = self.sb(es, "c_wkv", [128, 1, 1024], BF16)
            wvv = self.sb(es, "c_wvv", [128, 1, 512], BF16)
            lat = self.sb(es, "c_lat", [128, 3, TB], F32)
            qn = self.sb(es, "c_qn", [128, 2, T], BF16)
            kvn = self.sb(es, "c_kvn", [128, T], BF16)
            kpe = self.sb(es, "c_kpe", [64, T], BF16)
            rope = self.sb(es, "c_rope", [64, 2 * T], F32)
            vtok = self.sb(es, "c_v", [128, NT, 512], BF16)
            qno = self.sb(es, "c_qno", [128, T], BF16)
            qpe = self.sb(es, "c_qpe", [64, T], BF16)
            kno = self.sb(es, "c_kno", [128, T], BF16)
            ptbuf = [self.sb(es, "c_pt%d" % i, [128, TB], BF16) for i in range(3)]
            self.nsq = self.sb(es, "c_sq", [128, 2, TB], F32)
            self.nrs = self.sb(es, "c_rs", [128, TB], F32)
            t1 = self.sb(es, "c_t1", [64, TB], F32)
            t2 = self.sb(es, "c_t2", [64, TB], F32)
            rd = self.sb(es, "c_rd", [128, TB], F32)
            win = I["w_in"][l].rearrange("(kc p) n -> p kc n", p=128)
            self.loadw(wc[:, :, 0:448], win[:, :, OC_QA:OC_QA + 448], "c_w")
            self.loadw(wc[:, :, 448:512], I["w_in_kr_sw"][l].rearrange("(kc p) n -> p kc n", p=128), "c_w")
            self.loadw(wq[:, :, 0:768], I["mla_w_q_up"][l].rearrange("(kc p) n -> p kc n", p=128), "c_wq")
            self.loadw(wq[:, :, 768:1024], I["mla_w_q_up_sw"][l].rearrange("(kc p) n -> p kc n", p=128), "c_wq")
            self.loadw(wkv[:, 0, :], I["mla_w_kv_up"][l], "c_wkv")
            self.loadw(wvv[:, 0, :].rearrange("p (h e) -> p h e", h=4),
                       I["mla_w_kv_up"][l].rearrange("p (h e) -> p h e", h=4)[:, :, 128:256], "c_wvv")
            self.dma("sp", rope[:], I["rope"], (), ["c_rope"])
            ukeys = [("uT", kc) for kc in range(8)]
            urhs = lambda kc, b: self.U(kc, b * TB, (b + 1) * TB)
            for b in range(NB):
                sl = slice(b * TB, (b + 1) * TB)
                pst, pk = self.psb("x")
                pst2, pk2 = self.psb("x")
                for kc in range(8):
                    self.mm(pst[:64, :TB], wc[:, kc, 384:448], urhs(kc, b), kc == 0, kc == 7, ["c_w"] + ukeys, [pk])
                for kc in range(8):
                    self.mm(pst2[:64, :TB], wc[:, kc, 448:512], urhs(kc, b), kc == 0, kc == 7, ["c_w"] + ukeys, [pk2])
                self.tt("dve", t1[:, :], pst[:64, :TB], rope[:, sl], ALU.mult, [pk, "c_rope"], ["c_t1"])
                self.tt("dve", t2[:, :], pst2[:64, :TB], rope[:, T + b * TB:T + (b + 1) * TB], ALU.mult, [pk2, "c_rope"], ["c_t2"])
                self.tt("dve", kpe[:, sl], t1[:, :], t2[:, :], ALU.add, ["c_t1", "c_t2"], ["c_kpe"])
            sc = self.smallc
            for b in range(NB):
                sl = slice(b * TB, (b + 1) * TB)
                for ci in range(3):
                    pst, pk = self.psb("x")
                    for kc in range(8):
                        self.mm(pst[:, :TB], wc[:, kc, ci * 128:(ci + 1) * 128], urhs(kc, b), kc == 0, kc == 7, ["c_w"] + ukeys, [pk])
                    self.cp("act", lat[:, ci, :], pst[:, :TB], [pk], [("c_lat", ci)])
                self.rstd_from([lat[:, 0, :], lat[:, 1, :]], TB, 256, [("c_lat", 0), ("c_lat", 1)], self.nrs[:, :], "nrs", self.nsq, "nsq")
                for ci in range(2):
                    self.stt("dve", qn[:, ci, sl], lat[:, ci, :], sc[:, l * 8 + 3 + ci:l * 8 + 4 + ci], self.nrs[:, :],
                             ALU.mult, ALU.mult, [("c_lat", ci), "nrs", "smallc"], ["c_qn"])
                self.rstd_from([lat[:, 2, :]], TB, 128, [("c_lat", 2)], self.nrs[:, :], "nrs", self.nsq, "nsq")
                self.stt("dve", kvn[:, sl], lat[:, 2, :], sc[:, l * 8 + 2:l * 8 + 3], self.nrs[:, :],
                         ALU.mult, ALU.mult, [("c_lat", 2), "nrs", "smallc"], ["c_kvn"])
            tiles = [(128 * j, trows(j)) for j in range(NT)]
            self.proj_tm(wvv, "c_wvv", 1, 0, 512, lambda kc, t0, n: kvn[:, t0:t0 + n], ["c_kvn"], tiles,
                         lambda j, n, ps, pk: self.cp("act", vtok[:n, j, :], ps, [pk], ["c_v"]))
            scale = (128 + 64) ** -0.5
            self.ptidx = 0
            for h in range(4):
                qrhs = lambda kc, b: qn[:, kc, b * TB:(b + 1) * TB]
                self.proj_fm(wq, "c_wq", 2, h * 192, 128, qrhs, ["c_qn"],
                             lambda b, ps, pk: self.cp("act", qno[:, b * TB:(b + 1) * TB], ps, [pk], ["c_qno"]))
                for b in range(NB):
                    sl = slice(b * TB, (b + 1) * TB)
                    pst, pk = self.psb("x")
                    pst2, pk2 = self.psb("x")
                    for kc in range(2):
                        self.mm(pst[:64, :TB], wq[:, kc, h * 192 + 128:h * 192 + 192], qrhs(kc, b), kc == 0, kc == 1, ["c_wq", "c_qn"], [pk])
                    for kc in range(2):
                        self.mm(pst2[:64, :TB], wq[:, kc, 768 + h * 64:768 + h * 64 + 64], qrhs(kc, b), kc == 0, kc == 1, ["c_wq", "c_qn"], [pk2])
                    self.tt("dve", t1[:, :], pst[:64, :TB], rope[:, sl], ALU.mult, [pk, "c_rope"], ["c_t1"])
                    self.tt("dve", t2[:, :], pst2[:64, :TB], rope[:, T + b * TB:T + (b + 1) * TB], ALU.mult, [pk2, "c_rope"], ["c_t2"])
                    self.tt("dve", qpe[:, sl], t1[:, :], t2[:, :], ALU.add, ["c_t1", "c_t2"], ["c_qpe"])
                self.proj_fm(wkv, "c_wkv", 1, h * 256, 128, lambda kc, b: kvn[:, b * TB:(b + 1) * TB], ["c_kvn"],
                             lambda b, ps, pk: self.cp("act", kno[:, b * TB:(b + 1) * TB], ps, [pk], ["c_kno"]))
                for b in range(NB):
                    q0 = b * TB
                    tl = []
                    for j in range(NT):
                        kr = trows(j)
                        tl.append(dict(rows=kr, j=j, bias=None,
                                       mms=[(kno[:, 128 * j:128 * j + kr], qno[:, q0:q0 + TB]),
                                            (kpe[:, 128 * j:128 * j + kr], qpe[:, q0:q0 + TB])]))
                    ops_, ok, dps, dk = self.attn_inner(
                        "c", q0, TB, tl, ptbuf, 128,
                        lambda t, h=h: (vtok[:t["rows"], t["j"], h * 128:(h + 1) * 128], ["c_v"]),
                        lambda t: self.onesb[:t["rows"], :], ["c_kno", "c_qno", "c_kpe", "c_qpe"], scale)
                    self.recip(rd[:, :], dps[:, :TB], [dk], ["c_rd"])
                    self.tt("dve", self.oall[:, 8 + h, q0:q0 + TB], ops_[:, :TB], rd[:, :], ALU.mult, [ok, "c_rd"], ["oall"])

    def mix_a(self, l):
        I = self.I
        lam_init = 0.8 - 0.6 * math.exp(-0.3 * l)
        with ExitStack() as es:
            wqk = self.sb(es, "a_wqk", [128, 8, 256], BF16)
            wv = self.sb(es, "a_wv", [128, 8, 512], BF16)
            vtok = self.sb(es, "a_v", [128, NT, 512], BF16)
            qT = self.sb(es, "a_q", [128, T], BF16)
            kT = self.sb(es, "a_k", [128, T], BF16)
            slab = [self.sb(es, "a_slab%d" % i, [128, 2, A_W], F32) for i in range(2)]
            ptbuf = [self.sb(es, "a_pt%d" % i, [128, TB], BF16) for i in range(3)]
            self.sbias = [self.sb(es, "a_sb%d" % i, [128, TB], F32) for i in range(2)]
            self.nsq = self.sb(es, "a_sq", [128, 2, TB], F32)
            self.nrs = self.sb(es, "a_rs", [128, TB], F32)
            rd = self.sb(es, "a_rd", [128, TB], F32)
            on = [self.sb(es, "a_on%d" % i, [128, TB], F32) for i in range(2)]
            lam = self.sb(es, "a_lam", [128, 256], F32)
            lt = self.sb(es, "a_lt", [128, 128], F32)
            lv = self.sb(es, "a_lv", [128, 4], F32)
            gsub = self.sb(es, "a_gs", [128, 1], F32)
            win = I["w_in"][l].rearrange("(kc p) n -> p kc n", p=128)
            self.loadw(wv[:], win[:, :, OA_V:OA_V + 512], "a_wv")
            self.dma("sp", lam[:], I["lamrep"][:, l * 256:(l + 1) * 256], (), ["a_lam"])
            self.tt("dve", lt[:, 0:64], lam[:, 0:64], lam[:, 64:128], ALU.mult, ["a_lam"], ["a_lt"])
            self.tt("dve", lt[:, 64:128], lam[:, 128:192], lam[:, 192:256], ALU.mult, ["a_lam"], ["a_lt"])
            self.P.op("dve", lambda e: e.reduce_sum(out=lv[:, 0:1], in_=lt[:, 0:64], axis=mybir.AxisListType.X), ["a_lt"], ["a_lv"])
            self.P.op("dve", lambda e: e.reduce_sum(out=lv[:, 1:2], in_=lt[:, 64:128], axis=mybir.AxisListType.X), ["a_lt"], ["a_lv"])
            self.act(lv[:, 0:2], lv[:, 0:2], AF.Exp, ["a_lv"], ["a_lv"])
            self.tt("dve", lv[:, 2:3], lv[:, 1:2], lv[:, 0:1], ALU.subtract, ["a_lv"], ["a_lv"])
            self.ts("dve", lv[:, 3:4], lv[:, 2:3], -lam_init, None, ALU.add, None, ["a_lv"], ["a_lv"])
            self.ts("dve", gsub[:, :], self.smallc[:, l * 8:l * 8 + 1], 1.0 - lam_init, None, ALU.mult, None, ["smallc"], ["a_gs"])
            ukeys = [("uT", kc) for kc in range(8)]
            urhs = lambda kc, b: self.U(kc, b * TB, (b + 1) * TB)
            tiles = [(128 * j, trows(j)) for j in range(NT)]
            self.proj_tm(wv, "a_wv", 8, 0, 512, lambda kc, t0, n: self.U(kc, t0, t0 + n), ukeys, tiles,
                         lambda j, n, ps, pk: self.cp("act", vtok[:n, j, :], ps, [pk], ["a_v"]))
            self.ptidx = 0
            for h in range(4):
                self.loadw(wqk[:, :, 0:128], win[:, :, OA_Q + h * 128:OA_Q + (h + 1) * 128], "a_wqk")
                self.loadw(wqk[:, :, 128:256], win[:, :, OA_K + h * 128:OA_K + (h + 1) * 128], "a_wqk")
                sl_ = slab[h % 2]
                slk = ("a_slab", h % 2)
                for m in range(2):
                    self.dma("sp", sl_[:, m, :], I["slabA"][m * 4 + h], (), [slk])
                self.proj_fm(wqk, "a_wqk", 8, 0, 128, urhs, ukeys,
                             lambda b, ps, pk: self.cp("act", qT[:, b * TB:(b + 1) * TB], ps, [pk], ["a_q"]))
                self.proj_fm(wqk, "a_wqk", 8, 128, 128, urhs, ukeys,
                             lambda b, ps, pk: self.cp("act", kT[:, b * TB:(b + 1) * TB], ps, [pk], ["a_k"]))
                for b in range(NB):
                    q0 = b * TB
                    for m in range(2):
                        mh = m * 4 + h
                        tl = []
                        for j in range(NT):
                            kr = trows(j)
                            o = TB * b - 128 * j
                            if 127 - o <= -91:
                                bias = ("c", self.aconst[:, mh:mh + 1], ["aconst"])
                            elif -o - (TB - 1) >= 91:
                                bias = ("c", self.aconst[:, 8 + mh:9 + mh], ["aconst"])
                            else:
                                bias = ("s", sl_[:kr, m, o + A_C:o + A_C + TB], [slk])
                            tl.append(dict(rows=kr, j=j, bias=bias,
                                           mms=[(kT[64 * m:64 * m + 64, 128 * j:128 * j + kr], qT[64 * m:64 * m + 64, q0:q0 + TB])]))
                        ops_, ok, dps, dk = self.attn_inner(
                            "a", q0, TB, tl, ptbuf, 128,
                            lambda t, h=h: (vtok[:t["rows"], t["j"], h * 128:(h + 1) * 128], ["a_v"]),
                            lambda t: self.onesb[:t["rows"], :], ["a_q", "a_k"], 0.125)
                        self.recip(rd[:, :], dps[:, :TB], [dk], ["a_rd"])
                        self.tt("dve", on[m][:, :], ops_[:, :TB], rd[:, :], ALU.mult, [ok, "a_rd"], [("a_on", m)])
                    self.stt("dve", on[0][:, :], on[1][:, :], lv[:, 3:4], on[0][:, :], ALU.mult, ALU.add,
                             [("a_on", 0), ("a_on", 1), "a_lv"], [("a_on", 0)])
                    self.rstd_from([on[0][:, :]], TB, 128, [("a_on", 0)], self.nrs[:, :], "nrs", self.nsq, "nsq")
                    self.stt("dve", self.oall[:, h, q0:q0 + TB], on[0][:, :], gsub[:, 0:1], self.nrs[:, :], ALU.mult, ALU.mult,
                             [("a_on", 0), "nrs", "a_gs"], ["oall"])

    def mix_d(self, l):
        I = self.I
        with ExitStack() as es:
            wq = self.sb(es, "d_wq", [128, 8, 512], BF16)
            wkk = self.sb(es, "d_wkk", [128, 8, 2, 128], BF16)
            wv = self.sb(es, "d_wv", [128, 8, 128], BF16)
            qT = self.sb(es, "d_q", [128, 4, T], BF16)
            kT2 = self.sb(es, "d_k", [128, 2, T], BF16)
            vpad = self.sb(es, "d_v", [128, NT * 4, 128], BF16)
            slab = [self.sb(es, "d_slab%d" % i, [128, D_SLABW], F32) for i in range(2)]
            ptbuf = [self.sb(es, "d_pt%d" % i, [128, 256], BF16) for i in range(3)]
            self.sbias = [self.sb(es, "d_sb%d" % i, [128, 256], F32) for i in range(2)]
            oh = self.sb(es, "d_oh", [128, 2, 128], BF16)
            es8 = self.sb(es, "d_es8", [128, 8], F32)
            es2 = self.sb(es, "d_es2", [128, 4], F32)
            rd = self.sb(es, "d_rd", [128, 256], F32)
            win = I["w_in"][l].rearrange("(kc p) n -> p kc n", p=128)
            self.loadw(wq[:], win[:, :, OD_Q:OD_Q + 512], "d_wq")
            for kv in range(2):
                for e in range(2):
                    self.loadw(wkk[:, :, kv, e * 64:(e + 1) * 64], win[:, :, OD_K + kv * 64:OD_K + (kv + 1) * 64], "d_wkk")
            self.loadw(wv[:], win[:, :, OD_V:OD_V + 128], "d_wv")
            self.memset("dve", vpad[:], 0.0, ["d_v"])
            self.memset("dve", oh[:], 0.0, ["d_oh"])
            self.memset("dve", oh[:, 0, 0:64], 1.0, ["d_oh"])
            self.memset("dve", oh[:, 1, 64:128], 1.0, ["d_oh"])
            self.dma("sp", es8[:], I["sinkrep"][:, l * 8:(l + 1) * 8], (), ["d_es8"])
            self.act(es8[:], es8[:], AF.Exp, ["d_es8"], ["d_es8"])
            for p in range(4):
                self.cp("dve", es2[0:64, p:p + 1], es8[0:64, 2 * p:2 * p + 1], ["d_es8"], ["d_es2"])
                self.cp("dve", es2[64:128, p:p + 1], es8[64:128, 2 * p + 1:2 * p + 2], ["d_es8"], ["d_es2"])
            ukeys = [("uT", kc) for kc in range(8)]
            urhs = lambda kc, b: self.U(kc, b * TB, (b + 1) * TB)
            for p in range(4):
                self.proj_fm(wq, "d_wq", 8, p * 128, 128, urhs, ukeys,
                             lambda b, ps, pk, p=p: self.cp("act", qT[:, p, b * TB:(b + 1) * TB], ps, [pk], ["d_q"]))
            for kv in range(2):
                self.proj_fm(wkk[:, :, kv, :], "d_wkk", 8, 0, 128, urhs, ukeys,
                             lambda b, ps, pk, kv=kv: self.cp("act", kT2[:, kv, b * TB:(b + 1) * TB], ps, [pk], ["d_k"]))
            tiles = [(128 * j, trows(j)) for j in range(NT)]

            def vev(j, n, ps, pk):
                for kv in range(2):
                    self.cp("act", vpad[:n, j * 4 + kv * 2, 0:64], ps[:, kv * 64:(kv + 1) * 64], [pk], ["d_v"])
                    self.cp("dve", vpad[:n, j * 4 + kv * 2 + 1, 64:128], ps[:, kv * 64:(kv + 1) * 64], [pk], ["d_v"])
            self.proj_tm(wv, "d_wv", 8, 0, 128, lambda kc, t0, n: self.U(kc, t0, t0 + n), ukeys, tiles, vev)
            self.ptidx = 0
            for p in range(4):
                kv = p // 2
                for e in range(2):
                    self.dma("sp", slab[e][:], I["slabD"][2 * p + e], (), [("d_slab", e)])
                for qb, (q0, qn) in enumerate(D_QB):
                    tl = []
                    for e in range(2):
                        h = 2 * p + e
                        pr = slice(64 * e, 64 * e + 64)
                        if qb == 0:
                            mb = ("s", slab[e][:16, D_W + 256:D_W + 256 + qn], [("d_slab", e)])
                        else:
                            mb = ("c", self.dconst[:, h:h + 1], ["dconst"])
                        tl.append(dict(rows=16, j=0, e=e, bias=mb, mms=[(kT2[pr, kv, 0:16], qT[pr, p, q0:q0 + qn])]))
                        for j in range(max(0, q0 // 128 - 1), min(16, (q0 + qn + 127) // 128) + 1):
                            kr = trows(j)
                            o = q0 - 128 * j
                            if j == 0:
                                bs = slab[e][:kr, D_W:D_W + qn]
                            else:
                                bs = slab[e][:kr, o + D_C:o + D_C + qn]
                            tl.append(dict(rows=kr, j=j, e=e, bias=("s", bs, [("d_slab", e)]),
                                           mms=[(kT2[pr, kv, 128 * j:128 * j + kr], qT[pr, p, q0:q0 + qn])]))
                    ops_, ok, dps, dk = self.attn_inner(
                        "d", q0, qn, tl, ptbuf, 128,
                        lambda t, kv=kv: (vpad[:t["rows"], t["j"] * 4 + kv * 2 + t["e"], :], ["d_v"]),
                        lambda t: oh[:t["rows"], t["e"], :], ["d_q", "d_k", "d_oh"], 0.125)
                    self.ts("dve", rd[:, :qn], dps[:, :qn], es2[:, p:p + 1], None, ALU.add, None, [dk, "d_es2"], ["d_rd"])
                    self.recip(rd[:, :qn], rd[:, :qn], ["d_rd"], ["d_rd"])
                    self.tt("dve", self.oall[:, 12 + p, q0:q0 + qn], ops_[:, :qn], rd[:, :qn], ALU.mult, [ok, "d_rd"], ["oall"])

    def mix_b(self, l):
        I = self.I
        with ExitStack() as es:
            wb = self.sb(es, "b_w", [128, 8, 1568], BF16)
            wgu = self.sb(es, "b_wgu", [16, 2, 256], F32)
            gb = self.sb(es, "b_gb", [128, 512], F32)
            msk = self.sb(es, "b_msk", [128, 4, 128], F32)
            obw = self.sb(es, "b_obw", [128, 4, T], F32)
            S = self.sb(es, "b_S", [64, 4, 128], F32)
            Sbf = self.sb(es, "b_Sbf", [64, 4, 128], BF16)
            self.nsq = self.sb(es, "b_sq", [128, 2, 64], F32)
            self.nrs = self.sb(es, "b_rs", [128, 64], F32)
            NBUF = 2
            bufs = {}

            def tb(name, shape, dt, i):
                k = (name, i % NBUF)
                if k not in bufs:
                    bufs[k] = self.sb(es, "b_%s%d" % (name, i % NBUF), shape, dt)
                return bufs[k], ("b_" + name, i % NBUF)

            win = I["w_in"][l].rearrange("(kc p) n -> p kc n", p=128)
            self.loadw(wb[:], win[:, :, OB_Q:OB_Q + 1568], "b_w")
            self.dma("sp", wgu[:], I["gla_gate_up"][l].rearrange("g r c -> r g c"), (), ["b_wgu"])
            self.dma("sp", gb[:], I["gbias"][:, l * 512:(l + 1) * 512], (), ["b_gb"])
            self.dma("sp", msk[:], I["glam"].rearrange("p (m t) -> p m t", m=4), (), ["b_msk"])
            ukeys = [("uT", kc) for kc in range(8)]
            gch = [(0, 16)] + [(16 + 64 * (c - 1), 64) for c in range(1, 33)]
            cnt = 0
            for dr in (1, 0):
                self.memset("dve", S[:], 0.0, ["b_S"])
                self.memset("dve", Sbf[:], 0.0, ["b_Sbf"])
                mi_c = 0 if dr == 0 else 1
                mi_r = 2 if dr == 0 else 3
                order = range(32, -1, -1) if dr == 1 else range(33)
                for ci in order:
                    t0, n = gch[ci]
                    cnt += 1
                    ut = lambda kc: self.U(kc, t0, t0 + n)
                    pqk, pqkk = self.psb("x")
                    for qi in range(8):
                        for kc in range(8):
                            self.mm(pqk[:64, qi * 64:qi * 64 + n], wb[:, kc, qi * 64:(qi + 1) * 64], ut(kc), kc == 0, kc == 7, ["b_w"] + ukeys, [pqkk])
                    pgl, pglk = self.psb("x")
                    for kc in range(8):
                        self.mm(pgl[:16, :n], wb[:, kc, 1536 + 16 * dr:1552 + 16 * dr], ut(kc), kc == 0, kc == 7, ["b_w"] + ukeys, [pglk])
                    glT, glk = tb("glT", [16, 64], F32, cnt)
                    self.cp("act", glT[:, :n], pgl[:16, :n], [pglk], [glk])
                    ppre, pprek = self.psb("x")
                    self.mm(ppre[:n, :256], glT[:, :n], wgu[:, dr, :], True, True, [glk, "b_wgu"], [pprek])
                    xla, xlk = tb("xla", [64, 256], F32, cnt)
                    self.tt("dve", xla[:n, :], ppre[:n, :256], gb[:n, dr * 256:(dr + 1) * 256], ALU.add, [pprek, "b_gb"], [xlk])
                    self.act(xla[:n, :], xla[:n, :], AF.Exp, [xlk], [xlk], scale=-1.0)
                    sp_, spk = tb("sp", [64, 256], F32, cnt)
                    self.act(sp_[:n, :], xla[:n, :], AF.Ln, [xlk], [spk], bias=1.0)
                    pc, pck = self.psb("x")
                    for h in range(4):
                        self.mm(pc[:64, h * 64:h * 64 + n], sp_[:n, h * 64:(h + 1) * 64], msk[:n, mi_c, :n], True, True, [spk, "b_msk"], [pck])
                    eb, ebk = tb("eb", [64, 4, 64], F32, cnt)
                    einv, eik = tb("einv", [64, 4, 64], F32, cnt)
                    pc3 = pc[:64, 0:256].rearrange("p (h t) -> p h t", h=4)[:, :, :n]
                    self.act(eb[:, :, :n], pc3, AF.Exp, [pck], [ebk], scale=-1.0 / 16)
                    self.act(einv[:, :, :n], pc3, AF.Exp, [pck], [eik], scale=1.0 / 16)
                    qd, qdk = tb("qd", [64, 4, 64], BF16, cnt)
                    ki, kik = tb("ki", [64, 4, 64], BF16, cnt)
                    q3 = pqk[:64, 0:256].rearrange("p (h t) -> p h t", h=4)[:, :, :n]
                    k3 = pqk[:64, 256:512].rearrange("p (h t) -> p h t", h=4)[:, :, :n]
                    self.stt("dve", qd[:, :, :n], q3, 0.125, eb[:, :, :n], ALU.mult, ALU.mult, [pqkk, ebk], [qdk])
                    self.tt("dve", ki[:, :, :n], k3, einv[:, :, :n], ALU.mult, [pqkk, eik], [kik])
                    pkt, pktk = self.psb("x")
                    for kc in range(8):
                        self.mm(pkt[:n, :256], ut(kc), wb[:, kc, 256:512], kc == 0, kc == 7, ["b_w"] + ukeys, [pktk])
                    pvt, pvtk = self.psb("x")
                    for kc in range(8):
                        self.mm(pvt[:n, :512], ut(kc), wb[:, kc, 512:1024], kc == 0, kc == 7, ["b_w"] + ukeys, [pvtk])
                    vt, vtk = tb("vt", [64, 512], BF16, cnt)
                    self.cp("act", vt[:n, :], pvt[:n, :512], [pvtk], [vtk])
                    pr_, prk = self.psb("x")
                    self.mm(pr_[:n, :256], msk[:n, mi_r, :n], sp_[:n, :], True, True, [spk, "b_msk"], [prk])
                    eo, eok = tb("eo", [64, 256], F32, cnt)
                    self.act(eo[:n, :], pr_[:n, :256], AF.Exp, [prk], [eok], scale=-1.0 / 16)
                    ko, kok = tb("ko", [64, 256], BF16, cnt)
                    self.tt("dve", ko[:n, :], pkt[:n, :256], eo[:n, :], ALU.mult, [pktk, eok], [kok])
                    pat, patk = self.psb("x")
                    for h in range(4):
                        self.mm(pat[:n, h * 64:h * 64 + n], ki[:, h, :n], qd[:, h, :n], True, True, [kik, qdk], [patk])
                    att, atk = tb("att", [64, 4, 64], BF16, cnt)
                    for h in range(4):
                        self.tt("dve", att[:n, h, :n], pat[:n, h * 64:h * 64 + n], msk[:n, mi_c, :n], ALU.mult, [patk, "b_msk"], [atk])
                    po, pok = self.psb("x")
                    for h in range(4):
                        self.mm(po[:, h * 64:h * 64 + n], vt[:n, h * 128:(h + 1) * 128], att[:n, h, :n], True, False, [vtk, atk], [pok])
                        self.mm(po[:, h * 64:h * 64 + n], Sbf[:, h, :], qd[:, h, :n], False, True, ["b_Sbf", qdk], [pok])
                    pds, pdsk = self.psb("x")
                    for h in range(4):
                        self.mm(pds[:64, h * 128:(h + 1) * 128], ko[:n, h * 64:(h + 1) * 64], vt[:n, h * 128:(h + 1) * 128], True, True, [kok, vtk], [pdsk])
                    dcol = (n - 1) if dr == 0 else 0
                    for h in range(4):
                        self.stt("dve", S[:, h, :], S[:, h, :], eb[:, h, dcol:dcol + 1], pds[:64, h * 128:(h + 1) * 128], ALU.mult, ALU.add, ["b_S", ebk, pdsk], ["b_S"])
                    self.cp("act", Sbf[:], S[:], ["b_S"], ["b_Sbf"])
                    if dr == 1:
                        for h in range(4):
                            self.cp("act", obw[:, h, t0:t0 + n], po[:, h * 64:h * 64 + n], [pok], ["b_obw"])
                    else:
                        prr, prrk = self.psb("x")
                        for h in range(4):
                            for kc in range(8):
                                self.mm(prr[:, h * 64:h * 64 + n], wb[:, kc, 1024 + h * 128:1024 + (h + 1) * 128], ut(kc), kc == 0, kc == 7, ["b_w"] + ukeys, [prrk])
                        sr, srk = tb("sr", [128, 4, 64], F32, cnt)
                        for h in range(4):
                            self.act(sr[:, h, :n], prr[:, h * 64:h * 64 + n], AF.Silu, [prrk], [srk])
                        of, ofk = tb("of", [128, 4, 64], F32, cnt)
                        for h in range(4):
                            self.tt("dve", of[:, h, :n], po[:, h * 64:h * 64 + n], obw[:, h, t0:t0 + n], ALU.add, [pok, "b_obw"], [ofk])
                            self.rstd_from([of[:, h, :n]], n, 128, [ofk], self.nrs[:, :n], "nrs", self.nsq, "nsq")
                            self.stt("dve", of[:, h, :n], of[:, h, :n], self.smallc[:, l * 8 + 1:l * 8 + 2], self.nrs[:, :n], ALU.mult, ALU.mult, [ofk, "nrs", "smallc"], [ofk])
                            self.tt("dve", self.oall[:, 4 + h, t0:t0 + n], of[:, h, :n], sr[:, h, :n], ALU.mult, [ofk, srk], ["oall"])

    def resid_block(self, y, ykeys, b, l, gw, hsrc, hdst, bufs, nextnorm=None, final=False):
        hb, hk = bufs
        hv_s = hsrc.rearrange("(c p) t -> p c t", p=128)
        hv_d = hdst.rearrange("(c p) t -> p c t", p=128)
        sl = slice(b * TB, (b + 1) * TB)
        self.dma("sp", hb[:], hv_s[:, :, sl], (), [hk])
        self.rstd_from([y[:, kc, :] for kc in range(8)], TB, D, ykeys, self.nrs[:, :], "nrs", self.nsq, "nsq")
        for kc in range(8):
            self.stt("dve", y[:, kc, :], y[:, kc, :], self.gcol(l, gw, kc), self.nrs[:, :], ALU.mult, ALU.mult,
                     ykeys + ["nrs", "gains"], ykeys)
        self.tt("dve", hb[:], hb[:], y[:], ALU.add, [hk] + ykeys, [hk])
        st = self.dma("sp", hv_d[:, :, sl], hb[:], [hk], [("hdram", b)])
        if nextnorm is not None:
            self.norm_block(hb, hk, b, nextnorm[0], nextnorm[1], "nn")
        return st

    def merge_out(self, l, hsrc, hT, les):
        I = self.I
        with ExitStack() as es:
            merged = self.sb(es, "m_merged", [128, 8, T], BF16)
            with ExitStack() as es2:
                wbr = [self.sb(es2, "m_wbr%d" % i, [128, 16, 128], BF16) for i in range(2)]
                wg = [self.sb(es2, "m_wg%d" % i, [128, 8, 4, 128], BF16) for i in range(2)]
                sig = [self.sb(es2, "m_sig%d" % i, [128, TB], F32) for i in range(2)]
                acc = self.sb(es2, "m_acc", [128, TB], F32)
                prod = self.sb(es2, "m_prod", [128, TB], F32)
                ukeys = [("uT", kc) for kc in range(8)]
                for dc in range(8):
                    wb_, wbk = wbr[dc % 2], ("m_wbr", dc % 2)
                    wg_, wgk = wg[dc % 2], ("m_wg", dc % 2)
                    self.loadw(wb_[:], I["w_branch"][l].rearrange("n (ec p) d -> p (n ec) d", p=128)[:, :, dc * 128:(dc + 1) * 128], wbk)
                    for br in range(4):
                        self.loadw(wg_[:, :, br, :], I["w_in"][l].rearrange("(kc p) n -> p kc n", p=128)
                                   [:, :, O_GATE + br * 1024 + dc * 128:O_GATE + br * 1024 + (dc + 1) * 128], wgk)
                    for b in range(NB):
                        sl = slice(b * TB, (b + 1) * TB)
                        for br in range(4):
                            pg, pgk = self.psb("x")
                            for kc in range(8):
                                self.mm(pg[:, :TB], wg_[:, kc, br, :], self.U(kc, b * TB, (b + 1) * TB), kc == 0, kc == 7, [wgk] + ukeys, [pgk])
                            pp, ppk = self.psb("x")
                            for ec in range(4):
                                self.mm(pp[:, :TB], wb_[:, br * 4 + ec, :], self.oall[:, br * 4 + ec, sl], ec == 0, ec == 3, [wbk, "oall"], [ppk])
                            sg, sgk = sig[br % 2], ("m_sig", br % 2)
                            self.act(sg[:, :], pg[:, :TB], AF.Sigmoid, [pgk], [sgk])
                            if br == 0:
                                self.tt("dve", acc[:, :], pp[:, :TB], sg[:, :], ALU.mult, [ppk, sgk], ["m_acc"])
                            else:
                                self.tt("dve", prod[:, :], pp[:, :TB], sg[:, :], ALU.mult, [ppk, sgk], ["m_prod"])
                                if br < 3:
                                    self.tt("dve", acc[:, :], acc[:, :], prod[:, :], ALU.add, ["m_acc", "m_prod"], ["m_acc"])
                                else:
                                    self.tt("dve", merged[:, dc, sl], acc[:, :], prod[:, :], ALU.add, ["m_acc", "m_prod"], ["m_merged"])
                self.P.barrier()
            if "merged" in self.dbg_out and l == 0:
                with ExitStack() as es3:
                    tmp = self.sb(es3, "dbgtmp2", [128, T], F32)
                    for c in range(8):
                        self.cp("dve", tmp[:], merged[:, c, :], ["m_merged"], ["dbgtmp2"])
                        self.dma("sp", self.dbg_out["merged"][c * 128:(c + 1) * 128, :], tmp[:], ["dbgtmp2"], ())
                    self.P.barrier()
            with ExitStack() as es2:
                wo = self.sb(es2, "o_w", [128, 8, D], BF16)
                y2 = [self.sb(es2, "o_y%d" % i, [128, 8, TB], F32) for i in range(2)]
                hb2 = [self.sb(es2, "o_h%d" % i, [128, 8, TB], F32) for i in range(2)]
                self.nsq = self.sb(es2, "o_sq", [128, 2, TB], F32)
                self.nrs = self.sb(es2, "o_rs", [128, TB], F32)
                self.loadw(wo[:], I["w_out"][l].rearrange("(kc p) n -> p kc n", p=128), "o_w")
                for b in range(NB):
                    y, yk = y2[b % 2], ("o_y", b % 2)
                    sl = slice(b * TB, (b + 1) * TB)
                    for dc in range(8):
                        pst, pk = self.psb("x")
                        for kc in range(8):
                            self.mm(pst[:, :TB], wo[:, kc, dc * 128:(dc + 1) * 128], merged[:, kc, sl], kc == 0, kc == 7, ["o_w", "m_merged"], [pk])
                        self.cp("act", y[:, dc, :], pst[:, :TB], [pk], [yk])
                    self.resid_block(y, [yk], b, l, 1, hsrc, hT, (hb2[b % 2], ("o_h", b % 2)), nextnorm=(l, 2))
                self.P.barrier()

    def ffn(self, l, hT, hdst):
        I = self.I
        finals = []
        NJ = DFF // 128
        for half in range(2):
            with ExitStack() as es:
                actT = self.sb(es, "f_act", [128, NJ, 3 * TB], BF16)
                with ExitStack() as es2:
                    wu = [self.sb(es2, "f_wu%d" % i, [128, 8, 2, 128], BF16) for i in range(2)]
                    cg = self.sb(es2, "f_cg", [128, TB], F32)
                    cv = self.sb(es2, "f_cv", [128, TB], F32)
                    t1 = self.sb(es2, "f_t1", [128, TB], F32)
                    t2 = self.sb(es2, "f_t2", [128, TB], F32)
                    wup = I["ffn_w_up"][l].rearrange("(kc p) n -> p kc n", p=128)
                    ukeys = [("uT", kc) for kc in range(8)]
                    cw = self.convw
                    for j in range(NJ):
                        w_, wk = wu[j % 2], ("f_wu", j % 2)
                        self.loadw(w_[:, :, 0, :], wup[:, :, j * 128:(j + 1) * 128], wk)
                        self.loadw(w_[:, :, 1, :], wup[:, :, DFF + j * 128:DFF + (j + 1) * 128], wk)
                        for b in range(3 * half, 3 * half + 3):
                            sl = slice(b * TB, (b + 1) * TB)
                            res = []
                            for gv in range(2):
                                pst, pk = self.psb("x")
                                for kc in range(8):
                                    self.mm(pst[:, :TB + 2], w_[:, kc, gv, :], self.uT[:, kc, b * TB:b * TB + TB + 2], kc == 0, kc == 7, [wk] + ukeys, [pk])
                                ch = gv * NJ + j
                                base = (l * 4) * 44
                                c0 = cw[:, base + ch:base + ch + 1]
                                c1 = cw[:, base + 44 + ch:base + 44 + ch + 1]
                                c2 = cw[:, base + 88 + ch:base + 88 + ch + 1]
                                cb = cw[:, base + 132 + ch:base + 132 + ch + 1]
                                dst, dk = (cg, "f_cg") if gv == 0 else (cv, "f_cv")
                                self.ts("dve", dst[:, :], pst[:, 0:TB], c0, cb, ALU.mult, ALU.add, [pk, "convw"], [dk])
                                self.stt("dve", dst[:, :], pst[:, 1:TB + 1], c1, dst[:, :], ALU.mult, ALU.add, [pk, dk, "convw"], [dk])
                                self.stt("dve", dst[:, :], pst[:, 2:TB + 2], c2, dst[:, :], ALU.mult, ALU.add, [pk, dk, "convw"], [dk])
                            self.act(t1[:, :], cg[:, :], AF.Square, ["f_cg"], ["f_t1"])
                            self.ts("pool", t1[:, :], t1[:, :], 0.044715, 1.0, ALU.mult, ALU.add, ["f_t1"], ["f_t1"])
                            self.tt("pool", t1[:, :], t1[:, :], cg[:, :], ALU.mult, ["f_t1", "f_cg"], ["f_t1"])
                            self.act(t1[:, :], t1[:, :], AF.Sigmoid, ["f_t1"], ["f_t1"], scale=1.5957691216057308)
                            self.tt("pool", t2[:, :], cg[:, :], cv[:, :], ALU.mult, ["f_cg", "f_cv"], ["f_t2"])
                            self.tt("dve", actT[:, j, (b - 3 * half) * TB:(b - 3 * half + 1) * TB], t1[:, :], t2[:, :], ALU.mult, ["f_t1", "f_t2"], ["f_act"])
                    self.P.barrier()
                with ExitStack() as es2:
                    wd = self.sb(es2, "f_wd", [128, NJ, D], BF16)
                    y2 = [self.sb(es2, "f_y%d" % i, [128, 8, TB], F32) for i in range(2)]
                    hb2 = [self.sb(es2, "f_h%d" % i, [128, 8, TB], F32) for i in range(2)]
                    self.nsq = self.sb(es2, "f_sq", [128, 2, TB], F32)
                    self.nrs = self.sb(es2, "f_rs", [128, TB], F32)
                    self.loadw(wd[:], I["ffn_w_down"][l].rearrange("(j p) n -> p j n", p=128), "f_wd")
                    for b in range(3 * half, 3 * half + 3):
                        y, yk = y2[b % 2], ("f_y", b % 2)
                        sl = slice(b * TB, (b + 1) * TB)
                        for dc in range(8):
                            pst, pk = self.psb("x")
                            for j in range(NJ):
                                self.mm(pst[:, :TB], wd[:, j, dc * 128:(dc + 1) * 128], actT[:, j, (b - 3 * half) * TB:(b - 3 * half + 1) * TB], j == 0, j == NJ - 1, ["f_wd", "f_act"], [pk])
                            self.cp("act", y[:, dc, :], pst[:, :TB], [pk], [yk])
                        st = self.resid_block(y, [yk], b, l, 3, hT, hdst, (hb2[b % 2], ("f_h", b % 2)))
                        finals.append(st)
                    self.P.barrier()
        return finals


def host_consts(inp):
    f32 = np.float32
    c = {}
    tab = np.asarray(inp["rel_bias_table"], f32)
    kk = np.arange(128)[:, None]
    jj = np.arange(A_W)[None, :]
    bk = rel_bucket_jax(kk - jj + A_C)
    c["slabA"] = np.ascontiguousarray(np.transpose(tab[bk][:, :, 0:8], (2, 0, 1))).astype(f32)
    ac = np.concatenate([tab[15, 0:8], tab[31, 0:8]])
    c["aconst"] = np.ascontiguousarray(np.broadcast_to(ac[None, :], (128, 16))).astype(f32)
    tabd = tab[:, 8:16]
    jj = np.arange(D_W)[None, :]
    rel = kk - jj + D_C
    tz = np.where((np.abs(rel) <= 128)[:, :, None], tabd[rel_bucket_jax(rel)], f32(NEG))
    qq = np.arange(256)[None, :]
    rel0 = kk - qq
    t0 = np.where(((np.abs(rel0) <= 128) & (kk >= NMETA))[:, :, None], tabd[rel_bucket_jax(rel0)], f32(NEG))
    tm = np.where((kk < NMETA)[:, :, None], tabd[rel_bucket_jax(rel0)], f32(NEG))
    c["slabD"] = np.ascontiguousarray(np.transpose(np.concatenate([tz, t0, tm], axis=1), (2, 0, 1))).astype(f32)
    c["dconst"] = np.ascontiguousarray(np.broadcast_to(tabd[15][None, :], (128, 8))).astype(f32)
    half = 32
    inv = (10000.0 ** (-np.arange(half, dtype=np.float32) / half)).astype(f32)
    ang = np.arange(T, dtype=f32)[None, :] * inv[:, None]
    cos, sin = np.cos(ang).astype(f32), np.sin(ang).astype(f32)
    c["rope"] = np.ascontiguousarray(np.concatenate([np.concatenate([cos, cos], 0), np.concatenate([-sin, sin], 0)], 1)).astype(f32)
    s = np.arange(128)[:, None]
    t = np.arange(128)[None, :]
    same = (s // 64) == (t // 64)
    LT = (same & (s <= t)).astype(f32)
    L = (same & (s >= t)).astype(f32)
    SU = (same & (s > t)).astype(f32)
    SL = (same & (s < t)).astype(f32)
    c["glam"] = np.ascontiguousarray(np.concatenate([LT, L, SU, SL], 1))
    return c


def host_layout(inp):
    f32 = np.float32
    g = {}
    sw = np.concatenate([np.arange(32, 64), np.arange(0, 32)])
    wq = np.asarray(inp["mla_w_q_up"], f32).reshape(DEPTH, 256, 4, 192)
    g["mla_w_q_up_sw"] = np.ascontiguousarray(wq[:, :, :, 128:][:, :, :, sw].reshape(DEPTH, 256, 256))
    g["w_in_kr_sw"] = np.ascontiguousarray(np.asarray(inp["w_in"])[:, :, OC_KR:OC_KR + 64][:, :, sw])
    gains = np.stack([inp["norm_mix_pre"], inp["norm_mix_post"], inp["norm_ffn_pre"], inp["norm_ffn_post"]], 1)
    g["gains"] = np.ascontiguousarray(gains.reshape(DEPTH * 4 * 8, 128).T).astype(f32)
    cw = np.concatenate([np.asarray(inp["ffn_conv_w"], f32), np.asarray(inp["ffn_conv_b"], f32)[:, None, :]], 1)
    g["convw"] = np.ascontiguousarray(cw.reshape(DEPTH * 4 * 44, 128).T).astype(f32)
    sc = np.zeros((DEPTH, 8, 128), f32)
    sc[:, 0] = inp["diff_subln"]
    sc[:, 1] = inp["gla_norm"]
    sc[:, 2] = inp["mla_kv_norm"]
    sc[:, 3:5] = np.asarray(inp["mla_q_norm"]).reshape(DEPTH, 2, 128)
    g["smallc"] = np.ascontiguousarray(sc.reshape(DEPTH * 8, 128).T)
    g["lamrep"] = np.ascontiguousarray(np.broadcast_to(np.asarray(inp["diff_lambda"], f32).reshape(1, DEPTH * 256), (128, DEPTH * 256)))
    g["gbias"] = np.ascontiguousarray(np.broadcast_to(np.asarray(inp["gla_gate_bias"], f32).reshape(1, DEPTH * 512), (128, DEPTH * 512)))
    g["sinkrep"] = np.ascontiguousarray(np.broadcast_to(np.asarray(inp["swa_sinks"], f32).reshape(1, DEPTH * 8), (128, DEPTH * 8)))
    return g


_NC_CACHE = {}


def get_nc(layers=(0, 1), dbg=None):
    key = (tuple(layers), tuple(sorted((dbg or {}).items())))
    if key not in _NC_CACHE:
        nc = bass.Bass("TRN2", target_bir_lowering=False)
        KB(nc, dbg).build(layers)
        _NC_CACHE[key] = nc
    return _NC_CACHE[key]


def make_in_maps(inp, cores):
    shared = {}
    for k in ("w_in", "w_branch", "w_out", "ffn_w_up", "ffn_w_down", "mla_w_q_up", "mla_w_kv_up", "gla_gate_up"):
        shared[k] = np.ascontiguousarray(np.asarray(inp[k], np.float32))
    shared.update(host_layout(inp))
    shared.update(host_consts(inp))
    meta = np.asarray(inp["meta_tokens"], np.float32)
    x = np.asarray(inp["x"], np.float32)
    maps = []
    for b in cores:
        h0 = np.concatenate([meta, x[b]], axis=0)
        m = dict(shared)
        m["h0T"] = np.ascontiguousarray(h0.T)
        maps.append(m)
    return maps


def kernel(**inputs):
    nc = get_nc()
    maps = make_in_maps(inputs, list(range(8)))
    res = run_bass_kernel_spmd(nc, maps, core_ids=list(range(8)))
    out = np.stack([np.ascontiguousarray(r["outT"][:, NMETA:].T) for r in res.results], axis=0)
    return out.astype(np.float32)
```

```python
import math
import os
import numpy as np
from contextlib import ExitStack
import concourse.bass as bass
import concourse.mybir as mybir
from concourse.bass_utils import run_bass_kernel_spmd

F32 = mybir.dt.float32
BF16 = mybir.dt.bfloat16
AF = mybir.ActivationFunctionType
ALU = mybir.AluOpType

DEPTH = 2
D = 1024
SEQ = 2048
NMETA = 16
T = SEQ + NMETA
TB = 344
NB = 6
NT = 17
EPS = 1e-6
DFF = 2816
NIN = 8416
OA_Q, OA_K, OA_V = 0, 512, 1024
OB_Q, OB_K, OB_V, OB_R, OB_G = 1536, 1792, 2048, 2560, 3072
OC_QA, OC_KVA, OC_KR = 3104, 3360, 3488
OD_Q, OD_K, OD_V = 3552, 4064, 4192
O_GATE = 4320
NEG = -30000.0


def trows(j):
    return 128 if j < 16 else 16


class Dep:
    __slots__ = ("w", "r")

    def __init__(self):
        self.w = None
        self.r = []


class Op:
    __slots__ = ("eng", "fn", "deps", "ms", "val", "sem", "is_dma")

    def __init__(self, eng, fn, is_dma):
        self.eng = eng
        self.fn = fn
        self.deps = []
        self.ms = False
        self.val = 0
        self.sem = None
        self.is_dma = is_dma


ENGS = ("pe", "act", "dve", "pool", "sp")
NDMASEM = 8


class Prog:
    def __init__(self, nc, es):
        self.nc = nc
        self.es = es
        self.ops = {e: [] for e in ENGS}
        self.dma_hist = {e: [] for e in ENGS}
        self.dd = {}

    def D(self, key):
        d = self.dd.get(key)
        if d is None:
            d = self.dd[key] = Dep()
        return d

    def op(self, eng, fn, r=(), w=(), dma=False, extra=()):
        o = Op(eng, fn, dma)
        need = list(extra)
        for k in r:
            d = self.D(k)
            if d.w is not None:
                need.append(d.w)
        for k in w:
            d = self.D(k)
            if d.w is not None:
                need.append(d.w)
            for q in d.r:
                need.append(q)
        if dma:
            h = self.dma_hist[eng]
            if len(h) >= NDMASEM:
                need.append(h[-NDMASEM])
            h.append(o)
        seen = set()
        for p in need:
            if p is o or id(p) in seen:
                continue
            seen.add(id(p))
            if (not dma) and eng == "pe" and p.eng == "pe" and not p.is_dma:
                continue
            o.deps.append(p)
        for k in r:
            self.D(k).r.append(o)
        for k in w:
            d = self.D(k)
            d.w = o
            d.r = []
        self.ops[eng].append(o)
        return o

    def barrier(self):
        lasts = []
        for e in ENGS:
            cl = [o for o in self.ops[e] if not o.is_dma]
            if cl:
                lasts.append(cl[-1])
            lasts.extend(self.dma_hist[e][-NDMASEM:])
        for e in ENGS:
            if self.ops[e]:
                self.op(e, lambda eng: eng.nop(), extra=lasts)

    def finalize(self, final_ops=()):
        nc, es = self.nc, self.es
        for e in ENGS:
            for o in self.ops[e]:
                for p in o.deps:
                    p.ms = True
        esem = {e: es.enter_context(nc.semaphore("s_" + e)) for e in ENGS}
        dsem = {e: [es.enter_context(nc.semaphore("d_%s%d" % (e, i))) for i in range(NDMASEM)]
                for e in ENGS if self.dma_hist[e]}
        for e in ENGS:
            cnt = 0
            dcnt = [0] * NDMASEM
            k = 0
            for o in self.ops[e]:
                if o.is_dma:
                    s = k % NDMASEM
                    k += 1
                    dcnt[s] += 16
                    o.sem = dsem[e][s]
                    o.val = dcnt[s]
                elif o.ms:
                    cnt += 1
                    o.sem = esem[e]
                    o.val = cnt
        engobj = {"pe": "tensor", "act": "scalar", "dve": "vector", "pool": "gpsimd", "sp": "sync"}
        block = es.enter_context(nc.Block())

        def emit(e):
            def body(eng):
                known = {}
                for o in self.ops[e]:
                    wl = {}
                    for p in o.deps:
                        key = id(p.sem)
                        if known.get(key, 0) >= p.val:
                            continue
                        if key not in wl or wl[key][1] < p.val:
                            wl[key] = (p.sem, p.val)
                    for key, (s, v) in wl.items():
                        eng.wait_ge(s, v)
                        known[key] = v
                    ins = o.fn(eng)
                    if o.is_dma:
                        ins.then_inc(o.sem, 16)
                    elif o.ms:
                        ins.then_inc(o.sem, 1)
                if e == "sp":
                    for o in final_ops:
                        eng.wait_ge(o.sem, o.val)
            return body

        for e in ENGS:
            if self.ops[e] or e == "sp":
                getattr(block, engobj[e])(emit(e))


def rel_bucket(rel):
    rel = np.asarray(rel, dtype=np.int64)
    half, max_exact = 16, 8
    ret = np.where(rel > 0, half, 0)
    n = np.abs(rel)
    nf = np.maximum(n, 1).astype(np.float32)
    large = max_exact + (np.log(nf / np.float32(max_exact)) / np.float32(math.log(128 / max_exact))
                         * (half - max_exact)).astype(np.int32)
    large = np.minimum(large, half - 1)
    return ret + np.where(n < max_exact, n, large)


def rel_bucket_jax(rel):
    import jax
    import jax.numpy as jnp
    with jax.default_device(jax.devices("cpu")[0]):
        rel = jnp.asarray(np.asarray(rel, dtype=np.int32))
        half, max_exact = 16, 8
        ret = jnp.where(rel > 0, half, 0)
        n = jnp.abs(rel)
        nf = jnp.maximum(n, 1).astype(jnp.float32)
        large = max_exact + (jnp.log(nf / max_exact) / math.log(128 / max_exact) * (half - max_exact)).astype(jnp.int32)
        large = jnp.minimum(large, half - 1)
        return np.asarray(ret + jnp.where(n < max_exact, n, large))


A_OS = sorted(set(TB * b - 128 * j for b in range(NB) for j in range(NT)))
A_NEAR = [o for o in A_OS if not (127 - o <= -91 or -o - (TB - 1) >= 91)]
A_C = -min(A_NEAR)
A_W = TB + max(A_NEAR) + A_C
D_QB = [(256 * i, 256) for i in range(8)] + [(2048, 16)]
D_C = 256
D_W = 256 + 384
D_SLABW = D_W + 256 + 256


class KB:
    def __init__(self, nc, dbg=None):
        self.nc = nc
        self.dbg = dbg or {}

    def mm(self, out, lhsT, rhs, start, stop, r, w):
        return self.P.op("pe", lambda e: e.matmul(out, lhsT=lhsT, rhs=rhs, start=start, stop=stop), r, w)

    def act(self, out, in_, func, r, w, bias=None, scale=1.0, accum=None):
        def f(e):
            kw = {}
            if bias is not None:
                kw["bias"] = bias
            if accum is not None:
                kw["accum_out"] = accum
            return e.activation(out=out, in_=in_, func=func, scale=scale, **kw)
        return self.P.op("act", f, r, w)

    def stt(self, eng, out, in0, scalar, in1, op0, op1, r, w):
        nm = {"dve": "vector", "pool": "gpsimd"}[eng]
        return self.P.op(eng, lambda e: e.scalar_tensor_tensor(out=out, in0=in0, scalar=scalar, in1=in1, op0=op0, op1=op1), r, w)

    def ts(self, eng, out, in0, s1, s2, op0, op1, r, w):
        if s2 is None:
            return self.P.op(eng, lambda e: e.tensor_scalar(out=out, in0=in0, scalar1=s1, scalar2=None, op0=op0), r, w)
        return self.P.op(eng, lambda e: e.tensor_scalar(out=out, in0=in0, scalar1=s1, scalar2=s2, op0=op0, op1=op1), r, w)

    def tt(self, eng, out, in0, in1, op, r, w):
        return self.P.op(eng, lambda e: e.tensor_tensor(out=out, in0=in0, in1=in1, op=op), r, w)

    def cp(self, eng, out, in_, r, w):
        if eng == "act":
            return self.P.op("act", lambda e: e.copy(out=out, in_=in_), r, w)
        return self.P.op(eng, lambda e: e.tensor_copy(out=out, in_=in_), r, w)

    def recip(self, out, in_, r, w):
        return self.P.op("dve", lambda e: e.reciprocal(out=out, in_=in_), r, w)

    def memset(self, eng, ap, val, w):
        return self.P.op(eng, lambda e: e.memset(ap, val), (), w)

    def dma(self, q, out, in_, r, w):
        return self.P.op(q, lambda e: e.dma_start(out=out, in_=in_), r, w, dma=True)

    def sb(self, es, name, shape, dt):
        self.sbcnt = getattr(self, "sbcnt", 0) + 1
        return es.enter_context(self.nc.sbuf_tensor("sb%d_%s" % (self.sbcnt, name), shape, dt))

    def U(self, kc, t0, t1):
        return self.uT[:, kc, t0 + 1:t1 + 1]

    def psb(self, group):
        lst = self.psgroups[group]
        i = self.psidx.get(group, 0)
        self.psidx[group] = i + 1
        b = lst[i % len(lst)]
        return self.ps[b], ("ps", b)

    def rstd_from(self, srcs, n, Dn, rkeys, out_ap, out_key, sq_ap, sq_key, sq_eng="act"):
        pst, pk = self.ps[7], ("ps", 7)
        for i, s in enumerate(srcs):
            self.act(sq_ap[:, i % 2, :n], s, AF.Square, rkeys, [(sq_key, i % 2)])
            self.mm(pst[:, :n], self.onesf[:], sq_ap[:, i % 2, :n], i == 0, i == len(srcs) - 1, [(sq_key, i % 2), "onesf"], [pk])
        self.ts("dve", out_ap, pst[:, :n], 1.0 / Dn, EPS, ALU.mult, ALU.add, [pk], [out_key])
        self.recip(out_ap, out_ap, [out_key], [out_key])
        self.act(out_ap, out_ap, AF.Sqrt, [out_key], [out_key])

    def loadw(self, dst, src, wkey):
        return self.dma("pool", dst, src, (), [wkey])

    def build(self, layers=(0, 1)):
        nc = self.nc
        I = {}

        def din(name, shape):
            I[name] = nc.dram_tensor(name, list(shape), F32, kind="ExternalInput").ap()
            return I[name]

        din("h0T", [D, T])
        din("w_in", [DEPTH, D, NIN])
        din("w_branch", [DEPTH, 4, 512, D])
        din("w_out", [DEPTH, D, D])
        din("ffn_w_up", [DEPTH, D, 2 * DFF])
        din("ffn_w_down", [DEPTH, DFF, D])
        din("mla_w_q_up", [DEPTH, 256, 768])
        din("mla_w_q_up_sw", [DEPTH, 256, 256])
        din("mla_w_kv_up", [DEPTH, 128, 1024])
        din("w_in_kr_sw", [DEPTH, D, 64])
        din("gla_gate_up", [DEPTH, 2, 16, 256])
        din("gains", [128, DEPTH * 4 * 8])
        din("convw", [128, DEPTH * 4 * 44])
        din("smallc", [128, DEPTH * 8])
        din("lamrep", [128, DEPTH * 256])
        din("gbias", [128, DEPTH * 512])
        din("sinkrep", [128, DEPTH * 8])
        din("aconst", [128, 16])
        din("dconst", [128, 8])
        din("slabA", [8, 128, A_W])
        din("slabD", [8, 128, D_SLABW])
        din("rope", [64, 2 * T])
        din("glam", [128, 4 * 128])
        outT = nc.dram_tensor("outT", [D, T], F32, kind="ExternalOutput").ap()
        hT = nc.dram_tensor("hT_scr", [D, T], F32, kind="Internal").ap()
        dbg_out = {}
        for k, shp in self.dbg.items():
            dbg_out[k] = nc.dram_tensor("dbg_" + k, list(shp), F32, kind="ExternalOutput").ap()
        self.dbg_out = dbg_out
        self.I = I

        with ExitStack() as es:
            P = self.P = Prog(nc, es)
            self.ps = [es.enter_context(nc.psum_tensor("ps%d" % i, [128, 512], F32)) for i in range(8)]
            self.psgroups = {"s": [0, 1, 2], "o": [3, 4], "d": [5, 6], "x": [0, 1, 2, 3, 4, 5, 6]}
            self.psidx = {}
            self.uT = self.sb(es, "uT", [128, 8, T + 2], BF16)
            self.onesf = self.sb(es, "onesf", [128, 128], F32)
            self.onesb = self.sb(es, "onesb", [128, 128], BF16)
            self.gains = self.sb(es, "gains", [128, DEPTH * 32], F32)
            self.convw = self.sb(es, "convw", [128, DEPTH * 4 * 44], F32)
            self.smallc = self.sb(es, "smallc", [128, DEPTH * 8], F32)
            self.aconst = self.sb(es, "aconst", [128, 16], F32)
            self.dconst = self.sb(es, "dconst", [128, 8], F32)
            self.memset("dve", self.onesf[:], 1.0, ["onesf"])
            self.memset("dve", self.onesb[:], 1.0, ["onesb"])
            self.memset("dve", self.uT[:, :, 0:1], 0.0, ["uT"])
            self.memset("dve", self.uT[:, :, T + 1:T + 2], 0.0, ["uT"])
            self.dma("sp", self.gains[:], I["gains"], (), ["gains"])
            self.dma("sp", self.convw[:], I["convw"], (), ["convw"])
            self.dma("sp", self.smallc[:], I["smallc"], (), ["smallc"])
            self.dma("sp", self.aconst[:], I["aconst"], (), ["aconst"])
            self.dma("sp", self.dconst[:], I["dconst"], (), ["dconst"])

            finals = []
            nl = len(layers)
            for li, l in enumerate(layers):
                hsrc = I["h0T"] if li == 0 else hT
                last = (li == nl - 1)
                self.norm1(l, hsrc)
                with ExitStack() as les:
                    self.oall = self.sb(les, "oall", [128, 16, T], BF16)
                    self.mixers(l)
                    if not os.environ.get("ONLYMIX"):
                        self.merge_out(l, hsrc, hT, les)
                P.barrier()
                if not os.environ.get("ONLYMIX"):
                    finals += self.ffn(l, hT, outT if last else hT)
                P.barrier()
            P.finalize(finals)
        return nc

    def gcol(self, l, which, kc):
        i = (l * 4 + which) * 8 + kc
        return self.gains[:, i:i + 1]

    def norm_block(self, hb, hkey, b, gl, gw, tag):
        sq, rs = self.nsq, self.nrs
        self.rstd_from([hb[:, kc, :] for kc in range(8)], TB, D, [hkey], rs[:, :], "nrs", sq, "nsq")
        for kc in range(8):
            self.stt("dve", self.U(kc, b * TB, (b + 1) * TB), hb[:, kc, :], self.gcol(gl, gw, kc), rs[:, :],
                     ALU.mult, ALU.mult, [hkey, "nrs", "gains"], [("uT", kc)])

    def norm1(self, l, hsrc):
        with ExitStack() as es:
            hb2 = [self.sb(es, "n1h%d" % i, [128, 8, TB], F32) for i in range(2)]
            self.nsq = self.sb(es, "n1sq", [128, 2, TB], F32)
            self.nrs = self.sb(es, "n1rs", [128, TB], F32)
            hv = hsrc.rearrange("(c p) t -> p c t", p=128)
            for b in range(NB):
                hb = hb2[b % 2]
                hk = ("n1h", b % 2)
                self.dma("sp", hb[:], hv[:, :, b * TB:(b + 1) * TB], (), [hk])
                self.norm_block(hb, hk, b, l, 0, "n1")
            self.P.barrier()

    def mixers(self, l):
        import os
        sel = os.environ.get("MIX", "cadb")
        for nm, fn, c0 in (("c", self.mix_c, 8), ("a", self.mix_a, 0), ("d", self.mix_d, 12), ("b", self.mix_b, 4)):
            if nm in sel:
                fn(l)
            else:
                self.memset("dve", self.oall[:, c0:c0 + 4, :], 0.0, ["oall"])
            self.P.barrier()
        if "oall" in self.dbg_out and l == 0:
            self.dbgdump_oall()

    def dbgdump_oall(self):
        with ExitStack() as es:
            tmp = self.sb(es, "dbgtmp", [128, T], F32)
            for c in range(16):
                self.cp("dve", tmp[:], self.oall[:, c, :], ["oall"], ["dbgtmp"])
                self.dma("sp", self.dbg_out["oall"][c * 128:(c + 1) * 128, :], tmp[:], ["dbgtmp"], ())
            self.P.barrier()

    def proj_fm(self, w, wkey, ncontr, col0, M, rhs_fn, rkeys, evac):
        for b in range(NB):
            pst, pk = self.psb("x")
            for kc in range(ncontr):
                self.mm(pst[:M, :TB], w[:, kc, col0:col0 + M], rhs_fn(kc, b), kc == 0, kc == ncontr - 1,
                        [wkey] + rkeys, [pk])
            evac(b, pst[:M, :TB], pk)

    def proj_tm(self, w, wkey, ncontr, col0, N, lhs_fn, rkeys, tiles, evac):
        for j, (t0, n) in enumerate(tiles):
            pst, pk = self.psb("x")
            for kc in range(ncontr):
                self.mm(pst[:n, :N], lhs_fn(kc, t0, n), w[:, kc, col0:col0 + N], kc == 0, kc == ncontr - 1,
                        [wkey] + rkeys, [pk])
            evac(j, n, pst[:n, :N], pk)

    def attn_stream(self, tag, jobs, ptbuf, LOOK=2, DEFER=5):
        flat = []
        for ji, jb in enumerate(jobs):
            n = len(jb["tiles"])
            for i, tl in enumerate(jb["tiles"]):
                flat.append((ji, i, n, tl))
        pend = []
        state = {}
        pts = {}

        def issue_s(idx):
            ji, i, n, tl = flat[idx]
            jb = jobs[ji]
            if i == 0 and jb.get("pre") is not None:
                jb["pre"]()
            qn, scale = jb["qn"], jb["scale"]
            kr = tl["rows"]
            pst, pk = self.psb("s")
            nm = len(tl["mms"])
            for mi, (lt, rh) in enumerate(tl["mms"]):
                self.mm(pst[:kr, :qn], lt, rh, mi == 0, mi == nm - 1, jb["rkeys"], [pk])
            pi = self.ptidx
            self.ptidx += 1
            pt = ptbuf[pi % len(ptbuf)]
            ptk = (tag + "pt", pi % len(ptbuf))
            bias = tl["bias"]
            if bias is None:
                self.act(pt[:kr, :qn], pst[:kr, :qn], AF.Exp, [pk], [ptk], scale=scale)
            elif bias[0] == "c":
                self.act(pt[:kr, :qn], pst[:kr, :qn], AF.Exp, [pk] + bias[2], [ptk], bias=bias[1][:kr, :], scale=scale)
            else:
                tmp = self.sbias[pi % len(self.sbias)]
                tk = (tag + "sb", pi % len(self.sbias))
                self.stt("dve", tmp[:kr, :qn], pst[:kr, :qn], scale, bias[1], ALU.mult, ALU.add, [pk] + bias[2], [tk])
                self.act(pt[:kr, :qn], tmp[:kr, :qn], AF.Exp, [tk], [ptk])
            pts[idx] = (pt, ptk)

        def issue_pv(idx):
            ji, i, n, tl = flat[idx]
            jb = jobs[ji]
            qn, o_M = jb["qn"], jb["o_M"]
            kr = tl["rows"]
            if i == 0:
                state[ji] = self.psb("o") + self.psb("d")
            ops_, ok, dps, dk = state[ji]
            pt, ptk = pts.pop(idx)
            vl, vkeys = jb["v_fn"](tl)
            self.mm(ops_[:o_M, :qn], vl, pt[:kr, :qn], i == 0, i == n - 1, [ptk] + vkeys, [ok])
            self.mm(dps[:o_M, :qn], jb["ones_fn"](tl), pt[:kr, :qn], i == 0, i == n - 1, [ptk, "onesb"] + jb.get("okeys", []), [dk])
            if i == n - 1:
                jb["fin1"](ops_, ok, dps, dk)
                if jb.get("fin2") is not None:
                    pend.append((idx + DEFER, jb["fin2"]))
                del state[ji]

        N = len(flat)
        for idx in range(N + LOOK):
            if idx < N:
                issue_s(idx)
            if idx - LOOK >= 0:
                issue_pv(idx - LOOK)
            while pend and pend[0][0] <= idx - LOOK:
                pend.pop(0)[1]()
        for _, fn in pend:
            fn()

    def mix_c(self, l):
        I = self.I
        with ExitStack() as es:
            wc = self.sb(es, "c_w", [128, 8, 512], BF16)
            wq = self.sb(es, "c_wq", [128, 2, 768 + 256], BF16)
            wkv = self.sb(es, "c_wkv", [128, 1, 1024], BF16)
            wvv = self.sb(es, "c_wvv", [128, 1, 512], BF16)
            lat = self.sb(es, "c_lat", [128, 3, TB], F32)
            qn = self.sb(es, "c_qn", [128, 2, T], BF16)
            kvn = self.sb(es, "c_kvn", [128, T], BF16)
            kpe = self.sb(es, "c_kpe", [64, T], BF16)
            rope = self.sb(es, "c_rope", [64, 2 * T], F32)
            vtok = self.sb(es, "c_v", [128, NT, 512], BF16)
            qno = self.sb(es, "c_qno", [128, 2, T], BF16)
            qpe = self.sb(es, "c_qpe", [64, 2, T], BF16)
            kno = self.sb(es, "c_kno", [128, 2, T], BF16)
            ptbuf = [self.sb(es, "c_pt%d" % i, [128, TB], BF16) for i in range(3)]
            self.nsq = self.sb(es, "c_sq", [128, 2, TB], F32)
            self.nrs = self.sb(es, "c_rs", [128, TB], F32)
            t1 = self.sb(es, "c_t1", [64, TB], F32)
            t2 = self.sb(es, "c_t2", [64, TB], F32)
            rd = [self.sb(es, "c_rd%d" % i, [128, TB], F32) for i in range(2)]
            win = I["w_in"][l].rearrange("(kc p) n -> p kc n", p=128)
            self.loadw(wc[:, :, 0:448], win[:, :, OC_QA:OC_QA + 448], "c_w")
            self.loadw(wc[:, :, 448:512], I["w_in_kr_sw"][l].rearrange("(kc p) n -> p kc n", p=128), "c_w")
            self.loadw(wq[:, :, 0:768], I["mla_w_q_up"][l].rearrange("(kc p) n -> p kc n", p=128), "c_wq")
            self.loadw(wq[:, :, 768:1024], I["mla_w_q_up_sw"][l].rearrange("(kc p) n -> p kc n", p=128), "c_wq")
            self.loadw(wkv[:, 0, :], I["mla_w_kv_up"][l], "c_wkv")
            self.loadw(wvv[:, 0, :].rearrange("p (h e) -> p h e", h=4),
                       I["mla_w_kv_up"][l].rearrange("p (h e) -> p h e", h=4)[:, :, 128:256], "c_wvv")
            self.dma("sp", rope[:], I["rope"], (), ["c_rope"])
            ukeys = [("uT", kc) for kc in range(8)]
            urhs = lambda kc, b: self.U(kc, b * TB, (b + 1) * TB)
            for b in range(NB):
                sl = slice(b * TB, (b + 1) * TB)
                pst, pk = self.psb("x")
                pst2, pk2 = self.psb("x")
                for kc in range(8):
                    self.mm(pst[:64, :TB], wc[:, kc, 384:448], urhs(kc, b), kc == 0, kc == 7, ["c_w"] + ukeys, [pk])
                for kc in range(8):
                    self.mm(pst2[:64, :TB], wc[:, kc, 448:512], urhs(kc, b), kc == 0, kc == 7, ["c_w"] + ukeys, [pk2])
                self.tt("dve", t1[:, :], pst[:64, :TB], rope[:, sl], ALU.mult, [pk, "c_rope"], ["c_t1"])
                self.tt("dve", t2[:, :], pst2[:64, :TB], rope[:, T + b * TB:T + (b + 1) * TB], ALU.mult, [pk2, "c_rope"], ["c_t2"])
                self.tt("dve", kpe[:, sl], t1[:, :], t2[:, :], ALU.add, ["c_t1", "c_t2"], ["c_kpe"])
            sc = self.smallc
            for b in range(NB):
                sl = slice(b * TB, (b + 1) * TB)
                for ci in range(3):
                    pst, pk = self.psb("x")
                    for kc in range(8):
                        self.mm(pst[:, :TB], wc[:, kc, ci * 128:(ci + 1) * 128], urhs(kc, b), kc == 0, kc == 7, ["c_w"] + ukeys, [pk])
                    self.cp("act", lat[:, ci, :], pst[:, :TB], [pk], [("c_lat", ci)])
                self.rstd_from([lat[:, 0, :], lat[:, 1, :]], TB, 256, [("c_lat", 0), ("c_lat", 1)], self.nrs[:, :], "nrs", self.nsq, "nsq")
                for ci in range(2):
                    self.stt("dve", qn[:, ci, sl], lat[:, ci, :], sc[:, l * 8 + 3 + ci:l * 8 + 4 + ci], self.nrs[:, :],
                             ALU.mult, ALU.mult, [("c_lat", ci), "nrs", "smallc"], ["c_qn"])
                self.rstd_from([lat[:, 2, :]], TB, 128, [("c_lat", 2)], self.nrs[:, :], "nrs", self.nsq, "nsq")
                self.stt("dve", kvn[:, sl], lat[:, 2, :], sc[:, l * 8 + 2:l * 8 + 3], self.nrs[:, :],
                         ALU.mult, ALU.mult, [("c_lat", 2), "nrs", "smallc"], ["c_kvn"])
            tiles = [(128 * j, trows(j)) for j in range(NT)]
            self.proj_tm(wvv, "c_wvv", 1, 0, 512, lambda kc, t0, n: kvn[:, t0:t0 + n], ["c_kvn"], tiles,
                         lambda j, n, ps, pk: self.cp("act", vtok[:n, j, :], ps, [pk], ["c_v"]))
            scale = (128 + 64) ** -0.5
            self.ptidx = 0
            qrhs = lambda kc, b: qn[:, kc, b * TB:(b + 1) * TB]

            def cproj(h):
                hb_ = h % 2
                self.proj_fm(wq, "c_wq", 2, h * 192, 128, qrhs, ["c_qn"],
                             lambda b, ps, pk: self.cp("act", qno[:, hb_, b * TB:(b + 1) * TB], ps, [pk], [("c_qno", hb_)]))
                for b in range(NB):
                    sl = slice(b * TB, (b + 1) * TB)
                    pst, pk = self.psb("x")
                    pst2, pk2 = self.psb("x")
                    for kc in range(2):
                        self.mm(pst[:64, :TB], wq[:, kc, h * 192 + 128:h * 192 + 192], qrhs(kc, b), kc == 0, kc == 1, ["c_wq", "c_qn"], [pk])
                    for kc in range(2):
                        self.mm(pst2[:64, :TB], wq[:, kc, 768 + h * 64:768 + h * 64 + 64], qrhs(kc, b), kc == 0, kc == 1, ["c_wq", "c_qn"], [pk2])
                    self.tt("dve", t1[:, :], pst[:64, :TB], rope[:, sl], ALU.mult, [pk, "c_rope"], ["c_t1"])
                    self.tt("dve", t2[:, :], pst2[:64, :TB], rope[:, T + b * TB:T + (b + 1) * TB], ALU.mult, [pk2, "c_rope"], ["c_t2"])
                    self.tt("dve", qpe[:, hb_, sl], t1[:, :], t2[:, :], ALU.add, ["c_t1", "c_t2"], [("c_qpe", hb_)])
                self.proj_fm(wkv, "c_wkv", 1, h * 256, 128, lambda kc, b: kvn[:, b * TB:(b + 1) * TB], ["c_kvn"],
                             lambda b, ps, pk: self.cp("act", kno[:, hb_, b * TB:(b + 1) * TB], ps, [pk], [("c_kno", hb_)]))

            cproj(0)
            for h in range(4):
                if h + 1 < 4:
                    cproj(h + 1)
                hb_ = h % 2
                jobs = []
                for b in range(NB):
                    q0 = b * TB
                    tl = []
                    for j in range(NT):
                        kr = trows(j)
                        tl.append(dict(rows=kr, j=j, bias=None,
                                       mms=[(kno[:, hb_, 128 * j:128 * j + kr], qno[:, hb_, q0:q0 + TB]),
                                            (kpe[:, 128 * j:128 * j + kr], qpe[:, hb_, q0:q0 + TB])]))

                    def fin1(ops_, ok, dps, dk, q0=q0, h=h, b=b):
                        rd_ = rd[b % 2]
                        self.recip(rd_[:, :], dps[:, :TB], [dk], [("c_rd", b % 2)])
                        self.tt("dve", self.oall[:, 8 + h, q0:q0 + TB], ops_[:, :TB], rd_[:, :], ALU.mult, [ok, ("c_rd", b % 2)], ["oall"])
                    jobs.append(dict(qn=TB, tiles=tl, o_M=128, scale=scale, fin1=fin1, fin2=None,
                                     v_fn=lambda t, h=h: (vtok[:t["rows"], t["j"], h * 128:(h + 1) * 128], ["c_v"]),
                                     ones_fn=lambda t: self.onesb[:t["rows"], :],
                                     rkeys=[("c_kno", hb_), ("c_qno", hb_), "c_kpe", ("c_qpe", hb_)]))
                self.attn_stream("c", jobs, ptbuf)

    def mix_a(self, l):
        I = self.I
        lam_init = 0.8 - 0.6 * math.exp(-0.3 * l)
        with ExitStack() as es:
            wqk = [self.sb(es, "a_wqk%d" % i, [128, 8, 256], BF16) for i in range(2)]
            wv = self.sb(es, "a_wv", [128, 8, 512], BF16)
            vtok = self.sb(es, "a_v", [128, NT, 512], BF16)
            qT = self.sb(es, "a_q", [128, 4, T], BF16)
            kT = self.sb(es, "a_k", [128, 4, T], BF16)
            slab = [self.sb(es, "a_slab%d" % i, [128, 2, A_W], F32) for i in range(2)]
            ptbuf = [self.sb(es, "a_pt%d" % i, [128, TB], BF16) for i in range(3)]
            self.sbias = [self.sb(es, "a_sb%d" % i, [128, TB], F32) for i in range(2)]
            self.nsq = self.sb(es, "a_sq", [128, 2, TB], F32)
            self.nrs = self.sb(es, "a_rs", [128, TB], F32)
            rd = [self.sb(es, "a_rd%d" % i, [128, TB], F32) for i in range(2)]
            on = [self.sb(es, "a_on%d" % i, [128, TB], F32) for i in range(4)]
            lam = self.sb(es, "a_lam", [128, 256], F32)
            lt = self.sb(es, "a_lt", [128, 128], F32)
            lv = self.sb(es, "a_lv", [128, 4], F32)
            gsub = self.sb(es, "a_gs", [128, 1], F32)
            win = I["w_in"][l].rearrange("(kc p) n -> p kc n", p=128)
            self.loadw(wv[:], win[:, :, OA_V:OA_V + 512], "a_wv")
            self.dma("sp", lam[:], I["lamrep"][:, l * 256:(l + 1) * 256], (), ["a_lam"])
            self.tt("dve", lt[:, 0:64], lam[:, 0:64], lam[:, 64:128], ALU.mult, ["a_lam"], ["a_lt"])
            self.tt("dve", lt[:, 64:128], lam[:, 128:192], lam[:, 192:256], ALU.mult, ["a_lam"], ["a_lt"])
            self.P.op("dve", lambda e: e.reduce_sum(out=lv[:, 0:1], in_=lt[:, 0:64], axis=mybir.AxisListType.X), ["a_lt"], ["a_lv"])
            self.P.op("dve", lambda e: e.reduce_sum(out=lv[:, 1:2], in_=lt[:, 64:128], axis=mybir.AxisListType.X), ["a_lt"], ["a_lv"])
            self.act(lv[:, 0:2], lv[:, 0:2], AF.Exp, ["a_lv"], ["a_lv"])
            self.tt("dve", lv[:, 2:3], lv[:, 1:2], lv[:, 0:1], ALU.subtract, ["a_lv"], ["a_lv"])
            self.ts("dve", lv[:, 3:4], lv[:, 2:3], -lam_init, None, ALU.add, None, ["a_lv"], ["a_lv"])
            self.ts("dve", gsub[:, :], self.smallc[:, l * 8:l * 8 + 1], 1.0 - lam_init, None, ALU.mult, None, ["smallc"], ["a_gs"])
            ukeys = [("uT", kc) for kc in range(8)]
            urhs = lambda kc, b: self.U(kc, b * TB, (b + 1) * TB)
            tiles = [(128 * j, trows(j)) for j in range(NT)]
            self.proj_tm(wv, "a_wv", 8, 0, 512, lambda kc, t0, n: self.U(kc, t0, t0 + n), ukeys, tiles,
                         lambda j, n, ps, pk: self.cp("act", vtok[:n, j, :], ps, [pk], ["a_v"]))
            self.ptidx = 0
            for h in range(4):
                wq_, wqkk = wqk[h % 2], ("a_wqk", h % 2)
                self.loadw(wq_[:, :, 0:128], win[:, :, OA_Q + h * 128:OA_Q + (h + 1) * 128], wqkk)
                self.loadw(wq_[:, :, 128:256], win[:, :, OA_K + h * 128:OA_K + (h + 1) * 128], wqkk)
                self.proj_fm(wq_, wqkk, 8, 0, 128, urhs, ukeys,
                             lambda b, ps, pk, h=h: self.cp("act", qT[:, h, b * TB:(b + 1) * TB], ps, [pk], ["a_q"]))
                self.proj_fm(wq_, wqkk, 8, 128, 128, urhs, ukeys,
                             lambda b, ps, pk, h=h: self.cp("act", kT[:, h, b * TB:(b + 1) * TB], ps, [pk], ["a_k"]))
            jobs = []
            for h in range(4):
                sl_ = slab[h % 2]
                slk = ("a_slab", h % 2)
                slab_loaded = [False]
                for b in range(NB):
                    q0 = b * TB
                    for m in range(2):
                        mh = m * 4 + h
                        tl = []
                        for j in range(NT):
                            kr = trows(j)
                            o = TB * b - 128 * j
                            if 127 - o <= -91:
                                bias = ("c", self.aconst[:, mh:mh + 1], ["aconst"])
                            elif -o - (TB - 1) >= 91:
                                bias = ("c", self.aconst[:, 8 + mh:9 + mh], ["aconst"])
                            else:
                                bias = ("s", sl_[:kr, m, o + A_C:o + A_C + TB], [slk])
                            tl.append(dict(rows=kr, j=j, bias=bias,
                                           mms=[(kT[64 * m:64 * m + 64, h, 128 * j:128 * j + kr], qT[64 * m:64 * m + 64, h, q0:q0 + TB])]))
                        oi = (b % 2) * 2 + m

                        def fin1(ops_, ok, dps, dk, oi=oi):
                            rd_ = rd[oi % 2]
                            self.recip(rd_[:, :], dps[:, :TB], [dk], [("a_rd", oi % 2)])
                            self.tt("dve", on[oi][:, :], ops_[:, :TB], rd_[:, :], ALU.mult, [ok, ("a_rd", oi % 2)], [("a_on", oi)])

                        def fin2(b=b, h=h, q0=q0):
                            o0, o1 = (b % 2) * 2, (b % 2) * 2 + 1
                            self.stt("dve", on[o0][:, :], on[o1][:, :], lv[:, 3:4], on[o0][:, :], ALU.mult, ALU.add,
                                     [("a_on", o0), ("a_on", o1), "a_lv"], [("a_on", o0)])
                            self.rstd_from([on[o0][:, :]], TB, 128, [("a_on", o0)], self.nrs[:, :], "nrs", self.nsq, "nsq")
                            self.stt("dve", self.oall[:, h, q0:q0 + TB], on[o0][:, :], gsub[:, 0:1], self.nrs[:, :], ALU.mult, ALU.mult,
                                     [("a_on", o0), "nrs", "a_gs"], ["oall"])
                        jobs.append(dict(qn=TB, tiles=tl, o_M=128, scale=0.125, fin1=fin1, fin2=(fin2 if m == 1 else None),
                                         v_fn=lambda t, h=h: (vtok[:t["rows"], t["j"], h * 128:(h + 1) * 128], ["a_v"]),
                                         ones_fn=lambda t: self.onesb[:t["rows"], :], rkeys=["a_q", "a_k"],
                                         pre=((lambda h=h: [self.dma("sp", slab[h % 2][:, mm_, :], self.I["slabA"][mm_ * 4 + h], (), [("a_slab", h % 2)]) for mm_ in range(2)])
                                              if (b == 0 and m == 0) else None)))
            self.attn_stream("a", jobs, ptbuf)

    def mix_d(self, l):
        I = self.I
        with ExitStack() as es:
            wq = self.sb(es, "d_wq", [128, 8, 512], BF16)
            wkk = self.sb(es, "d_wkk", [128, 8, 2, 128], BF16)
            wv = self.sb(es, "d_wv", [128, 8, 128], BF16)
            qT = self.sb(es, "d_q", [128, 4, T], BF16)
            kT2 = self.sb(es, "d_k", [128, 2, T], BF16)
            vpad = self.sb(es, "d_v", [128, NT * 4, 128], BF16)
            slab = [self.sb(es, "d_slab%d" % i, [128, D_SLABW], F32) for i in range(4)]
            ptbuf = [self.sb(es, "d_pt%d" % i, [128, 256], BF16) for i in range(3)]
            self.sbias = [self.sb(es, "d_sb%d" % i, [128, 256], F32) for i in range(2)]
            oh = self.sb(es, "d_oh", [128, 2, 128], BF16)
            es8 = self.sb(es, "d_es8", [128, 8], F32)
            es2 = self.sb(es, "d_es2", [128, 4], F32)
            rd = [self.sb(es, "d_rd%d" % i, [128, 256], F32) for i in range(2)]
            win = I["w_in"][l].rearrange("(kc p) n -> p kc n", p=128)
            self.loadw(wq[:], win[:, :, OD_Q:OD_Q + 512], "d_wq")
            for kv in range(2):
                for e in range(2):
                    self.loadw(wkk[:, :, kv, e * 64:(e + 1) * 64], win[:, :, OD_K + kv * 64:OD_K + (kv + 1) * 64], "d_wkk")
            self.loadw(wv[:], win[:, :, OD_V:OD_V + 128], "d_wv")
            self.memset("dve", vpad[:], 0.0, ["d_v"])
            self.memset("dve", oh[:], 0.0, ["d_oh"])
            self.memset("dve", oh[:, 0, 0:64], 1.0, ["d_oh"])
            self.memset("dve", oh[:, 1, 64:128], 1.0, ["d_oh"])
            self.dma("sp", es8[:], I["sinkrep"][:, l * 8:(l + 1) * 8], (), ["d_es8"])
            self.act(es8[:], es8[:], AF.Exp, ["d_es8"], ["d_es8"])
            for p in range(4):
                self.cp("dve", es2[0:64, p:p + 1], es8[0:64, 2 * p:2 * p + 1], ["d_es8"], ["d_es2"])
                self.cp("dve", es2[64:128, p:p + 1], es8[64:128, 2 * p + 1:2 * p + 2], ["d_es8"], ["d_es2"])
            ukeys = [("uT", kc) for kc in range(8)]
            urhs = lambda kc, b: self.U(kc, b * TB, (b + 1) * TB)
            for p in range(4):
                self.proj_fm(wq, "d_wq", 8, p * 128, 128, urhs, ukeys,
                             lambda b, ps, pk, p=p: self.cp("act", qT[:, p, b * TB:(b + 1) * TB], ps, [pk], ["d_q"]))
            for kv in range(2):
                self.proj_fm(wkk[:, :, kv, :], "d_wkk", 8, 0, 128, urhs, ukeys,
                             lambda b, ps, pk, kv=kv: self.cp("act", kT2[:, kv, b * TB:(b + 1) * TB], ps, [pk], ["d_k"]))
            tiles = [(128 * j, trows(j)) for j in range(NT)]

            def vev(j, n, ps, pk):
                for kv in range(2):
                    self.cp("act", vpad[:n, j * 4 + kv * 2, 0:64], ps[:, kv * 64:(kv + 1) * 64], [pk], ["d_v"])
                    self.cp("dve", vpad[:n, j * 4 + kv * 2 + 1, 64:128], ps[:, kv * 64:(kv + 1) * 64], [pk], ["d_v"])
            self.proj_tm(wv, "d_wv", 8, 0, 128, lambda kc, t0, n: self.U(kc, t0, t0 + n), ukeys, tiles, vev)
            self.ptidx = 0
            jobs = []
            for p in range(4):
                kv = p // 2
                sls = [slab[(p % 2) * 2 + e] for e in range(2)]
                slks = [("d_slab", (p % 2) * 2 + e) for e in range(2)]
                pre_p = (lambda p=p, sls=sls, slks=slks: [self.dma("sp", sls[e][:], I["slabD"][2 * p + e], (), [slks[e]]) for e in range(2)])
                for qb, (q0, qn) in enumerate(D_QB):
                    tl = []
                    for e in range(2):
                        h = 2 * p + e
                        pr = slice(64 * e, 64 * e + 64)
                        if qb == 0:
                            mb = ("s", sls[e][:16, D_W + 256:D_W + 256 + qn], [slks[e]])
                        else:
                            mb = ("c", self.dconst[:, h:h + 1], ["dconst"])
                        tl.append(dict(rows=16, j=0, e=e, bias=mb, mms=[(kT2[pr, kv, 0:16], qT[pr, p, q0:q0 + qn])]))
                        for j in range(max(0, q0 // 128 - 1), min(16, (q0 + qn + 127) // 128) + 1):
                            kr = trows(j)
                            o = q0 - 128 * j
                            if j == 0:
                                bs = sls[e][:kr, D_W:D_W + qn]
                            else:
                                bs = sls[e][:kr, o + D_C:o + D_C + qn]
                            tl.append(dict(rows=kr, j=j, e=e, bias=("s", bs, [slks[e]]),
                                           mms=[(kT2[pr, kv, 128 * j:128 * j + kr], qT[pr, p, q0:q0 + qn])]))

                    def fin1(ops_, ok, dps, dk, p=p, q0=q0, qn=qn, qb=qb):
                        rd_ = rd[qb % 2]
                        rk = ("d_rd", qb % 2)
                        self.ts("dve", rd_[:, :qn], dps[:, :qn], es2[:, p:p + 1], None, ALU.add, None, [dk, "d_es2"], [rk])
                        self.recip(rd_[:, :qn], rd_[:, :qn], [rk], [rk])
                        self.tt("dve", self.oall[:, 12 + p, q0:q0 + qn], ops_[:, :qn], rd_[:, :qn], ALU.mult, [ok, rk], ["oall"])
                    jobs.append(dict(qn=qn, tiles=tl, o_M=128, scale=0.125, fin1=fin1, fin2=None,
                                     v_fn=lambda t, kv=kv: (vpad[:t["rows"], t["j"] * 4 + kv * 2 + t["e"], :], ["d_v"]),
                                     ones_fn=lambda t: oh[:t["rows"], t["e"], :], okeys=["d_oh"], rkeys=["d_q", "d_k"],
                                     pre=(pre_p if qb == 0 else None)))
            self.attn_stream("d", jobs, ptbuf)

    def mix_b(self, l):
        I = self.I
        with ExitStack() as es:
            wb = self.sb(es, "b_w", [128, 8, 1568], BF16)
            wgu = self.sb(es, "b_wgu", [16, 2, 256], F32)
            gb = self.sb(es, "b_gb", [128, 512], F32)
            msk = self.sb(es, "b_msk", [128, 4, 128], F32)
            obw = self.sb(es, "b_obw", [128, 4, T], F32)
            S = self.sb(es, "b_S", [64, 4, 128], F32)
            Sbf = self.sb(es, "b_Sbf", [64, 4, 128], BF16)
            self.nsq = self.sb(es, "b_sq", [128, 2, 64], F32)
            self.nrs = self.sb(es, "b_rs", [128, 64], F32)
            NBUF = 2
            bufs = {}

            def tb(name, shape, dt, i):
                k = (name, i % NBUF)
                if k not in bufs:
                    bufs[k] = self.sb(es, "b_%s%d" % (name, i % NBUF), shape, dt)
                return bufs[k], ("b_" + name, i % NBUF)

            win = I["w_in"][l].rearrange("(kc p) n -> p kc n", p=128)
            self.loadw(wb[:], win[:, :, OB_Q:OB_Q + 1568], "b_w")
            self.dma("sp", wgu[:], I["gla_gate_up"][l].rearrange("g r c -> r g c"), (), ["b_wgu"])
            self.dma("sp", gb[:], I["gbias"][:, l * 512:(l + 1) * 512], (), ["b_gb"])
            self.dma("sp", msk[:], I["glam"].rearrange("p (m t) -> p m t", m=4), (), ["b_msk"])
            ukeys = [("uT", kc) for kc in range(8)]
            gch = [(0, 16)] + [(16 + 64 * (c - 1), 64) for c in range(1, 33)]
            cnt = 0
            for dr in (1, 0):
                self.memset("dve", S[:], 0.0, ["b_S"])
                self.memset("dve", Sbf[:], 0.0, ["b_Sbf"])
                mi_c = 0 if dr == 0 else 1
                mi_r = 2 if dr == 0 else 3
                order = range(32, -1, -1) if dr == 1 else range(33)
                for ci in order:
                    t0, n = gch[ci]
                    cnt += 1
                    ut = lambda kc: self.U(kc, t0, t0 + n)
                    pqk, pqkk = self.psb("x")
                    for qi in range(8):
                        for kc in range(8):
                            self.mm(pqk[:64, qi * 64:qi * 64 + n], wb[:, kc, qi * 64:(qi + 1) * 64], ut(kc), kc == 0, kc == 7, ["b_w"] + ukeys, [pqkk])
                    pgl, pglk = self.psb("x")
                    for kc in range(8):
                        self.mm(pgl[:16, :n], wb[:, kc, 1536 + 16 * dr:1552 + 16 * dr], ut(kc), kc == 0, kc == 7, ["b_w"] + ukeys, [pglk])
                    glT, glk = tb("glT", [16, 64], F32, cnt)
                    self.cp("act", glT[:, :n], pgl[:16, :n], [pglk], [glk])
                    ppre, pprek = self.psb("x")
                    self.mm(ppre[:n, :256], glT[:, :n], wgu[:, dr, :], True, True, [glk, "b_wgu"], [pprek])
                    xla, xlk = tb("xla", [64, 256], F32, cnt)
                    self.tt("dve", xla[:n, :], ppre[:n, :256], gb[:n, dr * 256:(dr + 1) * 256], ALU.add, [pprek, "b_gb"], [xlk])
                    self.act(xla[:n, :], xla[:n, :], AF.Exp, [xlk], [xlk], scale=-1.0)
                    sp_, spk = tb("sp", [64, 256], F32, cnt)
                    self.act(sp_[:n, :], xla[:n, :], AF.Ln, [xlk], [spk], bias=1.0)
                    pc, pck = self.psb("x")
                    for h in range(4):
                        self.mm(pc[:64, h * 64:h * 64 + n], sp_[:n, h * 64:(h + 1) * 64], msk[:n, mi_c, :n], True, True, [spk, "b_msk"], [pck])
                    eb, ebk = tb("eb", [64, 4, 64], F32, cnt)
                    einv, eik = tb("einv", [64, 4, 64], F32, cnt)
                    pc3 = pc[:64, 0:256].rearrange("p (h t) -> p h t", h=4)[:, :, :n]
                    self.act(eb[:, :, :n], pc3, AF.Exp, [pck], [ebk], scale=-1.0 / 16)
                    self.act(einv[:, :, :n], pc3, AF.Exp, [pck], [eik], scale=1.0 / 16)
                    qd, qdk = tb("qd", [64, 4, 64], BF16, cnt)
                    ki, kik = tb("ki", [64, 4, 64], BF16, cnt)
                    q3 = pqk[:64, 0:256].rearrange("p (h t) -> p h t", h=4)[:, :, :n]
                    k3 = pqk[:64, 256:512].rearrange("p (h t) -> p h t", h=4)[:, :, :n]
                    self.stt("dve", qd[:, :, :n], q3, 0.125, eb[:, :, :n], ALU.mult, ALU.mult, [pqkk, ebk], [qdk])
                    self.tt("dve", ki[:, :, :n], k3, einv[:, :, :n], ALU.mult, [pqkk, eik], [kik])
                    pkt, pktk = self.psb("x")
                    for kc in range(8):
                        self.mm(pkt[:n, :256], ut(kc), wb[:, kc, 256:512], kc == 0, kc == 7, ["b_w"] + ukeys, [pktk])
                    pvt, pvtk = self.psb("x")
                    for kc in range(8):
                        self.mm(pvt[:n, :512], ut(kc), wb[:, kc, 512:1024], kc == 0, kc == 7, ["b_w"] + ukeys, [pvtk])
                    vt, vtk = tb("vt", [64, 512], BF16, cnt)
                    self.cp("act", vt[:n, :], pvt[:n, :512], [pvtk], [vtk])
                    pr_, prk = self.psb("x")
                    self.mm(pr_[:n, :256], msk[:n, mi_r, :n], sp_[:n, :], True, True, [spk, "b_msk"], [prk])
                    eo, eok = tb("eo", [64, 256], F32, cnt)
                    self.act(eo[:n, :], pr_[:n, :256], AF.Exp, [prk], [eok], scale=-1.0 / 16)
                    ko, kok = tb("ko", [64, 256], BF16, cnt)
                    self.tt("dve", ko[:n, :], pkt[:n, :256], eo[:n, :], ALU.mult, [pktk, eok], [kok])
                    pat, patk = self.psb("x")
                    for h in range(4):
                        self.mm(pat[:n, h * 64:h * 64 + n], ki[:, h, :n], qd[:, h, :n], True, True, [kik, qdk], [patk])
                    att, atk = tb("att", [64, 4, 64], BF16, cnt)
                    for h in range(4):
                        self.tt("dve", att[:n, h, :n], pat[:n, h * 64:h * 64 + n], msk[:n, mi_c, :n], ALU.mult, [patk, "b_msk"], [atk])
                    po, pok = self.psb("x")
                    for h in range(4):
                        self.mm(po[:, h * 64:h * 64 + n], vt[:n, h * 128:(h + 1) * 128], att[:n, h, :n], True, False, [vtk, atk], [pok])
                        self.mm(po[:, h * 64:h * 64 + n], Sbf[:, h, :], qd[:, h, :n], False, True, ["b_Sbf", qdk], [pok])
                    pds, pdsk = self.psb("x")
                    for h in range(4):
                        self.mm(pds[:64, h * 128:(h + 1) * 128], ko[:n, h * 64:(h + 1) * 64], vt[:n, h * 128:(h + 1) * 128], True, True, [kok, vtk], [pdsk])
                    dcol = (n - 1) if dr == 0 else 0
                    for h in range(4):
                        self.stt("dve", S[:, h, :], S[:, h, :], eb[:, h, dcol:dcol + 1], pds[:64, h * 128:(h + 1) * 128], ALU.mult, ALU.add, ["b_S", ebk, pdsk], ["b_S"])
                    self.cp("act", Sbf[:], S[:], ["b_S"], ["b_Sbf"])
                    if dr == 1:
                        for h in range(4):
                            self.cp("act", obw[:, h, t0:t0 + n], po[:, h * 64:h * 64 + n], [pok], ["b_obw"])
                    else:
                        prr, prrk = self.psb("x")
                        for h in range(4):
                            for kc in range(8):
                                self.mm(prr[:, h * 64:h * 64 + n], wb[:, kc, 1024 + h * 128:1024 + (h + 1) * 128], ut(kc), kc == 0, kc == 7, ["b_w"] + ukeys, [prrk])
                        sr, srk = tb("sr", [128, 4, 64], F32, cnt)
                        for h in range(4):
                            self.act(sr[:, h, :n], prr[:, h * 64:h * 64 + n], AF.Silu, [prrk], [srk])
                        of, ofk = tb("of", [128, 4, 64], F32, cnt)
                        for h in range(4):
                            self.tt("dve", of[:, h, :n], po[:, h * 64:h * 64 + n], obw[:, h, t0:t0 + n], ALU.add, [pok, "b_obw"], [ofk])
                            self.rstd_from([of[:, h, :n]], n, 128, [ofk], self.nrs[:, :n], "nrs", self.nsq, "nsq")
                            self.stt("dve", of[:, h, :n], of[:, h, :n], self.smallc[:, l * 8 + 1:l * 8 + 2], self.nrs[:, :n], ALU.mult, ALU.mult, [ofk, "nrs", "smallc"], [ofk])
                            self.tt("dve", self.oall[:, 4 + h, t0:t0 + n], of[:, h, :n], sr[:, h, :n], ALU.mult, [ofk, srk], ["oall"])

    def resid_block(self, y, ykeys, b, l, gw, hsrc, hdst, bufs, nextnorm=None, final=False):
        hb, hk = bufs
        hv_s = hsrc.rearrange("(c p) t -> p c t", p=128)
        hv_d = hdst.rearrange("(c p) t -> p c t", p=128)
        sl = slice(b * TB, (b + 1) * TB)
        self.dma("sp", hb[:], hv_s[:, :, sl], (), [hk])
        self.rstd_from([y[:, kc, :] for kc in range(8)], TB, D, ykeys, self.nrs[:, :], "nrs", self.nsq, "nsq")
        for kc in range(8):
            self.stt("dve", y[:, kc, :], y[:, kc, :], self.gcol(l, gw, kc), self.nrs[:, :], ALU.mult, ALU.mult,
                     ykeys + ["nrs", "gains"], ykeys)
        self.tt("dve", hb[:], hb[:], y[:], ALU.add, [hk] + ykeys, [hk])
        st = self.dma("sp", hv_d[:, :, sl], hb[:], [hk], [("hdram", b)])
        if nextnorm is not None:
            self.norm_block(hb, hk, b, nextnorm[0], nextnorm[1], "nn")
        return st

    def merge_out(self, l, hsrc, hT, les):
        I = self.I
        with ExitStack() as es:
            merged = self.sb(es, "m_merged", [128, 8, T], BF16)
            with ExitStack() as es2:
                wbr = [self.sb(es2, "m_wbr%d" % i, [128, 16, 128], BF16) for i in range(2)]
                wg = [self.sb(es2, "m_wg%d" % i, [128, 8, 4, 128], BF16) for i in range(2)]
                sig = [self.sb(es2, "m_sig%d" % i, [128, TB], F32) for i in range(2)]
                acc = self.sb(es2, "m_acc", [128, TB], F32)
                prod = self.sb(es2, "m_prod", [128, TB], F32)
                ukeys = [("uT", kc) for kc in range(8)]
                for dc in range(8):
                    wb_, wbk = wbr[dc % 2], ("m_wbr", dc % 2)
                    wg_, wgk = wg[dc % 2], ("m_wg", dc % 2)
                    self.loadw(wb_[:], I["w_branch"][l].rearrange("n (ec p) d -> p (n ec) d", p=128)[:, :, dc * 128:(dc + 1) * 128], wbk)
                    for br in range(4):
                        self.loadw(wg_[:, :, br, :], I["w_in"][l].rearrange("(kc p) n -> p kc n", p=128)
                                   [:, :, O_GATE + br * 1024 + dc * 128:O_GATE + br * 1024 + (dc + 1) * 128], wgk)
                    for b in range(NB):
                        sl = slice(b * TB, (b + 1) * TB)
                        for br in range(4):
                            pg, pgk = self.psb("x")
                            for kc in range(8):
                                self.mm(pg[:, :TB], wg_[:, kc, br, :], self.U(kc, b * TB, (b + 1) * TB), kc == 0, kc == 7, [wgk] + ukeys, [pgk])
                            pp, ppk = self.psb("x")
                            for ec in range(4):
                                self.mm(pp[:, :TB], wb_[:, br * 4 + ec, :], self.oall[:, br * 4 + ec, sl], ec == 0, ec == 3, [wbk, "oall"], [ppk])
                            sg, sgk = sig[br % 2], ("m_sig", br % 2)
                            self.act(sg[:, :], pg[:, :TB], AF.Sigmoid, [pgk], [sgk])
                            if br == 0:
                                self.tt("dve", acc[:, :], pp[:, :TB], sg[:, :], ALU.mult, [ppk, sgk], ["m_acc"])
                            else:
                                self.tt("dve", prod[:, :], pp[:, :TB], sg[:, :], ALU.mult, [ppk, sgk], ["m_prod"])
                                if br < 3:
                                    self.tt("dve", acc[:, :], acc[:, :], prod[:, :], ALU.add, ["m_acc", "m_prod"], ["m_acc"])
                                else:
                                    self.tt("dve", merged[:, dc, sl], acc[:, :], prod[:, :], ALU.add, ["m_acc", "m_prod"], ["m_merged"])
                self.P.barrier()
            if "merged" in self.dbg_out and l == 0:
                with ExitStack() as es3:
                    tmp = self.sb(es3, "dbgtmp2", [128, T], F32)
                    for c in range(8):
                        self.cp("dve", tmp[:], merged[:, c, :], ["m_merged"], ["dbgtmp2"])
                        self.dma("sp", self.dbg_out["merged"][c * 128:(c + 1) * 128, :], tmp[:], ["dbgtmp2"], ())
                    self.P.barrier()
            with ExitStack() as es2:
                wo = self.sb(es2, "o_w", [128, 8, D], BF16)
                y2 = [self.sb(es2, "o_y%d" % i, [128, 8, TB], F32) for i in range(2)]
                hb2 = [self.sb(es2, "o_h%d" % i, [128, 8, TB], F32) for i in range(2)]
                self.nsq = self.sb(es2, "o_sq", [128, 2, TB], F32)
                self.nrs = self.sb(es2, "o_rs", [128, TB], F32)
                self.loadw(wo[:], I["w_out"][l].rearrange("(kc p) n -> p kc n", p=128), "o_w")
                for b in range(NB):
                    y, yk = y2[b % 2], ("o_y", b % 2)
                    sl = slice(b * TB, (b + 1) * TB)
                    for dc in range(8):
                        pst, pk = self.psb("x")
                        for kc in range(8):
                            self.mm(pst[:, :TB], wo[:, kc, dc * 128:(dc + 1) * 128], merged[:, kc, sl], kc == 0, kc == 7, ["o_w", "m_merged"], [pk])
                        self.cp("act", y[:, dc, :], pst[:, :TB], [pk], [yk])
                    self.resid_block(y, [yk], b, l, 1, hsrc, hT, (hb2[b % 2], ("o_h", b % 2)), nextnorm=(l, 2))
                self.P.barrier()

    def ffn(self, l, hT, hdst):
        I = self.I
        finals = []
        NJ = DFF // 128
        for half in range(2):
            with ExitStack() as es:
                actT = self.sb(es, "f_act", [128, NJ, 3 * TB], BF16)
                with ExitStack() as es2:
                    wu = [self.sb(es2, "f_wu%d" % i, [128, 8, 2, 128], BF16) for i in range(2)]
                    cg = self.sb(es2, "f_cg", [128, TB], F32)
                    cv = self.sb(es2, "f_cv", [128, TB], F32)
                    t1 = self.sb(es2, "f_t1", [128, TB], F32)
                    t2 = self.sb(es2, "f_t2", [128, TB], F32)
                    wup = I["ffn_w_up"][l].rearrange("(kc p) n -> p kc n", p=128)
                    ukeys = [("uT", kc) for kc in range(8)]
                    cw = self.convw
                    for j in range(NJ):
                        w_, wk = wu[j % 2], ("f_wu", j % 2)
                        self.loadw(w_[:, :, 0, :], wup[:, :, j * 128:(j + 1) * 128], wk)
                        self.loadw(w_[:, :, 1, :], wup[:, :, DFF + j * 128:DFF + (j + 1) * 128], wk)
                        for b in range(3 * half, 3 * half + 3):
                            sl = slice(b * TB, (b + 1) * TB)
                            res = []
                            for gv in range(2):
                                pst, pk = self.psb("x")
                                for kc in range(8):
                                    self.mm(pst[:, :TB + 2], w_[:, kc, gv, :], self.uT[:, kc, b * TB:b * TB + TB + 2], kc == 0, kc == 7, [wk] + ukeys, [pk])
                                ch = gv * NJ + j
                                base = (l * 4) * 44
                                c0 = cw[:, base + ch:base + ch + 1]
                                c1 = cw[:, base + 44 + ch:base + 44 + ch + 1]
                                c2 = cw[:, base + 88 + ch:base + 88 + ch + 1]
                                cb = cw[:, base + 132 + ch:base + 132 + ch + 1]
                                dst, dk = (cg, "f_cg") if gv == 0 else (cv, "f_cv")
                                self.ts("dve", dst[:, :], pst[:, 0:TB], c0, cb, ALU.mult, ALU.add, [pk, "convw"], [dk])
                                self.stt("dve", dst[:, :], pst[:, 1:TB + 1], c1, dst[:, :], ALU.mult, ALU.add, [pk, dk, "convw"], [dk])
                                self.stt("dve", dst[:, :], pst[:, 2:TB + 2], c2, dst[:, :], ALU.mult, ALU.add, [pk, dk, "convw"], [dk])
                            self.act(t1[:, :], cg[:, :], AF.Square, ["f_cg"], ["f_t1"])
                            self.ts("pool", t1[:, :], t1[:, :], 0.044715, 1.0, ALU.mult, ALU.add, ["f_t1"], ["f_t1"])
                            self.tt("pool", t1[:, :], t1[:, :], cg[:, :], ALU.mult, ["f_t1", "f_cg"], ["f_t1"])
                            self.act(t1[:, :], t1[:, :], AF.Sigmoid, ["f_t1"], ["f_t1"], scale=1.5957691216057308)
                            self.tt("pool", t2[:, :], cg[:, :], cv[:, :], ALU.mult, ["f_cg", "f_cv"], ["f_t2"])
                            self.tt("dve", actT[:, j, (b - 3 * half) * TB:(b - 3 * half + 1) * TB], t1[:, :], t2[:, :], ALU.mult, ["f_t1", "f_t2"], ["f_act"])
                    self.P.barrier()
                with ExitStack() as es2:
                    wd = self.sb(es2, "f_wd", [128, NJ, D], BF16)
                    y2 = [self.sb(es2, "f_y%d" % i, [128, 8, TB], F32) for i in range(2)]
                    hb2 = [self.sb(es2, "f_h%d" % i, [128, 8, TB], F32) for i in range(2)]
                    self.nsq = self.sb(es2, "f_sq", [128, 2, TB], F32)
                    self.nrs = self.sb(es2, "f_rs", [128, TB], F32)
                    self.loadw(wd[:], I["ffn_w_down"][l].rearrange("(j p) n -> p j n", p=128), "f_wd")
                    for b in range(3 * half, 3 * half + 3):
                        y, yk = y2[b % 2], ("f_y", b % 2)
                        sl = slice(b * TB, (b + 1) * TB)
                        for dc in range(8):
                            pst, pk = self.psb("x")
                            for j in range(NJ):
                                self.mm(pst[:, :TB], wd[:, j, dc * 128:(dc + 1) * 128], actT[:, j, (b - 3 * half) * TB:(b - 3 * half + 1) * TB], j == 0, j == NJ - 1, ["f_wd", "f_act"], [pk])
                            self.cp("act", y[:, dc, :], pst[:, :TB], [pk], [yk])
                        st = self.resid_block(y, [yk], b, l, 3, hT, hdst, (hb2[b % 2], ("f_h", b % 2)))
                        finals.append(st)
                    self.P.barrier()
        return finals


def host_consts(inp):
    f32 = np.float32
    c = {}
    tab = np.asarray(inp["rel_bias_table"], f32)
    kk = np.arange(128)[:, None]
    jj = np.arange(A_W)[None, :]
    bk = rel_bucket_jax(kk - jj + A_C)
    c["slabA"] = np.ascontiguousarray(np.transpose(tab[bk][:, :, 0:8], (2, 0, 1))).astype(f32)
    ac = np.concatenate([tab[15, 0:8], tab[31, 0:8]])
    c["aconst"] = np.ascontiguousarray(np.broadcast_to(ac[None, :], (128, 16))).astype(f32)
    tabd = tab[:, 8:16]
    jj = np.arange(D_W)[None, :]
    rel = kk - jj + D_C
    tz = np.where((np.abs(rel) <= 128)[:, :, None], tabd[rel_bucket_jax(rel)], f32(NEG))
    qq = np.arange(256)[None, :]
    rel0 = kk - qq
    t0 = np.where(((np.abs(rel0) <= 128) & (kk >= NMETA))[:, :, None], tabd[rel_bucket_jax(rel0)], f32(NEG))
    tm = np.where((kk < NMETA)[:, :, None], tabd[rel_bucket_jax(rel0)], f32(NEG))
    c["slabD"] = np.ascontiguousarray(np.transpose(np.concatenate([tz, t0, tm], axis=1), (2, 0, 1))).astype(f32)
    c["dconst"] = np.ascontiguousarray(np.broadcast_to(tabd[15][None, :], (128, 8))).astype(f32)
    half = 32
    inv = (10000.0 ** (-np.arange(half, dtype=np.float32) / half)).astype(f32)
    ang = np.arange(T, dtype=f32)[None, :] * inv[:, None]
    cos, sin = np.cos(ang).astype(f32), np.sin(ang).astype(f32)
    c["rope"] = np.ascontiguousarray(np.concatenate([np.concatenate([cos, cos], 0), np.concatenate([-sin, sin], 0)], 1)).astype(f32)
    s = np.arange(128)[:, None]
    t = np.arange(128)[None, :]
    same = (s // 64) == (t // 64)
    LT = (same & (s <= t)).astype(f32)
    L = (same & (s >= t)).astype(f32)
    SU = (same & (s > t)).astype(f32)
    SL = (same & (s < t)).astype(f32)
    c["glam"] = np.ascontiguousarray(np.concatenate([LT, L, SU, SL], 1))
    return c


def host_layout(inp):
    f32 = np.float32
    g = {}
    sw = np.concatenate([np.arange(32, 64), np.arange(0, 32)])
    wq = np.asarray(inp["mla_w_q_up"], f32).reshape(DEPTH, 256, 4, 192)
    g["mla_w_q_up_sw"] = np.ascontiguousarray(wq[:, :, :, 128:][:, :, :, sw].reshape(DEPTH, 256, 256))
    g["w_in_kr_sw"] = np.ascontiguousarray(np.asarray(inp["w_in"])[:, :, OC_KR:OC_KR + 64][:, :, sw])
    gains = np.stack([inp["norm_mix_pre"], inp["norm_mix_post"], inp["norm_ffn_pre"], inp["norm_ffn_post"]], 1)
    g["gains"] = np.ascontiguousarray(gains.reshape(DEPTH * 4 * 8, 128).T).astype(f32)
    cw = np.concatenate([np.asarray(inp["ffn_conv_w"], f32), np.asarray(inp["ffn_conv_b"], f32)[:, None, :]], 1)
    g["convw"] = np.ascontiguousarray(cw.reshape(DEPTH * 4 * 44, 128).T).astype(f32)
    sc = np.zeros((DEPTH, 8, 128), f32)
    sc[:, 0] = inp["diff_subln"]
    sc[:, 1] = inp["gla_norm"]
    sc[:, 2] = inp["mla_kv_norm"]
    sc[:, 3:5] = np.asarray(inp["mla_q_norm"]).reshape(DEPTH, 2, 128)
    g["smallc"] = np.ascontiguousarray(sc.reshape(DEPTH * 8, 128).T)
    g["lamrep"] = np.ascontiguousarray(np.broadcast_to(np.asarray(inp["diff_lambda"], f32).reshape(1, DEPTH * 256), (128, DEPTH * 256)))
    g["gbias"] = np.ascontiguousarray(np.broadcast_to(np.asarray(inp["gla_gate_bias"], f32).reshape(1, DEPTH * 512), (128, DEPTH * 512)))
    g["sinkrep"] = np.ascontiguousarray(np.broadcast_to(np.asarray(inp["swa_sinks"], f32).reshape(1, DEPTH * 8), (128, DEPTH * 8)))
    return g


_NC_CACHE = {}


def get_nc(layers=(0, 1), dbg=None):
    key = (tuple(layers), tuple(sorted((dbg or {}).items())))
    if key not in _NC_CACHE:
        nc = bass.Bass("TRN2", target_bir_lowering=False)
        KB(nc, dbg).build(layers)
        _NC_CACHE[key] = nc
    return _NC_CACHE[key]


def make_in_maps(inp, cores):
    shared = {}
    for k in ("w_in", "w_branch", "w_out", "ffn_w_up", "ffn_w_down", "mla_w_q_up", "mla_w_kv_up", "gla_gate_up"):
        shared[k] = np.ascontiguousarray(np.asarray(inp[k], np.float32))
    shared.update(host_layout(inp))
    shared.update(host_consts(inp))
    meta = np.asarray(inp["meta_tokens"], np.float32)
    x = np.asarray(inp["x"], np.float32)
    maps = []
    for b in cores:
        h0 = np.concatenate([meta, x[b]], axis=0)
        m = dict(shared)
        m["h0T"] = np.ascontiguousarray(h0.T)
        maps.append(m)
    return maps


def kernel(**inputs):
    nc = get_nc()
    maps = make_in_maps(inputs, list(range(8)))
    res = run_bass_kernel_spmd(nc, maps, core_ids=list(range(8)))
    out = np.stack([np.ascontiguousarray(r["outT"][:, NMETA:].T) for r in res.results], axis=0)
    return out.astype(np.float32)
```

```python
import math
import os
import numpy as np
from contextlib import ExitStack
import concourse.bass as bass
import concourse.mybir as mybir
from concourse.bass_utils import run_bass_kernel_spmd

F32 = mybir.dt.float32
BF16 = mybir.dt.bfloat16
AF = mybir.ActivationFunctionType
ALU = mybir.AluOpType

DEPTH = 2
D = 1024
SEQ = 2048
NMETA = 16
T = SEQ + NMETA
TB = 344
NB = 6
NT = 17
EPS = 1e-6
DFF = 2816
NIN = 8416
OA_Q, OA_K, OA_V = 0, 512, 1024
OB_Q, OB_K, OB_V, OB_R, OB_G = 1536, 1792, 2048, 2560, 3072
OC_QA, OC_KVA, OC_KR = 3104, 3360, 3488
OD_Q, OD_K, OD_V = 3552, 4064, 4192
O_GATE = 4320
NEG = -30000.0


def trows(j):
    return 128 if j < 16 else 16


class Dep:
    __slots__ = ("w", "r")

    def __init__(self):
        self.w = None
        self.r = []


class Op:
    __slots__ = ("eng", "fn", "deps", "ms", "val", "sem", "is_dma")

    def __init__(self, eng, fn, is_dma):
        self.eng = eng
        self.fn = fn
        self.deps = []
        self.ms = False
        self.val = 0
        self.sem = None
        self.is_dma = is_dma


ENGS = ("pe", "act", "dve", "pool", "sp")
NDMASEM = 8


class Prog:
    def __init__(self, nc, es):
        self.nc = nc
        self.es = es
        self.ops = {e: [] for e in ENGS}
        self.dma_hist = {e: [] for e in ENGS}
        self.dd = {}

    def D(self, key):
        d = self.dd.get(key)
        if d is None:
            d = self.dd[key] = Dep()
        return d

    def op(self, eng, fn, r=(), w=(), dma=False, extra=()):
        o = Op(eng, fn, dma)
        need = list(extra)
        for k in r:
            d = self.D(k)
            if d.w is not None:
                need.append(d.w)
        for k in w:
            d = self.D(k)
            if d.w is not None:
                need.append(d.w)
            for q in d.r:
                need.append(q)
        if dma:
            h = self.dma_hist[eng]
            if len(h) >= NDMASEM:
                need.append(h[-NDMASEM])
            h.append(o)
        seen = set()
        for p in need:
            if p is o or id(p) in seen:
                continue
            seen.add(id(p))
            if (not dma) and eng == "pe" and p.eng == "pe" and not p.is_dma:
                continue
            o.deps.append(p)
        for k in r:
            self.D(k).r.append(o)
        for k in w:
            d = self.D(k)
            d.w = o
            d.r = []
        self.ops[eng].append(o)
        return o

    def barrier(self):
        lasts = []
        for e in ENGS:
            cl = [o for o in self.ops[e] if not o.is_dma]
            if cl:
                lasts.append(cl[-1])
            lasts.extend(self.dma_hist[e][-NDMASEM:])
        for e in ENGS:
            if self.ops[e]:
                self.op(e, lambda eng: eng.nop(), extra=lasts)

    def finalize(self, final_ops=()):
        nc, es = self.nc, self.es
        for e in ENGS:
            for o in self.ops[e]:
                for p in o.deps:
                    p.ms = True
        esem = {e: es.enter_context(nc.semaphore("s_" + e)) for e in ENGS}
        dsem = {e: [es.enter_context(nc.semaphore("d_%s%d" % (e, i))) for i in range(NDMASEM)]
                for e in ENGS if self.dma_hist[e]}
        for e in ENGS:
            cnt = 0
            dcnt = [0] * NDMASEM
            k = 0
            for o in self.ops[e]:
                if o.is_dma:
                    s = k % NDMASEM
                    k += 1
                    dcnt[s] += 16
                    o.sem = dsem[e][s]
                    o.val = dcnt[s]
                elif o.ms:
                    cnt += 1
                    o.sem = esem[e]
                    o.val = cnt
        engobj = {"pe": "tensor", "act": "scalar", "dve": "vector", "pool": "gpsimd", "sp": "sync"}
        block = es.enter_context(nc.Block())

        def emit(e):
            def body(eng):
                known = {}
                for o in self.ops[e]:
                    wl = {}
                    for p in o.deps:
                        key = id(p.sem)
                        if known.get(key, 0) >= p.val:
                            continue
                        if key not in wl or wl[key][1] < p.val:
                            wl[key] = (p.sem, p.val)
                    for key, (s, v) in wl.items():
                        eng.wait_ge(s, v)
                        known[key] = v
                    ins = o.fn(eng)
                    if o.is_dma:
                        ins.then_inc(o.sem, 16)
                    elif o.ms:
                        ins.then_inc(o.sem, 1)
                if e == "sp":
                    for o in final_ops:
                        eng.wait_ge(o.sem, o.val)
            return body

        for e in ENGS:
            if self.ops[e] or e == "sp":
                getattr(block, engobj[e])(emit(e))


def rel_bucket(rel):
    rel = np.asarray(rel, dtype=np.int64)
    half, max_exact = 16, 8
    ret = np.where(rel > 0, half, 0)
    n = np.abs(rel)
    nf = np.maximum(n, 1).astype(np.float32)
    large = max_exact + (np.log(nf / np.float32(max_exact)) / np.float32(math.log(128 / max_exact))
                         * (half - max_exact)).astype(np.int32)
    large = np.minimum(large, half - 1)
    return ret + np.where(n < max_exact, n, large)


def rel_bucket_jax(rel):
    import jax
    import jax.numpy as jnp
    with jax.default_device(jax.devices("cpu")[0]):
        rel = jnp.asarray(np.asarray(rel, dtype=np.int32))
        half, max_exact = 16, 8
        ret = jnp.where(rel > 0, half, 0)
        n = jnp.abs(rel)
        nf = jnp.maximum(n, 1).astype(jnp.float32)
        large = max_exact + (jnp.log(nf / max_exact) / math.log(128 / max_exact) * (half - max_exact)).astype(jnp.int32)
        large = jnp.minimum(large, half - 1)
        return np.asarray(ret + jnp.where(n < max_exact, n, large))


A_OS = sorted(set(TB * b - 128 * j for b in range(NB) for j in range(NT)))
A_NEAR = [o for o in A_OS if not (127 - o <= -91 or -o - (TB - 1) >= 91)]
A_C = -min(A_NEAR)
A_W = TB + max(A_NEAR) + A_C
D_QB = [(256 * i, 256) for i in range(8)] + [(2048, 16)]
D_C = 256
D_W = 256 + 384
D_SLABW = D_W + 256 + 256


class KB:
    def __init__(self, nc, dbg=None):
        self.nc = nc
        self.dbg = dbg or {}

    def mm(self, out, lhsT, rhs, start, stop, r, w):
        return self.P.op("pe", lambda e: e.matmul(out, lhsT=lhsT, rhs=rhs, start=start, stop=stop), r, w)

    def act(self, out, in_, func, r, w, bias=None, scale=1.0, accum=None):
        def f(e):
            kw = {}
            if bias is not None:
                kw["bias"] = bias
            if accum is not None:
                kw["accum_out"] = accum
            return e.activation(out=out, in_=in_, func=func, scale=scale, **kw)
        return self.P.op("act", f, r, w)

    def stt(self, eng, out, in0, scalar, in1, op0, op1, r, w):
        nm = {"dve": "vector", "pool": "gpsimd"}[eng]
        return self.P.op(eng, lambda e: e.scalar_tensor_tensor(out=out, in0=in0, scalar=scalar, in1=in1, op0=op0, op1=op1), r, w)

    def ts(self, eng, out, in0, s1, s2, op0, op1, r, w):
        if s2 is None:
            return self.P.op(eng, lambda e: e.tensor_scalar(out=out, in0=in0, scalar1=s1, scalar2=None, op0=op0), r, w)
        return self.P.op(eng, lambda e: e.tensor_scalar(out=out, in0=in0, scalar1=s1, scalar2=s2, op0=op0, op1=op1), r, w)

    def tt(self, eng, out, in0, in1, op, r, w):
        return self.P.op(eng, lambda e: e.tensor_tensor(out=out, in0=in0, in1=in1, op=op), r, w)

    def cp(self, eng, out, in_, r, w):
        if eng == "act":
            return self.P.op("act", lambda e: e.copy(out=out, in_=in_), r, w)
        return self.P.op(eng, lambda e: e.tensor_copy(out=out, in_=in_), r, w)

    def recip(self, out, in_, r, w):
        return self.P.op("dve", lambda e: e.reciprocal(out=out, in_=in_), r, w)

    def memset(self, eng, ap, val, w):
        return self.P.op(eng, lambda e: e.memset(ap, val), (), w)

    def dma(self, q, out, in_, r, w):
        return self.P.op(q, lambda e: e.dma_start(out=out, in_=in_), r, w, dma=True)

    def sb(self, es, name, shape, dt):
        self.sbcnt = getattr(self, "sbcnt", 0) + 1
        return es.enter_context(self.nc.sbuf_tensor("sb%d_%s" % (self.sbcnt, name), shape, dt))

    def U(self, kc, t0, t1):
        return self.uT[:, kc, t0 + 1:t1 + 1]

    def psb(self, group):
        lst = self.psgroups[group]
        i = self.psidx.get(group, 0)
        self.psidx[group] = i + 1
        b = lst[i % len(lst)]
        return self.ps[b], ("ps", b)

    def rstd_from(self, srcs, n, Dn, rkeys, out_ap, out_key, sq_ap, sq_key, sq_eng="act"):
        pst, pk = self.ps[7], ("ps", 7)
        for i, s in enumerate(srcs):
            self.act(sq_ap[:, i % 2, :n], s, AF.Square, rkeys, [(sq_key, i % 2)])
            self.mm(pst[:, :n], self.onesf[:], sq_ap[:, i % 2, :n], i == 0, i == len(srcs) - 1, [(sq_key, i % 2), "onesf"], [pk])
        self.ts("dve", out_ap, pst[:, :n], 1.0 / Dn, EPS, ALU.mult, ALU.add, [pk], [out_key])
        self.recip(out_ap, out_ap, [out_key], [out_key])
        self.act(out_ap, out_ap, AF.Sqrt, [out_key], [out_key])

    def loadw(self, dst, src, wkey):
        return self.dma("pool", dst, src, (), [wkey])

    def build(self, layers=(0, 1)):
        nc = self.nc
        I = {}

        def din(name, shape):
            I[name] = nc.dram_tensor(name, list(shape), F32, kind="ExternalInput").ap()
            return I[name]

        din("h0T", [D, T])
        din("w_in", [DEPTH, D, NIN])
        din("w_branch", [DEPTH, 4, 512, D])
        din("w_out", [DEPTH, D, D])
        din("ffn_w_up", [DEPTH, D, 2 * DFF])
        din("ffn_w_down", [DEPTH, DFF, D])
        din("mla_w_q_up", [DEPTH, 256, 768])
        din("mla_w_q_up_sw", [DEPTH, 256, 256])
        din("mla_w_kv_up", [DEPTH, 128, 1024])
        din("w_in_kr_sw", [DEPTH, D, 64])
        din("gla_gate_up", [DEPTH, 2, 16, 256])
        din("gains", [128, DEPTH * 4 * 8])
        din("convw", [128, DEPTH * 4 * 44])
        din("smallc", [128, DEPTH * 8])
        din("lamrep", [128, DEPTH * 256])
        din("gbias", [128, DEPTH * 512])
        din("sinkrep", [128, DEPTH * 8])
        din("aconst", [128, 16])
        din("dconst", [128, 8])
        din("slabA", [8, 128, A_W])
        din("slabD", [8, 128, D_SLABW])
        din("rope", [64, 2 * T])
        din("glam", [128, 4 * 128])
        outT = nc.dram_tensor("outT", [D, T], F32, kind="ExternalOutput").ap()
        hT = nc.dram_tensor("hT_scr", [D, T], F32, kind="Internal").ap()
        dbg_out = {}
        for k, shp in self.dbg.items():
            dbg_out[k] = nc.dram_tensor("dbg_" + k, list(shp), F32, kind="ExternalOutput").ap()
        self.dbg_out = dbg_out
        self.I = I

        with ExitStack() as es:
            P = self.P = Prog(nc, es)
            self.ps = [es.enter_context(nc.psum_tensor("ps%d" % i, [128, 512], F32)) for i in range(8)]
            self.psgroups = {"s": [0, 1, 2], "o": [3, 4], "d": [5, 6], "x": [0, 1, 2, 3, 4, 5, 6]}
            self.psidx = {}
            self.uT = self.sb(es, "uT", [128, 8, T + 2], BF16)
            self.onesf = self.sb(es, "onesf", [128, 128], F32)
            self.onesb = self.sb(es, "onesb", [128, 128], BF16)
            self.gains = self.sb(es, "gains", [128, DEPTH * 32], F32)
            self.convw = self.sb(es, "convw", [128, DEPTH * 4 * 44], F32)
            self.smallc = self.sb(es, "smallc", [128, DEPTH * 8], F32)
            self.aconst = self.sb(es, "aconst", [128, 16], F32)
            self.dconst = self.sb(es, "dconst", [128, 8], F32)
            self.memset("dve", self.onesf[:], 1.0, ["onesf"])
            self.memset("dve", self.onesb[:], 1.0, ["onesb"])
            self.memset("dve", self.uT[:, :, 0:1], 0.0, ["uT"])
            self.memset("dve", self.uT[:, :, T + 1:T + 2], 0.0, ["uT"])
            self.dma("sp", self.gains[:], I["gains"], (), ["gains"])
            self.dma("sp", self.convw[:], I["convw"], (), ["convw"])
            self.dma("sp", self.smallc[:], I["smallc"], (), ["smallc"])
            self.dma("sp", self.aconst[:], I["aconst"], (), ["aconst"])
            self.dma("sp", self.dconst[:], I["dconst"], (), ["dconst"])

            finals = []
            nl = len(layers)
            for li, l in enumerate(layers):
                hsrc = I["h0T"] if li == 0 else hT
                last = (li == nl - 1)
                self.norm1(l, hsrc)
                with ExitStack() as les:
                    self.oall = self.sb(les, "oall", [128, 16, T], BF16)
                    self.mixers(l)
                    if not os.environ.get("ONLYMIX"):
                        self.merge_out(l, hsrc, hT, les)
                P.barrier()
                if not os.environ.get("ONLYMIX"):
                    finals += self.ffn(l, hT, outT if last else hT)
                P.barrier()
            P.finalize(finals)
        return nc

    def gcol(self, l, which, kc):
        i = (l * 4 + which) * 8 + kc
        return self.gains[:, i:i + 1]

    def norm_block(self, hb, hkey, b, gl, gw, tag):
        sq, rs = self.nsq, self.nrs
        self.rstd_from([hb[:, kc, :] for kc in range(8)], TB, D, [hkey], rs[:, :], "nrs", sq, "nsq")
        for kc in range(8):
            self.stt("dve", self.U(kc, b * TB, (b + 1) * TB), hb[:, kc, :], self.gcol(gl, gw, kc), rs[:, :],
                     ALU.mult, ALU.mult, [hkey, "nrs", "gains"], [("uT", kc)])

    def norm1(self, l, hsrc):
        with ExitStack() as es:
            hb2 = [self.sb(es, "n1h%d" % i, [128, 8, TB], F32) for i in range(2)]
            self.nsq = self.sb(es, "n1sq", [128, 2, TB], F32)
            self.nrs = self.sb(es, "n1rs", [128, TB], F32)
            hv = hsrc.rearrange("(c p) t -> p c t", p=128)
            for b in range(NB):
                hb = hb2[b % 2]
                hk = ("n1h", b % 2)
                self.dma("sp", hb[:], hv[:, :, b * TB:(b + 1) * TB], (), [hk])
                self.norm_block(hb, hk, b, l, 0, "n1")
            self.P.barrier()

    def mixers(self, l):
        import os
        sel = os.environ.get("MIX", "cadb")
        for nm, fn, c0 in (("c", self.mix_c, 8), ("a", self.mix_a, 0), ("d", self.mix_d, 12), ("b", self.mix_b, 4)):
            if nm in sel:
                fn(l)
            else:
                self.memset("dve", self.oall[:, c0:c0 + 4, :], 0.0, ["oall"])
            self.P.barrier()
        if "oall" in self.dbg_out and l == 0:
            self.dbgdump_oall()

    def dbgdump_oall(self):
        with ExitStack() as es:
            tmp = self.sb(es, "dbgtmp", [128, T], F32)
            for c in range(16):
                self.cp("dve", tmp[:], self.oall[:, c, :], ["oall"], ["dbgtmp"])
                self.dma("sp", self.dbg_out["oall"][c * 128:(c + 1) * 128, :], tmp[:], ["dbgtmp"], ())
            self.P.barrier()

    def proj_fm(self, w, wkey, ncontr, col0, M, rhs_fn, rkeys, evac):
        for b in range(NB):
            pst, pk = self.psb("x")
            for kc in range(ncontr):
                self.mm(pst[:M, :TB], w[:, kc, col0:col0 + M], rhs_fn(kc, b), kc == 0, kc == ncontr - 1,
                        [wkey] + rkeys, [pk])
            evac(b, pst[:M, :TB], pk)

    def proj_tm(self, w, wkey, ncontr, col0, N, lhs_fn, rkeys, tiles, evac):
        for j, (t0, n) in enumerate(tiles):
            pst, pk = self.psb("x")
            for kc in range(ncontr):
                self.mm(pst[:n, :N], lhs_fn(kc, t0, n), w[:, kc, col0:col0 + N], kc == 0, kc == ncontr - 1,
                        [wkey] + rkeys, [pk])
            evac(j, n, pst[:n, :N], pk)

    def attn_stream(self, tag, jobs, ptbuf, LOOK=2, DEFER=5):
        flat = []
        for ji, jb in enumerate(jobs):
            n = len(jb["tiles"])
            for i, tl in enumerate(jb["tiles"]):
                flat.append((ji, i, n, tl))
        pend = []
        state = {}
        pts = {}

        def issue_s(idx):
            ji, i, n, tl = flat[idx]
            jb = jobs[ji]
            if i == 0 and jb.get("pre") is not None:
                jb["pre"]()
            qn, scale = jb["qn"], jb["scale"]
            kr = tl["rows"]
            pst, pk = self.psb("s")
            nm = len(tl["mms"])
            for mi, (lt, rh) in enumerate(tl["mms"]):
                self.mm(pst[:kr, :qn], lt, rh, mi == 0, mi == nm - 1, jb["rkeys"], [pk])
            pi = self.ptidx
            self.ptidx += 1
            pt = ptbuf[pi % len(ptbuf)]
            ptk = (tag + "pt", pi % len(ptbuf))
            bias = tl["bias"]
            if bias is None:
                self.act(pt[:kr, :qn], pst[:kr, :qn], AF.Exp, [pk], [ptk], scale=scale)
            elif bias[0] == "c":
                self.act(pt[:kr, :qn], pst[:kr, :qn], AF.Exp, [pk] + bias[2], [ptk], bias=bias[1][:kr, :], scale=scale)
            else:
                tmp = self.sbias[pi % len(self.sbias)]
                tk = (tag + "sb", pi % len(self.sbias))
                self.stt("dve", tmp[:kr, :qn], pst[:kr, :qn], scale, bias[1], ALU.mult, ALU.add, [pk] + bias[2], [tk])
                self.act(pt[:kr, :qn], tmp[:kr, :qn], AF.Exp, [tk], [ptk])
            pts[idx] = (pt, ptk)

        def issue_pv(idx):
            ji, i, n, tl = flat[idx]
            jb = jobs[ji]
            qn, o_M = jb["qn"], jb["o_M"]
            kr = tl["rows"]
            if i == 0:
                state[ji] = self.psb("o") + self.psb("d")
            ops_, ok, dps, dk = state[ji]
            pt, ptk = pts.pop(idx)
            vl, vkeys = jb["v_fn"](tl)
            self.mm(ops_[:o_M, :qn], vl, pt[:kr, :qn], i == 0, i == n - 1, [ptk] + vkeys, [ok])
            self.mm(dps[:o_M, :qn], jb["ones_fn"](tl), pt[:kr, :qn], i == 0, i == n - 1, [ptk, "onesb"] + jb.get("okeys", []), [dk])
            if i == n - 1:
                jb["fin1"](ops_, ok, dps, dk)
                if jb.get("fin2") is not None:
                    pend.append((idx + DEFER, jb["fin2"]))
                del state[ji]

        N = len(flat)
        for idx in range(N + LOOK):
            if idx < N:
                issue_s(idx)
            if idx - LOOK >= 0:
                issue_pv(idx - LOOK)
            while pend and pend[0][0] <= idx - LOOK:
                pend.pop(0)[1]()
        for _, fn in pend:
            fn()

    def mix_c(self, l):
        I = self.I
        with ExitStack() as es:
            wc = self.sb(es, "c_w", [128, 8, 512], BF16)
            wq = self.sb(es, "c_wq", [128, 2, 768 + 256], BF16)
            wkv = self.sb(es, "c_wkv", [128, 1, 1024], BF16)
            wvv = self.sb(es, "c_wvv", [128, 1, 512], BF16)
            lat = self.sb(es, "c_lat", [128, 3, TB], F32)
            qn = self.sb(es, "c_qn", [128, 2, T], BF16)
            kvn = self.sb(es, "c_kvn", [128, T], BF16)
            kpe = self.sb(es, "c_kpe", [128, T], BF16)
            rope = self.sb(es, "c_rope", [64, 2 * T], F32)
            vtok = self.sb(es, "c_v", [128, NT, 512], BF16)
            qno = self.sb(es, "c_qno", [128, 2, T], BF16)
            qpe = self.sb(es, "c_qpe", [128, 2, T], BF16)
            kno = self.sb(es, "c_kno", [128, 2, T], BF16)
            ptbuf = [self.sb(es, "c_pt%d" % i, [128, TB], BF16) for i in range(3)]
            self.nsq = self.sb(es, "c_sq", [128, 2, TB], F32)
            self.nrs = self.sb(es, "c_rs", [128, TB], F32)
            t1 = self.sb(es, "c_t1", [64, TB], F32)
            t2 = self.sb(es, "c_t2", [64, TB], F32)
            rd = [self.sb(es, "c_rd%d" % i, [128, TB], F32) for i in range(2)]
            win = I["w_in"][l].rearrange("(kc p) n -> p kc n", p=128)
            self.loadw(wc[:, :, 0:448], win[:, :, OC_QA:OC_QA + 448], "c_w")
            self.loadw(wc[:, :, 448:512], I["w_in_kr_sw"][l].rearrange("(kc p) n -> p kc n", p=128), "c_w")
            self.loadw(wq[:, :, 0:768], I["mla_w_q_up"][l].rearrange("(kc p) n -> p kc n", p=128), "c_wq")
            self.loadw(wq[:, :, 768:1024], I["mla_w_q_up_sw"][l].rearrange("(kc p) n -> p kc n", p=128), "c_wq")
            self.loadw(wkv[:, 0, :], I["mla_w_kv_up"][l], "c_wkv")
            self.loadw(wvv[:, 0, :].rearrange("p (h e) -> p h e", h=4),
                       I["mla_w_kv_up"][l].rearrange("p (h e) -> p h e", h=4)[:, :, 128:256], "c_wvv")
            self.dma("sp", rope[:], I["rope"], (), ["c_rope"])
            self.memset("dve", kpe[64:128, :], 0.0, ["c_kpe"])
            for i_ in range(2):
                self.memset("dve", qpe[64:128, i_, :], 0.0, [("c_qpe", i_)])
            ukeys = [("uT", kc) for kc in range(8)]
            urhs = lambda kc, b: self.U(kc, b * TB, (b + 1) * TB)
            for b in range(NB):
                sl = slice(b * TB, (b + 1) * TB)
                pst, pk = self.psb("x")
                pst2, pk2 = self.psb("x")
                for kc in range(8):
                    self.mm(pst[:64, :TB], wc[:, kc, 384:448], urhs(kc, b), kc == 0, kc == 7, ["c_w"] + ukeys, [pk])
                for kc in range(8):
                    self.mm(pst2[:64, :TB], wc[:, kc, 448:512], urhs(kc, b), kc == 0, kc == 7, ["c_w"] + ukeys, [pk2])
                self.tt("dve", t1[:, :], pst[:64, :TB], rope[:, sl], ALU.mult, [pk, "c_rope"], ["c_t1"])
                self.tt("dve", t2[:, :], pst2[:64, :TB], rope[:, T + b * TB:T + (b + 1) * TB], ALU.mult, [pk2, "c_rope"], ["c_t2"])
                self.tt("dve", kpe[0:64, sl], t1[:, :], t2[:, :], ALU.add, ["c_t1", "c_t2"], ["c_kpe"])
            sc = self.smallc
            for b in range(NB):
                sl = slice(b * TB, (b + 1) * TB)
                for ci in range(3):
                    pst, pk = self.psb("x")
                    for kc in range(8):
                        self.mm(pst[:, :TB], wc[:, kc, ci * 128:(ci + 1) * 128], urhs(kc, b), kc == 0, kc == 7, ["c_w"] + ukeys, [pk])
                    self.cp("act", lat[:, ci, :], pst[:, :TB], [pk], [("c_lat", ci)])
                self.rstd_from([lat[:, 0, :], lat[:, 1, :]], TB, 256, [("c_lat", 0), ("c_lat", 1)], self.nrs[:, :], "nrs", self.nsq, "nsq")
                for ci in range(2):
                    self.stt("dve", qn[:, ci, sl], lat[:, ci, :], sc[:, l * 8 + 3 + ci:l * 8 + 4 + ci], self.nrs[:, :],
                             ALU.mult, ALU.mult, [("c_lat", ci), "nrs", "smallc"], ["c_qn"])
                self.rstd_from([lat[:, 2, :]], TB, 128, [("c_lat", 2)], self.nrs[:, :], "nrs", self.nsq, "nsq")
                self.stt("dve", kvn[:, sl], lat[:, 2, :], sc[:, l * 8 + 2:l * 8 + 3], self.nrs[:, :],
                         ALU.mult, ALU.mult, [("c_lat", 2), "nrs", "smallc"], ["c_kvn"])
            tiles = [(128 * j, trows(j)) for j in range(NT)]
            self.proj_tm(wvv, "c_wvv", 1, 0, 512, lambda kc, t0, n: kvn[:, t0:t0 + n], ["c_kvn"], tiles,
                         lambda j, n, ps, pk: self.cp("act", vtok[:n, j, :], ps, [pk], ["c_v"]))
            scale = (128 + 64) ** -0.5
            self.ptidx = 0
            qrhs = lambda kc, b: qn[:, kc, b * TB:(b + 1) * TB]

            def cproj(h):
                hb_ = h % 2
                self.proj_fm(wq, "c_wq", 2, h * 192, 128, qrhs, ["c_qn"],
                             lambda b, ps, pk: self.cp("act", qno[:, hb_, b * TB:(b + 1) * TB], ps, [pk], [("c_qno", hb_)]))
                for b in range(NB):
                    sl = slice(b * TB, (b + 1) * TB)
                    pst, pk = self.psb("x")
                    pst2, pk2 = self.psb("x")
                    for kc in range(2):
                        self.mm(pst[:64, :TB], wq[:, kc, h * 192 + 128:h * 192 + 192], qrhs(kc, b), kc == 0, kc == 1, ["c_wq", "c_qn"], [pk])
                    for kc in range(2):
                        self.mm(pst2[:64, :TB], wq[:, kc, 768 + h * 64:768 + h * 64 + 64], qrhs(kc, b), kc == 0, kc == 1, ["c_wq", "c_qn"], [pk2])
                    self.tt("dve", t1[:, :], pst[:64, :TB], rope[:, sl], ALU.mult, [pk, "c_rope"], ["c_t1"])
                    self.tt("dve", t2[:, :], pst2[:64, :TB], rope[:, T + b * TB:T + (b + 1) * TB], ALU.mult, [pk2, "c_rope"], ["c_t2"])
                    self.tt("dve", qpe[0:64, hb_, sl], t1[:, :], t2[:, :], ALU.add, ["c_t1", "c_t2"], [("c_qpe", hb_)])
                self.proj_fm(wkv, "c_wkv", 1, h * 256, 128, lambda kc, b: kvn[:, b * TB:(b + 1) * TB], ["c_kvn"],
                             lambda b, ps, pk: self.cp("act", kno[:, hb_, b * TB:(b + 1) * TB], ps, [pk], [("c_kno", hb_)]))

            cproj(0)
            for h in range(4):
                if h + 1 < 4:
                    cproj(h + 1)
                hb_ = h % 2
                jobs = []
                for b in range(NB):
                    q0 = b * TB
                    tl = []
                    for j in range(NT):
                        kr = trows(j)
                        tl.append(dict(rows=kr, j=j, bias=None,
                                       mms=[(kno[:, hb_, 128 * j:128 * j + kr], qno[:, hb_, q0:q0 + TB]),
                                            (kpe[:, 128 * j:128 * j + kr], qpe[:, hb_, q0:q0 + TB])]))

                    def fin1(ops_, ok, dps, dk, q0=q0, h=h, b=b):
                        rd_ = rd[b % 2]
                        self.recip(rd_[:, :], dps[:, :TB], [dk], [("c_rd", b % 2)])
                        self.tt("dve", self.oall[:, 8 + h, q0:q0 + TB], ops_[:, :TB], rd_[:, :], ALU.mult, [ok, ("c_rd", b % 2)], ["oall"])
                    jobs.append(dict(qn=TB, tiles=tl, o_M=128, scale=scale, fin1=fin1, fin2=None,
                                     v_fn=lambda t, h=h: (vtok[:t["rows"], t["j"], h * 128:(h + 1) * 128], ["c_v"]),
                                     ones_fn=lambda t: self.onesb[:t["rows"], :],
                                     rkeys=[("c_kno", hb_), ("c_qno", hb_), "c_kpe", ("c_qpe", hb_)]))
                self.attn_stream("c", jobs, ptbuf)

    def mix_a(self, l):
        I = self.I
        lam_init = 0.8 - 0.6 * math.exp(-0.3 * l)
        with ExitStack() as es:
            vtok = self.sb(es, "a_v", [128, NT, 512], BF16)
            qT = self.sb(es, "a_q", [128, 8, T], BF16)
            kT = self.sb(es, "a_k", [128, 4, T], BF16)
            lam = self.sb(es, "a_lam", [128, 256], F32)
            lt = self.sb(es, "a_lt", [128, 128], F32)
            lv = self.sb(es, "a_lv", [128, 4], F32)
            gsub = self.sb(es, "a_gs", [128, 1], F32)
            win = I["w_in"][l].rearrange("(kc p) n -> p kc n", p=128)
            self.dma("sp", lam[:], I["lamrep"][:, l * 256:(l + 1) * 256], (), ["a_lam"])
            self.tt("dve", lt[:, 0:64], lam[:, 0:64], lam[:, 64:128], ALU.mult, ["a_lam"], ["a_lt"])
            self.tt("dve", lt[:, 64:128], lam[:, 128:192], lam[:, 192:256], ALU.mult, ["a_lam"], ["a_lt"])
            self.P.op("dve", lambda e: e.reduce_sum(out=lv[:, 0:1], in_=lt[:, 0:64], axis=mybir.AxisListType.X), ["a_lt"], ["a_lv"])
            self.P.op("dve", lambda e: e.reduce_sum(out=lv[:, 1:2], in_=lt[:, 64:128], axis=mybir.AxisListType.X), ["a_lt"], ["a_lv"])
            self.act(lv[:, 0:2], lv[:, 0:2], AF.Exp, ["a_lv"], ["a_lv"])
            self.tt("dve", lv[:, 2:3], lv[:, 1:2], lv[:, 0:1], ALU.subtract, ["a_lv"], ["a_lv"])
            self.ts("dve", lv[:, 3:4], lv[:, 2:3], -lam_init, None, ALU.add, None, ["a_lv"], ["a_lv"])
            self.ts("dve", gsub[:, :], self.smallc[:, l * 8:l * 8 + 1], 1.0 - lam_init, None, ALU.mult, None, ["smallc"], ["a_gs"])
            self.memset("dve", qT[:], 0.0, ["a_q"])
            ukeys = [("uT", kc) for kc in range(8)]
            urhs = lambda kc, b: self.U(kc, b * TB, (b + 1) * TB)
            tiles = [(128 * j, trows(j)) for j in range(NT)]
            with ExitStack() as es1:
                wqk = [self.sb(es1, "a_wqk%d" % i, [128, 8, 256], BF16) for i in range(2)]
                wv = self.sb(es1, "a_wv", [128, 8, 512], BF16)
                self.loadw(wv[:], win[:, :, OA_V:OA_V + 512], "a_wv")
                for h in range(4):
                    wq_, wqkk = wqk[h % 2], ("a_wqk", h % 2)
                    self.loadw(wq_[:, :, 0:128], win[:, :, OA_Q + h * 128:OA_Q + (h + 1) * 128], wqkk)
                    self.loadw(wq_[:, :, 128:256], win[:, :, OA_K + h * 128:OA_K + (h + 1) * 128], wqkk)
                    if h == 0:
                        self.proj_tm(wv, "a_wv", 8, 0, 512, lambda kc, t0, n: self.U(kc, t0, t0 + n), ukeys, tiles,
                                     lambda j, n, ps, pk: self.cp("act", vtok[:n, j, :], ps, [pk], ["a_v"]))

                    def qev(b, ps, pk, h=h):
                        self.cp("act", qT[0:64, 2 * h, b * TB:(b + 1) * TB], ps[0:64, :], [pk], ["a_q"])
                        self.cp("dve", qT[64:128, 2 * h + 1, b * TB:(b + 1) * TB], ps[64:128, :], [pk], ["a_q"])
                    self.proj_fm(wq_, wqkk, 8, 0, 128, urhs, ukeys, qev)
                    self.proj_fm(wq_, wqkk, 8, 128, 128, urhs, ukeys,
                                 lambda b, ps, pk, h=h: self.cp("act", kT[:, h, b * TB:(b + 1) * TB], ps, [pk], ["a_k"]))
                self.P.barrier()
            slab = [self.sb(es, "a_slab%d" % i, [128, 2, A_W], F32) for i in range(2)]
            ptbuf = [self.sb(es, "a_pt%d" % i, [128, TB], BF16) for i in range(3)]
            self.sbias = [self.sb(es, "a_sb%d" % i, [128, TB], F32) for i in range(2)]
            self.nsq = self.sb(es, "a_sq", [128, 2, TB], F32)
            self.nrs = self.sb(es, "a_rs", [128, TB], F32)
            rd = [self.sb(es, "a_rd%d" % i, [128, TB], F32) for i in range(2)]
            on = [self.sb(es, "a_on%d" % i, [128, TB], F32) for i in range(4)]
            self.ptidx = 0
            jobs = []
            for h in range(4):
                sl_ = slab[h % 2]
                slk = ("a_slab", h % 2)
                slab_loaded = [False]
                for b in range(NB):
                    q0 = b * TB
                    for m in range(2):
                        mh = m * 4 + h
                        tl = []
                        for j in range(NT):
                            kr = trows(j)
                            o = TB * b - 128 * j
                            if 127 - o <= -91:
                                bias = ("c", self.aconst[:, mh:mh + 1], ["aconst"])
                            elif -o - (TB - 1) >= 91:
                                bias = ("c", self.aconst[:, 8 + mh:9 + mh], ["aconst"])
                            else:
                                bias = ("s", sl_[:kr, m, o + A_C:o + A_C + TB], [slk])
                            tl.append(dict(rows=kr, j=j, bias=bias,
                                           mms=[(kT[:, h, 128 * j:128 * j + kr], qT[:, 2 * h + m, q0:q0 + TB])]))
                        oi = (b % 2) * 2 + m

                        def fin1(ops_, ok, dps, dk, oi=oi):
                            rd_ = rd[oi % 2]
                            self.recip(rd_[:, :], dps[:, :TB], [dk], [("a_rd", oi % 2)])
                            self.tt("dve", on[oi][:, :], ops_[:, :TB], rd_[:, :], ALU.mult, [ok, ("a_rd", oi % 2)], [("a_on", oi)])

                        def fin2(b=b, h=h, q0=q0):
                            o0, o1 = (b % 2) * 2, (b % 2) * 2 + 1
                            self.stt("dve", on[o0][:, :], on[o1][:, :], lv[:, 3:4], on[o0][:, :], ALU.mult, ALU.add,
                                     [("a_on", o0), ("a_on", o1), "a_lv"], [("a_on", o0)])
                            self.rstd_from([on[o0][:, :]], TB, 128, [("a_on", o0)], self.nrs[:, :], "nrs", self.nsq, "nsq")
                            self.stt("dve", self.oall[:, h, q0:q0 + TB], on[o0][:, :], gsub[:, 0:1], self.nrs[:, :], ALU.mult, ALU.mult,
                                     [("a_on", o0), "nrs", "a_gs"], ["oall"])
                        jobs.append(dict(qn=TB, tiles=tl, o_M=128, scale=0.125, fin1=fin1, fin2=(fin2 if m == 1 else None),
                                         v_fn=lambda t, h=h: (vtok[:t["rows"], t["j"], h * 128:(h + 1) * 128], ["a_v"]),
                                         ones_fn=lambda t: self.onesb[:t["rows"], :], rkeys=["a_q", "a_k"],
                                         pre=((lambda h=h: [self.dma("sp", slab[h % 2][:, mm_, :], self.I["slabA"][mm_ * 4 + h], (), [("a_slab", h % 2)]) for mm_ in range(2)])
                                              if (b == 0 and m == 0) else None)))
            self.attn_stream("a", jobs, ptbuf)

    def mix_d(self, l):
        I = self.I
        with ExitStack() as es:
            wq = self.sb(es, "d_wq", [128, 8, 512], BF16)
            wkk = self.sb(es, "d_wkk", [128, 8, 2, 128], BF16)
            wv = self.sb(es, "d_wv", [128, 8, 128], BF16)
            qT = self.sb(es, "d_q", [128, 8, T], BF16)
            kT2 = self.sb(es, "d_k", [128, 2, T], BF16)
            vpad = self.sb(es, "d_v", [128, NT * 4, 128], BF16)
            slab = [self.sb(es, "d_slab%d" % i, [128, D_SLABW], F32) for i in range(4)]
            ptbuf = [self.sb(es, "d_pt%d" % i, [128, 256], BF16) for i in range(3)]
            self.sbias = [self.sb(es, "d_sb%d" % i, [128, 256], F32) for i in range(2)]
            oh = self.sb(es, "d_oh", [128, 2, 128], BF16)
            es8 = self.sb(es, "d_es8", [128, 8], F32)
            es2 = self.sb(es, "d_es2", [128, 4], F32)
            rd = [self.sb(es, "d_rd%d" % i, [128, 256], F32) for i in range(2)]
            win = I["w_in"][l].rearrange("(kc p) n -> p kc n", p=128)
            self.loadw(wq[:], win[:, :, OD_Q:OD_Q + 512], "d_wq")
            for kv in range(2):
                for e in range(2):
                    self.loadw(wkk[:, :, kv, e * 64:(e + 1) * 64], win[:, :, OD_K + kv * 64:OD_K + (kv + 1) * 64], "d_wkk")
            self.loadw(wv[:], win[:, :, OD_V:OD_V + 128], "d_wv")
            self.memset("dve", vpad[:], 0.0, ["d_v"])
            self.memset("dve", oh[:], 0.0, ["d_oh"])
            self.memset("dve", oh[:, 0, 0:64], 1.0, ["d_oh"])
            self.memset("dve", oh[:, 1, 64:128], 1.0, ["d_oh"])
            self.dma("sp", es8[:], I["sinkrep"][:, l * 8:(l + 1) * 8], (), ["d_es8"])
            self.act(es8[:], es8[:], AF.Exp, ["d_es8"], ["d_es8"])
            for p in range(4):
                self.cp("dve", es2[0:64, p:p + 1], es8[0:64, 2 * p:2 * p + 1], ["d_es8"], ["d_es2"])
                self.cp("dve", es2[64:128, p:p + 1], es8[64:128, 2 * p + 1:2 * p + 2], ["d_es8"], ["d_es2"])
            ukeys = [("uT", kc) for kc in range(8)]
            urhs = lambda kc, b: self.U(kc, b * TB, (b + 1) * TB)
            self.memset("dve", qT[:], 0.0, ["d_q"])

            def qev(b, ps, pk, p):
                self.cp("act", qT[0:64, 2 * p, b * TB:(b + 1) * TB], ps[0:64, :], [pk], ["d_q"])
                self.cp("act", qT[64:128, 2 * p + 1, b * TB:(b + 1) * TB], ps[64:128, :], [pk], ["d_q"])
            for p in range(4):
                self.proj_fm(wq, "d_wq", 8, p * 128, 128, urhs, ukeys, lambda b, ps, pk, p=p: qev(b, ps, pk, p))
            for kv in range(2):
                self.proj_fm(wkk[:, :, kv, :], "d_wkk", 8, 0, 128, urhs, ukeys,
                             lambda b, ps, pk, kv=kv: self.cp("act", kT2[:, kv, b * TB:(b + 1) * TB], ps, [pk], ["d_k"]))
            tiles = [(128 * j, trows(j)) for j in range(NT)]

            def vev(j, n, ps, pk):
                for kv in range(2):
                    self.cp("act", vpad[:n, j * 4 + kv * 2, 0:64], ps[:, kv * 64:(kv + 1) * 64], [pk], ["d_v"])
                    self.cp("dve", vpad[:n, j * 4 + kv * 2 + 1, 64:128], ps[:, kv * 64:(kv + 1) * 64], [pk], ["d_v"])
            self.proj_tm(wv, "d_wv", 8, 0, 128, lambda kc, t0, n: self.U(kc, t0, t0 + n), ukeys, tiles, vev)
            self.ptidx = 0
            jobs = []
            for p in range(4):
                kv = p // 2
                sls = [slab[(p % 2) * 2 + e] for e in range(2)]
                slks = [("d_slab", (p % 2) * 2 + e) for e in range(2)]
                pre_p = (lambda p=p, sls=sls, slks=slks: [self.dma("sp", sls[e][:], I["slabD"][2 * p + e], (), [slks[e]]) for e in range(2)])
                for qb, (q0, qn) in enumerate(D_QB):
                    tl = []
                    for e in range(2):
                        h = 2 * p + e
                        pr = slice(64 * e, 64 * e + 64)
                        if qb == 0:
                            mb = ("s", sls[e][:16, D_W + 256:D_W + 256 + qn], [slks[e]])
                        else:
                            mb = ("c", self.dconst[:, h:h + 1], ["dconst"])
                        tl.append(dict(rows=16, j=0, e=e, bias=mb, mms=[(kT2[:, kv, 0:16], qT[:, h, q0:q0 + qn])]))
                        for j in range(max(0, q0 // 128 - 1), min(16, (q0 + qn + 127) // 128) + 1):
                            kr = trows(j)
                            o = q0 - 128 * j
                            if j == 0:
                                bs = sls[e][:kr, D_W:D_W + qn]
                            else:
                                bs = sls[e][:kr, o + D_C:o + D_C + qn]
                            tl.append(dict(rows=kr, j=j, e=e, bias=("s", bs, [slks[e]]),
                                           mms=[(kT2[:, kv, 128 * j:128 * j + kr], qT[:, h, q0:q0 + qn])]))

                    def fin1(ops_, ok, dps, dk, p=p, q0=q0, qn=qn, qb=qb):
                        rd_ = rd[qb % 2]
                        rk = ("d_rd", qb % 2)
                        self.ts("dve", rd_[:, :qn], dps[:, :qn], es2[:, p:p + 1], None, ALU.add, None, [dk, "d_es2"], [rk])
                        self.recip(rd_[:, :qn], rd_[:, :qn], [rk], [rk])
                        self.tt("dve", self.oall[:, 12 + p, q0:q0 + qn], ops_[:, :qn], rd_[:, :qn], ALU.mult, [ok, rk], ["oall"])
                    jobs.append(dict(qn=qn, tiles=tl, o_M=128, scale=0.125, fin1=fin1, fin2=None,
                                     v_fn=lambda t, kv=kv: (vpad[:t["rows"], t["j"] * 4 + kv * 2 + t["e"], :], ["d_v"]),
                                     ones_fn=lambda t: oh[:t["rows"], t["e"], :], okeys=["d_oh"], rkeys=["d_q", "d_k"],
                                     pre=(pre_p if qb == 0 else None)))
            self.attn_stream("d", jobs, ptbuf)

    def mix_b(self, l):
        I = self.I
        with ExitStack() as es:
            wb = self.sb(es, "b_w", [128, 8, 1568], BF16)
            wgu = self.sb(es, "b_wgu", [16, 2, 256], F32)
            gb = self.sb(es, "b_gb", [128, 512], F32)
            msk = self.sb(es, "b_msk", [128, 4, 128], F32)
            obw = self.sb(es, "b_obw", [128, 4, T], F32)
            S = self.sb(es, "b_S", [64, 4, 128], F32)
            Sbf = self.sb(es, "b_Sbf", [64, 4, 128], BF16)
            self.nsq = self.sb(es, "b_sq", [128, 2, 64], F32)
            self.nrs = self.sb(es, "b_rs", [128, 64], F32)
            NBUF = 2
            bufs = {}

            def tb(name, shape, dt, i):
                k = (name, i % NBUF)
                if k not in bufs:
                    bufs[k] = self.sb(es, "b_%s%d" % (name, i % NBUF), shape, dt)
                return bufs[k], ("b_" + name, i % NBUF)

            win = I["w_in"][l].rearrange("(kc p) n -> p kc n", p=128)
            self.loadw(wb[:], win[:, :, OB_Q:OB_Q + 1568], "b_w")
            self.dma("sp", wgu[:], I["gla_gate_up"][l].rearrange("g r c -> r g c"), (), ["b_wgu"])
            self.dma("sp", gb[:], I["gbias"][:, l * 512:(l + 1) * 512], (), ["b_gb"])
            self.dma("sp", msk[:], I["glam"].rearrange("p (m t) -> p m t", m=4), (), ["b_msk"])
            ukeys = [("uT", kc) for kc in range(8)]
            gch = [(0, 16)] + [(16 + 64 * (c - 1), 64) for c in range(1, 33)]
            cnt = 0
            for dr in (1, 0):
                self.memset("dve", S[:], 0.0, ["b_S"])
                self.memset("dve", Sbf[:], 0.0, ["b_Sbf"])
                mi_c = 0 if dr == 0 else 1
                mi_r = 2 if dr == 0 else 3
                order = range(32, -1, -1) if dr == 1 else range(33)
                for ci in order:
                    t0, n = gch[ci]
                    cnt += 1
                    ut = lambda kc: self.U(kc, t0, t0 + n)
                    pqk, pqkk = self.psb("x")
                    for qi in range(8):
                        for kc in range(8):
                            self.mm(pqk[:64, qi * 64:qi * 64 + n], wb[:, kc, qi * 64:(qi + 1) * 64], ut(kc), kc == 0, kc == 7, ["b_w"] + ukeys, [pqkk])
                    pgl, pglk = self.psb("x")
                    for kc in range(8):
                        self.mm(pgl[:16, :n], wb[:, kc, 1536 + 16 * dr:1552 + 16 * dr], ut(kc), kc == 0, kc == 7, ["b_w"] + ukeys, [pglk])
                    glT, glk = tb("glT", [16, 64], F32, cnt)
                    self.cp("act", glT[:, :n], pgl[:16, :n], [pglk], [glk])
                    ppre, pprek = self.psb("x")
                    self.mm(ppre[:n, :256], glT[:, :n], wgu[:, dr, :], True, True, [glk, "b_wgu"], [pprek])
                    xla, xlk = tb("xla", [64, 256], F32, cnt)
                    self.tt("dve", xla[:n, :], ppre[:n, :256], gb[:n, dr * 256:(dr + 1) * 256], ALU.add, [pprek, "b_gb"], [xlk])
                    self.act(xla[:n, :], xla[:n, :], AF.Exp, [xlk], [xlk], scale=-1.0)
                    sp_, spk = tb("sp", [64, 256], F32, cnt)
                    self.act(sp_[:n, :], xla[:n, :], AF.Ln, [xlk], [spk], bias=1.0)
                    pc, pck = self.psb("x")
                    for h in range(4):
                        self.mm(pc[:64, h * 64:h * 64 + n], sp_[:n, h * 64:(h + 1) * 64], msk[:n, mi_c, :n], True, True, [spk, "b_msk"], [pck])
                    eb, ebk = tb("eb", [64, 4, 64], F32, cnt)
                    einv, eik = tb("einv", [64, 4, 64], F32, cnt)
                    pc3 = pc[:64, 0:256].rearrange("p (h t) -> p h t", h=4)[:, :, :n]
                    self.act(eb[:, :, :n], pc3, AF.Exp, [pck], [ebk], scale=-1.0 / 16)
                    self.act(einv[:, :, :n], pc3, AF.Exp, [pck], [eik], scale=1.0 / 16)
                    qd, qdk = tb("qd", [64, 4, 64], BF16, cnt)
                    ki, kik = tb("ki", [64, 4, 64], BF16, cnt)
                    q3 = pqk[:64, 0:256].rearrange("p (h t) -> p h t", h=4)[:, :, :n]
                    k3 = pqk[:64, 256:512].rearrange("p (h t) -> p h t", h=4)[:, :, :n]
                    self.stt("dve", qd[:, :, :n], q3, 0.125, eb[:, :, :n], ALU.mult, ALU.mult, [pqkk, ebk], [qdk])
                    self.tt("dve", ki[:, :, :n], k3, einv[:, :, :n], ALU.mult, [pqkk, eik], [kik])
                    pkt, pktk = self.psb("x")
                    for kc in range(8):
                        self.mm(pkt[:n, :256], ut(kc), wb[:, kc, 256:512], kc == 0, kc == 7, ["b_w"] + ukeys, [pktk])
                    pvt, pvtk = self.psb("x")
                    for kc in range(8):
                        self.mm(pvt[:n, :512], ut(kc), wb[:, kc, 512:1024], kc == 0, kc == 7, ["b_w"] + ukeys, [pvtk])
                    vt, vtk = tb("vt", [64, 512], BF16, cnt)
                    self.cp("act", vt[:n, :], pvt[:n, :512], [pvtk], [vtk])
                    pr_, prk = self.psb("x")
                    self.mm(pr_[:n, :256], msk[:n, mi_r, :n], sp_[:n, :], True, True, [spk, "b_msk"], [prk])
                    eo, eok = tb("eo", [64, 256], F32, cnt)
                    self.act(eo[:n, :], pr_[:n, :256], AF.Exp, [prk], [eok], scale=-1.0 / 16)
                    ko, kok = tb("ko", [64, 256], BF16, cnt)
                    self.tt("dve", ko[:n, :], pkt[:n, :256], eo[:n, :], ALU.mult, [pktk, eok], [kok])
                    pat, patk = self.psb("x")
                    for h in range(4):
                        self.mm(pat[:n, h * 64:h * 64 + n], ki[:, h, :n], qd[:, h, :n], True, True, [kik, qdk], [patk])
                    att, atk = tb("att", [64, 4, 64], BF16, cnt)
                    for h in range(4):
                        self.tt("dve", att[:n, h, :n], pat[:n, h * 64:h * 64 + n], msk[:n, mi_c, :n], ALU.mult, [patk, "b_msk"], [atk])
                    po, pok = self.psb("x")
                    for h in range(4):
                        self.mm(po[:, h * 64:h * 64 + n], vt[:n, h * 128:(h + 1) * 128], att[:n, h, :n], True, False, [vtk, atk], [pok])
                        self.mm(po[:, h * 64:h * 64 + n], Sbf[:, h, :], qd[:, h, :n], False, True, ["b_Sbf", qdk], [pok])
                    pds, pdsk = self.psb("x")
                    for h in range(4):
                        self.mm(pds[:64, h * 128:(h + 1) * 128], ko[:n, h * 64:(h + 1) * 64], vt[:n, h * 128:(h + 1) * 128], True, True, [kok, vtk], [pdsk])
                    dcol = (n - 1) if dr == 0 else 0
                    for h in range(4):
                        self.stt("dve", S[:, h, :], S[:, h, :], eb[:, h, dcol:dcol + 1], pds[:64, h * 128:(h + 1) * 128], ALU.mult, ALU.add, ["b_S", ebk, pdsk], ["b_S"])
                    self.cp("act", Sbf[:], S[:], ["b_S"], ["b_Sbf"])
                    if dr == 1:
                        for h in range(4):
                            self.cp("act", obw[:, h, t0:t0 + n], po[:, h * 64:h * 64 + n], [pok], ["b_obw"])
                    else:
                        prr, prrk = self.psb("x")
                        for h in range(4):
                            for kc in range(8):
                                self.mm(prr[:, h * 64:h * 64 + n], wb[:, kc, 1024 + h * 128:1024 + (h + 1) * 128], ut(kc), kc == 0, kc == 7, ["b_w"] + ukeys, [prrk])
                        sr, srk = tb("sr", [128, 4, 64], F32, cnt)
                        for h in range(4):
                            self.act(sr[:, h, :n], prr[:, h * 64:h * 64 + n], AF.Silu, [prrk], [srk])
                        of, ofk = tb("of", [128, 4, 64], F32, cnt)
                        for h in range(4):
                            self.tt("dve", of[:, h, :n], po[:, h * 64:h * 64 + n], obw[:, h, t0:t0 + n], ALU.add, [pok, "b_obw"], [ofk])
                            self.rstd_from([of[:, h, :n]], n, 128, [ofk], self.nrs[:, :n], "nrs", self.nsq, "nsq")
                            self.stt("dve", of[:, h, :n], of[:, h, :n], self.smallc[:, l * 8 + 1:l * 8 + 2], self.nrs[:, :n], ALU.mult, ALU.mult, [ofk, "nrs", "smallc"], [ofk])
                            self.tt("dve", self.oall[:, 4 + h, t0:t0 + n], of[:, h, :n], sr[:, h, :n], ALU.mult, [ofk, srk], ["oall"])

    def resid_block(self, y, ykeys, b, l, gw, hsrc, hdst, bufs, nextnorm=None, final=False):
        hb, hk = bufs
        hv_s = hsrc.rearrange("(c p) t -> p c t", p=128)
        hv_d = hdst.rearrange("(c p) t -> p c t", p=128)
        sl = slice(b * TB, (b + 1) * TB)
        self.dma("sp", hb[:], hv_s[:, :, sl], (), [hk])
        self.rstd_from([y[:, kc, :] for kc in range(8)], TB, D, ykeys, self.nrs[:, :], "nrs", self.nsq, "nsq")
        for kc in range(8):
            self.stt("dve", y[:, kc, :], y[:, kc, :], self.gcol(l, gw, kc), self.nrs[:, :], ALU.mult, ALU.mult,
                     ykeys + ["nrs", "gains"], ykeys)
        self.tt("dve", hb[:], hb[:], y[:], ALU.add, [hk] + ykeys, [hk])
        st = self.dma("sp", hv_d[:, :, sl], hb[:], [hk], [("hdram", b)])
        if nextnorm is not None:
            self.norm_block(hb, hk, b, nextnorm[0], nextnorm[1], "nn")
        return st

    def merge_out(self, l, hsrc, hT, les):
        I = self.I
        with ExitStack() as es:
            merged = self.sb(es, "m_merged", [128, 8, T], BF16)
            with ExitStack() as es2:
                wbr = [self.sb(es2, "m_wbr%d" % i, [128, 16, 128], BF16) for i in range(2)]
                wg = [self.sb(es2, "m_wg%d" % i, [128, 8, 4, 128], BF16) for i in range(2)]
                sig = [self.sb(es2, "m_sig%d" % i, [128, TB], F32) for i in range(2)]
                acc = self.sb(es2, "m_acc", [128, TB], F32)
                prod = self.sb(es2, "m_prod", [128, TB], F32)
                ukeys = [("uT", kc) for kc in range(8)]
                for dc in range(8):
                    wb_, wbk = wbr[dc % 2], ("m_wbr", dc % 2)
                    wg_, wgk = wg[dc % 2], ("m_wg", dc % 2)
                    self.loadw(wb_[:], I["w_branch"][l].rearrange("n (ec p) d -> p (n ec) d", p=128)[:, :, dc * 128:(dc + 1) * 128], wbk)
                    for br in range(4):
                        self.loadw(wg_[:, :, br, :], I["w_in"][l].rearrange("(kc p) n -> p kc n", p=128)
                                   [:, :, O_GATE + br * 1024 + dc * 128:O_GATE + br * 1024 + (dc + 1) * 128], wgk)
                    for b in range(NB):
                        sl = slice(b * TB, (b + 1) * TB)
                        for br in range(4):
                            pg, pgk = self.psb("x")
                            for kc in range(8):
                                self.mm(pg[:, :TB], wg_[:, kc, br, :], self.U(kc, b * TB, (b + 1) * TB), kc == 0, kc == 7, [wgk] + ukeys, [pgk])
                            pp, ppk = self.psb("x")
                            for ec in range(4):
                                self.mm(pp[:, :TB], wb_[:, br * 4 + ec, :], self.oall[:, br * 4 + ec, sl], ec == 0, ec == 3, [wbk, "oall"], [ppk])
                            sg, sgk = sig[br % 2], ("m_sig", br % 2)
                            self.act(sg[:, :], pg[:, :TB], AF.Sigmoid, [pgk], [sgk])
                            if br == 0:
                                self.tt("dve", acc[:, :], pp[:, :TB], sg[:, :], ALU.mult, [ppk, sgk], ["m_acc"])
                            else:
                                self.tt("dve", prod[:, :], pp[:, :TB], sg[:, :], ALU.mult, [ppk, sgk], ["m_prod"])
                                if br < 3:
                                    self.tt("dve", acc[:, :], acc[:, :], prod[:, :], ALU.add, ["m_acc", "m_prod"], ["m_acc"])
                                else:
                                    self.tt("dve", merged[:, dc, sl], acc[:, :], prod[:, :], ALU.add, ["m_acc", "m_prod"], ["m_merged"])
                self.P.barrier()
            if "merged" in self.dbg_out and l == 0:
                with ExitStack() as es3:
                    tmp = self.sb(es3, "dbgtmp2", [128, T], F32)
                    for c in range(8):
                        self.cp("dve", tmp[:], merged[:, c, :], ["m_merged"], ["dbgtmp2"])
                        self.dma("sp", self.dbg_out["merged"][c * 128:(c + 1) * 128, :], tmp[:], ["dbgtmp2"], ())
                    self.P.barrier()
            with ExitStack() as es2:
                wo = self.sb(es2, "o_w", [128, 8, D], BF16)
                y2 = [self.sb(es2, "o_y%d" % i, [128, 8, TB], F32) for i in range(2)]
                hb2 = [self.sb(es2, "o_h%d" % i, [128, 8, TB], F32) for i in range(2)]
                self.nsq = self.sb(es2, "o_sq", [128, 2, TB], F32)
                self.nrs = self.sb(es2, "o_rs", [128, TB], F32)
                self.loadw(wo[:], I["w_out"][l].rearrange("(kc p) n -> p kc n", p=128), "o_w")
                for b in range(NB):
                    y, yk = y2[b % 2], ("o_y", b % 2)
                    sl = slice(b * TB, (b + 1) * TB)
                    for dc in range(8):
                        pst, pk = self.psb("x")
                        for kc in range(8):
                            self.mm(pst[:, :TB], wo[:, kc, dc * 128:(dc + 1) * 128], merged[:, kc, sl], kc == 0, kc == 7, ["o_w", "m_merged"], [pk])
                        self.cp("act", y[:, dc, :], pst[:, :TB], [pk], [yk])
                    self.resid_block(y, [yk], b, l, 1, hsrc, hT, (hb2[b % 2], ("o_h", b % 2)), nextnorm=(l, 2))
                self.P.barrier()

    def ffn(self, l, hT, hdst):
        I = self.I
        finals = []
        NJ = DFF // 128
        for half in range(2):
            with ExitStack() as es:
                actT = self.sb(es, "f_act", [128, NJ, 3 * TB], BF16)
                with ExitStack() as es2:
                    wu = [self.sb(es2, "f_wu%d" % i, [128, 8, 2, 128], BF16) for i in range(2)]
                    cgs = [self.sb(es2, "f_cg%d" % i, [128, TB], F32) for i in range(2)]
                    cvs = [self.sb(es2, "f_cv%d" % i, [128, TB], F32) for i in range(2)]
                    t1s = [self.sb(es2, "f_t1%d" % i, [128, TB], F32) for i in range(2)]
                    wup = I["ffn_w_up"][l].rearrange("(kc p) n -> p kc n", p=128)
                    ukeys = [("uT", kc) for kc in range(8)]
                    cw = self.convw
                    it = 0
                    for j in range(NJ):
                        w_, wk = wu[j % 2], ("f_wu", j % 2)
                        self.loadw(w_[:, :, 0, :], wup[:, :, j * 128:(j + 1) * 128], wk)
                        self.loadw(w_[:, :, 1, :], wup[:, :, DFF + j * 128:DFF + (j + 1) * 128], wk)
                        for b in range(3 * half, 3 * half + 3):
                            it += 1
                            cg, cv, t1 = cgs[it % 2], cvs[it % 2], t1s[it % 2]
                            cgk, cvk, t1k = ("f_cg", it % 2), ("f_cv", it % 2), ("f_t1", it % 2)
                            for gv in range(2):
                                pst, pk = self.psb("x")
                                for kc in range(8):
                                    self.mm(pst[:, :TB + 2], w_[:, kc, gv, :], self.uT[:, kc, b * TB:b * TB + TB + 2], kc == 0, kc == 7, [wk] + ukeys, [pk])
                                ch = gv * NJ + j
                                base = (l * 4) * 44
                                c0 = cw[:, base + ch:base + ch + 1]
                                c1 = cw[:, base + 44 + ch:base + 44 + ch + 1]
                                c2 = cw[:, base + 88 + ch:base + 88 + ch + 1]
                                cb = cw[:, base + 132 + ch:base + 132 + ch + 1]
                                dst, dk = (cg, cgk) if gv == 0 else (cv, cvk)
                                self.act(dst[:, :], pst[:, 0:TB], AF.Identity, [pk, "convw"], [dk], bias=cb, scale=c0)
                                self.stt("dve", dst[:, :], pst[:, 1:TB + 1], c1, dst[:, :], ALU.mult, ALU.add, [pk, dk, "convw"], [dk])
                                self.stt("dve", dst[:, :], pst[:, 2:TB + 2], c2, dst[:, :], ALU.mult, ALU.add, [pk, dk, "convw"], [dk])
                            self.act(t1[:, :], cg[:, :], AF.Gelu_apprx_tanh, [cgk], [t1k])
                            self.tt("pool", actT[:, j, (b - 3 * half) * TB:(b - 3 * half + 1) * TB], t1[:, :], cv[:, :], ALU.mult, [t1k, cvk], ["f_act"])
                    self.P.barrier()
                with ExitStack() as es2:
                    wd = self.sb(es2, "f_wd", [128, NJ, D], BF16)
                    y2 = [self.sb(es2, "f_y%d" % i, [128, 8, TB], F32) for i in range(2)]
                    hb2 = [self.sb(es2, "f_h%d" % i, [128, 8, TB], F32) for i in range(2)]
                    self.nsq = self.sb(es2, "f_sq", [128, 2, TB], F32)
                    self.nrs = self.sb(es2, "f_rs", [128, TB], F32)
                    self.loadw(wd[:], I["ffn_w_down"][l].rearrange("(j p) n -> p j n", p=128), "f_wd")
                    for b in range(3 * half, 3 * half + 3):
                        y, yk = y2[b % 2], ("f_y", b % 2)
                        sl = slice(b * TB, (b + 1) * TB)
                        for dc in range(8):
                            pst, pk = self.psb("x")
                            for j in range(NJ):
                                self.mm(pst[:, :TB], wd[:, j, dc * 128:(dc + 1) * 128], actT[:, j, (b - 3 * half) * TB:(b - 3 * half + 1) * TB], j == 0, j == NJ - 1, ["f_wd", "f_act"], [pk])
                            self.cp("act", y[:, dc, :], pst[:, :TB], [pk], [yk])
                        st = self.resid_block(y, [yk], b, l, 3, hT, hdst, (hb2[b % 2], ("f_h", b % 2)))
                        finals.append(st)
                    self.P.barrier()
        return finals


def host_consts(inp):
    f32 = np.float32
    c = {}
    tab = np.asarray(inp["rel_bias_table"], f32)
    kk = np.arange(128)[:, None]
    jj = np.arange(A_W)[None, :]
    bk = rel_bucket_jax(kk - jj + A_C)
    c["slabA"] = np.ascontiguousarray(np.transpose(tab[bk][:, :, 0:8], (2, 0, 1))).astype(f32)
    ac = np.concatenate([tab[15, 0:8], tab[31, 0:8]])
    c["aconst"] = np.ascontiguousarray(np.broadcast_to(ac[None, :], (128, 16))).astype(f32)
    tabd = tab[:, 8:16]
    jj = np.arange(D_W)[None, :]
    rel = kk - jj + D_C
    tz = np.where((np.abs(rel) <= 128)[:, :, None], tabd[rel_bucket_jax(rel)], f32(NEG))
    qq = np.arange(256)[None, :]
    rel0 = kk - qq
    t0 = np.where(((np.abs(rel0) <= 128) & (kk >= NMETA))[:, :, None], tabd[rel_bucket_jax(rel0)], f32(NEG))
    tm = np.where((kk < NMETA)[:, :, None], tabd[rel_bucket_jax(rel0)], f32(NEG))
    c["slabD"] = np.ascontiguousarray(np.transpose(np.concatenate([tz, t0, tm], axis=1), (2, 0, 1))).astype(f32)
    c["dconst"] = np.ascontiguousarray(np.broadcast_to(tabd[15][None, :], (128, 8))).astype(f32)
    half = 32
    inv = (10000.0 ** (-np.arange(half, dtype=np.float32) / half)).astype(f32)
    ang = np.arange(T, dtype=f32)[None, :] * inv[:, None]
    cos, sin = np.cos(ang).astype(f32), np.sin(ang).astype(f32)
    c["rope"] = np.ascontiguousarray(np.concatenate([np.concatenate([cos, cos], 0), np.concatenate([-sin, sin], 0)], 1)).astype(f32)
    s = np.arange(128)[:, None]
    t = np.arange(128)[None, :]
    same = (s // 64) == (t // 64)
    LT = (same & (s <= t)).astype(f32)
    L = (same & (s >= t)).astype(f32)
    SU = (same & (s > t)).astype(f32)
    SL = (same & (s < t)).astype(f32)
    c["glam"] = np.ascontiguousarray(np.concatenate([LT, L, SU, SL], 1))
    return c


def host_layout(inp):
    f32 = np.float32
    g = {}
    sw = np.concatenate([np.arange(32, 64), np.arange(0, 32)])
    wq = np.asarray(inp["mla_w_q_up"], f32).reshape(DEPTH, 256, 4, 192)
    g["mla_w_q_up_sw"] = np.ascontiguousarray(wq[:, :, :, 128:][:, :, :, sw].reshape(DEPTH, 256, 256))
    g["w_in_kr_sw"] = np.ascontiguousarray(np.asarray(inp["w_in"])[:, :, OC_KR:OC_KR + 64][:, :, sw])
    gains = np.stack([inp["norm_mix_pre"], inp["norm_mix_post"], inp["norm_ffn_pre"], inp["norm_ffn_post"]], 1)
    g["gains"] = np.ascontiguousarray(gains.reshape(DEPTH * 4 * 8, 128).T).astype(f32)
    cw = np.concatenate([np.asarray(inp["ffn_conv_w"], f32), np.asarray(inp["ffn_conv_b"], f32)[:, None, :]], 1)
    g["convw"] = np.ascontiguousarray(cw.reshape(DEPTH * 4 * 44, 128).T).astype(f32)
    sc = np.zeros((DEPTH, 8, 128), f32)
    sc[:, 0] = inp["diff_subln"]
    sc[:, 1] = inp["gla_norm"]
    sc[:, 2] = inp["mla_kv_norm"]
    sc[:, 3:5] = np.asarray(inp["mla_q_norm"]).reshape(DEPTH, 2, 128)
    g["smallc"] = np.ascontiguousarray(sc.reshape(DEPTH * 8, 128).T)
    g["lamrep"] = np.ascontiguousarray(np.broadcast_to(np.asarray(inp["diff_lambda"], f32).reshape(1, DEPTH * 256), (128, DEPTH * 256)))
    g["gbias"] = np.ascontiguousarray(np.broadcast_to(np.asarray(inp["gla_gate_bias"], f32).reshape(1, DEPTH * 512), (128, DEPTH * 512)))
    g["sinkrep"] = np.ascontiguousarray(np.broadcast_to(np.asarray(inp["swa_sinks"], f32).reshape(1, DEPTH * 8), (128, DEPTH * 8)))
    return g


_NC_CACHE = {}


def get_nc(layers=(0, 1), dbg=None):
    key = (tuple(layers), tuple(sorted((dbg or {}).items())))
    if key not in _NC_CACHE:
        nc = bass.Bass("TRN2", target_bir_lowering=False)
        KB(nc, dbg).build(layers)
        _NC_CACHE[key] = nc
    return _NC_CACHE[key]


def make_in_maps(inp, cores):
    shared = {}
    for k in ("w_in", "w_branch", "w_out", "ffn_w_up", "ffn_w_down", "mla_w_q_up", "mla_w_kv_up", "gla_gate_up"):
        shared[k] = np.ascontiguousarray(np.asarray(inp[k], np.float32))
    shared.update(host_layout(inp))
    shared.update(host_consts(inp))
    meta = np.asarray(inp["meta_tokens"], np.float32)
    x = np.asarray(inp["x"], np.float32)
    maps = []
    for b in cores:
        h0 = np.concatenate([meta, x[b]], axis=0)
        m = dict(shared)
        m["h0T"] = np.ascontiguousarray(h0.T)
        maps.append(m)
    return maps


def kernel(**inputs):
    nc = get_nc()
    maps = make_in_maps(inputs, list(range(8)))
    res = run_bass_kernel_spmd(nc, maps, core_ids=list(range(8)))
    out = np.stack([np.ascontiguousarray(r["outT"][:, NMETA:].T) for r in res.results], axis=0)
    return out.astype(np.float32)
```

```python
import math
import os
import numpy as np
from contextlib import ExitStack
import concourse.bass as bass
import concourse.mybir as mybir
from concourse.bass_utils import run_bass_kernel_spmd

F32 = mybir.dt.float32
BF16 = mybir.dt.bfloat16
AF = mybir.ActivationFunctionType
ALU = mybir.AluOpType

DEPTH = 2
D = 1024
SEQ = 2048
NMETA = 16
T = SEQ + NMETA
TB = 344
NB = 6
NT = 17
EPS = 1e-6
DFF = 2816
NIN = 8416
OA_Q, OA_K, OA_V = 0, 512, 1024
OB_Q, OB_K, OB_V, OB_R, OB_G = 1536, 1792, 2048, 2560, 3072
OC_QA, OC_KVA, OC_KR = 3104, 3360, 3488
OD_Q, OD_K, OD_V = 3552, 4064, 4192
O_GATE = 4320
NEG = -30000.0


def trows(j):
    return 128 if j < 16 else 16


class Dep:
    __slots__ = ("w", "r")

    def __init__(self):
        self.w = None
        self.r = []


class Op:
    __slots__ = ("eng", "fn", "deps", "ms", "val", "sem", "is_dma")

    def __init__(self, eng, fn, is_dma):
        self.eng = eng
        self.fn = fn
        self.deps = []
        self.ms = False
        self.val = 0
        self.sem = None
        self.is_dma = is_dma


ENGS = ("pe", "act", "dve", "pool", "sp")
NDMASEM = 8


class Prog:
    def __init__(self, nc, es):
        self.nc = nc
        self.es = es
        self.ops = {e: [] for e in ENGS}
        self.dma_hist = {e: [] for e in ENGS}
        self.dd = {}

    def D(self, key):
        d = self.dd.get(key)
        if d is None:
            d = self.dd[key] = Dep()
        return d

    def op(self, eng, fn, r=(), w=(), dma=False, extra=()):
        o = Op(eng, fn, dma)
        need = list(extra)
        for k in r:
            d = self.D(k)
            if d.w is not None:
                need.append(d.w)
        for k in w:
            d = self.D(k)
            if d.w is not None:
                need.append(d.w)
            for q in d.r:
                need.append(q)
        if dma:
            h = self.dma_hist[eng]
            if len(h) >= NDMASEM:
                need.append(h[-NDMASEM])
            h.append(o)
        seen = set()
        for p in need:
            if p is o or id(p) in seen:
                continue
            seen.add(id(p))
            if (not dma) and eng == "pe" and p.eng == "pe" and not p.is_dma:
                continue
            o.deps.append(p)
        for k in r:
            lst = self.D(k).r
            if not dma:
                lst[:] = [q for q in lst if q.is_dma or q.eng != eng]
            lst.append(o)
        for k in w:
            d = self.D(k)
            d.w = o
            d.r = []
        self.ops[eng].append(o)
        return o

    def barrier(self):
        lasts = []
        for e in ENGS:
            cl = [o for o in self.ops[e] if not o.is_dma]
            if cl:
                lasts.append(cl[-1])
            lasts.extend(self.dma_hist[e][-NDMASEM:])
        for e in ENGS:
            if self.ops[e]:
                self.op(e, lambda eng: eng.nop(), extra=lasts)

    def finalize(self, final_ops=()):
        nc, es = self.nc, self.es
        for e in ENGS:
            for o in self.ops[e]:
                for p in o.deps:
                    p.ms = True
        esem = {e: es.enter_context(nc.semaphore("s_" + e)) for e in ENGS}
        dsem = {e: [es.enter_context(nc.semaphore("d_%s%d" % (e, i))) for i in range(NDMASEM)]
                for e in ENGS if self.dma_hist[e]}
        for e in ENGS:
            cnt = 0
            dcnt = [0] * NDMASEM
            k = 0
            for o in self.ops[e]:
                if o.is_dma:
                    s = k % NDMASEM
                    k += 1
                    dcnt[s] += 16
                    o.sem = dsem[e][s]
                    o.val = dcnt[s]
                elif o.ms:
                    cnt += 1
                    o.sem = esem[e]
                    o.val = cnt
        engobj = {"pe": "tensor", "act": "scalar", "dve": "vector", "pool": "gpsimd", "sp": "sync"}
        block = es.enter_context(nc.Block())

        def emit(e):
            def body(eng):
                known = {}
                for o in self.ops[e]:
                    wl = {}
                    for p in o.deps:
                        key = id(p.sem)
                        if known.get(key, 0) >= p.val:
                            continue
                        if key not in wl or wl[key][1] < p.val:
                            wl[key] = (p.sem, p.val)
                    for key, (s, v) in wl.items():
                        eng.wait_ge(s, v)
                        known[key] = v
                    ins = o.fn(eng)
                    if o.is_dma:
                        ins.then_inc(o.sem, 16)
                    elif o.ms:
                        ins.then_inc(o.sem, 1)
                if e == "sp":
                    for o in final_ops:
                        eng.wait_ge(o.sem, o.val)
            return body

        for e in ENGS:
            if self.ops[e] or e == "sp":
                getattr(block, engobj[e])(emit(e))


def rel_bucket(rel):
    rel = np.asarray(rel, dtype=np.int64)
    half, max_exact = 16, 8
    ret = np.where(rel > 0, half, 0)
    n = np.abs(rel)
    nf = np.maximum(n, 1).astype(np.float32)
    large = max_exact + (np.log(nf / np.float32(max_exact)) / np.float32(math.log(128 / max_exact))
                         * (half - max_exact)).astype(np.int32)
    large = np.minimum(large, half - 1)
    return ret + np.where(n < max_exact, n, large)


def rel_bucket_jax(rel):
    import jax
    import jax.numpy as jnp
    with jax.default_device(jax.devices("cpu")[0]):
        rel = jnp.asarray(np.asarray(rel, dtype=np.int32))
        half, max_exact = 16, 8
        ret = jnp.where(rel > 0, half, 0)
        n = jnp.abs(rel)
        nf = jnp.maximum(n, 1).astype(jnp.float32)
        large = max_exact + (jnp.log(nf / max_exact) / math.log(128 / max_exact) * (half - max_exact)).astype(jnp.int32)
        large = jnp.minimum(large, half - 1)
        return np.asarray(ret + jnp.where(n < max_exact, n, large))


A_OS = sorted(set(TB * b - 128 * j for b in range(NB) for j in range(NT)))
A_NEAR = [o for o in A_OS if not (127 - o <= -91 or -o - (TB - 1) >= 91)]
A_C = -min(A_NEAR)
A_W = TB + max(A_NEAR) + A_C
D_QB = [(256 * i, 256) for i in range(8)] + [(2048, 16)]
D_C = 256
D_W = 256 + 384
D_SLABW = D_W + 256 + 256


class KB:
    def __init__(self, nc, dbg=None):
        self.nc = nc
        self.dbg = dbg or {}

    def mm(self, out, lhsT, rhs, start, stop, r, w):
        return self.P.op("pe", lambda e: e.matmul(out, lhsT=lhsT, rhs=rhs, start=start, stop=stop), r, w)

    def act(self, out, in_, func, r, w, bias=None, scale=1.0, accum=None):
        def f(e):
            kw = {}
            if bias is not None:
                kw["bias"] = bias
            if accum is not None:
                kw["accum_out"] = accum
            return e.activation(out=out, in_=in_, func=func, scale=scale, **kw)
        return self.P.op("act", f, r, w)

    def stt(self, eng, out, in0, scalar, in1, op0, op1, r, w):
        nm = {"dve": "vector", "pool": "gpsimd"}[eng]
        return self.P.op(eng, lambda e: e.scalar_tensor_tensor(out=out, in0=in0, scalar=scalar, in1=in1, op0=op0, op1=op1), r, w)

    def ts(self, eng, out, in0, s1, s2, op0, op1, r, w):
        if s2 is None:
            return self.P.op(eng, lambda e: e.tensor_scalar(out=out, in0=in0, scalar1=s1, scalar2=None, op0=op0), r, w)
        return self.P.op(eng, lambda e: e.tensor_scalar(out=out, in0=in0, scalar1=s1, scalar2=s2, op0=op0, op1=op1), r, w)

    def tt(self, eng, out, in0, in1, op, r, w):
        return self.P.op(eng, lambda e: e.tensor_tensor(out=out, in0=in0, in1=in1, op=op), r, w)

    def cp(self, eng, out, in_, r, w):
        if eng == "act":
            return self.P.op("act", lambda e: e.copy(out=out, in_=in_), r, w)
        return self.P.op(eng, lambda e: e.tensor_copy(out=out, in_=in_), r, w)

    def recip(self, out, in_, r, w):
        return self.P.op("dve", lambda e: e.reciprocal(out=out, in_=in_), r, w)

    def memset(self, eng, ap, val, w):
        return self.P.op(eng, lambda e: e.memset(ap, val), (), w)

    def dma(self, q, out, in_, r, w):
        return self.P.op(q, lambda e: e.dma_start(out=out, in_=in_), r, w, dma=True)

    def sb(self, es, name, shape, dt):
        self.sbcnt = getattr(self, "sbcnt", 0) + 1
        return es.enter_context(self.nc.sbuf_tensor("sb%d_%s" % (self.sbcnt, name), shape, dt))

    def U(self, kc, t0, t1):
        return self.uT[:, kc, t0 + 1:t1 + 1]

    def psb(self, group):
        lst = self.psgroups[group]
        i = self.psidx.get(group, 0)
        self.psidx[group] = i + 1
        b = lst[i % len(lst)]
        return self.ps[b], ("ps", b)

    def rstd_from(self, srcs, n, Dn, rkeys, out_ap, out_key, sq_ap, sq_key, sq_eng="act"):
        pst, pk = self.ps[7], ("ps", 7)
        for i, s in enumerate(srcs):
            self.act(sq_ap[:, i % 2, :n], s, AF.Square, rkeys, [(sq_key, i % 2)])
            self.mm(pst[:, :n], self.onesf[:], sq_ap[:, i % 2, :n], i == 0, i == len(srcs) - 1, [(sq_key, i % 2), "onesf"], [pk])
        self.ts("dve", out_ap, pst[:, :n], 1.0 / Dn, EPS, ALU.mult, ALU.add, [pk], [out_key])
        self.recip(out_ap, out_ap, [out_key], [out_key])
        self.act(out_ap, out_ap, AF.Sqrt, [out_key], [out_key])

    def loadw(self, dst, src, wkey):
        return self.dma("pool", dst, src, (), [wkey])

    def build(self, layers=(0, 1)):
        nc = self.nc
        I = {}

        def din(name, shape):
            I[name] = nc.dram_tensor(name, list(shape), F32, kind="ExternalInput").ap()
            return I[name]

        din("h0T", [D, T])
        din("w_in", [DEPTH, D, NIN])
        din("w_branch", [DEPTH, 4, 512, D])
        din("w_out", [DEPTH, D, D])
        din("ffn_w_up", [DEPTH, D, 2 * DFF])
        din("ffn_w_down", [DEPTH, DFF, D])
        din("mla_w_q_up", [DEPTH, 256, 768])
        din("mla_w_q_up_sw", [DEPTH, 256, 256])
        din("mla_w_kv_up", [DEPTH, 128, 1024])
        din("w_in_kr_sw", [DEPTH, D, 64])
        din("gla_gate_up", [DEPTH, 2, 16, 256])
        din("gains", [128, DEPTH * 4 * 8])
        din("convw", [128, DEPTH * 4 * 44])
        din("smallc", [128, DEPTH * 8])
        din("lamrep", [128, DEPTH * 256])
        din("gbias", [128, DEPTH * 512])
        din("sinkrep", [128, DEPTH * 8])
        din("aconst", [128, 16])
        din("dconst", [128, 8])
        din("slabA", [8, 128, A_W])
        din("slabD", [8, 128, D_SLABW])
        din("rope", [64, 2 * T])
        din("glam", [128, 4 * 128])
        outT = nc.dram_tensor("outT", [D, T], F32, kind="ExternalOutput").ap()
        hT = nc.dram_tensor("hT_scr", [D, T], F32, kind="Internal").ap()
        dbg_out = {}
        for k, shp in self.dbg.items():
            dbg_out[k] = nc.dram_tensor("dbg_" + k, list(shp), F32, kind="ExternalOutput").ap()
        self.dbg_out = dbg_out
        self.I = I

        with ExitStack() as es:
            P = self.P = Prog(nc, es)
            self.ps = [es.enter_context(nc.psum_tensor("ps%d" % i, [128, 512], F32)) for i in range(8)]
            self.psgroups = {"s": [0, 1, 2], "o": [3, 4], "d": [5, 6], "x": [0, 1, 2, 3, 4, 5, 6]}
            self.psidx = {}
            self.uT = self.sb(es, "uT", [128, 8, T + 2], BF16)
            self.onesf = self.sb(es, "onesf", [128, 128], F32)
            self.onesb = self.sb(es, "onesb", [128, 128], BF16)
            self.gains = self.sb(es, "gains", [128, DEPTH * 32], F32)
            self.convw = self.sb(es, "convw", [128, DEPTH * 4 * 44], F32)
            self.smallc = self.sb(es, "smallc", [128, DEPTH * 8], F32)
            self.aconst = self.sb(es, "aconst", [128, 16], F32)
            self.dconst = self.sb(es, "dconst", [128, 8], F32)
            self.memset("dve", self.onesf[:], 1.0, ["onesf"])
            self.memset("dve", self.onesb[:], 1.0, ["onesb"])
            self.memset("dve", self.uT[:, :, 0:1], 0.0, ["uT"])
            self.memset("dve", self.uT[:, :, T + 1:T + 2], 0.0, ["uT"])
            self.dma("sp", self.gains[:], I["gains"], (), ["gains"])
            self.dma("sp", self.convw[:], I["convw"], (), ["convw"])
            self.dma("sp", self.smallc[:], I["smallc"], (), ["smallc"])
            self.dma("sp", self.aconst[:], I["aconst"], (), ["aconst"])
            self.dma("sp", self.dconst[:], I["dconst"], (), ["dconst"])

            finals = []
            nl = len(layers)
            for li, l in enumerate(layers):
                hsrc = I["h0T"] if li == 0 else hT
                last = (li == nl - 1)
                self.norm1(l, hsrc)
                with ExitStack() as les:
                    self.oall = self.sb(les, "oall", [128, 16, T], BF16)
                    self.mixers(l)
                    if not os.environ.get("ONLYMIX"):
                        self.merge_out(l, hsrc, hT, les)
                P.barrier()
                if not os.environ.get("ONLYMIX"):
                    finals += self.ffn(l, hT, outT if last else hT)
                P.barrier()
            P.finalize(finals)
        return nc

    def gcol(self, l, which, kc):
        i = (l * 4 + which) * 8 + kc
        return self.gains[:, i:i + 1]

    def norm_block(self, hb, hkey, b, gl, gw, tag):
        sq, rs = self.nsq, self.nrs
        self.rstd_from([hb[:, kc, :] for kc in range(8)], TB, D, [hkey], rs[:, :], "nrs", sq, "nsq")
        for kc in range(8):
            self.stt("dve", self.U(kc, b * TB, (b + 1) * TB), hb[:, kc, :], self.gcol(gl, gw, kc), rs[:, :],
                     ALU.mult, ALU.mult, [hkey, "nrs", "gains"], [("uT", kc)])

    def norm1(self, l, hsrc):
        with ExitStack() as es:
            hb2 = [self.sb(es, "n1h%d" % i, [128, 8, TB], F32) for i in range(2)]
            self.nsq = self.sb(es, "n1sq", [128, 2, TB], F32)
            self.nrs = self.sb(es, "n1rs", [128, TB], F32)
            hv = hsrc.rearrange("(c p) t -> p c t", p=128)
            for b in range(NB):
                hb = hb2[b % 2]
                hk = ("n1h", b % 2)
                self.dma("sp", hb[:], hv[:, :, b * TB:(b + 1) * TB], (), [hk])
                self.norm_block(hb, hk, b, l, 0, "n1")
            self.P.barrier()

    def mixers(self, l):
        import os
        sel = os.environ.get("MIX", "cadb")
        for nm, fn, c0 in (("c", self.mix_c, 8), ("a", self.mix_a, 0), ("d", self.mix_d, 12), ("b", self.mix_b, 4)):
            if nm in sel:
                fn(l)
            else:
                self.memset("dve", self.oall[:, c0:c0 + 4, :], 0.0, ["oall"])
            self.P.barrier()
        if "oall" in self.dbg_out and l == 0:
            self.dbgdump_oall()

    def dbgdump_oall(self):
        with ExitStack() as es:
            tmp = self.sb(es, "dbgtmp", [128, T], F32)
            for c in range(16):
                self.cp("dve", tmp[:], self.oall[:, c, :], ["oall"], ["dbgtmp"])
                self.dma("sp", self.dbg_out["oall"][c * 128:(c + 1) * 128, :], tmp[:], ["dbgtmp"], ())
            self.P.barrier()

    def proj_fm(self, w, wkey, ncontr, col0, M, rhs_fn, rkeys, evac):
        for b in range(NB):
            pst, pk = self.psb("x")
            for kc in range(ncontr):
                self.mm(pst[:M, :TB], w[:, kc, col0:col0 + M], rhs_fn(kc, b), kc == 0, kc == ncontr - 1,
                        [wkey] + rkeys, [pk])
            evac(b, pst[:M, :TB], pk)

    def proj_tm(self, w, wkey, ncontr, col0, N, lhs_fn, rkeys, tiles, evac):
        for j, (t0, n) in enumerate(tiles):
            pst, pk = self.psb("x")
            for kc in range(ncontr):
                self.mm(pst[:n, :N], lhs_fn(kc, t0, n), w[:, kc, col0:col0 + N], kc == 0, kc == ncontr - 1,
                        [wkey] + rkeys, [pk])
            evac(j, n, pst[:n, :N], pk)

    def attn_stream(self, tag, jobs, ptbuf, LOOK=2, DEFER=5):
        flat = []
        for ji, jb in enumerate(jobs):
            n = len(jb["tiles"])
            for i, tl in enumerate(jb["tiles"]):
                flat.append((ji, i, n, tl))
        pend = []
        state = {}
        pts = {}

        def issue_s(idx):
            ji, i, n, tl = flat[idx]
            jb = jobs[ji]
            if i == 0 and jb.get("pre") is not None:
                jb["pre"]()
            qn, scale = jb["qn"], jb["scale"]
            kr = tl["rows"]
            pst, pk = self.psb("s")
            nm = len(tl["mms"])
            for mi, (lt, rh) in enumerate(tl["mms"]):
                self.mm(pst[:kr, :qn], lt, rh, mi == 0, mi == nm - 1, jb["rkeys"], [pk])
            pi = self.ptidx
            self.ptidx += 1
            pt = ptbuf[pi % len(ptbuf)]
            ptk = (tag + "pt", pi % len(ptbuf))
            bias = tl["bias"]
            if bias is None:
                self.act(pt[:kr, :qn], pst[:kr, :qn], AF.Exp, [pk], [ptk], scale=scale)
            elif bias[0] == "c":
                self.act(pt[:kr, :qn], pst[:kr, :qn], AF.Exp, [pk] + bias[2], [ptk], bias=bias[1][:kr, :], scale=scale)
            else:
                tmp = self.sbias[pi % len(self.sbias)]
                tk = (tag + "sb", pi % len(self.sbias))
                self.stt("dve", tmp[:kr, :qn], pst[:kr, :qn], scale, bias[1], ALU.mult, ALU.add, [pk] + bias[2], [tk])
                self.act(pt[:kr, :qn], tmp[:kr, :qn], AF.Exp, [tk], [ptk])
            pts[idx] = (pt, ptk)

        def issue_pv(idx):
            ji, i, n, tl = flat[idx]
            jb = jobs[ji]
            qn, o_M = jb["qn"], jb["o_M"]
            kr = tl["rows"]
            if i == 0:
                state[ji] = self.psb("o") + self.psb("d")
            ops_, ok, dps, dk = state[ji]
            pt, ptk = pts.pop(idx)
            vl, vkeys = jb["v_fn"](tl)
            self.mm(ops_[:o_M, :qn], vl, pt[:kr, :qn], i == 0, i == n - 1, [ptk] + vkeys, [ok])
            self.mm(dps[:o_M, :qn], jb["ones_fn"](tl), pt[:kr, :qn], i == 0, i == n - 1, [ptk, "onesb"] + jb.get("okeys", []), [dk])
            if i == n - 1:
                jb["fin1"](ops_, ok, dps, dk)
                if jb.get("fin2") is not None:
                    pend.append((idx + DEFER, jb["fin2"]))
                del state[ji]

        N = len(flat)
        for idx in range(N + LOOK):
            if idx < N:
                issue_s(idx)
            if idx - LOOK >= 0:
                issue_pv(idx - LOOK)
            while pend and pend[0][0] <= idx - LOOK:
                pend.pop(0)[1]()
        for _, fn in pend:
            fn()

    def mix_c(self, l):
        I = self.I
        with ExitStack() as es:
            wc = self.sb(es, "c_w", [128, 8, 512], BF16)
            wq = self.sb(es, "c_wq", [128, 2, 768 + 256], BF16)
            wkv = self.sb(es, "c_wkv", [128, 1, 1024], BF16)
            wvv = self.sb(es, "c_wvv", [128, 1, 512], BF16)
            lat = self.sb(es, "c_lat", [128, 3, TB], F32)
            qn = self.sb(es, "c_qn", [128, 2, T], BF16)
            kvn = self.sb(es, "c_kvn", [128, T], BF16)
            kpe = self.sb(es, "c_kpe", [128, T], BF16)
            rope = self.sb(es, "c_rope", [64, 2 * T], F32)
            vtok = self.sb(es, "c_v", [128, NT, 512], BF16)
            qno = self.sb(es, "c_qno", [128, 2, T], BF16)
            qpe = self.sb(es, "c_qpe", [128, 2, T], BF16)
            kno = self.sb(es, "c_kno", [128, 2, T], BF16)
            ptbuf = [self.sb(es, "c_pt%d" % i, [128, TB], BF16) for i in range(3)]
            self.nsq = self.sb(es, "c_sq", [128, 2, TB], F32)
            self.nrs = self.sb(es, "c_rs", [128, TB], F32)
            t1 = self.sb(es, "c_t1", [64, TB], F32)
            t2 = self.sb(es, "c_t2", [64, TB], F32)
            rd = [self.sb(es, "c_rd%d" % i, [128, TB], F32) for i in range(2)]
            win = I["w_in"][l].rearrange("(kc p) n -> p kc n", p=128)
            self.loadw(wc[:, :, 0:448], win[:, :, OC_QA:OC_QA + 448], "c_w")
            self.loadw(wc[:, :, 448:512], I["w_in_kr_sw"][l].rearrange("(kc p) n -> p kc n", p=128), "c_w")
            self.loadw(wq[:, :, 0:768], I["mla_w_q_up"][l].rearrange("(kc p) n -> p kc n", p=128), "c_wq")
            self.loadw(wq[:, :, 768:1024], I["mla_w_q_up_sw"][l].rearrange("(kc p) n -> p kc n", p=128), "c_wq")
            self.loadw(wkv[:, 0, :], I["mla_w_kv_up"][l], "c_wkv")
            self.loadw(wvv[:, 0, :].rearrange("p (h e) -> p h e", h=4),
                       I["mla_w_kv_up"][l].rearrange("p (h e) -> p h e", h=4)[:, :, 128:256], "c_wvv")
            self.dma("sp", rope[:], I["rope"], (), ["c_rope"])
            self.memset("dve", kpe[64:128, :], 0.0, ["c_kpe"])
            for i_ in range(2):
                self.memset("dve", qpe[64:128, i_, :], 0.0, [("c_qpe", i_)])
            ukeys = [("uT", kc) for kc in range(8)]
            urhs = lambda kc, b: self.U(kc, b * TB, (b + 1) * TB)
            for b in range(NB):
                sl = slice(b * TB, (b + 1) * TB)
                pst, pk = self.psb("x")
                pst2, pk2 = self.psb("x")
                for kc in range(8):
                    self.mm(pst[:64, :TB], wc[:, kc, 384:448], urhs(kc, b), kc == 0, kc == 7, ["c_w"] + ukeys, [pk])
                for kc in range(8):
                    self.mm(pst2[:64, :TB], wc[:, kc, 448:512], urhs(kc, b), kc == 0, kc == 7, ["c_w"] + ukeys, [pk2])
                self.tt("dve", t1[:, :], pst[:64, :TB], rope[:, sl], ALU.mult, [pk, "c_rope"], ["c_t1"])
                self.tt("dve", t2[:, :], pst2[:64, :TB], rope[:, T + b * TB:T + (b + 1) * TB], ALU.mult, [pk2, "c_rope"], ["c_t2"])
                self.tt("dve", kpe[0:64, sl], t1[:, :], t2[:, :], ALU.add, ["c_t1", "c_t2"], ["c_kpe"])
            sc = self.smallc
            for b in range(NB):
                sl = slice(b * TB, (b + 1) * TB)
                for ci in range(3):
                    pst, pk = self.psb("x")
                    for kc in range(8):
                        self.mm(pst[:, :TB], wc[:, kc, ci * 128:(ci + 1) * 128], urhs(kc, b), kc == 0, kc == 7, ["c_w"] + ukeys, [pk])
                    self.cp("act", lat[:, ci, :], pst[:, :TB], [pk], [("c_lat", ci)])
                self.rstd_from([lat[:, 0, :], lat[:, 1, :]], TB, 256, [("c_lat", 0), ("c_lat", 1)], self.nrs[:, :], "nrs", self.nsq, "nsq")
                for ci in range(2):
                    self.stt("dve", qn[:, ci, sl], lat[:, ci, :], sc[:, l * 8 + 3 + ci:l * 8 + 4 + ci], self.nrs[:, :],
                             ALU.mult, ALU.mult, [("c_lat", ci), "nrs", "smallc"], ["c_qn"])
                self.rstd_from([lat[:, 2, :]], TB, 128, [("c_lat", 2)], self.nrs[:, :], "nrs", self.nsq, "nsq")
                self.stt("dve", kvn[:, sl], lat[:, 2, :], sc[:, l * 8 + 2:l * 8 + 3], self.nrs[:, :],
                         ALU.mult, ALU.mult, [("c_lat", 2), "nrs", "smallc"], ["c_kvn"])
            tiles = [(128 * j, trows(j)) for j in range(NT)]
            self.proj_tm(wvv, "c_wvv", 1, 0, 512, lambda kc, t0, n: kvn[:, t0:t0 + n], ["c_kvn"], tiles,
                         lambda j, n, ps, pk: self.cp("act", vtok[:n, j, :], ps, [pk], ["c_v"]))
            scale = (128 + 64) ** -0.5
            self.ptidx = 0
            qrhs = lambda kc, b: qn[:, kc, b * TB:(b + 1) * TB]

            def cproj(h):
                hb_ = h % 2
                self.proj_fm(wq, "c_wq", 2, h * 192, 128, qrhs, ["c_qn"],
                             lambda b, ps, pk: self.cp("act", qno[:, hb_, b * TB:(b + 1) * TB], ps, [pk], [("c_qno", hb_)]))
                for b in range(NB):
                    sl = slice(b * TB, (b + 1) * TB)
                    pst, pk = self.psb("x")
                    pst2, pk2 = self.psb("x")
                    for kc in range(2):
                        self.mm(pst[:64, :TB], wq[:, kc, h * 192 + 128:h * 192 + 192], qrhs(kc, b), kc == 0, kc == 1, ["c_wq", "c_qn"], [pk])
                    for kc in range(2):
                        self.mm(pst2[:64, :TB], wq[:, kc, 768 + h * 64:768 + h * 64 + 64], qrhs(kc, b), kc == 0, kc == 1, ["c_wq", "c_qn"], [pk2])
                    self.tt("dve", t1[:, :], pst[:64, :TB], rope[:, sl], ALU.mult, [pk, "c_rope"], ["c_t1"])
                    self.tt("dve", t2[:, :], pst2[:64, :TB], rope[:, T + b * TB:T + (b + 1) * TB], ALU.mult, [pk2, "c_rope"], ["c_t2"])
                    self.tt("dve", qpe[0:64, hb_, sl], t1[:, :], t2[:, :], ALU.add, ["c_t1", "c_t2"], [("c_qpe", hb_)])
                self.proj_fm(wkv, "c_wkv", 1, h * 256, 128, lambda kc, b: kvn[:, b * TB:(b + 1) * TB], ["c_kvn"],
                             lambda b, ps, pk: self.cp("act", kno[:, hb_, b * TB:(b + 1) * TB], ps, [pk], [("c_kno", hb_)]))

            cproj(0)
            for h in range(4):
                if h + 1 < 4:
                    cproj(h + 1)
                hb_ = h % 2
                jobs = []
                for b in range(NB):
                    q0 = b * TB
                    tl = []
                    for j in range(NT):
                        kr = trows(j)
                        tl.append(dict(rows=kr, j=j, bias=None,
                                       mms=[(kno[:, hb_, 128 * j:128 * j + kr], qno[:, hb_, q0:q0 + TB]),
                                            (kpe[:, 128 * j:128 * j + kr], qpe[:, hb_, q0:q0 + TB])]))

                    def fin1(ops_, ok, dps, dk, q0=q0, h=h, b=b):
                        rd_ = rd[b % 2]
                        self.recip(rd_[:, :], dps[:, :TB], [dk], [("c_rd", b % 2)])
                        self.tt("dve", self.oall[:, 8 + h, q0:q0 + TB], ops_[:, :TB], rd_[:, :], ALU.mult, [ok, ("c_rd", b % 2)], ["oall"])
                    jobs.append(dict(qn=TB, tiles=tl, o_M=128, scale=scale, fin1=fin1, fin2=None,
                                     v_fn=lambda t, h=h: (vtok[:t["rows"], t["j"], h * 128:(h + 1) * 128], ["c_v"]),
                                     ones_fn=lambda t: self.onesb[:t["rows"], :],
                                     rkeys=[("c_kno", hb_), ("c_qno", hb_), "c_kpe", ("c_qpe", hb_)]))
                self.attn_stream("c", jobs, ptbuf)

    def mix_a(self, l):
        I = self.I
        lam_init = 0.8 - 0.6 * math.exp(-0.3 * l)
        with ExitStack() as es:
            vtok = self.sb(es, "a_v", [128, NT, 512], BF16)
            qT = self.sb(es, "a_q", [128, 8, T], BF16)
            kT = self.sb(es, "a_k", [128, 4, T], BF16)
            lam = self.sb(es, "a_lam", [128, 256], F32)
            lt = self.sb(es, "a_lt", [128, 128], F32)
            lv = self.sb(es, "a_lv", [128, 4], F32)
            gsub = self.sb(es, "a_gs", [128, 1], F32)
            win = I["w_in"][l].rearrange("(kc p) n -> p kc n", p=128)
            self.dma("sp", lam[:], I["lamrep"][:, l * 256:(l + 1) * 256], (), ["a_lam"])
            self.tt("dve", lt[:, 0:64], lam[:, 0:64], lam[:, 64:128], ALU.mult, ["a_lam"], ["a_lt"])
            self.tt("dve", lt[:, 64:128], lam[:, 128:192], lam[:, 192:256], ALU.mult, ["a_lam"], ["a_lt"])
            self.P.op("dve", lambda e: e.reduce_sum(out=lv[:, 0:1], in_=lt[:, 0:64], axis=mybir.AxisListType.X), ["a_lt"], ["a_lv"])
            self.P.op("dve", lambda e: e.reduce_sum(out=lv[:, 1:2], in_=lt[:, 64:128], axis=mybir.AxisListType.X), ["a_lt"], ["a_lv"])
            self.act(lv[:, 0:2], lv[:, 0:2], AF.Exp, ["a_lv"], ["a_lv"])
            self.tt("dve", lv[:, 2:3], lv[:, 1:2], lv[:, 0:1], ALU.subtract, ["a_lv"], ["a_lv"])
            self.ts("dve", lv[:, 3:4], lv[:, 2:3], -lam_init, None, ALU.add, None, ["a_lv"], ["a_lv"])
            self.ts("dve", gsub[:, :], self.smallc[:, l * 8:l * 8 + 1], 1.0 - lam_init, None, ALU.mult, None, ["smallc"], ["a_gs"])
            self.memset("dve", qT[:], 0.0, ["a_q"])
            ukeys = [("uT", kc) for kc in range(8)]
            urhs = lambda kc, b: self.U(kc, b * TB, (b + 1) * TB)
            tiles = [(128 * j, trows(j)) for j in range(NT)]
            with ExitStack() as es1:
                wqk = [self.sb(es1, "a_wqk%d" % i, [128, 8, 256], BF16) for i in range(2)]
                wv = self.sb(es1, "a_wv", [128, 8, 512], BF16)
                self.loadw(wv[:], win[:, :, OA_V:OA_V + 512], "a_wv")
                for h in range(4):
                    wq_, wqkk = wqk[h % 2], ("a_wqk", h % 2)
                    self.loadw(wq_[:, :, 0:128], win[:, :, OA_Q + h * 128:OA_Q + (h + 1) * 128], wqkk)
                    self.loadw(wq_[:, :, 128:256], win[:, :, OA_K + h * 128:OA_K + (h + 1) * 128], wqkk)
                    if h == 0:
                        self.proj_tm(wv, "a_wv", 8, 0, 512, lambda kc, t0, n: self.U(kc, t0, t0 + n), ukeys, tiles,
                                     lambda j, n, ps, pk: self.cp("act", vtok[:n, j, :], ps, [pk], ["a_v"]))

                    def qev(b, ps, pk, h=h):
                        self.cp("act", qT[0:64, 2 * h, b * TB:(b + 1) * TB], ps[0:64, :], [pk], ["a_q"])
                        self.cp("dve", qT[64:128, 2 * h + 1, b * TB:(b + 1) * TB], ps[64:128, :], [pk], ["a_q"])
                    self.proj_fm(wq_, wqkk, 8, 0, 128, urhs, ukeys, qev)
                    self.proj_fm(wq_, wqkk, 8, 128, 128, urhs, ukeys,
                                 lambda b, ps, pk, h=h: self.cp("act", kT[:, h, b * TB:(b + 1) * TB], ps, [pk], ["a_k"]))
                self.P.barrier()
            slab = [self.sb(es, "a_slab%d" % i, [128, 2, A_W], F32) for i in range(2)]
            ptbuf = [self.sb(es, "a_pt%d" % i, [128, TB], BF16) for i in range(3)]
            self.sbias = [self.sb(es, "a_sb%d" % i, [128, TB], F32) for i in range(2)]
            self.nsq = self.sb(es, "a_sq", [128, 2, TB], F32)
            self.nrs = self.sb(es, "a_rs", [128, TB], F32)
            rd = [self.sb(es, "a_rd%d" % i, [128, TB], F32) for i in range(2)]
            on = [self.sb(es, "a_on%d" % i, [128, TB], F32) for i in range(4)]
            self.ptidx = 0
            jobs = []
            for h in range(4):
                sl_ = slab[h % 2]
                slk = ("a_slab", h % 2)
                slab_loaded = [False]
                for b in range(NB):
                    q0 = b * TB
                    for m in range(2):
                        mh = m * 4 + h
                        tl = []
                        for j in range(NT):
                            kr = trows(j)
                            o = TB * b - 128 * j
                            if 127 - o <= -91:
                                bias = ("c", self.aconst[:, mh:mh + 1], ["aconst"])
                            elif -o - (TB - 1) >= 91:
                                bias = ("c", self.aconst[:, 8 + mh:9 + mh], ["aconst"])
                            else:
                                bias = ("s", sl_[:kr, m, o + A_C:o + A_C + TB], [slk])
                            tl.append(dict(rows=kr, j=j, bias=bias,
                                           mms=[(kT[:, h, 128 * j:128 * j + kr], qT[:, 2 * h + m, q0:q0 + TB])]))
                        oi = (b % 2) * 2 + m

                        def fin1(ops_, ok, dps, dk, oi=oi):
                            rd_ = rd[oi % 2]
                            self.recip(rd_[:, :], dps[:, :TB], [dk], [("a_rd", oi % 2)])
                            self.tt("dve", on[oi][:, :], ops_[:, :TB], rd_[:, :], ALU.mult, [ok, ("a_rd", oi % 2)], [("a_on", oi)])

                        def fin2(b=b, h=h, q0=q0):
                            o0, o1 = (b % 2) * 2, (b % 2) * 2 + 1
                            self.stt("dve", on[o0][:, :], on[o1][:, :], lv[:, 3:4], on[o0][:, :], ALU.mult, ALU.add,
                                     [("a_on", o0), ("a_on", o1), "a_lv"], [("a_on", o0)])
                            self.rstd_from([on[o0][:, :]], TB, 128, [("a_on", o0)], self.nrs[:, :], "nrs", self.nsq, "nsq")
                            self.stt("dve", self.oall[:, h, q0:q0 + TB], on[o0][:, :], gsub[:, 0:1], self.nrs[:, :], ALU.mult, ALU.mult,
                                     [("a_on", o0), "nrs", "a_gs"], ["oall"])
                        jobs.append(dict(qn=TB, tiles=tl, o_M=128, scale=0.125, fin1=fin1, fin2=(fin2 if m == 1 else None),
                                         v_fn=lambda t, h=h: (vtok[:t["rows"], t["j"], h * 128:(h + 1) * 128], ["a_v"]),
                                         ones_fn=lambda t: self.onesb[:t["rows"], :], rkeys=["a_q", "a_k"],
                                         pre=((lambda h=h: [self.dma("sp", slab[h % 2][:, mm_, :], self.I["slabA"][mm_ * 4 + h], (), [("a_slab", h % 2)]) for mm_ in range(2)])
                                              if (b == 0 and m == 0) else None)))
            self.attn_stream("a", jobs, ptbuf)

    def mix_d(self, l):
        I = self.I
        with ExitStack() as es:
            wq = self.sb(es, "d_wq", [128, 8, 512], BF16)
            wkk = self.sb(es, "d_wkk", [128, 8, 2, 128], BF16)
            wv = self.sb(es, "d_wv", [128, 8, 128], BF16)
            qT = self.sb(es, "d_q", [128, 8, T], BF16)
            kT2 = self.sb(es, "d_k", [128, 2, T], BF16)
            vpad = self.sb(es, "d_v", [128, NT * 4, 128], BF16)
            slab = [self.sb(es, "d_slab%d" % i, [128, D_SLABW], F32) for i in range(4)]
            ptbuf = [self.sb(es, "d_pt%d" % i, [128, 256], BF16) for i in range(3)]
            self.sbias = [self.sb(es, "d_sb%d" % i, [128, 256], F32) for i in range(2)]
            oh = self.sb(es, "d_oh", [128, 2, 128], BF16)
            es8 = self.sb(es, "d_es8", [128, 8], F32)
            es2 = self.sb(es, "d_es2", [128, 4], F32)
            rd = [self.sb(es, "d_rd%d" % i, [128, 256], F32) for i in range(2)]
            win = I["w_in"][l].rearrange("(kc p) n -> p kc n", p=128)
            self.loadw(wq[:], win[:, :, OD_Q:OD_Q + 512], "d_wq")
            for kv in range(2):
                for e in range(2):
                    self.loadw(wkk[:, :, kv, e * 64:(e + 1) * 64], win[:, :, OD_K + kv * 64:OD_K + (kv + 1) * 64], "d_wkk")
            self.loadw(wv[:], win[:, :, OD_V:OD_V + 128], "d_wv")
            self.memset("dve", vpad[:], 0.0, ["d_v"])
            self.memset("dve", oh[:], 0.0, ["d_oh"])
            self.memset("dve", oh[:, 0, 0:64], 1.0, ["d_oh"])
            self.memset("dve", oh[:, 1, 64:128], 1.0, ["d_oh"])
            self.dma("sp", es8[:], I["sinkrep"][:, l * 8:(l + 1) * 8], (), ["d_es8"])
            self.act(es8[:], es8[:], AF.Exp, ["d_es8"], ["d_es8"])
            for p in range(4):
                self.cp("dve", es2[0:64, p:p + 1], es8[0:64, 2 * p:2 * p + 1], ["d_es8"], ["d_es2"])
                self.cp("dve", es2[64:128, p:p + 1], es8[64:128, 2 * p + 1:2 * p + 2], ["d_es8"], ["d_es2"])
            ukeys = [("uT", kc) for kc in range(8)]
            urhs = lambda kc, b: self.U(kc, b * TB, (b + 1) * TB)
            self.memset("dve", qT[:], 0.0, ["d_q"])

            def qev(b, ps, pk, p):
                self.cp("act", qT[0:64, 2 * p, b * TB:(b + 1) * TB], ps[0:64, :], [pk], ["d_q"])
                self.cp("act", qT[64:128, 2 * p + 1, b * TB:(b + 1) * TB], ps[64:128, :], [pk], ["d_q"])
            for p in range(4):
                self.proj_fm(wq, "d_wq", 8, p * 128, 128, urhs, ukeys, lambda b, ps, pk, p=p: qev(b, ps, pk, p))
            for kv in range(2):
                self.proj_fm(wkk[:, :, kv, :], "d_wkk", 8, 0, 128, urhs, ukeys,
                             lambda b, ps, pk, kv=kv: self.cp("act", kT2[:, kv, b * TB:(b + 1) * TB], ps, [pk], ["d_k"]))
            tiles = [(128 * j, trows(j)) for j in range(NT)]

            def vev(j, n, ps, pk):
                for kv in range(2):
                    self.cp("act", vpad[:n, j * 4 + kv * 2, 0:64], ps[:, kv * 64:(kv + 1) * 64], [pk], ["d_v"])
                    self.cp("dve", vpad[:n, j * 4 + kv * 2 + 1, 64:128], ps[:, kv * 64:(kv + 1) * 64], [pk], ["d_v"])
            self.proj_tm(wv, "d_wv", 8, 0, 128, lambda kc, t0, n: self.U(kc, t0, t0 + n), ukeys, tiles, vev)
            self.ptidx = 0
            jobs = []
            for p in range(4):
                kv = p // 2
                sls = [slab[(p % 2) * 2 + e] for e in range(2)]
                slks = [("d_slab", (p % 2) * 2 + e) for e in range(2)]
                pre_p = (lambda p=p, sls=sls, slks=slks: [self.dma("sp", sls[e][:], I["slabD"][2 * p + e], (), [slks[e]]) for e in range(2)])
                for qb, (q0, qn) in enumerate(D_QB):
                    tl = []
                    for e in range(2):
                        h = 2 * p + e
                        pr = slice(64 * e, 64 * e + 64)
                        if qb == 0:
                            mb = ("s", sls[e][:16, D_W + 256:D_W + 256 + qn], [slks[e]])
                        else:
                            mb = ("c", self.dconst[:, h:h + 1], ["dconst"])
                        tl.append(dict(rows=16, j=0, e=e, bias=mb, mms=[(kT2[:, kv, 0:16], qT[:, h, q0:q0 + qn])]))
                        for j in range(max(0, q0 // 128 - 1), min(16, (q0 + qn + 127) // 128) + 1):
                            kr = trows(j)
                            o = q0 - 128 * j
                            if j == 0:
                                bs = sls[e][:kr, D_W:D_W + qn]
                            else:
                                bs = sls[e][:kr, o + D_C:o + D_C + qn]
                            tl.append(dict(rows=kr, j=j, e=e, bias=("s", bs, [slks[e]]),
                                           mms=[(kT2[:, kv, 128 * j:128 * j + kr], qT[:, h, q0:q0 + qn])]))

                    def fin1(ops_, ok, dps, dk, p=p, q0=q0, qn=qn, qb=qb):
                        rd_ = rd[qb % 2]
                        rk = ("d_rd", qb % 2)
                        self.ts("dve", rd_[:, :qn], dps[:, :qn], es2[:, p:p + 1], None, ALU.add, None, [dk, "d_es2"], [rk])
                        self.recip(rd_[:, :qn], rd_[:, :qn], [rk], [rk])
                        self.tt("dve", self.oall[:, 12 + p, q0:q0 + qn], ops_[:, :qn], rd_[:, :qn], ALU.mult, [ok, rk], ["oall"])
                    jobs.append(dict(qn=qn, tiles=tl, o_M=128, scale=0.125, fin1=fin1, fin2=None,
                                     v_fn=lambda t, kv=kv: (vpad[:t["rows"], t["j"] * 4 + kv * 2 + t["e"], :], ["d_v"]),
                                     ones_fn=lambda t: oh[:t["rows"], t["e"], :], okeys=["d_oh"], rkeys=["d_q", "d_k"],
                                     pre=(pre_p if qb == 0 else None)))
            self.attn_stream("d", jobs, ptbuf)

    def mix_b(self, l):
        I = self.I
        with ExitStack() as es:
            wb = self.sb(es, "b_w", [128, 8, 1568], BF16)
            wgu = self.sb(es, "b_wgu", [16, 2, 256], F32)
            gb = self.sb(es, "b_gb", [128, 512], F32)
            msk = self.sb(es, "b_msk", [128, 4, 128], F32)
            obw = self.sb(es, "b_obw", [128, 4, T], F32)
            S = self.sb(es, "b_S", [64, 4, 128], F32)
            Sbf = self.sb(es, "b_Sbf", [64, 4, 128], BF16)
            self.nsq = self.sb(es, "b_sq", [128, 2, 64], F32)
            self.nrs = self.sb(es, "b_rs", [128, 64], F32)
            NBUF = 2
            bufs = {}

            def tb(name, shape, dt, i):
                k = (name, i % NBUF)
                if k not in bufs:
                    bufs[k] = self.sb(es, "b_%s%d" % (name, i % NBUF), shape, dt)
                return bufs[k], ("b_" + name, i % NBUF)

            win = I["w_in"][l].rearrange("(kc p) n -> p kc n", p=128)
            self.loadw(wb[:], win[:, :, OB_Q:OB_Q + 1568], "b_w")
            self.dma("sp", wgu[:], I["gla_gate_up"][l].rearrange("g r c -> r g c"), (), ["b_wgu"])
            self.dma("sp", gb[:], I["gbias"][:, l * 512:(l + 1) * 512], (), ["b_gb"])
            self.dma("sp", msk[:], I["glam"].rearrange("p (m t) -> p m t", m=4), (), ["b_msk"])
            ukeys = [("uT", kc) for kc in range(8)]
            gch = [(0, 16)] + [(16 + 64 * (c - 1), 64) for c in range(1, 33)]
            seq = [(1, ci) for ci in range(32, -1, -1)] + [(0, ci) for ci in range(33)]
            NS = len(seq)
            cxs = {}

            def ctx(i):
                if i not in cxs:
                    dr, ci = seq[i]
                    t0, n = gch[ci]
                    cxs[i] = dict(i=i, dr=dr, ci=ci, t0=t0, n=n, mi_c=(0 if dr == 0 else 1), mi_r=(2 if dr == 0 else 3))
                return cxs[i]

            def P1(cx):
                i, dr, t0, n = cx["i"], cx["dr"], cx["t0"], cx["n"]
                ut = lambda kc: self.U(kc, t0, t0 + n)
                pgl, pglk = self.psb("x")
                for kc in range(8):
                    self.mm(pgl[:16, :n], wb[:, kc, 1536 + 16 * dr:1552 + 16 * dr], ut(kc), kc == 0, kc == 7, ["b_w"] + ukeys, [pglk])
                cx["glT"], cx["glk"] = tb3("glT", [16, 64], F32, i)
                self.cp("act", cx["glT"][:, :n], pgl[:16, :n], [pglk], [cx["glk"]])
                pvt, pvtk = self.psb("x")
                for kc in range(8):
                    self.mm(pvt[:n, :512], ut(kc), wb[:, kc, 512:1024], kc == 0, kc == 7, ["b_w"] + ukeys, [pvtk])
                cx["vt"], cx["vtk"] = tb3("vt", [64, 512], BF16, i)
                self.cp("act", cx["vt"][:n, :], pvt[:n, :512], [pvtk], [cx["vtk"]])
                if dr == 0:
                    prr, prrk = self.psb("x")
                    for h in range(4):
                        for kc in range(8):
                            self.mm(prr[:, h * 64:h * 64 + n], wb[:, kc, 1024 + h * 128:1024 + (h + 1) * 128], ut(kc), kc == 0, kc == 7, ["b_w"] + ukeys, [prrk])
                    cx["sr"], cx["srk"] = tb3("sr", [128, 4, 64], F32, i)
                    self.act(cx["sr"][:, :, :n], prr[:, 0:256].rearrange("p (h t) -> p h t", h=4)[:, :, :n], AF.Silu, [prrk], [cx["srk"]])

            def P2a(cx):
                i, dr, n = cx["i"], cx["dr"], cx["n"]
                ppre, pprek = self.psb("x")
                self.mm(ppre[:n, :256], cx["glT"][:, :n], wgu[:, dr, :], True, True, [cx["glk"], "b_wgu"], [pprek])
                xla, xlk = tb("xla", [64, 256], F32, i)
                self.tt("dve", xla[:n, :], ppre[:n, :256], gb[:n, dr * 256:(dr + 1) * 256], ALU.add, [pprek, "b_gb"], [xlk])
                self.act(xla[:n, :], xla[:n, :], AF.Exp, [xlk], [xlk], scale=-1.0)
                cx["sp"], cx["spk"] = tb("sp", [64, 256], F32, i)
                self.act(cx["sp"][:n, :], xla[:n, :], AF.Ln, [xlk], [cx["spk"]], bias=1.0)

            def P2b(cx):
                i, dr, t0, n = cx["i"], cx["dr"], cx["t0"], cx["n"]
                sp_, spk = cx["sp"], cx["spk"]
                ut = lambda kc: self.U(kc, t0, t0 + n)
                pqk, pqkk = self.psb("x")
                for qi in range(8):
                    for kc in range(8):
                        self.mm(pqk[:64, qi * 64:qi * 64 + n], wb[:, kc, qi * 64:(qi + 1) * 64], ut(kc), kc == 0, kc == 7, ["b_w"] + ukeys, [pqkk])
                pkt, pktk = self.psb("x")
                for kc in range(8):
                    self.mm(pkt[:n, :256], ut(kc), wb[:, kc, 256:512], kc == 0, kc == 7, ["b_w"] + ukeys, [pktk])
                pc, pck = self.psb("x")
                for h in range(4):
                    self.mm(pc[:64, h * 64:h * 64 + n], sp_[:n, h * 64:(h + 1) * 64], msk[:n, cx["mi_c"], :n], True, True, [spk, "b_msk"], [pck])
                pr_, prk = self.psb("x")
                self.mm(pr_[:n, :256], msk[:n, cx["mi_r"], :n], sp_[:n, :], True, True, [spk, "b_msk"], [prk])
                cx["eb"], cx["ebk"] = tb("eb", [64, 4, 64], F32, i)
                einv, eik = tb("einv", [64, 4, 64], F32, i)
                pc3 = pc[:64, 0:256].rearrange("p (h t) -> p h t", h=4)[:, :, :n]
                self.act(cx["eb"][:, :, :n], pc3, AF.Exp, [pck], [cx["ebk"]], scale=-1.0 / 16)
                self.act(einv[:, :, :n], pc3, AF.Exp, [pck], [eik], scale=1.0 / 16)
                eo, eok = tb("eo", [64, 256], F32, i)
                self.act(eo[:n, :], pr_[:n, :256], AF.Exp, [prk], [eok], scale=-1.0 / 16)
                cx["qd"], cx["qdk"] = tb("qd", [64, 4, 64], BF16, i)
                cx["ki"], cx["kik"] = tb("ki", [64, 4, 64], BF16, i)
                q3 = pqk[:64, 0:256].rearrange("p (h t) -> p h t", h=4)[:, :, :n]
                k3 = pqk[:64, 256:512].rearrange("p (h t) -> p h t", h=4)[:, :, :n]
                self.stt("dve", cx["qd"][:, :, :n], q3, 0.125, cx["eb"][:, :, :n], ALU.mult, ALU.mult, [pqkk, cx["ebk"]], [cx["qdk"]])
                self.tt("dve", cx["ki"][:, :, :n], k3, einv[:, :, :n], ALU.mult, [pqkk, eik], [cx["kik"]])
                cx["ko"], cx["kok"] = tb("ko", [64, 256], BF16, i)
                self.tt("dve", cx["ko"][:n, :], pkt[:n, :256], eo[:n, :], ALU.mult, [pktk, eok], [cx["kok"]])

            def P3a(cx):
                i, n = cx["i"], cx["n"]
                pat, patk = self.psb("x")
                for h in range(4):
                    self.mm(pat[:n, h * 64:h * 64 + n], cx["ki"][:, h, :n], cx["qd"][:, h, :n], True, True, [cx["kik"], cx["qdk"]], [patk])
                cx["att"], cx["atk"] = tb("att", [64, 4, 64], BF16, i)
                for h in range(4):
                    self.tt("dve", cx["att"][:n, h, :n], pat[:n, h * 64:h * 64 + n], msk[:n, cx["mi_c"], :n], ALU.mult, [patk, "b_msk"], [cx["atk"]])
                pds, pdsk = self.psb("x")
                for h in range(4):
                    self.mm(pds[:64, h * 128:(h + 1) * 128], cx["ko"][:n, h * 64:(h + 1) * 64], cx["vt"][:n, h * 128:(h + 1) * 128], True, True, [cx["kok"], cx["vtk"]], [pdsk])
                cx["pds"], cx["pdsk"] = pds, pdsk

            def P3b(cx):
                i, dr, t0, n = cx["i"], cx["dr"], cx["t0"], cx["n"]
                vt, vtk, att, atk, qd, qdk = cx["vt"], cx["vtk"], cx["att"], cx["atk"], cx["qd"], cx["qdk"]
                if i == 0 or seq[i][0] != seq[i - 1][0]:
                    self.memset("dve", S[:], 0.0, ["b_S"])
                    self.memset("dve", Sbf[:], 0.0, ["b_Sbf"])
                po, pok = self.psb("x")
                for h in range(4):
                    self.mm(po[:, h * 64:h * 64 + n], vt[:n, h * 128:(h + 1) * 128], att[:n, h, :n], True, False, [vtk, atk], [pok])
                    self.mm(po[:, h * 64:h * 64 + n], Sbf[:, h, :], qd[:, h, :n], False, True, ["b_Sbf", qdk], [pok])
                dcol = (n - 1) if dr == 0 else 0
                pds, pdsk = cx["pds"], cx["pdsk"]
                for h in range(4):
                    self.stt("dve", S[:, h, :], S[:, h, :], cx["eb"][:, h, dcol:dcol + 1], pds[:64, h * 128:(h + 1) * 128], ALU.mult, ALU.add, ["b_S", cx["ebk"], pdsk], ["b_S"])
                self.cp("act", Sbf[:], S[:], ["b_S"], ["b_Sbf"])
                if dr == 1:
                    self.cp("act", obw[:, :, t0:t0 + n], po[:, 0:256].rearrange("p (h t) -> p h t", h=4)[:, :, :n], [pok], ["b_obw"])
                else:
                    of, ofk = tb("of", [128, 4, 64], F32, i)
                    self.tt("dve", of[:, :, :n], po[:, 0:256].rearrange("p (h t) -> p h t", h=4)[:, :, :n], obw[:, :, t0:t0 + n], ALU.add, [pok, "b_obw"], [ofk])
                    for h in range(4):
                        self.rstd_from([of[:, h, :n]], n, 128, [ofk], self.nrs[:, :n], "nrs", self.nsq, "nsq")
                        self.stt("dve", of[:, h, :n], of[:, h, :n], self.smallc[:, l * 8 + 1:l * 8 + 2], self.nrs[:, :n], ALU.mult, ALU.mult, [ofk, "nrs", "smallc"], [ofk])
                    self.tt("dve", self.oall[:, 4:8, t0:t0 + n], of[:, :, :n], cx["sr"][:, :, :n], ALU.mult, [ofk, cx["srk"]], ["oall"])
                del cxs[i]

            def tb3(name, shape, dt, i):
                k = (name, i % 3)
                if k not in bufs:
                    bufs[k] = self.sb(es, "b_%s%d" % (name, i % 3), shape, dt)
                return bufs[k], ("b_" + name, i % 3)

            P1(ctx(0))
            P1(ctx(1))
            P2a(ctx(0))
            P2b(ctx(0))
            for i in range(NS):
                if i + 2 < NS:
                    P1(ctx(i + 2))
                if i + 1 < NS:
                    P2a(ctx(i + 1))
                P3a(ctx(i))
                if i + 1 < NS:
                    P2b(ctx(i + 1))
                P3b(ctx(i))

    def resid_block(self, y, ykeys, b, l, gw, hsrc, hdst, bufs, nextnorm=None, final=False):
        hb, hk = bufs
        hv_s = hsrc.rearrange("(c p) t -> p c t", p=128)
        hv_d = hdst.rearrange("(c p) t -> p c t", p=128)
        sl = slice(b * TB, (b + 1) * TB)
        self.dma("sp", hb[:], hv_s[:, :, sl], (), [hk])
        self.rstd_from([y[:, kc, :] for kc in range(8)], TB, D, ykeys, self.nrs[:, :], "nrs", self.nsq, "nsq")
        for kc in range(8):
            self.stt("dve", y[:, kc, :], y[:, kc, :], self.gcol(l, gw, kc), self.nrs[:, :], ALU.mult, ALU.mult,
                     ykeys + ["nrs", "gains"], ykeys)
        self.tt("dve", hb[:], hb[:], y[:], ALU.add, [hk] + ykeys, [hk])
        st = self.dma("sp", hv_d[:, :, sl], hb[:], [hk], [("hdram", b)])
        if nextnorm is not None:
            self.norm_block(hb, hk, b, nextnorm[0], nextnorm[1], "nn")
        return st

    def merge_out(self, l, hsrc, hT, les):
        I = self.I
        with ExitStack() as es:
            merged = self.sb(es, "m_merged", [128, 8, T], BF16)
            with ExitStack() as es2:
                wbr = [self.sb(es2, "m_wbr%d" % i, [128, 16, 128], BF16) for i in range(2)]
                wg = [self.sb(es2, "m_wg%d" % i, [128, 8, 4, 128], BF16) for i in range(2)]
                sig = [self.sb(es2, "m_sig%d" % i, [128, TB], F32) for i in range(2)]
                acc = self.sb(es2, "m_acc", [128, TB], F32)
                prod = self.sb(es2, "m_prod", [128, TB], F32)
                ukeys = [("uT", kc) for kc in range(8)]
                for dc in range(8):
                    wb_, wbk = wbr[dc % 2], ("m_wbr", dc % 2)
                    wg_, wgk = wg[dc % 2], ("m_wg", dc % 2)
                    self.loadw(wb_[:], I["w_branch"][l].rearrange("n (ec p) d -> p (n ec) d", p=128)[:, :, dc * 128:(dc + 1) * 128], wbk)
                    for br in range(4):
                        self.loadw(wg_[:, :, br, :], I["w_in"][l].rearrange("(kc p) n -> p kc n", p=128)
                                   [:, :, O_GATE + br * 1024 + dc * 128:O_GATE + br * 1024 + (dc + 1) * 128], wgk)
                    for b in range(NB):
                        sl = slice(b * TB, (b + 1) * TB)
                        for br in range(4):
                            pg, pgk = self.psb("x")
                            for kc in range(8):
                                self.mm(pg[:, :TB], wg_[:, kc, br, :], self.U(kc, b * TB, (b + 1) * TB), kc == 0, kc == 7, [wgk] + ukeys, [pgk])
                            pp, ppk = self.psb("x")
                            for ec in range(4):
                                self.mm(pp[:, :TB], wb_[:, br * 4 + ec, :], self.oall[:, br * 4 + ec, sl], ec == 0, ec == 3, [wbk, "oall"], [ppk])
                            sg, sgk = sig[br % 2], ("m_sig", br % 2)
                            self.act(sg[:, :], pg[:, :TB], AF.Sigmoid, [pgk], [sgk])
                            if br == 0:
                                self.tt("dve", acc[:, :], pp[:, :TB], sg[:, :], ALU.mult, [ppk, sgk], ["m_acc"])
                            else:
                                self.tt("dve", prod[:, :], pp[:, :TB], sg[:, :], ALU.mult, [ppk, sgk], ["m_prod"])
                                if br < 3:
                                    self.tt("dve", acc[:, :], acc[:, :], prod[:, :], ALU.add, ["m_acc", "m_prod"], ["m_acc"])
                                else:
                                    self.tt("dve", merged[:, dc, sl], acc[:, :], prod[:, :], ALU.add, ["m_acc", "m_prod"], ["m_merged"])
                self.P.barrier()
            if "merged" in self.dbg_out and l == 0:
                with ExitStack() as es3:
                    tmp = self.sb(es3, "dbgtmp2", [128, T], F32)
                    for c in range(8):
                        self.cp("dve", tmp[:], merged[:, c, :], ["m_merged"], ["dbgtmp2"])
                        self.dma("sp", self.dbg_out["merged"][c * 128:(c + 1) * 128, :], tmp[:], ["dbgtmp2"], ())
                    self.P.barrier()
            with ExitStack() as es2:
                wo = self.sb(es2, "o_w", [128, 8, D], BF16)
                y2 = [self.sb(es2, "o_y%d" % i, [128, 8, TB], F32) for i in range(2)]
                hb2 = [self.sb(es2, "o_h%d" % i, [128, 8, TB], F32) for i in range(2)]
                self.nsq = self.sb(es2, "o_sq", [128, 2, TB], F32)
                self.nrs = self.sb(es2, "o_rs", [128, TB], F32)
                self.loadw(wo[:], I["w_out"][l].rearrange("(kc p) n -> p kc n", p=128), "o_w")
                for b in range(NB):
                    y, yk = y2[b % 2], ("o_y", b % 2)
                    sl = slice(b * TB, (b + 1) * TB)
                    for dc in range(8):
                        pst, pk = self.psb("x")
                        for kc in range(8):
                            self.mm(pst[:, :TB], wo[:, kc, dc * 128:(dc + 1) * 128], merged[:, kc, sl], kc == 0, kc == 7, ["o_w", "m_merged"], [pk])
                        self.cp("act", y[:, dc, :], pst[:, :TB], [pk], [yk])
                    self.resid_block(y, [yk], b, l, 1, hsrc, hT, (hb2[b % 2], ("o_h", b % 2)), nextnorm=(l, 2))
                self.P.barrier()

    def ffn(self, l, hT, hdst):
        I = self.I
        finals = []
        NJ = DFF // 128
        for half in range(2):
            with ExitStack() as es:
                actT = self.sb(es, "f_act", [128, NJ, 3 * TB], BF16)
                with ExitStack() as es2:
                    wu = [self.sb(es2, "f_wu%d" % i, [128, 8, 2, 128], BF16) for i in range(2)]
                    cgs = [self.sb(es2, "f_cg%d" % i, [128, TB], F32) for i in range(2)]
                    cvs = [self.sb(es2, "f_cv%d" % i, [128, TB], F32) for i in range(2)]
                    t1s = [self.sb(es2, "f_t1%d" % i, [128, TB], F32) for i in range(2)]
                    wup = I["ffn_w_up"][l].rearrange("(kc p) n -> p kc n", p=128)
                    ukeys = [("uT", kc) for kc in range(8)]
                    cw = self.convw
                    it = 0
                    for j in range(NJ):
                        w_, wk = wu[j % 2], ("f_wu", j % 2)
                        self.loadw(w_[:, :, 0, :], wup[:, :, j * 128:(j + 1) * 128], wk)
                        self.loadw(w_[:, :, 1, :], wup[:, :, DFF + j * 128:DFF + (j + 1) * 128], wk)
                        for b in range(3 * half, 3 * half + 3):
                            it += 1
                            cg, cv, t1 = cgs[it % 2], cvs[it % 2], t1s[it % 2]
                            cgk, cvk, t1k = ("f_cg", it % 2), ("f_cv", it % 2), ("f_t1", it % 2)
                            for gv in range(2):
                                pst, pk = self.psb("x")
                                for kc in range(8):
                                    self.mm(pst[:, :TB + 2], w_[:, kc, gv, :], self.uT[:, kc, b * TB:b * TB + TB + 2], kc == 0, kc == 7, [wk] + ukeys, [pk])
                                ch = gv * NJ + j
                                base = (l * 4) * 44
                                c0 = cw[:, base + ch:base + ch + 1]
                                c1 = cw[:, base + 44 + ch:base + 44 + ch + 1]
                                c2 = cw[:, base + 88 + ch:base + 88 + ch + 1]
                                cb = cw[:, base + 132 + ch:base + 132 + ch + 1]
                                dst, dk = (cg, cgk) if gv == 0 else (cv, cvk)
                                self.act(dst[:, :], pst[:, 0:TB], AF.Identity, [pk, "convw"], [dk], bias=cb, scale=c0)
                                self.stt("dve", dst[:, :], pst[:, 1:TB + 1], c1, dst[:, :], ALU.mult, ALU.add, [pk, dk, "convw"], [dk])
                                self.stt("dve", dst[:, :], pst[:, 2:TB + 2], c2, dst[:, :], ALU.mult, ALU.add, [pk, dk, "convw"], [dk])
                            self.act(t1[:, :], cg[:, :], AF.Gelu_apprx_tanh, [cgk], [t1k])
                            self.tt("pool", actT[:, j, (b - 3 * half) * TB:(b - 3 * half + 1) * TB], t1[:, :], cv[:, :], ALU.mult, [t1k, cvk], ["f_act"])
                    self.P.barrier()
                with ExitStack() as es2:
                    wd = self.sb(es2, "f_wd", [128, NJ, D], BF16)
                    y2 = [self.sb(es2, "f_y%d" % i, [128, 8, TB], F32) for i in range(2)]
                    hb2 = [self.sb(es2, "f_h%d" % i, [128, 8, TB], F32) for i in range(2)]
                    self.nsq = self.sb(es2, "f_sq", [128, 2, TB], F32)
                    self.nrs = self.sb(es2, "f_rs", [128, TB], F32)
                    self.loadw(wd[:], I["ffn_w_down"][l].rearrange("(j p) n -> p j n", p=128), "f_wd")
                    for b in range(3 * half, 3 * half + 3):
                        y, yk = y2[b % 2], ("f_y", b % 2)
                        sl = slice(b * TB, (b + 1) * TB)
                        for dc in range(8):
                            pst, pk = self.psb("x")
                            for j in range(NJ):
                                self.mm(pst[:, :TB], wd[:, j, dc * 128:(dc + 1) * 128], actT[:, j, (b - 3 * half) * TB:(b - 3 * half + 1) * TB], j == 0, j == NJ - 1, ["f_wd", "f_act"], [pk])
                            self.cp("act", y[:, dc, :], pst[:, :TB], [pk], [yk])
                        st = self.resid_block(y, [yk], b, l, 3, hT, hdst, (hb2[b % 2], ("f_h", b % 2)))
                        finals.append(st)
                    self.P.barrier()
        return finals


def host_consts(inp):
    f32 = np.float32
    c = {}
    tab = np.asarray(inp["rel_bias_table"], f32)
    kk = np.arange(128)[:, None]
    jj = np.arange(A_W)[None, :]
    bk = rel_bucket_jax(kk - jj + A_C)
    c["slabA"] = np.ascontiguousarray(np.transpose(tab[bk][:, :, 0:8], (2, 0, 1))).astype(f32)
    ac = np.concatenate([tab[15, 0:8], tab[31, 0:8]])
    c["aconst"] = np.ascontiguousarray(np.broadcast_to(ac[None, :], (128, 16))).astype(f32)
    tabd = tab[:, 8:16]
    jj = np.arange(D_W)[None, :]
    rel = kk - jj + D_C
    tz = np.where((np.abs(rel) <= 128)[:, :, None], tabd[rel_bucket_jax(rel)], f32(NEG))
    qq = np.arange(256)[None, :]
    rel0 = kk - qq
    t0 = np.where(((np.abs(rel0) <= 128) & (kk >= NMETA))[:, :, None], tabd[rel_bucket_jax(rel0)], f32(NEG))
    tm = np.where((kk < NMETA)[:, :, None], tabd[rel_bucket_jax(rel0)], f32(NEG))
    c["slabD"] = np.ascontiguousarray(np.transpose(np.concatenate([tz, t0, tm], axis=1), (2, 0, 1))).astype(f32)
    c["dconst"] = np.ascontiguousarray(np.broadcast_to(tabd[15][None, :], (128, 8))).astype(f32)
    half = 32
    inv = (10000.0 ** (-np.arange(half, dtype=np.float32) / half)).astype(f32)
    ang = np.arange(T, dtype=f32)[None, :] * inv[:, None]
    cos, sin = np.cos(ang).astype(f32), np.sin(ang).astype(f32)
    c["rope"] = np.ascontiguousarray(np.concatenate([np.concatenate([cos, cos], 0), np.concatenate([-sin, sin], 0)], 1)).astype(f32)
    s = np.arange(128)[:, None]
    t = np.arange(128)[None, :]
    same = (s // 64) == (t // 64)
    LT = (same & (s <= t)).astype(f32)
    L = (same & (s >= t)).astype(f32)
    SU = (same & (s > t)).astype(f32)
    SL = (same & (s < t)).astype(f32)
    c["glam"] = np.ascontiguousarray(np.concatenate([LT, L, SU, SL], 1))
    return c


def host_layout(inp):
    f32 = np.float32
    g = {}
    sw = np.concatenate([np.arange(32, 64), np.arange(0, 32)])
    wq = np.asarray(inp["mla_w_q_up"], f32).reshape(DEPTH, 256, 4, 192)
    g["mla_w_q_up_sw"] = np.ascontiguousarray(wq[:, :, :, 128:][:, :, :, sw].reshape(DEPTH, 256, 256))
    g["w_in_kr_sw"] = np.ascontiguousarray(np.asarray(inp["w_in"])[:, :, OC_KR:OC_KR + 64][:, :, sw])
    gains = np.stack([inp["norm_mix_pre"], inp["norm_mix_post"], inp["norm_ffn_pre"], inp["norm_ffn_post"]], 1)
    g["gains"] = np.ascontiguousarray(gains.reshape(DEPTH * 4 * 8, 128).T).astype(f32)
    cw = np.concatenate([np.asarray(inp["ffn_conv_w"], f32), np.asarray(inp["ffn_conv_b"], f32)[:, None, :]], 1)
    g["convw"] = np.ascontiguousarray(cw.reshape(DEPTH * 4 * 44, 128).T).astype(f32)
    sc = np.zeros((DEPTH, 8, 128), f32)
    sc[:, 0] = inp["diff_subln"]
    sc[:, 1] = inp["gla_norm"]
    sc[:, 2] = inp["mla_kv_norm"]
    sc[:, 3:5] = np.asarray(inp["mla_q_norm"]).reshape(DEPTH, 2, 128)
    g["smallc"] = np.ascontiguousarray(sc.reshape(DEPTH * 8, 128).T)
    g["lamrep"] = np.ascontiguousarray(np.broadcast_to(np.asarray(inp["diff_lambda"], f32).reshape(1, DEPTH * 256), (128, DEPTH * 256)))
    g["gbias"] = np.ascontiguousarray(np.broadcast_to(np.asarray(inp["gla_gate_bias"], f32).reshape(1, DEPTH * 512), (128, DEPTH * 512)))
    g["sinkrep"] = np.ascontiguousarray(np.broadcast_to(np.asarray(inp["swa_sinks"], f32).reshape(1, DEPTH * 8), (128, DEPTH * 8)))
    return g


_NC_CACHE = {}


def get_nc(layers=(0, 1), dbg=None):
    key = (tuple(layers), tuple(sorted((dbg or {}).items())))
    if key not in _NC_CACHE:
        nc = bass.Bass("TRN2", target_bir_lowering=False)
        KB(nc, dbg).build(layers)
        _NC_CACHE[key] = nc
    return _NC_CACHE[key]


def make_in_maps(inp, cores):
    shared = {}
    for k in ("w_in", "w_branch", "w_out", "ffn_w_up", "ffn_w_down", "mla_w_q_up", "mla_w_kv_up", "gla_gate_up"):
        shared[k] = np.ascontiguousarray(np.asarray(inp[k], np.float32))
    shared.update(host_layout(inp))
    shared.update(host_consts(inp))
    meta = np.asarray(inp["meta_tokens"], np.float32)
    x = np.asarray(inp["x"], np.float32)
    maps = []
    for b in cores:
        h0 = np.concatenate([meta, x[b]], axis=0)
        m = dict(shared)
        m["h0T"] = np.ascontiguousarray(h0.T)
        maps.append(m)
    return maps


def kernel(**inputs):
    nc = get_nc()
    maps = make_in_maps(inputs, list(range(8)))
    res = run_bass_kernel_spmd(nc, maps, core_ids=list(range(8)))
    out = np.stack([np.ascontiguousarray(r["outT"][:, NMETA:].T) for r in res.results], axis=0)
    return out.astype(np.float32)
```

```python
import math
import os
import numpy as np
from contextlib import ExitStack
import concourse.bass as bass
import concourse.mybir as mybir
from concourse.bass_utils import run_bass_kernel_spmd

F32 = mybir.dt.float32
BF16 = mybir.dt.bfloat16
AF = mybir.ActivationFunctionType
ALU = mybir.AluOpType

DEPTH = 2
D = 1024
SEQ = 2048
NMETA = 16
T = SEQ + NMETA
TB = 344
NB = 6
NT = 17
EPS = 1e-6
DFF = 2816
NIN = 8416
OA_Q, OA_K, OA_V = 0, 512, 1024
OB_Q, OB_K, OB_V, OB_R, OB_G = 1536, 1792, 2048, 2560, 3072
OC_QA, OC_KVA, OC_KR = 3104, 3360, 3488
OD_Q, OD_K, OD_V = 3552, 4064, 4192
O_GATE = 4320
NEG = -30000.0


def trows(j):
    return 128 if j < 16 else 16


class Dep:
    __slots__ = ("w", "r")

    def __init__(self):
        self.w = None
        self.r = []


class Op:
    __slots__ = ("eng", "fn", "deps", "ms", "val", "sem", "is_dma")

    def __init__(self, eng, fn, is_dma):
        self.eng = eng
        self.fn = fn
        self.deps = []
        self.ms = False
        self.val = 0
        self.sem = None
        self.is_dma = is_dma


ENGS = ("pe", "act", "dve", "pool", "sp")
NDMASEM = 8


class Prog:
    def __init__(self, nc, es):
        self.nc = nc
        self.es = es
        self.ops = {e: [] for e in ENGS}
        self.dma_hist = {e: [] for e in ENGS}
        self.dd = {}

    def D(self, key):
        d = self.dd.get(key)
        if d is None:
            d = self.dd[key] = Dep()
        return d

    def op(self, eng, fn, r=(), w=(), dma=False, extra=()):
        o = Op(eng, fn, dma)
        need = list(extra)
        for k in r:
            d = self.D(k)
            if d.w is not None:
                need.append(d.w)
        for k in w:
            d = self.D(k)
            if d.w is not None:
                need.append(d.w)
            for q in d.r:
                need.append(q)
        if dma:
            h = self.dma_hist[eng]
            if len(h) >= NDMASEM:
                need.append(h[-NDMASEM])
            h.append(o)
        seen = set()
        for p in need:
            if p is o or id(p) in seen:
                continue
            seen.add(id(p))
            if (not dma) and eng == "pe" and p.eng == "pe" and not p.is_dma:
                continue
            o.deps.append(p)
        for k in r:
            lst = self.D(k).r
            if not dma:
                lst[:] = [q for q in lst if q.is_dma or q.eng != eng]
            lst.append(o)
        for k in w:
            d = self.D(k)
            d.w = o
            d.r = []
        self.ops[eng].append(o)
        return o

    def barrier(self):
        lasts = []
        for e in ENGS:
            cl = [o for o in self.ops[e] if not o.is_dma]
            if cl:
                lasts.append(cl[-1])
            lasts.extend(self.dma_hist[e][-NDMASEM:])
        for e in ENGS:
            if self.ops[e]:
                self.op(e, lambda eng: eng.nop(), extra=lasts)

    def finalize(self, final_ops=()):
        nc, es = self.nc, self.es
        for e in ENGS:
            for o in self.ops[e]:
                for p in o.deps:
                    p.ms = True
        esem = {e: es.enter_context(nc.semaphore("s_" + e)) for e in ENGS}
        dsem = {e: [es.enter_context(nc.semaphore("d_%s%d" % (e, i))) for i in range(NDMASEM)]
                for e in ENGS if self.dma_hist[e]}
        for e in ENGS:
            cnt = 0
            dcnt = [0] * NDMASEM
            k = 0
            for o in self.ops[e]:
                if o.is_dma:
                    s = k % NDMASEM
                    k += 1
                    dcnt[s] += 16
                    o.sem = dsem[e][s]
                    o.val = dcnt[s]
                elif o.ms:
                    cnt += 1
                    o.sem = esem[e]
                    o.val = cnt
        engobj = {"pe": "tensor", "act": "scalar", "dve": "vector", "pool": "gpsimd", "sp": "sync"}
        block = es.enter_context(nc.Block())

        def emit(e):
            def body(eng):
                known = {}
                for o in self.ops[e]:
                    wl = {}
                    for p in o.deps:
                        key = id(p.sem)
                        if known.get(key, 0) >= p.val:
                            continue
                        if key not in wl or wl[key][1] < p.val:
                            wl[key] = (p.sem, p.val)
                    for key, (s, v) in wl.items():
                        eng.wait_ge(s, v)
                        known[key] = v
                    ins = o.fn(eng)
                    if o.is_dma:
                        ins.then_inc(o.sem, 16)
                    elif o.ms:
                        ins.then_inc(o.sem, 1)
                if e == "sp":
                    for o in final_ops:
                        eng.wait_ge(o.sem, o.val)
            return body

        for e in ENGS:
            if self.ops[e] or e == "sp":
                getattr(block, engobj[e])(emit(e))


def rel_bucket(rel):
    rel = np.asarray(rel, dtype=np.int64)
    half, max_exact = 16, 8
    ret = np.where(rel > 0, half, 0)
    n = np.abs(rel)
    nf = np.maximum(n, 1).astype(np.float32)
    large = max_exact + (np.log(nf / np.float32(max_exact)) / np.float32(math.log(128 / max_exact))
                         * (half - max_exact)).astype(np.int32)
    large = np.minimum(large, half - 1)
    return ret + np.where(n < max_exact, n, large)


def rel_bucket_jax(rel):
    import jax
    import jax.numpy as jnp
    with jax.default_device(jax.devices("cpu")[0]):
        rel = jnp.asarray(np.asarray(rel, dtype=np.int32))
        half, max_exact = 16, 8
        ret = jnp.where(rel > 0, half, 0)
        n = jnp.abs(rel)
        nf = jnp.maximum(n, 1).astype(jnp.float32)
        large = max_exact + (jnp.log(nf / max_exact) / math.log(128 / max_exact) * (half - max_exact)).astype(jnp.int32)
        large = jnp.minimum(large, half - 1)
        return np.asarray(ret + jnp.where(n < max_exact, n, large))


A_OS = sorted(set(TB * b - 128 * j for b in range(NB) for j in range(NT)))
A_NEAR = [o for o in A_OS if not (127 - o <= -91 or -o - (TB - 1) >= 91)]
A_C = -min(A_NEAR)
A_W = TB + max(A_NEAR) + A_C
D_QB = [(256 * i, 256) for i in range(8)] + [(2048, 16)]
D_C = 256
D_W = 256 + 384
D_SLABW = D_W + 256 + 256


class KB:
    def __init__(self, nc, dbg=None):
        self.nc = nc
        self.dbg = dbg or {}

    def mm(self, out, lhsT, rhs, start, stop, r, w):
        return self.P.op("pe", lambda e: e.matmul(out, lhsT=lhsT, rhs=rhs, start=start, stop=stop), r, w)

    def act(self, out, in_, func, r, w, bias=None, scale=1.0, accum=None):
        def f(e):
            kw = {}
            if bias is not None:
                kw["bias"] = bias
            if accum is not None:
                kw["accum_out"] = accum
            return e.activation(out=out, in_=in_, func=func, scale=scale, **kw)
        return self.P.op("act", f, r, w)

    def stt(self, eng, out, in0, scalar, in1, op0, op1, r, w):
        nm = {"dve": "vector", "pool": "gpsimd"}[eng]
        return self.P.op(eng, lambda e: e.scalar_tensor_tensor(out=out, in0=in0, scalar=scalar, in1=in1, op0=op0, op1=op1), r, w)

    def ts(self, eng, out, in0, s1, s2, op0, op1, r, w):
        if s2 is None:
            return self.P.op(eng, lambda e: e.tensor_scalar(out=out, in0=in0, scalar1=s1, scalar2=None, op0=op0), r, w)
        return self.P.op(eng, lambda e: e.tensor_scalar(out=out, in0=in0, scalar1=s1, scalar2=s2, op0=op0, op1=op1), r, w)

    def tt(self, eng, out, in0, in1, op, r, w):
        return self.P.op(eng, lambda e: e.tensor_tensor(out=out, in0=in0, in1=in1, op=op), r, w)

    def cp(self, eng, out, in_, r, w):
        if eng == "act":
            return self.P.op("act", lambda e: e.copy(out=out, in_=in_), r, w)
        return self.P.op(eng, lambda e: e.tensor_copy(out=out, in_=in_), r, w)

    def recip(self, out, in_, r, w):
        return self.P.op("dve", lambda e: e.reciprocal(out=out, in_=in_), r, w)

    def memset(self, eng, ap, val, w):
        return self.P.op(eng, lambda e: e.memset(ap, val), (), w)

    def dma(self, q, out, in_, r, w):
        return self.P.op(q, lambda e: e.dma_start(out=out, in_=in_), r, w, dma=True)

    def sb(self, es, name, shape, dt):
        self.sbcnt = getattr(self, "sbcnt", 0) + 1
        return es.enter_context(self.nc.sbuf_tensor("sb%d_%s" % (self.sbcnt, name), shape, dt))

    def U(self, kc, t0, t1):
        return self.uT[:, kc, t0 + 1:t1 + 1]

    def psb(self, group):
        lst = self.psgroups[group]
        i = self.psidx.get(group, 0)
        self.psidx[group] = i + 1
        b = lst[i % len(lst)]
        return self.ps[b], ("ps", b)

    def rstd_from(self, srcs, n, Dn, rkeys, out_ap, out_key, sq_ap, sq_key, sq_eng="pool"):
        pst, pk = self.ps[7], ("ps", 7)
        for i, s in enumerate(srcs):
            self.tt(sq_eng, sq_ap[:, i % 2, :n], s, s, ALU.mult, rkeys, [(sq_key, i % 2)])
            self.mm(pst[:, :n], self.onesf[:], sq_ap[:, i % 2, :n], i == 0, i == len(srcs) - 1, [(sq_key, i % 2), "onesf"], [pk])
        self.ts("dve", out_ap, pst[:, :n], 1.0 / Dn, EPS, ALU.mult, ALU.add, [pk], [out_key])
        self.act(out_ap, out_ap, AF.Ln, [out_key], [out_key])
        self.act(out_ap, out_ap, AF.Exp, [out_key], [out_key], scale=-0.5)

    def loadw(self, dst, src, wkey):
        return self.dma("pool", dst, src, (), [wkey])

    def build(self, layers=(0, 1)):
        nc = self.nc
        I = {}

        def din(name, shape):
            I[name] = nc.dram_tensor(name, list(shape), F32, kind="ExternalInput").ap()
            return I[name]

        din("h0T", [D, T])
        din("w_in", [DEPTH, D, NIN])
        din("w_branch", [DEPTH, 4, 512, D])
        din("w_out", [DEPTH, D, D])
        din("ffn_w_up", [DEPTH, D, 2 * DFF])
        din("ffn_w_down", [DEPTH, DFF, D])
        din("mla_w_q_up", [DEPTH, 256, 768])
        din("mla_w_q_up_sw", [DEPTH, 256, 256])
        din("mla_w_kv_up", [DEPTH, 128, 1024])
        din("w_in_kr_sw", [DEPTH, D, 64])
        din("gla_gate_up", [DEPTH, 2, 16, 256])
        din("gains", [128, DEPTH * 4 * 8])
        din("convw", [128, DEPTH * 4 * 44])
        din("smallc", [128, DEPTH * 8])
        din("lamrep", [128, DEPTH * 256])
        din("gbias", [128, DEPTH * 512])
        din("sinkrep", [128, DEPTH * 8])
        din("aconst", [128, 16])
        din("dconst", [128, 8])
        din("slabA", [8, 128, A_W])
        din("slabD", [8, 128, D_SLABW])
        din("rope", [64, 2 * T])
        din("glam", [128, 4 * 128])
        outT = nc.dram_tensor("outT", [D, T], F32, kind="ExternalOutput").ap()
        hT = nc.dram_tensor("hT_scr", [D, T], F32, kind="Internal").ap()
        dbg_out = {}
        for k, shp in self.dbg.items():
            dbg_out[k] = nc.dram_tensor("dbg_" + k, list(shp), F32, kind="ExternalOutput").ap()
        self.dbg_out = dbg_out
        self.I = I

        with ExitStack() as es:
            P = self.P = Prog(nc, es)
            self.ps = [es.enter_context(nc.psum_tensor("ps%d" % i, [128, 512], F32)) for i in range(8)]
            self.psgroups = {"s": [0, 1, 2], "o": [3, 4], "d": [5, 6], "x": [0, 1, 2, 3, 4, 5, 6]}
            self.psidx = {}
            self.uT = self.sb(es, "uT", [128, 8, T + 2], BF16)
            self.onesf = self.sb(es, "onesf", [128, 128], F32)
            self.onesb = self.sb(es, "onesb", [128, 128], BF16)
            self.gains = self.sb(es, "gains", [128, DEPTH * 32], F32)
            self.convw = self.sb(es, "convw", [128, DEPTH * 4 * 44], F32)
            self.smallc = self.sb(es, "smallc", [128, DEPTH * 8], F32)
            self.aconst = self.sb(es, "aconst", [128, 16], F32)
            self.dconst = self.sb(es, "dconst", [128, 8], F32)
            self.memset("dve", self.onesf[:], 1.0, ["onesf"])
            self.memset("dve", self.onesb[:], 1.0, ["onesb"])
            self.ones16 = self.sb(es, "ones16", [128, 128], BF16)
            self.memset("dve", self.ones16[:], 0.0, ["onesb"])
            self.memset("dve", self.ones16[0:16, :], 1.0, ["onesb"])
            self.memset("dve", self.uT[:, :, 0:1], 0.0, ["uT"])
            self.memset("dve", self.uT[:, :, T + 1:T + 2], 0.0, ["uT"])
            self.dma("sp", self.gains[:], I["gains"], (), ["gains"])
            self.dma("sp", self.convw[:], I["convw"], (), ["convw"])
            self.dma("sp", self.smallc[:], I["smallc"], (), ["smallc"])
            self.dma("sp", self.aconst[:], I["aconst"], (), ["aconst"])
            self.dma("sp", self.dconst[:], I["dconst"], (), ["dconst"])

            finals = []
            nl = len(layers)
            for li, l in enumerate(layers):
                hsrc = I["h0T"] if li == 0 else hT
                last = (li == nl - 1)
                self.norm1(l, hsrc)
                with ExitStack() as les:
                    self.oall = self.sb(les, "oall", [128, 16, T], BF16)
                    self.mixers(l)
                    if not os.environ.get("ONLYMIX"):
                        self.merge_out(l, hsrc, hT, les)
                P.barrier()
                if not os.environ.get("ONLYMIX"):
                    finals += self.ffn(l, hT, outT if last else hT)
                P.barrier()
            P.finalize(finals)
        return nc

    def gcol(self, l, which, kc):
        i = (l * 4 + which) * 8 + kc
        return self.gains[:, i:i + 1]

    def norm_block(self, hb, hkey, b, gl, gw, tag):
        sq, rs = self.nsq, self.nrs
        self.rstd_from([hb[:, kc, :] for kc in range(8)], TB, D, [hkey], rs[:, :], "nrs", sq, "nsq")
        for kc in range(8):
            self.stt("dve", self.U(kc, b * TB, (b + 1) * TB), hb[:, kc, :], self.gcol(gl, gw, kc), rs[:, :],
                     ALU.mult, ALU.mult, [hkey, "nrs", "gains"], [("uT", kc)])

    def norm1(self, l, hsrc):
        with ExitStack() as es:
            hb2 = [self.sb(es, "n1h%d" % i, [128, 8, TB], F32) for i in range(2)]
            self.nsq = self.sb(es, "n1sq", [128, 2, TB], F32)
            self.nrs = self.sb(es, "n1rs", [128, TB], F32)
            hv = hsrc.rearrange("(c p) t -> p c t", p=128)
            for b in range(NB):
                hb = hb2[b % 2]
                hk = ("n1h", b % 2)
                self.dma("sp", hb[:], hv[:, :, b * TB:(b + 1) * TB], (), [hk])
                self.norm_block(hb, hk, b, l, 0, "n1")
            self.P.barrier()

    def mixers(self, l):
        import os
        sel = os.environ.get("MIX", "cadb")
        for nm, fn, c0 in (("c", self.mix_c, 8), ("a", self.mix_a, 0), ("d", self.mix_d, 12), ("b", self.mix_b, 4)):
            if nm in sel:
                fn(l)
            else:
                self.memset("dve", self.oall[:, c0:c0 + 4, :], 0.0, ["oall"])
            self.P.barrier()
        if "oall" in self.dbg_out and l == 0:
            self.dbgdump_oall()

    def dbgdump_oall(self):
        with ExitStack() as es:
            tmp = self.sb(es, "dbgtmp", [128, T], F32)
            for c in range(16):
                self.cp("dve", tmp[:], self.oall[:, c, :], ["oall"], ["dbgtmp"])
                self.dma("sp", self.dbg_out["oall"][c * 128:(c + 1) * 128, :], tmp[:], ["dbgtmp"], ())
            self.P.barrier()

    def proj_fm(self, w, wkey, ncontr, col0, M, rhs_fn, rkeys, evac):
        for b in range(NB):
            pst, pk = self.psb("x")
            for kc in range(ncontr):
                self.mm(pst[:M, :TB], w[:, kc, col0:col0 + M], rhs_fn(kc, b), kc == 0, kc == ncontr - 1,
                        [wkey] + rkeys, [pk])
            evac(b, pst[:M, :TB], pk)

    def proj_tm(self, w, wkey, ncontr, col0, N, lhs_fn, rkeys, tiles, evac):
        for j, (t0, n) in enumerate(tiles):
            pst, pk = self.psb("x")
            for kc in range(ncontr):
                self.mm(pst[:n, :N], lhs_fn(kc, t0, n), w[:, kc, col0:col0 + N], kc == 0, kc == ncontr - 1,
                        [wkey] + rkeys, [pk])
            evac(j, n, pst[:n, :N], pk)

    def attn_stream(self, tag, jobs, ptbuf, LOOK=2, DEFER=5):
        flat = []
        for ji, jb in enumerate(jobs):
            n = len(jb["tiles"])
            for i, tl in enumerate(jb["tiles"]):
                flat.append((ji, i, n, tl))
        pend = []
        state = {}
        pts = {}

        def issue_s(idx):
            ji, i, n, tl = flat[idx]
            jb = jobs[ji]
            if i == 0 and jb.get("pre") is not None:
                jb["pre"]()
            qn, scale = jb["qn"], jb["scale"]
            kr = tl["rows"]
            pst, pk = self.psb("s")
            nm = len(tl["mms"])
            for mi, (lt, rh) in enumerate(tl["mms"]):
                self.mm(pst[:kr, :qn], lt, rh, mi == 0, mi == nm - 1, jb["rkeys"], [pk])
            pi = self.ptidx
            self.ptidx += 1
            pt = ptbuf[pi % len(ptbuf)]
            ptk = (tag + "pt", pi % len(ptbuf))
            bias = tl["bias"]
            if bias is None:
                self.act(pt[:kr, :qn], pst[:kr, :qn], AF.Exp, [pk], [ptk], scale=scale)
            elif bias[0] == "c":
                self.act(pt[:kr, :qn], pst[:kr, :qn], AF.Exp, [pk] + bias[2], [ptk], bias=bias[1][:kr, :], scale=scale)
            else:
                tmp = self.sbias[pi % len(self.sbias)]
                tk = (tag + "sb", pi % len(self.sbias))
                self.stt("dve", tmp[:kr, :qn], pst[:kr, :qn], scale, bias[1], ALU.mult, ALU.add, [pk] + bias[2], [tk])
                self.act(pt[:kr, :qn], tmp[:kr, :qn], AF.Exp, [tk], [ptk])
            pts[idx] = (pt, ptk)

        def issue_pv(idx):
            ji, i, n, tl = flat[idx]
            jb = jobs[ji]
            qn, o_M = jb["qn"], jb["o_M"]
            kr = tl["rows"]
            if i == 0:
                state[ji] = self.psb("o") + self.psb("d")
            ops_, ok, dps, dk = state[ji]
            pt, ptk = pts.pop(idx)
            vl, vkeys = jb["v_fn"](tl)
            self.mm(ops_[:o_M, :qn], vl, pt[:kr, :qn], i == 0, i == n - 1, [ptk] + vkeys, [ok])
            self.mm(dps[:o_M, :qn], jb["ones_fn"](tl), pt[:kr, :qn], i == 0, i == n - 1, [ptk, "onesb"] + jb.get("okeys", []), [dk])
            if i == n - 1:
                jb["fin1"](ops_, ok, dps, dk)
                if jb.get("fin2") is not None:
                    pend.append((idx + DEFER, jb["fin2"]))
                del state[ji]

        N = len(flat)
        for idx in range(N + LOOK):
            if idx < N:
                issue_s(idx)
            if idx - LOOK >= 0:
                issue_pv(idx - LOOK)
            while pend and pend[0][0] <= idx - LOOK:
                pend.pop(0)[1]()
        for _, fn in pend:
            fn()

    def mix_c(self, l):
        I = self.I
        with ExitStack() as es:
            wc = self.sb(es, "c_w", [128, 8, 512], BF16)
            wq = self.sb(es, "c_wq", [128, 2, 768 + 256], BF16)
            wkv = self.sb(es, "c_wkv", [128, 1, 1024], BF16)
            wvv = self.sb(es, "c_wvv", [128, 1, 512], BF16)
            lat = self.sb(es, "c_lat", [128, 3, TB], F32)
            qn = self.sb(es, "c_qn", [128, 2, T], BF16)
            kvn = self.sb(es, "c_kvn", [128, T], BF16)
            kpe = self.sb(es, "c_kpe", [128, 17 * 128], BF16)
            rope = self.sb(es, "c_rope", [64, 2 * T], F32)
            vtok = self.sb(es, "c_v", [128, NT, 512], BF16)
            qno = self.sb(es, "c_qno", [128, 2, T], BF16)
            qpe = self.sb(es, "c_qpe", [128, 2, T], BF16)
            kno = self.sb(es, "c_kno", [128, 2, 17 * 128], BF16)
            ptbuf = [self.sb(es, "c_pt%d" % i, [128, TB], BF16) for i in range(3)]
            self.nsq = self.sb(es, "c_sq", [128, 2, TB], F32)
            self.nrs = self.sb(es, "c_rs", [128, TB], F32)
            t1 = self.sb(es, "c_t1", [64, TB], F32)
            t2 = self.sb(es, "c_t2", [64, TB], F32)
            rd = [self.sb(es, "c_rd%d" % i, [128, TB], F32) for i in range(2)]
            win = I["w_in"][l].rearrange("(kc p) n -> p kc n", p=128)
            self.loadw(wc[:, :, 0:448], win[:, :, OC_QA:OC_QA + 448], "c_w")
            self.loadw(wc[:, :, 448:512], I["w_in_kr_sw"][l].rearrange("(kc p) n -> p kc n", p=128), "c_w")
            self.loadw(wq[:, :, 0:768], I["mla_w_q_up"][l].rearrange("(kc p) n -> p kc n", p=128), "c_wq")
            self.loadw(wq[:, :, 768:1024], I["mla_w_q_up_sw"][l].rearrange("(kc p) n -> p kc n", p=128), "c_wq")
            self.loadw(wkv[:, 0, :], I["mla_w_kv_up"][l], "c_wkv")
            self.loadw(wvv[:, 0, :].rearrange("p (h e) -> p h e", h=4),
                       I["mla_w_kv_up"][l].rearrange("p (h e) -> p h e", h=4)[:, :, 128:256], "c_wvv")
            self.dma("sp", rope[:], I["rope"], (), ["c_rope"])
            self.memset("dve", kpe[:], 0.0, ["c_kpe"])
            for i_ in range(2):
                self.memset("dve", kno[:, i_, T:17 * 128], 0.0, [("c_kno", i_)])
            self.memset("dve", vtok[:, NT - 1, :], 0.0, ["c_v"])
            for i_ in range(2):
                self.memset("dve", qpe[64:128, i_, :], 0.0, [("c_qpe", i_)])
            ukeys = [("uT", kc) for kc in range(8)]
            urhs = lambda kc, b: self.U(kc, b * TB, (b + 1) * TB)
            for b in range(NB):
                sl = slice(b * TB, (b + 1) * TB)
                pst, pk = self.psb("x")
                pst2, pk2 = self.psb("x")
                for kc in range(8):
                    self.mm(pst[:64, :TB], wc[:, kc, 384:448], urhs(kc, b), kc == 0, kc == 7, ["c_w"] + ukeys, [pk])
                for kc in range(8):
                    self.mm(pst2[:64, :TB], wc[:, kc, 448:512], urhs(kc, b), kc == 0, kc == 7, ["c_w"] + ukeys, [pk2])
                self.tt("dve", t1[:, :], pst[:64, :TB], rope[:, sl], ALU.mult, [pk, "c_rope"], ["c_t1"])
                self.tt("dve", t2[:, :], pst2[:64, :TB], rope[:, T + b * TB:T + (b + 1) * TB], ALU.mult, [pk2, "c_rope"], ["c_t2"])
                self.tt("dve", kpe[0:64, sl], t1[:, :], t2[:, :], ALU.add, ["c_t1", "c_t2"], ["c_kpe"])
            sc = self.smallc
            for b in range(NB):
                sl = slice(b * TB, (b + 1) * TB)
                for ci in range(3):
                    pst, pk = self.psb("x")
                    for kc in range(8):
                        self.mm(pst[:, :TB], wc[:, kc, ci * 128:(ci + 1) * 128], urhs(kc, b), kc == 0, kc == 7, ["c_w"] + ukeys, [pk])
                    self.cp("act", lat[:, ci, :], pst[:, :TB], [pk], [("c_lat", ci)])
                self.rstd_from([lat[:, 0, :], lat[:, 1, :]], TB, 256, [("c_lat", 0), ("c_lat", 1)], self.nrs[:, :], "nrs", self.nsq, "nsq")
                for ci in range(2):
                    self.stt("dve", qn[:, ci, sl], lat[:, ci, :], sc[:, l * 8 + 3 + ci:l * 8 + 4 + ci], self.nrs[:, :],
                             ALU.mult, ALU.mult, [("c_lat", ci), "nrs", "smallc"], ["c_qn"])
                self.rstd_from([lat[:, 2, :]], TB, 128, [("c_lat", 2)], self.nrs[:, :], "nrs", self.nsq, "nsq")
                self.stt("dve", kvn[:, sl], lat[:, 2, :], sc[:, l * 8 + 2:l * 8 + 3], self.nrs[:, :],
                         ALU.mult, ALU.mult, [("c_lat", 2), "nrs", "smallc"], ["c_kvn"])
            tiles = [(128 * j, trows(j)) for j in range(NT)]
            self.proj_tm(wvv, "c_wvv", 1, 0, 512, lambda kc, t0, n: kvn[:, t0:t0 + n], ["c_kvn"], tiles,
                         lambda j, n, ps, pk: self.cp("act", vtok[:n, j, :], ps, [pk], ["c_v"]))
            scale = (128 + 64) ** -0.5
            self.ptidx = 0
            qrhs = lambda kc, b: qn[:, kc, b * TB:(b + 1) * TB]

            def cproj(h):
                hb_ = h % 2
                self.proj_fm(wq, "c_wq", 2, h * 192, 128, qrhs, ["c_qn"],
                             lambda b, ps, pk: self.cp("act", qno[:, hb_, b * TB:(b + 1) * TB], ps, [pk], [("c_qno", hb_)]))
                for b in range(NB):
                    sl = slice(b * TB, (b + 1) * TB)
                    pst, pk = self.psb("x")
                    pst2, pk2 = self.psb("x")
                    for kc in range(2):
                        self.mm(pst[:64, :TB], wq[:, kc, h * 192 + 128:h * 192 + 192], qrhs(kc, b), kc == 0, kc == 1, ["c_wq", "c_qn"], [pk])
                    for kc in range(2):
                        self.mm(pst2[:64, :TB], wq[:, kc, 768 + h * 64:768 + h * 64 + 64], qrhs(kc, b), kc == 0, kc == 1, ["c_wq", "c_qn"], [pk2])
                    self.tt("dve", t1[:, :], pst[:64, :TB], rope[:, sl], ALU.mult, [pk, "c_rope"], ["c_t1"])
                    self.tt("dve", t2[:, :], pst2[:64, :TB], rope[:, T + b * TB:T + (b + 1) * TB], ALU.mult, [pk2, "c_rope"], ["c_t2"])
                    self.tt("dve", qpe[0:64, hb_, sl], t1[:, :], t2[:, :], ALU.add, ["c_t1", "c_t2"], [("c_qpe", hb_)])
                self.proj_fm(wkv, "c_wkv", 1, h * 256, 128, lambda kc, b: kvn[:, b * TB:(b + 1) * TB], ["c_kvn"],
                             lambda b, ps, pk: self.cp("act", kno[:, hb_, b * TB:(b + 1) * TB], ps, [pk], [("c_kno", hb_)]))

            cproj(0)
            for h in range(4):
                if h + 1 < 4:
                    cproj(h + 1)
                hb_ = h % 2
                jobs = []
                for b in range(NB):
                    q0 = b * TB
                    tl = []
                    for j in range(NT):
                        kr = 128
                        tl.append(dict(rows=kr, j=j, bias=None,
                                       mms=[(kno[:, hb_, 128 * j:128 * j + kr], qno[:, hb_, q0:q0 + TB]),
                                            (kpe[:, 128 * j:128 * j + kr], qpe[:, hb_, q0:q0 + TB])]))

                    def fin1(ops_, ok, dps, dk, q0=q0, h=h, b=b):
                        rd_ = rd[b % 2]
                        self.recip(rd_[:, :], dps[:, :TB], [dk], [("c_rd", b % 2)])
                        self.tt("dve", self.oall[:, 8 + h, q0:q0 + TB], ops_[:, :TB], rd_[:, :], ALU.mult, [ok, ("c_rd", b % 2)], ["oall"])
                    jobs.append(dict(qn=TB, tiles=tl, o_M=128, scale=scale, fin1=fin1, fin2=None,
                                     v_fn=lambda t, h=h: (vtok[:t["rows"], t["j"], h * 128:(h + 1) * 128], ["c_v"]),
                                     ones_fn=lambda t: (self.ones16 if t["j"] == NT - 1 else self.onesb)[:, :],
                                     rkeys=[("c_kno", hb_), ("c_qno", hb_), "c_kpe", ("c_qpe", hb_)]))
                self.attn_stream("c", jobs, ptbuf)

    def mix_a(self, l):
        I = self.I
        lam_init = 0.8 - 0.6 * math.exp(-0.3 * l)
        with ExitStack() as es:
            vtok = self.sb(es, "a_v", [128, NT, 512], BF16)
            qT = self.sb(es, "a_q", [128, 8, T], BF16)
            kT = self.sb(es, "a_k", [128, 4, 17 * 128], BF16)
            lam = self.sb(es, "a_lam", [128, 256], F32)
            lt = self.sb(es, "a_lt", [128, 128], F32)
            lv = self.sb(es, "a_lv", [128, 4], F32)
            gsub = self.sb(es, "a_gs", [128, 1], F32)
            win = I["w_in"][l].rearrange("(kc p) n -> p kc n", p=128)
            self.dma("sp", lam[:], I["lamrep"][:, l * 256:(l + 1) * 256], (), ["a_lam"])
            self.tt("dve", lt[:, 0:64], lam[:, 0:64], lam[:, 64:128], ALU.mult, ["a_lam"], ["a_lt"])
            self.tt("dve", lt[:, 64:128], lam[:, 128:192], lam[:, 192:256], ALU.mult, ["a_lam"], ["a_lt"])
            self.P.op("dve", lambda e: e.reduce_sum(out=lv[:, 0:1], in_=lt[:, 0:64], axis=mybir.AxisListType.X), ["a_lt"], ["a_lv"])
            self.P.op("dve", lambda e: e.reduce_sum(out=lv[:, 1:2], in_=lt[:, 64:128], axis=mybir.AxisListType.X), ["a_lt"], ["a_lv"])
            self.act(lv[:, 0:2], lv[:, 0:2], AF.Exp, ["a_lv"], ["a_lv"])
            self.tt("dve", lv[:, 2:3], lv[:, 1:2], lv[:, 0:1], ALU.subtract, ["a_lv"], ["a_lv"])
            self.ts("dve", lv[:, 3:4], lv[:, 2:3], -lam_init, None, ALU.add, None, ["a_lv"], ["a_lv"])
            self.ts("dve", gsub[:, :], self.smallc[:, l * 8:l * 8 + 1], 1.0 - lam_init, None, ALU.mult, None, ["smallc"], ["a_gs"])
            self.memset("dve", qT[:], 0.0, ["a_q"])
            self.memset("dve", kT[:], 0.0, ["a_k"])
            self.memset("dve", vtok[:, NT - 1, :], 0.0, ["a_v"])
            ukeys = [("uT", kc) for kc in range(8)]
            urhs = lambda kc, b: self.U(kc, b * TB, (b + 1) * TB)
            tiles = [(128 * j, trows(j)) for j in range(NT)]
            with ExitStack() as es1:
                wqk = [self.sb(es1, "a_wqk%d" % i, [128, 8, 256], BF16) for i in range(2)]
                wv = self.sb(es1, "a_wv", [128, 8, 512], BF16)
                self.loadw(wv[:], win[:, :, OA_V:OA_V + 512], "a_wv")
                for h in range(4):
                    wq_, wqkk = wqk[h % 2], ("a_wqk", h % 2)
                    self.loadw(wq_[:, :, 0:128], win[:, :, OA_Q + h * 128:OA_Q + (h + 1) * 128], wqkk)
                    self.loadw(wq_[:, :, 128:256], win[:, :, OA_K + h * 128:OA_K + (h + 1) * 128], wqkk)
                    if h == 0:
                        self.proj_tm(wv, "a_wv", 8, 0, 512, lambda kc, t0, n: self.U(kc, t0, t0 + n), ukeys, tiles,
                                     lambda j, n, ps, pk: self.cp("act", vtok[:n, j, :], ps, [pk], ["a_v"]))

                    def qev(b, ps, pk, h=h):
                        self.cp("act", qT[0:64, 2 * h, b * TB:(b + 1) * TB], ps[0:64, :], [pk], ["a_q"])
                        self.cp("dve", qT[64:128, 2 * h + 1, b * TB:(b + 1) * TB], ps[64:128, :], [pk], ["a_q"])
                    self.proj_fm(wq_, wqkk, 8, 0, 128, urhs, ukeys, qev)
                    self.proj_fm(wq_, wqkk, 8, 128, 128, urhs, ukeys,
                                 lambda b, ps, pk, h=h: self.cp("act", kT[:, h, b * TB:(b + 1) * TB], ps, [pk], ["a_k"]))
                self.P.barrier()
            slab = [self.sb(es, "a_slab%d" % i, [128, 2, A_W], F32) for i in range(2)]
            ptbuf = [self.sb(es, "a_pt%d" % i, [128, TB], BF16) for i in range(3)]
            self.sbias = [self.sb(es, "a_sb%d" % i, [128, TB], F32) for i in range(2)]
            self.nsq = self.sb(es, "a_sq", [128, 2, TB], F32)
            self.nrs = self.sb(es, "a_rs", [128, TB], F32)
            rd = [self.sb(es, "a_rd%d" % i, [128, TB], F32) for i in range(2)]
            on = [self.sb(es, "a_on%d" % i, [128, TB], F32) for i in range(4)]
            self.ptidx = 0
            jobs = []
            for h in range(4):
                sl_ = slab[h % 2]
                slk = ("a_slab", h % 2)
                slab_loaded = [False]
                for b in range(NB):
                    q0 = b * TB
                    for m in range(2):
                        mh = m * 4 + h
                        tl = []
                        for j in range(NT):
                            kr = 128
                            o = TB * b - 128 * j
                            if 127 - o <= -91:
                                bias = ("c", self.aconst[:, mh:mh + 1], ["aconst"])
                            elif -o - (TB - 1) >= 91:
                                bias = ("c", self.aconst[:, 8 + mh:9 + mh], ["aconst"])
                            else:
                                bias = ("s", sl_[:kr, m, o + A_C:o + A_C + TB], [slk])
                            tl.append(dict(rows=kr, j=j, bias=bias,
                                           mms=[(kT[:, h, 128 * j:128 * j + kr], qT[:, 2 * h + m, q0:q0 + TB])]))
                        oi = (b % 2) * 2 + m

                        def fin1(ops_, ok, dps, dk, oi=oi):
                            rd_ = rd[oi % 2]
                            self.recip(rd_[:, :], dps[:, :TB], [dk], [("a_rd", oi % 2)])
                            self.tt("dve", on[oi][:, :], ops_[:, :TB], rd_[:, :], ALU.mult, [ok, ("a_rd", oi % 2)], [("a_on", oi)])

                        def fin2(b=b, h=h, q0=q0):
                            o0, o1 = (b % 2) * 2, (b % 2) * 2 + 1
                            self.stt("dve", on[o0][:, :], on[o1][:, :], lv[:, 3:4], on[o0][:, :], ALU.mult, ALU.add,
                                     [("a_on", o0), ("a_on", o1), "a_lv"], [("a_on", o0)])
                            self.rstd_from([on[o0][:, :]], TB, 128, [("a_on", o0)], self.nrs[:, :], "nrs", self.nsq, "nsq")
                            self.stt("dve", self.oall[:, h, q0:q0 + TB], on[o0][:, :], gsub[:, 0:1], self.nrs[:, :], ALU.mult, ALU.mult,
                                     [("a_on", o0), "nrs", "a_gs"], ["oall"])
                        jobs.append(dict(qn=TB, tiles=tl, o_M=128, scale=0.125, fin1=fin1, fin2=(fin2 if m == 1 else None),
                                         v_fn=lambda t, h=h: (vtok[:t["rows"], t["j"], h * 128:(h + 1) * 128], ["a_v"]),
                                         ones_fn=lambda t: (self.ones16 if t["j"] == NT - 1 else self.onesb)[:, :], rkeys=["a_q", "a_k"],
                                         pre=((lambda h=h: [self.dma("sp", slab[h % 2][:, mm_, :], self.I["slabA"][mm_ * 4 + h], (), [("a_slab", h % 2)]) for mm_ in range(2)])
                                              if (b == 0 and m == 0) else None)))
            self.attn_stream("a", jobs, ptbuf)

    def mix_d(self, l):
        I = self.I
        with ExitStack() as es:
            wq = self.sb(es, "d_wq", [128, 8, 512], BF16)
            wkk = self.sb(es, "d_wkk", [128, 8, 2, 128], BF16)
            wv = self.sb(es, "d_wv", [128, 8, 128], BF16)
            qT = self.sb(es, "d_q", [128, 8, T], BF16)
            kT2 = self.sb(es, "d_k", [128, 2, T], BF16)
            vpad = self.sb(es, "d_v", [128, NT * 4, 128], BF16)
            slab = [self.sb(es, "d_slab%d" % i, [128, D_SLABW], F32) for i in range(4)]
            ptbuf = [self.sb(es, "d_pt%d" % i, [128, 256], BF16) for i in range(3)]
            self.sbias = [self.sb(es, "d_sb%d" % i, [128, 256], F32) for i in range(2)]
            oh = self.sb(es, "d_oh", [128, 2, 128], BF16)
            es8 = self.sb(es, "d_es8", [128, 8], F32)
            es2 = self.sb(es, "d_es2", [128, 4], F32)
            rd = [self.sb(es, "d_rd%d" % i, [128, 256], F32) for i in range(2)]
            win = I["w_in"][l].rearrange("(kc p) n -> p kc n", p=128)
            self.loadw(wq[:], win[:, :, OD_Q:OD_Q + 512], "d_wq")
            for kv in range(2):
                for e in range(2):
                    self.loadw(wkk[:, :, kv, e * 64:(e + 1) * 64], win[:, :, OD_K + kv * 64:OD_K + (kv + 1) * 64], "d_wkk")
            self.loadw(wv[:], win[:, :, OD_V:OD_V + 128], "d_wv")
            self.memset("dve", vpad[:], 0.0, ["d_v"])
            self.memset("dve", oh[:], 0.0, ["d_oh"])
            self.memset("dve", oh[:, 0, 0:64], 1.0, ["d_oh"])
            self.memset("dve", oh[:, 1, 64:128], 1.0, ["d_oh"])
            self.dma("sp", es8[:], I["sinkrep"][:, l * 8:(l + 1) * 8], (), ["d_es8"])
            self.act(es8[:], es8[:], AF.Exp, ["d_es8"], ["d_es8"])
            for p in range(4):
                self.cp("dve", es2[0:64, p:p + 1], es8[0:64, 2 * p:2 * p + 1], ["d_es8"], ["d_es2"])
                self.cp("dve", es2[64:128, p:p + 1], es8[64:128, 2 * p + 1:2 * p + 2], ["d_es8"], ["d_es2"])
            ukeys = [("uT", kc) for kc in range(8)]
            urhs = lambda kc, b: self.U(kc, b * TB, (b + 1) * TB)
            self.memset("dve", qT[:], 0.0, ["d_q"])

            def qev(b, ps, pk, p):
                self.cp("act", qT[0:64, 2 * p, b * TB:(b + 1) * TB], ps[0:64, :], [pk], ["d_q"])
                self.cp("act", qT[64:128, 2 * p + 1, b * TB:(b + 1) * TB], ps[64:128, :], [pk], ["d_q"])
            for p in range(4):
                self.proj_fm(wq, "d_wq", 8, p * 128, 128, urhs, ukeys, lambda b, ps, pk, p=p: qev(b, ps, pk, p))
            for kv in range(2):
                self.proj_fm(wkk[:, :, kv, :], "d_wkk", 8, 0, 128, urhs, ukeys,
                             lambda b, ps, pk, kv=kv: self.cp("act", kT2[:, kv, b * TB:(b + 1) * TB], ps, [pk], ["d_k"]))
            tiles = [(128 * j, trows(j)) for j in range(NT)]

            def vev(j, n, ps, pk):
                for kv in range(2):
                    self.cp("act", vpad[:n, j * 4 + kv * 2, 0:64], ps[:, kv * 64:(kv + 1) * 64], [pk], ["d_v"])
                    self.cp("dve", vpad[:n, j * 4 + kv * 2 + 1, 64:128], ps[:, kv * 64:(kv + 1) * 64], [pk], ["d_v"])
            self.proj_tm(wv, "d_wv", 8, 0, 128, lambda kc, t0, n: self.U(kc, t0, t0 + n), ukeys, tiles, vev)
            self.ptidx = 0
            jobs = []
            for p in range(4):
                kv = p // 2
                sls = [slab[(p % 2) * 2 + e] for e in range(2)]
                slks = [("d_slab", (p % 2) * 2 + e) for e in range(2)]
                pre_p = (lambda p=p, sls=sls, slks=slks: [self.dma("sp", sls[e][:], I["slabD"][2 * p + e], (), [slks[e]]) for e in range(2)])
                for qb, (q0, qn) in enumerate(D_QB):
                    tl = []
                    for e in range(2):
                        h = 2 * p + e
                        pr = slice(64 * e, 64 * e + 64)
                        if qb == 0:
                            mb = ("s", sls[e][:16, D_W + 256:D_W + 256 + qn], [slks[e]])
                        else:
                            mb = ("c", self.dconst[:, h:h + 1], ["dconst"])
                        tl.append(dict(rows=16, j=0, e=e, bias=mb, mms=[(kT2[:, kv, 0:16], qT[:, h, q0:q0 + qn])]))
                        for j in range(max(0, q0 // 128 - 1), min(16, (q0 + qn + 127) // 128) + 1):
                            kr = trows(j)
                            o = q0 - 128 * j
                            if j == 0:
                                bs = sls[e][:kr, D_W:D_W + qn]
                            else:
                                bs = sls[e][:kr, o + D_C:o + D_C + qn]
                            tl.append(dict(rows=kr, j=j, e=e, bias=("s", bs, [slks[e]]),
                                           mms=[(kT2[:, kv, 128 * j:128 * j + kr], qT[:, h, q0:q0 + qn])]))

                    def fin1(ops_, ok, dps, dk, p=p, q0=q0, qn=qn, qb=qb):
                        rd_ = rd[qb % 2]
                        rk = ("d_rd", qb % 2)
                        self.ts("dve", rd_[:, :qn], dps[:, :qn], es2[:, p:p + 1], None, ALU.add, None, [dk, "d_es2"], [rk])
                        self.recip(rd_[:, :qn], rd_[:, :qn], [rk], [rk])
                        self.tt("dve", self.oall[:, 12 + p, q0:q0 + qn], ops_[:, :qn], rd_[:, :qn], ALU.mult, [ok, rk], ["oall"])
                    jobs.append(dict(qn=qn, tiles=tl, o_M=128, scale=0.125, fin1=fin1, fin2=None,
                                     v_fn=lambda t, kv=kv: (vpad[:t["rows"], t["j"] * 4 + kv * 2 + t["e"], :], ["d_v"]),
                                     ones_fn=lambda t: oh[:t["rows"], t["e"], :], okeys=["d_oh"], rkeys=["d_q", "d_k"],
                                     pre=(pre_p if qb == 0 else None)))
            self.attn_stream("d", jobs, ptbuf)

    def mix_b(self, l):
        I = self.I
        with ExitStack() as es:
            wb = self.sb(es, "b_w", [128, 8, 1568], BF16)
            wgu = self.sb(es, "b_wgu", [16, 2, 256], F32)
            gb = self.sb(es, "b_gb", [128, 512], F32)
            msk = self.sb(es, "b_msk", [128, 4, 128], F32)
            obw = self.sb(es, "b_obw", [128, 4, T], F32)
            S = self.sb(es, "b_S", [64, 4, 128], F32)
            Sbf = self.sb(es, "b_Sbf", [64, 4, 128], BF16)
            self.nsq = self.sb(es, "b_sq", [128, 2, 64], F32)
            self.nrs = self.sb(es, "b_rs", [128, 64], F32)
            NBUF = 2
            bufs = {}

            def tb(name, shape, dt, i):
                k = (name, i % NBUF)
                if k not in bufs:
                    bufs[k] = self.sb(es, "b_%s%d" % (name, i % NBUF), shape, dt)
                return bufs[k], ("b_" + name, i % NBUF)

            win = I["w_in"][l].rearrange("(kc p) n -> p kc n", p=128)
            self.loadw(wb[:], win[:, :, OB_Q:OB_Q + 1568], "b_w")
            self.dma("sp", wgu[:], I["gla_gate_up"][l].rearrange("g r c -> r g c"), (), ["b_wgu"])
            self.dma("sp", gb[:], I["gbias"][:, l * 512:(l + 1) * 512], (), ["b_gb"])
            self.dma("sp", msk[:], I["glam"].rearrange("p (m t) -> p m t", m=4), (), ["b_msk"])
            ukeys = [("uT", kc) for kc in range(8)]
            gch = [(0, 16)] + [(16 + 64 * (c - 1), 64) for c in range(1, 33)]
            seq = [(1, ci) for ci in range(32, -1, -1)] + [(0, ci) for ci in range(33)]
            NS = len(seq)
            cxs = {}

            def ctx(i):
                if i not in cxs:
                    dr, ci = seq[i]
                    t0, n = gch[ci]
                    cxs[i] = dict(i=i, dr=dr, ci=ci, t0=t0, n=n, mi_c=(0 if dr == 0 else 1), mi_r=(2 if dr == 0 else 3))
                return cxs[i]

            def P1(cx):
                i, dr, t0, n = cx["i"], cx["dr"], cx["t0"], cx["n"]
                ut = lambda kc: self.U(kc, t0, t0 + n)
                pgl, pglk = self.psb("x")
                for kc in range(8):
                    self.mm(pgl[:16, :n], wb[:, kc, 1536 + 16 * dr:1552 + 16 * dr], ut(kc), kc == 0, kc == 7, ["b_w"] + ukeys, [pglk])
                cx["glT"], cx["glk"] = tb3("glT", [16, 64], F32, i)
                self.cp("act", cx["glT"][:, :n], pgl[:16, :n], [pglk], [cx["glk"]])
                pvt, pvtk = self.psb("x")
                for kc in range(8):
                    self.mm(pvt[:n, :512], ut(kc), wb[:, kc, 512:1024], kc == 0, kc == 7, ["b_w"] + ukeys, [pvtk])
                cx["vt"], cx["vtk"] = tb3("vt", [64, 512], BF16, i)
                self.cp("act", cx["vt"][:n, :], pvt[:n, :512], [pvtk], [cx["vtk"]])
                if dr == 0:
                    prr, prrk = self.psb("x")
                    for h in range(4):
                        for kc in range(8):
                            self.mm(prr[:, h * 64:h * 64 + n], wb[:, kc, 1024 + h * 128:1024 + (h + 1) * 128], ut(kc), kc == 0, kc == 7, ["b_w"] + ukeys, [prrk])
                    k4 = ("sr", i % 4)
                    if k4 not in bufs:
                        bufs[k4] = self.sb(es, "b_sr%d" % (i % 4), [128, 4, 64], F32)
                    cx["sr"], cx["srk"] = bufs[k4], ("b_sr", i % 4)
                    r3 = prr[:, 0:256].rearrange("p (h t) -> p h t", h=4)[:, :, :n]
                    sr_ = cx["sr"]
                    self.act(sr_[:, :, :n], r3, AF.Exp, [prrk], [cx["srk"]], scale=-1.0)
                    self.ts("dve", sr_[:, :, :n], sr_[:, :, :n], 1.0, None, ALU.add, None, [cx["srk"]], [cx["srk"]])
                    self.recip(sr_[:, :, :n], sr_[:, :, :n], [cx["srk"]], [cx["srk"]])
                    self.tt("dve", sr_[:, :, :n], sr_[:, :, :n], r3, ALU.mult, [cx["srk"], prrk], [cx["srk"]])

            def P2a(cx):
                i, dr, n = cx["i"], cx["dr"], cx["n"]
                ppre, pprek = self.psb("x")
                self.mm(ppre[:n, :256], cx["glT"][:, :n], wgu[:, dr, :], True, True, [cx["glk"], "b_wgu"], [pprek])
                xla, xlk = tb("xla", [64, 256], F32, i)
                self.tt("dve", xla[:n, :], ppre[:n, :256], gb[:n, dr * 256:(dr + 1) * 256], ALU.add, [pprek, "b_gb"], [xlk])
                self.act(xla[:n, :], xla[:n, :], AF.Exp, [xlk], [xlk], scale=-1.0)
                cx["sp"], cx["spk"] = tb("sp", [64, 256], F32, i)
                self.act(cx["sp"][:n, :], xla[:n, :], AF.Ln, [xlk], [cx["spk"]], bias=1.0)

            def P2b(cx):
                i, dr, t0, n = cx["i"], cx["dr"], cx["t0"], cx["n"]
                sp_, spk = cx["sp"], cx["spk"]
                ut = lambda kc: self.U(kc, t0, t0 + n)
                pqk, pqkk = self.psb("x")
                for qi in range(8):
                    for kc in range(8):
                        self.mm(pqk[:64, qi * 64:qi * 64 + n], wb[:, kc, qi * 64:(qi + 1) * 64], ut(kc), kc == 0, kc == 7, ["b_w"] + ukeys, [pqkk])
                pkt, pktk = self.psb("x")
                for kc in range(8):
                    self.mm(pkt[:n, :256], ut(kc), wb[:, kc, 256:512], kc == 0, kc == 7, ["b_w"] + ukeys, [pktk])
                pc, pck = self.psb("x")
                for h in range(4):
                    self.mm(pc[:64, h * 64:h * 64 + n], sp_[:n, h * 64:(h + 1) * 64], msk[:n, cx["mi_c"], :n], True, True, [spk, "b_msk"], [pck])
                pr_, prk = self.psb("x")
                self.mm(pr_[:n, :256], msk[:n, cx["mi_r"], :n], sp_[:n, :], True, True, [spk, "b_msk"], [prk])
                cx["eb"], cx["ebk"] = tb("eb", [64, 4, 64], F32, i)
                einv, eik = tb("einv", [64, 4, 64], F32, i)
                pc3 = pc[:64, 0:256].rearrange("p (h t) -> p h t", h=4)[:, :, :n]
                self.act(cx["eb"][:, :, :n], pc3, AF.Exp, [pck], [cx["ebk"]], scale=-1.0 / 16)
                self.act(einv[:, :, :n], pc3, AF.Exp, [pck], [eik], scale=1.0 / 16)
                eo, eok = tb("eo", [64, 256], F32, i)
                self.act(eo[:n, :], pr_[:n, :256], AF.Exp, [prk], [eok], scale=-1.0 / 16)
                cx["qd"], cx["qdk"] = tb("qd", [64, 4, 64], BF16, i)
                cx["ki"], cx["kik"] = tb("ki", [64, 4, 64], BF16, i)
                q3 = pqk[:64, 0:256].rearrange("p (h t) -> p h t", h=4)[:, :, :n]
                k3 = pqk[:64, 256:512].rearrange("p (h t) -> p h t", h=4)[:, :, :n]
                self.stt("dve", cx["qd"][:, :, :n], q3, 0.125, cx["eb"][:, :, :n], ALU.mult, ALU.mult, [pqkk, cx["ebk"]], [cx["qdk"]])
                self.tt("dve", cx["ki"][:, :, :n], k3, einv[:, :, :n], ALU.mult, [pqkk, eik], [cx["kik"]])
                cx["ko"], cx["kok"] = tb("ko", [64, 256], BF16, i)
                self.tt("dve", cx["ko"][:n, :], pkt[:n, :256], eo[:n, :], ALU.mult, [pktk, eok], [cx["kok"]])

            def P3a(cx):
                i, n = cx["i"], cx["n"]
                pat, patk = self.psb("x")
                for h in range(4):
                    self.mm(pat[:n, h * 64:h * 64 + n], cx["ki"][:, h, :n], cx["qd"][:, h, :n], True, True, [cx["kik"], cx["qdk"]], [patk])
                cx["att"], cx["atk"] = tb("att", [64, 4, 64], BF16, i)
                for h in range(4):
                    self.tt("dve", cx["att"][:n, h, :n], pat[:n, h * 64:h * 64 + n], msk[:n, cx["mi_c"], :n], ALU.mult, [patk, "b_msk"], [cx["atk"]])
                pds, pdsk = self.psb("x")
                for h in range(4):
                    self.mm(pds[:64, h * 128:(h + 1) * 128], cx["ko"][:n, h * 64:(h + 1) * 64], cx["vt"][:n, h * 128:(h + 1) * 128], True, True, [cx["kok"], cx["vtk"]], [pdsk])
                cx["pds"], cx["pdsk"] = pds, pdsk

            def P3b(cx):
                i, dr, t0, n = cx["i"], cx["dr"], cx["t0"], cx["n"]
                vt, vtk, att, atk, qd, qdk = cx["vt"], cx["vtk"], cx["att"], cx["atk"], cx["qd"], cx["qdk"]
                if i == 0 or seq[i][0] != seq[i - 1][0]:
                    self.memset("dve", S[:], 0.0, ["b_S"])
                    self.memset("dve", Sbf[:], 0.0, ["b_Sbf"])
                po, pok = self.psb("x")
                for h in range(4):
                    self.mm(po[:, h * 64:h * 64 + n], vt[:n, h * 128:(h + 1) * 128], att[:n, h, :n], True, False, [vtk, atk], [pok])
                    self.mm(po[:, h * 64:h * 64 + n], Sbf[:, h, :], qd[:, h, :n], False, True, ["b_Sbf", qdk], [pok])
                dcol = (n - 1) if dr == 0 else 0
                pds, pdsk = cx["pds"], cx["pdsk"]
                for h in range(4):
                    self.stt("dve", S[:, h, :], S[:, h, :], cx["eb"][:, h, dcol:dcol + 1], pds[:64, h * 128:(h + 1) * 128], ALU.mult, ALU.add, ["b_S", cx["ebk"], pdsk], ["b_S"])
                self.cp("act", Sbf[:], S[:], ["b_S"], ["b_Sbf"])
                if dr == 1:
                    self.cp("act", obw[:, :, t0:t0 + n], po[:, 0:256].rearrange("p (h t) -> p h t", h=4)[:, :, :n], [pok], ["b_obw"])
                else:
                    of, ofk = tb("of", [128, 4, 64], F32, i)
                    sq4, sqk = tb("sq4", [128, 4, 64], F32, i)
                    if ("of", i % NBUF) not in init_done:
                        init_done.add(("of", i % NBUF))
                        self.memset("dve", of[:], 0.0, [ofk])
                    self.tt("dve", of[:, :, :n], po[:, 0:256].rearrange("p (h t) -> p h t", h=4)[:, :, :n], obw[:, :, t0:t0 + n], ALU.add, [pok, "b_obw"], [ofk])
                    self.tt("pool", sq4[:], of[:], of[:], ALU.mult, [ofk], [sqk])
                    cx["of"], cx["ofk"], cx["sq4"], cx["sqk"] = of, ofk, sq4, sqk
                    pend3.append(cx)
                del cxs[i]

            def P3c(cx):
                i, t0, n = cx["i"], cx["t0"], cx["n"]
                of, ofk = cx["of"], cx["ofk"]
                pn, pnk = self.psb("x")
                self.mm(pn[:, 0:256], self.onesf[:], cx["sq4"][:].rearrange("p h t -> p (h t)"), True, True, [cx["sqk"], "onesf"], [pnk])
                rs4, rsk = tb("rs4", [128, 4, 64], F32, i)
                rs2 = rs4[:].rearrange("p h t -> p (h t)")
                self.ts("dve", rs2, pn[:, 0:256], 1.0 / 128, EPS, ALU.mult, ALU.add, [pnk], [rsk])
                self.act(rs2, rs2, AF.Ln, [rsk], [rsk])
                self.act(rs2, rs2, AF.Exp, [rsk], [rsk], scale=-0.5)
                self.stt("dve", of[:, :, :n], of[:, :, :n], self.smallc[:, l * 8 + 1:l * 8 + 2], rs4[:, :, :n], ALU.mult, ALU.mult, [ofk, rsk, "smallc"], [ofk])
                self.tt("dve", self.oall[:, 4:8, t0:t0 + n], of[:, :, :n], cx["sr"][:, :, :n], ALU.mult, [ofk, cx["srk"]], ["oall"])

            init_done = set()
            pend3 = []

            def tb3(name, shape, dt, i):
                k = (name, i % 3)
                if k not in bufs:
                    bufs[k] = self.sb(es, "b_%s%d" % (name, i % 3), shape, dt)
                return bufs[k], ("b_" + name, i % 3)

            P1(ctx(0))
            P1(ctx(1))
            P2a(ctx(0))
            P2b(ctx(0))
            for i in range(NS):
                if i + 2 < NS:
                    P1(ctx(i + 2))
                if i + 1 < NS:
                    P2a(ctx(i + 1))
                P3a(ctx(i))
                while pend3:
                    P3c(pend3.pop(0))
                if i + 1 < NS:
                    P2b(ctx(i + 1))
                P3b(ctx(i))
            while pend3:
                P3c(pend3.pop(0))

    def resid_block(self, y, ykeys, b, l, gw, hsrc, hdst, bufs, nextnorm=None, final=False):
        hb, hk = bufs
        hv_s = hsrc.rearrange("(c p) t -> p c t", p=128)
        hv_d = hdst.rearrange("(c p) t -> p c t", p=128)
        sl = slice(b * TB, (b + 1) * TB)
        self.dma("sp", hb[:], hv_s[:, :, sl], (), [hk])
        self.rstd_from([y[:, kc, :] for kc in range(8)], TB, D, ykeys, self.nrs[:, :], "nrs", self.nsq, "nsq")
        for kc in range(8):
            self.stt("dve", y[:, kc, :], y[:, kc, :], self.gcol(l, gw, kc), self.nrs[:, :], ALU.mult, ALU.mult,
                     ykeys + ["nrs", "gains"], ykeys)
        self.tt("dve", hb[:], hb[:], y[:], ALU.add, [hk] + ykeys, [hk])
        st = self.dma("sp", hv_d[:, :, sl], hb[:], [hk], [("hdram", b)])
        if nextnorm is not None:
            self.norm_block(hb, hk, b, nextnorm[0], nextnorm[1], "nn")
        return st

    def merge_out(self, l, hsrc, hT, les):
        I = self.I
        with ExitStack() as es:
            merged = self.sb(es, "m_merged", [128, 8, T], BF16)
            with ExitStack() as es2:
                wbr = [self.sb(es2, "m_wbr%d" % i, [128, 16, 128], BF16) for i in range(2)]
                wg = [self.sb(es2, "m_wg%d" % i, [128, 8, 4, 128], BF16) for i in range(2)]
                sig = [self.sb(es2, "m_sig%d" % i, [128, TB], F32) for i in range(2)]
                acc = self.sb(es2, "m_acc", [128, TB], F32)
                prod = self.sb(es2, "m_prod", [128, TB], F32)
                ukeys = [("uT", kc) for kc in range(8)]
                for dc in range(8):
                    wb_, wbk = wbr[dc % 2], ("m_wbr", dc % 2)
                    wg_, wgk = wg[dc % 2], ("m_wg", dc % 2)
                    self.loadw(wb_[:], I["w_branch"][l].rearrange("n (ec p) d -> p (n ec) d", p=128)[:, :, dc * 128:(dc + 1) * 128], wbk)
                    for br in range(4):
                        self.loadw(wg_[:, :, br, :], I["w_in"][l].rearrange("(kc p) n -> p kc n", p=128)
                                   [:, :, O_GATE + br * 1024 + dc * 128:O_GATE + br * 1024 + (dc + 1) * 128], wgk)
                    for b in range(NB):
                        sl = slice(b * TB, (b + 1) * TB)
                        for br in range(4):
                            pg, pgk = self.psb("x")
                            for kc in range(8):
                                self.mm(pg[:, :TB], wg_[:, kc, br, :], self.U(kc, b * TB, (b + 1) * TB), kc == 0, kc == 7, [wgk] + ukeys, [pgk])
                            pp, ppk = self.psb("x")
                            for ec in range(4):
                                self.mm(pp[:, :TB], wb_[:, br * 4 + ec, :], self.oall[:, br * 4 + ec, sl], ec == 0, ec == 3, [wbk, "oall"], [ppk])
                            sg, sgk = sig[br % 2], ("m_sig", br % 2)
                            self.act(sg[:, :], pg[:, :TB], AF.Sigmoid, [pgk], [sgk])
                            if br == 0:
                                self.tt("dve", acc[:, :], pp[:, :TB], sg[:, :], ALU.mult, [ppk, sgk], ["m_acc"])
                            else:
                                self.tt("dve", prod[:, :], pp[:, :TB], sg[:, :], ALU.mult, [ppk, sgk], ["m_prod"])
                                if br < 3:
                                    self.tt("dve", acc[:, :], acc[:, :], prod[:, :], ALU.add, ["m_acc", "m_prod"], ["m_acc"])
                                else:
                                    self.tt("dve", merged[:, dc, sl], acc[:, :], prod[:, :], ALU.add, ["m_acc", "m_prod"], ["m_merged"])
                self.P.barrier()
            if "merged" in self.dbg_out and l == 0:
                with ExitStack() as es3:
                    tmp = self.sb(es3, "dbgtmp2", [128, T], F32)
                    for c in range(8):
                        self.cp("dve", tmp[:], merged[:, c, :], ["m_merged"], ["dbgtmp2"])
                        self.dma("sp", self.dbg_out["merged"][c * 128:(c + 1) * 128, :], tmp[:], ["dbgtmp2"], ())
                    self.P.barrier()
            with ExitStack() as es2:
                wo = self.sb(es2, "o_w", [128, 8, D], BF16)
                y2 = [self.sb(es2, "o_y%d" % i, [128, 8, TB], F32) for i in range(2)]
                hb2 = [self.sb(es2, "o_h%d" % i, [128, 8, TB], F32) for i in range(2)]
                self.nsq = self.sb(es2, "o_sq", [128, 2, TB], F32)
                self.nrs = self.sb(es2, "o_rs", [128, TB], F32)
                self.loadw(wo[:], I["w_out"][l].rearrange("(kc p) n -> p kc n", p=128), "o_w")
                for b in range(NB):
                    y, yk = y2[b % 2], ("o_y", b % 2)
                    sl = slice(b * TB, (b + 1) * TB)
                    for dc in range(8):
                        pst, pk = self.psb("x")
                        for kc in range(8):
                            self.mm(pst[:, :TB], wo[:, kc, dc * 128:(dc + 1) * 128], merged[:, kc, sl], kc == 0, kc == 7, ["o_w", "m_merged"], [pk])
                        self.cp("act", y[:, dc, :], pst[:, :TB], [pk], [yk])
                    self.resid_block(y, [yk], b, l, 1, hsrc, hT, (hb2[b % 2], ("o_h", b % 2)), nextnorm=(l, 2))
                self.P.barrier()

    def ffn(self, l, hT, hdst):
        I = self.I
        finals = []
        NJ = DFF // 128
        for half in range(2):
            with ExitStack() as es:
                actT = self.sb(es, "f_act", [128, NJ, 3 * TB], BF16)
                with ExitStack() as es2:
                    wu = [self.sb(es2, "f_wu%d" % i, [128, 8, 2, 128], BF16) for i in range(2)]
                    cgs = [self.sb(es2, "f_cg%d" % i, [128, TB], F32) for i in range(2)]
                    cvs = [self.sb(es2, "f_cv%d" % i, [128, TB], F32) for i in range(2)]
                    t1s = [self.sb(es2, "f_t1%d" % i, [128, TB], F32) for i in range(2)]
                    wup = I["ffn_w_up"][l].rearrange("(kc p) n -> p kc n", p=128)
                    ukeys = [("uT", kc) for kc in range(8)]
                    cw = self.convw
                    it = 0
                    for j in range(NJ):
                        w_, wk = wu[j % 2], ("f_wu", j % 2)
                        self.loadw(w_[:, :, 0, :], wup[:, :, j * 128:(j + 1) * 128], wk)
                        self.loadw(w_[:, :, 1, :], wup[:, :, DFF + j * 128:DFF + (j + 1) * 128], wk)
                        for b in range(3 * half, 3 * half + 3):
                            it += 1
                            cg, cv, t1 = cgs[it % 2], cvs[it % 2], t1s[it % 2]
                            cgk, cvk, t1k = ("f_cg", it % 2), ("f_cv", it % 2), ("f_t1", it % 2)
                            for gv in range(2):
                                pst, pk = self.psb("x")
                                for kc in range(8):
                                    self.mm(pst[:, :TB + 2], w_[:, kc, gv, :], self.uT[:, kc, b * TB:b * TB + TB + 2], kc == 0, kc == 7, [wk] + ukeys, [pk])
                                ch = gv * NJ + j
                                base = (l * 4) * 44
                                c0 = cw[:, base + ch:base + ch + 1]
                                c1 = cw[:, base + 44 + ch:base + 44 + ch + 1]
                                c2 = cw[:, base + 88 + ch:base + 88 + ch + 1]
                                cb = cw[:, base + 132 + ch:base + 132 + ch + 1]
                                dst, dk = (cg, cgk) if gv == 0 else (cv, cvk)
                                self.act(dst[:, :], pst[:, 0:TB], AF.Identity, [pk, "convw"], [dk], bias=cb, scale=c0)
                                self.stt("dve", dst[:, :], pst[:, 1:TB + 1], c1, dst[:, :], ALU.mult, ALU.add, [pk, dk, "convw"], [dk])
                                self.stt("dve", dst[:, :], pst[:, 2:TB + 2], c2, dst[:, :], ALU.mult, ALU.add, [pk, dk, "convw"], [dk])
                            self.act(t1[:, :], cg[:, :], AF.Gelu_apprx_tanh, [cgk], [t1k])
                            self.tt("pool", actT[:, j, (b - 3 * half) * TB:(b - 3 * half + 1) * TB], t1[:, :], cv[:, :], ALU.mult, [t1k, cvk], ["f_act"])
                    self.P.barrier()
                with ExitStack() as es2:
                    wd = self.sb(es2, "f_wd", [128, NJ, D], BF16)
                    y2 = [self.sb(es2, "f_y%d" % i, [128, 8, TB], F32) for i in range(2)]
                    hb2 = [self.sb(es2, "f_h%d" % i, [128, 8, TB], F32) for i in range(2)]
                    self.nsq = self.sb(es2, "f_sq", [128, 2, TB], F32)
                    self.nrs = self.sb(es2, "f_rs", [128, TB], F32)
                    self.loadw(wd[:], I["ffn_w_down"][l].rearrange("(j p) n -> p j n", p=128), "f_wd")
                    for b in range(3 * half, 3 * half + 3):
                        y, yk = y2[b % 2], ("f_y", b % 2)
                        sl = slice(b * TB, (b + 1) * TB)
                        for dc in range(8):
                            pst, pk = self.psb("x")
                            for j in range(NJ):
                                self.mm(pst[:, :TB], wd[:, j, dc * 128:(dc + 1) * 128], actT[:, j, (b - 3 * half) * TB:(b - 3 * half + 1) * TB], j == 0, j == NJ - 1, ["f_wd", "f_act"], [pk])
                            self.cp("act", y[:, dc, :], pst[:, :TB], [pk], [yk])
                        st = self.resid_block(y, [yk], b, l, 3, hT, hdst, (hb2[b % 2], ("f_h", b % 2)))
                        finals.append(st)
                    self.P.barrier()
        return finals


def host_consts(inp):
    f32 = np.float32
    c = {}
    tab = np.asarray(inp["rel_bias_table"], f32)
    kk = np.arange(128)[:, None]
    jj = np.arange(A_W)[None, :]
    bk = rel_bucket_jax(kk - jj + A_C)
    c["slabA"] = np.ascontiguousarray(np.transpose(tab[bk][:, :, 0:8], (2, 0, 1))).astype(f32)
    ac = np.concatenate([tab[15, 0:8], tab[31, 0:8]])
    c["aconst"] = np.ascontiguousarray(np.broadcast_to(ac[None, :], (128, 16))).astype(f32)
    tabd = tab[:, 8:16]
    jj = np.arange(D_W)[None, :]
    rel = kk - jj + D_C
    tz = np.where((np.abs(rel) <= 128)[:, :, None], tabd[rel_bucket_jax(rel)], f32(NEG))
    qq = np.arange(256)[None, :]
    rel0 = kk - qq
    t0 = np.where(((np.abs(rel0) <= 128) & (kk >= NMETA))[:, :, None], tabd[rel_bucket_jax(rel0)], f32(NEG))
    tm = np.where((kk < NMETA)[:, :, None], tabd[rel_bucket_jax(rel0)], f32(NEG))
    c["slabD"] = np.ascontiguousarray(np.transpose(np.concatenate([tz, t0, tm], axis=1), (2, 0, 1))).astype(f32)
    c["dconst"] = np.ascontiguousarray(np.broadcast_to(tabd[15][None, :], (128, 8))).astype(f32)
    half = 32
    inv = (10000.0 ** (-np.arange(half, dtype=np.float32) / half)).astype(f32)
    ang = np.arange(T, dtype=f32)[None, :] * inv[:, None]
    cos, sin = np.cos(ang).astype(f32), np.sin(ang).astype(f32)
    c["rope"] = np.ascontiguousarray(np.concatenate([np.concatenate([cos, cos], 0), np.concatenate([-sin, sin], 0)], 1)).astype(f32)
    s = np.arange(128)[:, None]
    t = np.arange(128)[None, :]
    same = (s // 64) == (t // 64)
    LT = (same & (s <= t)).astype(f32)
    L = (same & (s >= t)).astype(f32)
    SU = (same & (s > t)).astype(f32)
    SL = (same & (s < t)).astype(f32)
    c["glam"] = np.ascontiguousarray(np.concatenate([LT, L, SU, SL], 1))
    return c


def host_layout(inp):
    f32 = np.float32
    g = {}
    sw = np.concatenate([np.arange(32, 64), np.arange(0, 32)])
    wq = np.asarray(inp["mla_w_q_up"], f32).reshape(DEPTH, 256, 4, 192)
    g["mla_w_q_up_sw"] = np.ascontiguousarray(wq[:, :, :, 128:][:, :, :, sw].reshape(DEPTH, 256, 256))
    g["w_in_kr_sw"] = np.ascontiguousarray(np.asarray(inp["w_in"])[:, :, OC_KR:OC_KR + 64][:, :, sw])
    gains = np.stack([inp["norm_mix_pre"], inp["norm_mix_post"], inp["norm_ffn_pre"], inp["norm_ffn_post"]], 1)
    g["gains"] = np.ascontiguousarray(gains.reshape(DEPTH * 4 * 8, 128).T).astype(f32)
    cw = np.concatenate([np.asarray(inp["ffn_conv_w"], f32), np.asarray(inp["ffn_conv_b"], f32)[:, None, :]], 1)
    g["convw"] = np.ascontiguousarray(cw.reshape(DEPTH * 4 * 44, 128).T).astype(f32)
    sc = np.zeros((DEPTH, 8, 128), f32)
    sc[:, 0] = inp["diff_subln"]
    sc[:, 1] = inp["gla_norm"]
    sc[:, 2] = inp["mla_kv_norm"]
    sc[:, 3:5] = np.asarray(inp["mla_q_norm"]).reshape(DEPTH, 2, 128)
    g["smallc"] = np.ascontiguousarray(sc.reshape(DEPTH * 8, 128).T)
    g["lamrep"] = np.ascontiguousarray(np.broadcast_to(np.asarray(inp["diff_lambda"], f32).reshape(1, DEPTH * 256), (128, DEPTH * 256)))
    g["gbias"] = np.ascontiguousarray(np.broadcast_to(np.asarray(inp["gla_gate_bias"], f32).reshape(1, DEPTH * 512), (128, DEPTH * 512)))
    g["sinkrep"] = np.ascontiguousarray(np.broadcast_to(np.asarray(inp["swa_sinks"], f32).reshape(1, DEPTH * 8), (128, DEPTH * 8)))
    return g


_NC_CACHE = {}


def get_nc(layers=(0, 1), dbg=None):
    key = (tuple(layers), tuple(sorted((dbg or {}).items())))
    if key not in _NC_CACHE:
        nc = bass.Bass("TRN2", target_bir_lowering=False)
        KB(nc, dbg).build(layers)
        _NC_CACHE[key] = nc
    return _NC_CACHE[key]


def make_in_maps(inp, cores):
    shared = {}
    for k in ("w_in", "w_branch", "w_out", "ffn_w_up", "ffn_w_down", "mla_w_q_up", "mla_w_kv_up", "gla_gate_up"):
        shared[k] = np.ascontiguousarray(np.asarray(inp[k], np.float32))
    shared.update(host_layout(inp))
    shared.update(host_consts(inp))
    meta = np.asarray(inp["meta_tokens"], np.float32)
    x = np.asarray(inp["x"], np.float32)
    maps = []
    for b in cores:
        h0 = np.concatenate([meta, x[b]], axis=0)
        m = dict(shared)
        m["h0T"] = np.ascontiguousarray(h0.T)
        maps.append(m)
    return maps


def kernel(**inputs):
    nc = get_nc()
    maps = make_in_maps(inputs, list(range(8)))
    res = run_bass_kernel_spmd(nc, maps, core_ids=list(range(8)))
    out = np.stack([np.ascontiguousarray(r["outT"][:, NMETA:].T) for r in res.results], axis=0)
    return out.astype(np.float32)
```

```python
import math
import os
import numpy as np
from contextlib import ExitStack
import concourse.bass as bass
import concourse.mybir as mybir
from concourse.bass_utils import run_bass_kernel_spmd

F32 = mybir.dt.float32
BF16 = mybir.dt.bfloat16
AF = mybir.ActivationFunctionType
ALU = mybir.AluOpType

DEPTH = 2
D = 1024
SEQ = 2048
NMETA = 16
T = SEQ + NMETA
TB = 344
NB = 6
NT = 17
EPS = 1e-6
DFF = 2816
NIN = 8416
OA_Q, OA_K, OA_V = 0, 512, 1024
OB_Q, OB_K, OB_V, OB_R, OB_G = 1536, 1792, 2048, 2560, 3072
OC_QA, OC_KVA, OC_KR = 3104, 3360, 3488
OD_Q, OD_K, OD_V = 3552, 4064, 4192
O_GATE = 4320
NEG = -30000.0


def trows(j):
    return 128 if j < 16 else 16


class Dep:
    __slots__ = ("w", "r")

    def __init__(self):
        self.w = None
        self.r = []


class Op:
    __slots__ = ("eng", "fn", "deps", "ms", "val", "sem", "is_dma")

    def __init__(self, eng, fn, is_dma):
        self.eng = eng
        self.fn = fn
        self.deps = []
        self.ms = False
        self.val = 0
        self.sem = None
        self.is_dma = is_dma


ENGS = ("pe", "act", "dve", "pool", "sp")
NDMASEM = 8


class Prog:
    def __init__(self, nc, es):
        self.nc = nc
        self.es = es
        self.ops = {e: [] for e in ENGS}
        self.dma_hist = {e: [] for e in ENGS}
        self.dd = {}

    def D(self, key):
        d = self.dd.get(key)
        if d is None:
            d = self.dd[key] = Dep()
        return d

    def op(self, eng, fn, r=(), w=(), dma=False, extra=()):
        o = Op(eng, fn, dma)
        need = list(extra)
        for k in r:
            d = self.D(k)
            if d.w is not None:
                need.append(d.w)
        for k in w:
            d = self.D(k)
            if d.w is not None:
                need.append(d.w)
            for q in d.r:
                need.append(q)
        if dma:
            h = self.dma_hist[eng]
            if len(h) >= NDMASEM:
                need.append(h[-NDMASEM])
            h.append(o)
        seen = set()
        for p in need:
            if p is o or id(p) in seen:
                continue
            seen.add(id(p))
            if (not dma) and eng == "pe" and p.eng == "pe" and not p.is_dma:
                continue
            o.deps.append(p)
        for k in r:
            lst = self.D(k).r
            if not dma:
                lst[:] = [q for q in lst if q.is_dma or q.eng != eng]
            lst.append(o)
        for k in w:
            d = self.D(k)
            d.w = o
            d.r = []
        self.ops[eng].append(o)
        return o

    def barrier(self):
        lasts = []
        for e in ENGS:
            cl = [o for o in self.ops[e] if not o.is_dma]
            if cl:
                lasts.append(cl[-1])
            lasts.extend(self.dma_hist[e][-NDMASEM:])
        for e in ENGS:
            if self.ops[e]:
                self.op(e, lambda eng: eng.nop(), extra=lasts)

    def finalize(self, final_ops=()):
        nc, es = self.nc, self.es
        for e in ENGS:
            for o in self.ops[e]:
                for p in o.deps:
                    p.ms = True
        esem = {e: es.enter_context(nc.semaphore("s_" + e)) for e in ENGS}
        dsem = {e: [es.enter_context(nc.semaphore("d_%s%d" % (e, i))) for i in range(NDMASEM)]
                for e in ENGS if self.dma_hist[e]}
        for e in ENGS:
            cnt = 0
            dcnt = [0] * NDMASEM
            k = 0
            for o in self.ops[e]:
                if o.is_dma:
                    s = k % NDMASEM
                    k += 1
                    dcnt[s] += 16
                    o.sem = dsem[e][s]
                    o.val = dcnt[s]
                elif o.ms:
                    cnt += 1
                    o.sem = esem[e]
                    o.val = cnt
        engobj = {"pe": "tensor", "act": "scalar", "dve": "vector", "pool": "gpsimd", "sp": "sync"}
        block = es.enter_context(nc.Block())

        def emit(e):
            def body(eng):
                known = {}
                for o in self.ops[e]:
                    wl = {}
                    for p in o.deps:
                        key = id(p.sem)
                        if known.get(key, 0) >= p.val:
                            continue
                        if key not in wl or wl[key][1] < p.val:
                            wl[key] = (p.sem, p.val)
                    for key, (s, v) in wl.items():
                        eng.wait_ge(s, v)
                        known[key] = v
                    ins = o.fn(eng)
                    if o.is_dma:
                        ins.then_inc(o.sem, 16)
                    elif o.ms:
                        ins.then_inc(o.sem, 1)
                if e == "sp":
                    for o in final_ops:
                        eng.wait_ge(o.sem, o.val)
            return body

        for e in ENGS:
            if self.ops[e] or e == "sp":
                getattr(block, engobj[e])(emit(e))


def rel_bucket(rel):
    rel = np.asarray(rel, dtype=np.int64)
    half, max_exact = 16, 8
    ret = np.where(rel > 0, half, 0)
    n = np.abs(rel)
    nf = np.maximum(n, 1).astype(np.float32)
    large = max_exact + (np.log(nf / np.float32(max_exact)) / np.float32(math.log(128 / max_exact))
                         * (half - max_exact)).astype(np.int32)
    large = np.minimum(large, half - 1)
    return ret + np.where(n < max_exact, n, large)


def rel_bucket_jax(rel):
    import jax
    import jax.numpy as jnp
    with jax.default_device(jax.devices("cpu")[0]):
        rel = jnp.asarray(np.asarray(rel, dtype=np.int32))
        half, max_exact = 16, 8
        ret = jnp.where(rel > 0, half, 0)
        n = jnp.abs(rel)
        nf = jnp.maximum(n, 1).astype(jnp.float32)
        large = max_exact + (jnp.log(nf / max_exact) / math.log(128 / max_exact) * (half - max_exact)).astype(jnp.int32)
        large = jnp.minimum(large, half - 1)
        return np.asarray(ret + jnp.where(n < max_exact, n, large))


A_OS = sorted(set(TB * b - 128 * j for b in range(NB) for j in range(NT)))
A_NEAR = [o for o in A_OS if not (127 - o <= -91 or -o - (TB - 1) >= 91)]
A_C = -min(A_NEAR)
A_W = TB + max(A_NEAR) + A_C
D_QB = [(256 * i, 256) for i in range(8)] + [(2048, 16)]
D_C = 256
D_W = 256 + 384
D_SLABW = D_W + 256 + 256


class KB:
    def __init__(self, nc, dbg=None):
        self.nc = nc
        self.dbg = dbg or {}

    def mm(self, out, lhsT, rhs, start, stop, r, w):
        return self.P.op("pe", lambda e: e.matmul(out, lhsT=lhsT, rhs=rhs, start=start, stop=stop), r, w)

    def act(self, out, in_, func, r, w, bias=None, scale=1.0, accum=None):
        def f(e):
            kw = {}
            if bias is not None:
                kw["bias"] = bias
            if accum is not None:
                kw["accum_out"] = accum
            return e.activation(out=out, in_=in_, func=func, scale=scale, **kw)
        return self.P.op("act", f, r, w)

    def stt(self, eng, out, in0, scalar, in1, op0, op1, r, w):
        nm = {"dve": "vector", "pool": "gpsimd"}[eng]
        return self.P.op(eng, lambda e: e.scalar_tensor_tensor(out=out, in0=in0, scalar=scalar, in1=in1, op0=op0, op1=op1), r, w)

    def ts(self, eng, out, in0, s1, s2, op0, op1, r, w):
        if s2 is None:
            return self.P.op(eng, lambda e: e.tensor_scalar(out=out, in0=in0, scalar1=s1, scalar2=None, op0=op0), r, w)
        return self.P.op(eng, lambda e: e.tensor_scalar(out=out, in0=in0, scalar1=s1, scalar2=s2, op0=op0, op1=op1), r, w)

    def tt(self, eng, out, in0, in1, op, r, w):
        return self.P.op(eng, lambda e: e.tensor_tensor(out=out, in0=in0, in1=in1, op=op), r, w)

    def cp(self, eng, out, in_, r, w):
        if eng == "act":
            return self.P.op("act", lambda e: e.copy(out=out, in_=in_), r, w)
        return self.P.op(eng, lambda e: e.tensor_copy(out=out, in_=in_), r, w)

    def recip(self, out, in_, r, w):
        return self.P.op("dve", lambda e: e.reciprocal(out=out, in_=in_), r, w)

    def memset(self, eng, ap, val, w):
        return self.P.op(eng, lambda e: e.memset(ap, val), (), w)

    def dma(self, q, out, in_, r, w):
        return self.P.op(q, lambda e: e.dma_start(out=out, in_=in_), r, w, dma=True)

    def sb(self, es, name, shape, dt):
        self.sbcnt = getattr(self, "sbcnt", 0) + 1
        return es.enter_context(self.nc.sbuf_tensor("sb%d_%s" % (self.sbcnt, name), shape, dt))

    def U(self, kc, t0, t1):
        return self.uT[:, kc, t0 + 1:t1 + 1]

    def psb(self, group):
        lst = self.psgroups[group]
        i = self.psidx.get(group, 0)
        self.psidx[group] = i + 1
        b = lst[i % len(lst)]
        return self.ps[b], ("ps", b)

    def rstd_from(self, srcs, n, Dn, rkeys, out_ap, out_key, sq_ap, sq_key, sq_eng="pool"):
        pst, pk = self.ps[7], ("ps", 7)
        for i, s in enumerate(srcs):
            self.tt(sq_eng, sq_ap[:, i % 2, :n], s, s, ALU.mult, rkeys, [(sq_key, i % 2)])
            self.mm(pst[:, :n], self.onesf[:], sq_ap[:, i % 2, :n], i == 0, i == len(srcs) - 1, [(sq_key, i % 2), "onesf"], [pk])
        self.ts("dve", out_ap, pst[:, :n], 1.0 / Dn, EPS, ALU.mult, ALU.add, [pk], [out_key])
        self.act(out_ap, out_ap, AF.Ln, [out_key], [out_key])
        self.act(out_ap, out_ap, AF.Exp, [out_key], [out_key], scale=-0.5)

    def loadw(self, dst, src, wkey):
        return self.dma("pool", dst, src, (), [wkey])

    def build(self, layers=(0, 1)):
        nc = self.nc
        I = {}

        def din(name, shape):
            I[name] = nc.dram_tensor(name, list(shape), F32, kind="ExternalInput").ap()
            return I[name]

        din("h0T", [D, T])
        din("w_in", [DEPTH, D, NIN])
        din("w_branch", [DEPTH, 4, 512, D])
        din("w_out", [DEPTH, D, D])
        din("ffn_w_up", [DEPTH, D, 2 * DFF])
        din("ffn_w_down", [DEPTH, DFF, D])
        din("mla_w_q_up", [DEPTH, 256, 768])
        din("mla_w_q_up_sw", [DEPTH, 256, 256])
        din("mla_w_kv_up", [DEPTH, 128, 1024])
        din("w_in_kr_sw", [DEPTH, D, 64])
        din("gla_gate_up", [DEPTH, 2, 16, 256])
        din("gains", [128, DEPTH * 4 * 8])
        din("convw", [128, DEPTH * 4 * 44])
        din("smallc", [128, DEPTH * 8])
        din("lamrep", [128, DEPTH * 256])
        din("gbias", [128, DEPTH * 512])
        din("sinkrep", [128, DEPTH * 8])
        din("aconst", [128, 16])
        din("dconst", [128, 8])
        din("slabA", [8, 128, A_W])
        din("slabD", [8, 128, D_SLABW])
        din("rope", [64, 2 * T])
        din("glam", [128, 4 * 128])
        outT = nc.dram_tensor("outT", [D, T], F32, kind="ExternalOutput").ap()
        hT = nc.dram_tensor("hT_scr", [D, T], F32, kind="Internal").ap()
        dbg_out = {}
        for k, shp in self.dbg.items():
            dbg_out[k] = nc.dram_tensor("dbg_" + k, list(shp), F32, kind="ExternalOutput").ap()
        self.dbg_out = dbg_out
        self.I = I

        with ExitStack() as es:
            P = self.P = Prog(nc, es)
            self.ps = [es.enter_context(nc.psum_tensor("ps%d" % i, [128, 512], F32)) for i in range(8)]
            self.psgroups = {"s": [0, 1, 2], "o": [3, 4], "d": [5, 6], "x": [0, 1, 2, 3, 4, 5, 6]}
            self.psidx = {}
            self.uT = self.sb(es, "uT", [128, 8, T + 2], BF16)
            self.onesf = self.sb(es, "onesf", [128, 128], F32)
            self.onesb = self.sb(es, "onesb", [128, 128], BF16)
            self.gains = self.sb(es, "gains", [128, DEPTH * 32], F32)
            self.convw = self.sb(es, "convw", [128, DEPTH * 4 * 44], F32)
            self.smallc = self.sb(es, "smallc", [128, DEPTH * 8], F32)
            self.aconst = self.sb(es, "aconst", [128, 16], F32)
            self.dconst = self.sb(es, "dconst", [128, 8], F32)
            self.memset("dve", self.onesf[:], 1.0, ["onesf"])
            self.memset("dve", self.onesb[:], 1.0, ["onesb"])
            self.ones16 = self.sb(es, "ones16", [128, 128], BF16)
            self.memset("dve", self.ones16[:], 0.0, ["onesb"])
            self.memset("dve", self.ones16[0:16, :], 1.0, ["onesb"])
            self.memset("dve", self.uT[:, :, 0:1], 0.0, ["uT"])
            self.memset("dve", self.uT[:, :, T + 1:T + 2], 0.0, ["uT"])
            self.dma("sp", self.gains[:], I["gains"], (), ["gains"])
            self.dma("sp", self.convw[:], I["convw"], (), ["convw"])
            self.dma("sp", self.smallc[:], I["smallc"], (), ["smallc"])
            self.dma("sp", self.aconst[:], I["aconst"], (), ["aconst"])
            self.dma("sp", self.dconst[:], I["dconst"], (), ["dconst"])

            finals = []
            nl = len(layers)
            for li, l in enumerate(layers):
                hsrc = I["h0T"] if li == 0 else hT
                last = (li == nl - 1)
                self.norm1(l, hsrc)
                with ExitStack() as les:
                    self.oall = self.sb(les, "oall", [128, 16, T], BF16)
                    self.mixers(l)
                    if not os.environ.get("ONLYMIX"):
                        self.merge_out(l, hsrc, hT, les)
                P.barrier()
                if not os.environ.get("ONLYMIX"):
                    finals += self.ffn(l, hT, outT if last else hT)
                P.barrier()
            P.finalize(finals)
        return nc

    def gcol(self, l, which, kc):
        i = (l * 4 + which) * 8 + kc
        return self.gains[:, i:i + 1]

    def norm_block(self, hb, hkey, b, gl, gw, tag):
        sq, rs = self.nsq, self.nrs
        self.rstd_from([hb[:, kc, :] for kc in range(8)], TB, D, [hkey], rs[:, :], "nrs", sq, "nsq")
        for kc in range(8):
            self.stt("dve", self.U(kc, b * TB, (b + 1) * TB), hb[:, kc, :], self.gcol(gl, gw, kc), rs[:, :],
                     ALU.mult, ALU.mult, [hkey, "nrs", "gains"], [("uT", kc)])

    def norm1(self, l, hsrc):
        with ExitStack() as es:
            hb2 = [self.sb(es, "n1h%d" % i, [128, 8, TB], F32) for i in range(2)]
            self.nsq = self.sb(es, "n1sq", [128, 2, TB], F32)
            self.nrs = self.sb(es, "n1rs", [128, TB], F32)
            hv = hsrc.rearrange("(c p) t -> p c t", p=128)
            for b in range(NB):
                hb = hb2[b % 2]
                hk = ("n1h", b % 2)
                self.dma("sp", hb[:], hv[:, :, b * TB:(b + 1) * TB], (), [hk])
                self.norm_block(hb, hk, b, l, 0, "n1")
            self.P.barrier()

    def mixers(self, l):
        import os
        sel = os.environ.get("MIX", "cadb")
        for nm, fn, c0 in (("c", self.mix_c, 8), ("a", self.mix_a, 0), ("d", self.mix_d, 12), ("b", self.mix_b, 4)):
            if nm in sel:
                fn(l)
            else:
                self.memset("dve", self.oall[:, c0:c0 + 4, :], 0.0, ["oall"])
            self.P.barrier()
        if "oall" in self.dbg_out and l == 0:
            self.dbgdump_oall()

    def dbgdump_oall(self):
        with ExitStack() as es:
            tmp = self.sb(es, "dbgtmp", [128, T], F32)
            for c in range(16):
                self.cp("dve", tmp[:], self.oall[:, c, :], ["oall"], ["dbgtmp"])
                self.dma("sp", self.dbg_out["oall"][c * 128:(c + 1) * 128, :], tmp[:], ["dbgtmp"], ())
            self.P.barrier()

    def proj_fm(self, w, wkey, ncontr, col0, M, rhs_fn, rkeys, evac):
        for b in range(NB):
            pst, pk = self.psb("x")
            for kc in range(ncontr):
                self.mm(pst[:M, :TB], w[:, kc, col0:col0 + M], rhs_fn(kc, b), kc == 0, kc == ncontr - 1,
                        [wkey] + rkeys, [pk])
            evac(b, pst[:M, :TB], pk)

    def proj_tm(self, w, wkey, ncontr, col0, N, lhs_fn, rkeys, tiles, evac):
        for j, (t0, n) in enumerate(tiles):
            pst, pk = self.psb("x")
            for kc in range(ncontr):
                self.mm(pst[:n, :N], lhs_fn(kc, t0, n), w[:, kc, col0:col0 + N], kc == 0, kc == ncontr - 1,
                        [wkey] + rkeys, [pk])
            evac(j, n, pst[:n, :N], pk)

    def attn_stream(self, tag, jobs, ptbuf, LOOK=2, DEFER=5):
        flat = []
        for ji, jb in enumerate(jobs):
            n = len(jb["tiles"])
            for i, tl in enumerate(jb["tiles"]):
                flat.append((ji, i, n, tl))
        pend = []
        state = {}
        pts = {}

        def issue_s(idx):
            ji, i, n, tl = flat[idx]
            jb = jobs[ji]
            if i == 0 and jb.get("pre") is not None:
                jb["pre"]()
            qn, scale = jb["qn"], jb["scale"]
            kr = tl["rows"]
            pst, pk = self.psb("s")
            nm = len(tl["mms"])
            for mi, (lt, rh) in enumerate(tl["mms"]):
                self.mm(pst[:kr, :qn], lt, rh, mi == 0, mi == nm - 1, jb["rkeys"], [pk])
            pi = self.ptidx
            self.ptidx += 1
            pt = ptbuf[pi % len(ptbuf)]
            ptk = (tag + "pt", pi % len(ptbuf))
            bias = tl["bias"]
            if bias is None:
                self.act(pt[:kr, :qn], pst[:kr, :qn], AF.Exp, [pk], [ptk], scale=scale)
            elif bias[0] == "c":
                self.act(pt[:kr, :qn], pst[:kr, :qn], AF.Exp, [pk] + bias[2], [ptk], bias=bias[1][:kr, :], scale=scale)
            else:
                tmp = self.sbias[pi % len(self.sbias)]
                tk = (tag + "sb", pi % len(self.sbias))
                self.stt("dve", tmp[:kr, :qn], pst[:kr, :qn], scale, bias[1], ALU.mult, ALU.add, [pk] + bias[2], [tk])
                self.act(pt[:kr, :qn], tmp[:kr, :qn], AF.Exp, [tk], [ptk])
            pts[idx] = (pt, ptk)

        def issue_pv(idx):
            ji, i, n, tl = flat[idx]
            jb = jobs[ji]
            qn, o_M = jb["qn"], jb["o_M"]
            kr = tl["rows"]
            if i == 0:
                state[ji] = self.psb("o") + self.psb("d")
            ops_, ok, dps, dk = state[ji]
            pt, ptk = pts.pop(idx)
            vl, vkeys = jb["v_fn"](tl)
            self.mm(ops_[:o_M, :qn], vl, pt[:kr, :qn], i == 0, i == n - 1, [ptk] + vkeys, [ok])
            self.mm(dps[:o_M, :qn], jb["ones_fn"](tl), pt[:kr, :qn], i == 0, i == n - 1, [ptk, "onesb"] + jb.get("okeys", []), [dk])
            if i == n - 1:
                jb["fin1"](ops_, ok, dps, dk)
                if jb.get("fin2") is not None:
                    pend.append((idx + DEFER, jb["fin2"]))
                del state[ji]

        N = len(flat)
        for idx in range(N + LOOK):
            if idx < N:
                issue_s(idx)
            if idx - LOOK >= 0:
                issue_pv(idx - LOOK)
            while pend and pend[0][0] <= idx - LOOK:
                pend.pop(0)[1]()
        for _, fn in pend:
            fn()

    def mix_c(self, l):
        I = self.I
        with ExitStack() as es:
            wc = self.sb(es, "c_w", [128, 8, 512], BF16)
            wq = self.sb(es, "c_wq", [128, 2, 768 + 256], BF16)
            wkv = self.sb(es, "c_wkv", [128, 1, 1024], BF16)
            wvv = self.sb(es, "c_wvv", [128, 1, 512], BF16)
            lat = self.sb(es, "c_lat", [128, 3, TB], F32)
            qn = self.sb(es, "c_qn", [128, 2, T], BF16)
            kvn = self.sb(es, "c_kvn", [128, T], BF16)
            kpe = self.sb(es, "c_kpe", [128, 17 * 128], BF16)
            rope = self.sb(es, "c_rope", [64, 2 * T], F32)
            vtok = self.sb(es, "c_v", [128, NT, 512], BF16)
            qno = self.sb(es, "c_qno", [128, 2, T], BF16)
            qpe = self.sb(es, "c_qpe", [128, 2, T], BF16)
            kno = self.sb(es, "c_kno", [128, 2, 17 * 128], BF16)
            ptbuf = [self.sb(es, "c_pt%d" % i, [128, TB], BF16) for i in range(3)]
            self.nsq = self.sb(es, "c_sq", [128, 2, TB], F32)
            self.nrs = self.sb(es, "c_rs", [128, TB], F32)
            t1 = self.sb(es, "c_t1", [64, TB], F32)
            t2 = self.sb(es, "c_t2", [64, TB], F32)
            rd = [self.sb(es, "c_rd%d" % i, [128, TB], F32) for i in range(2)]
            win = I["w_in"][l].rearrange("(kc p) n -> p kc n", p=128)
            self.loadw(wc[:, :, 0:448], win[:, :, OC_QA:OC_QA + 448], "c_w")
            self.loadw(wc[:, :, 448:512], I["w_in_kr_sw"][l].rearrange("(kc p) n -> p kc n", p=128), "c_w")
            self.loadw(wq[:, :, 0:768], I["mla_w_q_up"][l].rearrange("(kc p) n -> p kc n", p=128), "c_wq")
            self.loadw(wq[:, :, 768:1024], I["mla_w_q_up_sw"][l].rearrange("(kc p) n -> p kc n", p=128), "c_wq")
            self.loadw(wkv[:, 0, :], I["mla_w_kv_up"][l], "c_wkv")
            self.loadw(wvv[:, 0, :].rearrange("p (h e) -> p h e", h=4),
                       I["mla_w_kv_up"][l].rearrange("p (h e) -> p h e", h=4)[:, :, 128:256], "c_wvv")
            self.dma("sp", rope[:], I["rope"], (), ["c_rope"])
            self.memset("dve", kpe[:], 0.0, ["c_kpe"])
            for i_ in range(2):
                self.memset("dve", kno[:, i_, T:17 * 128], 0.0, [("c_kno", i_)])
            self.memset("dve", vtok[:, NT - 1, :], 0.0, ["c_v"])
            for i_ in range(2):
                self.memset("dve", qpe[64:128, i_, :], 0.0, [("c_qpe", i_)])
            ukeys = [("uT", kc) for kc in range(8)]
            urhs = lambda kc, b: self.U(kc, b * TB, (b + 1) * TB)
            for b in range(NB):
                sl = slice(b * TB, (b + 1) * TB)
                pst, pk = self.psb("x")
                pst2, pk2 = self.psb("x")
                for kc in range(8):
                    self.mm(pst[:64, :TB], wc[:, kc, 384:448], urhs(kc, b), kc == 0, kc == 7, ["c_w"] + ukeys, [pk])
                for kc in range(8):
                    self.mm(pst2[:64, :TB], wc[:, kc, 448:512], urhs(kc, b), kc == 0, kc == 7, ["c_w"] + ukeys, [pk2])
                self.tt("dve", t1[:, :], pst[:64, :TB], rope[:, sl], ALU.mult, [pk, "c_rope"], ["c_t1"])
                self.tt("dve", t2[:, :], pst2[:64, :TB], rope[:, T + b * TB:T + (b + 1) * TB], ALU.mult, [pk2, "c_rope"], ["c_t2"])
                self.tt("dve", kpe[0:64, sl], t1[:, :], t2[:, :], ALU.add, ["c_t1", "c_t2"], ["c_kpe"])
            sc = self.smallc
            for b in range(NB):
                sl = slice(b * TB, (b + 1) * TB)
                for ci in range(3):
                    pst, pk = self.psb("x")
                    for kc in range(8):
                        self.mm(pst[:, :TB], wc[:, kc, ci * 128:(ci + 1) * 128], urhs(kc, b), kc == 0, kc == 7, ["c_w"] + ukeys, [pk])
                    self.cp("act", lat[:, ci, :], pst[:, :TB], [pk], [("c_lat", ci)])
                self.rstd_from([lat[:, 0, :], lat[:, 1, :]], TB, 256, [("c_lat", 0), ("c_lat", 1)], self.nrs[:, :], "nrs", self.nsq, "nsq")
                for ci in range(2):
                    self.stt("dve", qn[:, ci, sl], lat[:, ci, :], sc[:, l * 8 + 3 + ci:l * 8 + 4 + ci], self.nrs[:, :],
                             ALU.mult, ALU.mult, [("c_lat", ci), "nrs", "smallc"], ["c_qn"])
                self.rstd_from([lat[:, 2, :]], TB, 128, [("c_lat", 2)], self.nrs[:, :], "nrs", self.nsq, "nsq")
                self.stt("dve", kvn[:, sl], lat[:, 2, :], sc[:, l * 8 + 2:l * 8 + 3], self.nrs[:, :],
                         ALU.mult, ALU.mult, [("c_lat", 2), "nrs", "smallc"], ["c_kvn"])
            tiles = [(128 * j, trows(j)) for j in range(NT)]
            self.proj_tm(wvv, "c_wvv", 1, 0, 512, lambda kc, t0, n: kvn[:, t0:t0 + n], ["c_kvn"], tiles,
                         lambda j, n, ps, pk: self.cp("act", vtok[:n, j, :], ps, [pk], ["c_v"]))
            scale = (128 + 64) ** -0.5
            self.ptidx = 0
            qrhs = lambda kc, b: qn[:, kc, b * TB:(b + 1) * TB]

            def cproj(h):
                hb_ = h % 2
                self.proj_fm(wq, "c_wq", 2, h * 192, 128, qrhs, ["c_qn"],
                             lambda b, ps, pk: self.cp("act", qno[:, hb_, b * TB:(b + 1) * TB], ps, [pk], [("c_qno", hb_)]))
                for b in range(NB):
                    sl = slice(b * TB, (b + 1) * TB)
                    pst, pk = self.psb("x")
                    pst2, pk2 = self.psb("x")
                    for kc in range(2):
                        self.mm(pst[:64, :TB], wq[:, kc, h * 192 + 128:h * 192 + 192], qrhs(kc, b), kc == 0, kc == 1, ["c_wq", "c_qn"], [pk])
                    for kc in range(2):
                        self.mm(pst2[:64, :TB], wq[:, kc, 768 + h * 64:768 + h * 64 + 64], qrhs(kc, b), kc == 0, kc == 1, ["c_wq", "c_qn"], [pk2])
                    self.tt("dve", t1[:, :], pst[:64, :TB], rope[:, sl], ALU.mult, [pk, "c_rope"], ["c_t1"])
                    self.tt("dve", t2[:, :], pst2[:64, :TB], rope[:, T + b * TB:T + (b + 1) * TB], ALU.mult, [pk2, "c_rope"], ["c_t2"])
                    self.tt("dve", qpe[0:64, hb_, sl], t1[:, :], t2[:, :], ALU.add, ["c_t1", "c_t2"], [("c_qpe", hb_)])
                self.proj_fm(wkv, "c_wkv", 1, h * 256, 128, lambda kc, b: kvn[:, b * TB:(b + 1) * TB], ["c_kvn"],
                             lambda b, ps, pk: self.cp("act", kno[:, hb_, b * TB:(b + 1) * TB], ps, [pk], [("c_kno", hb_)]))

            cproj(0)
            for h in range(4):
                if h + 1 < 4:
                    cproj(h + 1)
                hb_ = h % 2
                jobs = []
                for b in range(NB):
                    q0 = b * TB
                    tl = []
                    for j in range(NT):
                        kr = 128
                        tl.append(dict(rows=kr, j=j, bias=None,
                                       mms=[(kno[:, hb_, 128 * j:128 * j + kr], qno[:, hb_, q0:q0 + TB]),
                                            (kpe[:, 128 * j:128 * j + kr], qpe[:, hb_, q0:q0 + TB])]))

                    def fin1(ops_, ok, dps, dk, q0=q0, h=h, b=b):
                        rd_ = rd[b % 2]
                        self.recip(rd_[:, :], dps[:, :TB], [dk], [("c_rd", b % 2)])
                        self.tt("dve", self.oall[:, 8 + h, q0:q0 + TB], ops_[:, :TB], rd_[:, :], ALU.mult, [ok, ("c_rd", b % 2)], ["oall"])
                    jobs.append(dict(qn=TB, tiles=tl, o_M=128, scale=scale, fin1=fin1, fin2=None,
                                     v_fn=lambda t, h=h: (vtok[:t["rows"], t["j"], h * 128:(h + 1) * 128], ["c_v"]),
                                     ones_fn=lambda t: (self.ones16 if t["j"] == NT - 1 else self.onesb)[:, :],
                                     rkeys=[("c_kno", hb_), ("c_qno", hb_), "c_kpe", ("c_qpe", hb_)]))
                self.attn_stream("c", jobs, ptbuf)

    def mix_a(self, l):
        I = self.I
        lam_init = 0.8 - 0.6 * math.exp(-0.3 * l)
        with ExitStack() as es:
            vtok = self.sb(es, "a_v", [128, NT, 512], BF16)
            qT = self.sb(es, "a_q", [128, 8, T], BF16)
            kT = self.sb(es, "a_k", [128, 4, 17 * 128], BF16)
            lam = self.sb(es, "a_lam", [128, 256], F32)
            lt = self.sb(es, "a_lt", [128, 128], F32)
            lv = self.sb(es, "a_lv", [128, 4], F32)
            gsub = self.sb(es, "a_gs", [128, 1], F32)
            win = I["w_in"][l].rearrange("(kc p) n -> p kc n", p=128)
            self.dma("sp", lam[:], I["lamrep"][:, l * 256:(l + 1) * 256], (), ["a_lam"])
            self.tt("dve", lt[:, 0:64], lam[:, 0:64], lam[:, 64:128], ALU.mult, ["a_lam"], ["a_lt"])
            self.tt("dve", lt[:, 64:128], lam[:, 128:192], lam[:, 192:256], ALU.mult, ["a_lam"], ["a_lt"])
            self.P.op("dve", lambda e: e.reduce_sum(out=lv[:, 0:1], in_=lt[:, 0:64], axis=mybir.AxisListType.X), ["a_lt"], ["a_lv"])
            self.P.op("dve", lambda e: e.reduce_sum(out=lv[:, 1:2], in_=lt[:, 64:128], axis=mybir.AxisListType.X), ["a_lt"], ["a_lv"])
            self.act(lv[:, 0:2], lv[:, 0:2], AF.Exp, ["a_lv"], ["a_lv"])
            self.tt("dve", lv[:, 2:3], lv[:, 1:2], lv[:, 0:1], ALU.subtract, ["a_lv"], ["a_lv"])
            self.ts("dve", lv[:, 3:4], lv[:, 2:3], -lam_init, None, ALU.add, None, ["a_lv"], ["a_lv"])
            self.ts("dve", gsub[:, :], self.smallc[:, l * 8:l * 8 + 1], 1.0 - lam_init, None, ALU.mult, None, ["smallc"], ["a_gs"])
            self.memset("dve", qT[:], 0.0, ["a_q"])
            self.memset("dve", kT[:], 0.0, ["a_k"])
            self.memset("dve", vtok[:, NT - 1, :], 0.0, ["a_v"])
            ukeys = [("uT", kc) for kc in range(8)]
            urhs = lambda kc, b: self.U(kc, b * TB, (b + 1) * TB)
            tiles = [(128 * j, trows(j)) for j in range(NT)]
            with ExitStack() as es1:
                wqk = [self.sb(es1, "a_wqk%d" % i, [128, 8, 256], BF16) for i in range(2)]
                wv = self.sb(es1, "a_wv", [128, 8, 512], BF16)
                self.loadw(wv[:], win[:, :, OA_V:OA_V + 512], "a_wv")
                for h in range(4):
                    wq_, wqkk = wqk[h % 2], ("a_wqk", h % 2)
                    self.loadw(wq_[:, :, 0:128], win[:, :, OA_Q + h * 128:OA_Q + (h + 1) * 128], wqkk)
                    self.loadw(wq_[:, :, 128:256], win[:, :, OA_K + h * 128:OA_K + (h + 1) * 128], wqkk)
                    if h == 0:
                        self.proj_tm(wv, "a_wv", 8, 0, 512, lambda kc, t0, n: self.U(kc, t0, t0 + n), ukeys, tiles,
                                     lambda j, n, ps, pk: self.cp("act", vtok[:n, j, :], ps, [pk], ["a_v"]))

                    def qev(b, ps, pk, h=h):
                        self.cp("act", qT[0:64, 2 * h, b * TB:(b + 1) * TB], ps[0:64, :], [pk], ["a_q"])
                        self.cp("dve", qT[64:128, 2 * h + 1, b * TB:(b + 1) * TB], ps[64:128, :], [pk], ["a_q"])
                    self.proj_fm(wq_, wqkk, 8, 0, 128, urhs, ukeys, qev)
                    self.proj_fm(wq_, wqkk, 8, 128, 128, urhs, ukeys,
                                 lambda b, ps, pk, h=h: self.cp("act", kT[:, h, b * TB:(b + 1) * TB], ps, [pk], ["a_k"]))
                self.P.barrier()
            slab = [self.sb(es, "a_slab%d" % i, [128, 2, A_W], F32) for i in range(2)]
            ptbuf = [self.sb(es, "a_pt%d" % i, [128, TB], BF16) for i in range(3)]
            self.sbias = [self.sb(es, "a_sb%d" % i, [128, TB], F32) for i in range(2)]
            self.nsq = self.sb(es, "a_sq", [128, 2, TB], F32)
            self.nrs = self.sb(es, "a_rs", [128, TB], F32)
            rd = [self.sb(es, "a_rd%d" % i, [128, TB], F32) for i in range(2)]
            on = [self.sb(es, "a_on%d" % i, [128, TB], F32) for i in range(4)]
            self.ptidx = 0
            jobs = []
            for h in range(4):
                sl_ = slab[h % 2]
                slk = ("a_slab", h % 2)
                slab_loaded = [False]
                for b in range(NB):
                    q0 = b * TB
                    for m in range(2):
                        mh = m * 4 + h
                        tl = []
                        for j in range(NT):
                            kr = 128
                            o = TB * b - 128 * j
                            if 127 - o <= -91:
                                bias = ("c", self.aconst[:, mh:mh + 1], ["aconst"])
                            elif -o - (TB - 1) >= 91:
                                bias = ("c", self.aconst[:, 8 + mh:9 + mh], ["aconst"])
                            else:
                                bias = ("s", sl_[:kr, m, o + A_C:o + A_C + TB], [slk])
                            tl.append(dict(rows=kr, j=j, bias=bias,
                                           mms=[(kT[:, h, 128 * j:128 * j + kr], qT[:, 2 * h + m, q0:q0 + TB])]))
                        oi = (b % 2) * 2 + m

                        def fin1(ops_, ok, dps, dk, oi=oi):
                            rd_ = rd[oi % 2]
                            self.recip(rd_[:, :], dps[:, :TB], [dk], [("a_rd", oi % 2)])
                            self.tt("dve", on[oi][:, :], ops_[:, :TB], rd_[:, :], ALU.mult, [ok, ("a_rd", oi % 2)], [("a_on", oi)])

                        def fin2(b=b, h=h, q0=q0):
                            o0, o1 = (b % 2) * 2, (b % 2) * 2 + 1
                            self.stt("dve", on[o0][:, :], on[o1][:, :], lv[:, 3:4], on[o0][:, :], ALU.mult, ALU.add,
                                     [("a_on", o0), ("a_on", o1), "a_lv"], [("a_on", o0)])
                            self.rstd_from([on[o0][:, :]], TB, 128, [("a_on", o0)], self.nrs[:, :], "nrs", self.nsq, "nsq")
                            self.stt("dve", self.oall[:, h, q0:q0 + TB], on[o0][:, :], gsub[:, 0:1], self.nrs[:, :], ALU.mult, ALU.mult,
                                     [("a_on", o0), "nrs", "a_gs"], ["oall"])
                        jobs.append(dict(qn=TB, tiles=tl, o_M=128, scale=0.125, fin1=fin1, fin2=(fin2 if m == 1 else None),
                                         v_fn=lambda t, h=h: (vtok[:t["rows"], t["j"], h * 128:(h + 1) * 128], ["a_v"]),
                                         ones_fn=lambda t: (self.ones16 if t["j"] == NT - 1 else self.onesb)[:, :], rkeys=["a_q", "a_k"],
                                         pre=((lambda h=h: [self.dma("sp", slab[h % 2][:, mm_, :], self.I["slabA"][mm_ * 4 + h], (), [("a_slab", h % 2)]) for mm_ in range(2)])
                                              if (b == 0 and m == 0) else None)))
            self.attn_stream("a", jobs, ptbuf)

    def mix_d(self, l):
        I = self.I
        with ExitStack() as es:
            wq = self.sb(es, "d_wq", [128, 8, 512], BF16)
            wkk = self.sb(es, "d_wkk", [128, 8, 2, 128], BF16)
            wv = self.sb(es, "d_wv", [128, 8, 128], BF16)
            qT = self.sb(es, "d_q", [128, 8, T], BF16)
            kT2 = self.sb(es, "d_k", [128, 2, T], BF16)
            vpad = self.sb(es, "d_v", [128, NT * 4, 128], BF16)
            slab = [self.sb(es, "d_slab%d" % i, [128, D_SLABW], F32) for i in range(4)]
            ptbuf = [self.sb(es, "d_pt%d" % i, [128, 256], BF16) for i in range(3)]
            self.sbias = [self.sb(es, "d_sb%d" % i, [128, 256], F32) for i in range(2)]
            oh = self.sb(es, "d_oh", [128, 2, 128], BF16)
            es8 = self.sb(es, "d_es8", [128, 8], F32)
            es2 = self.sb(es, "d_es2", [128, 4], F32)
            rd = [self.sb(es, "d_rd%d" % i, [128, 256], F32) for i in range(2)]
            win = I["w_in"][l].rearrange("(kc p) n -> p kc n", p=128)
            self.loadw(wq[:], win[:, :, OD_Q:OD_Q + 512], "d_wq")
            for kv in range(2):
                for e in range(2):
                    self.loadw(wkk[:, :, kv, e * 64:(e + 1) * 64], win[:, :, OD_K + kv * 64:OD_K + (kv + 1) * 64], "d_wkk")
            self.loadw(wv[:], win[:, :, OD_V:OD_V + 128], "d_wv")
            self.memset("dve", vpad[:], 0.0, ["d_v"])
            self.memset("dve", oh[:], 0.0, ["d_oh"])
            self.memset("dve", oh[:, 0, 0:64], 1.0, ["d_oh"])
            self.memset("dve", oh[:, 1, 64:128], 1.0, ["d_oh"])
            self.dma("sp", es8[:], I["sinkrep"][:, l * 8:(l + 1) * 8], (), ["d_es8"])
            self.act(es8[:], es8[:], AF.Exp, ["d_es8"], ["d_es8"])
            for p in range(4):
                self.cp("dve", es2[0:64, p:p + 1], es8[0:64, 2 * p:2 * p + 1], ["d_es8"], ["d_es2"])
                self.cp("dve", es2[64:128, p:p + 1], es8[64:128, 2 * p + 1:2 * p + 2], ["d_es8"], ["d_es2"])
            ukeys = [("uT", kc) for kc in range(8)]
            urhs = lambda kc, b: self.U(kc, b * TB, (b + 1) * TB)
            self.memset("dve", qT[:], 0.0, ["d_q"])

            def qev(b, ps, pk, p):
                self.cp("act", qT[0:64, 2 * p, b * TB:(b + 1) * TB], ps[0:64, :], [pk], ["d_q"])
                self.cp("act", qT[64:128, 2 * p + 1, b * TB:(b + 1) * TB], ps[64:128, :], [pk], ["d_q"])
            for p in range(4):
                self.proj_fm(wq, "d_wq", 8, p * 128, 128, urhs, ukeys, lambda b, ps, pk, p=p: qev(b, ps, pk, p))
            for kv in range(2):
                self.proj_fm(wkk[:, :, kv, :], "d_wkk", 8, 0, 128, urhs, ukeys,
                             lambda b, ps, pk, kv=kv: self.cp("act", kT2[:, kv, b * TB:(b + 1) * TB], ps, [pk], ["d_k"]))
            tiles = [(128 * j, trows(j)) for j in range(NT)]

            def vev(j, n, ps, pk):
                for kv in range(2):
                    self.cp("act", vpad[:n, j * 4 + kv * 2, 0:64], ps[:, kv * 64:(kv + 1) * 64], [pk], ["d_v"])
                    self.cp("dve", vpad[:n, j * 4 + kv * 2 + 1, 64:128], ps[:, kv * 64:(kv + 1) * 64], [pk], ["d_v"])
            self.proj_tm(wv, "d_wv", 8, 0, 128, lambda kc, t0, n: self.U(kc, t0, t0 + n), ukeys, tiles, vev)
            self.ptidx = 0
            jobs = []
            for p in range(4):
                kv = p // 2
                sls = [slab[(p % 2) * 2 + e] for e in range(2)]
                slks = [("d_slab", (p % 2) * 2 + e) for e in range(2)]
                pre_p = (lambda p=p, sls=sls, slks=slks: [self.dma("sp", sls[e][:], I["slabD"][2 * p + e], (), [slks[e]]) for e in range(2)])
                for qb, (q0, qn) in enumerate(D_QB):
                    tl = []
                    for e in range(2):
                        h = 2 * p + e
                        pr = slice(64 * e, 64 * e + 64)
                        if qb == 0:
                            mb = ("s", sls[e][:16, D_W + 256:D_W + 256 + qn], [slks[e]])
                        else:
                            mb = ("c", self.dconst[:, h:h + 1], ["dconst"])
                        tl.append(dict(rows=16, j=0, e=e, bias=mb, mms=[(kT2[:, kv, 0:16], qT[:, h, q0:q0 + qn])]))
                        for j in range(max(0, q0 // 128 - 1), min(16, (q0 + qn + 127) // 128) + 1):
                            kr = trows(j)
                            o = q0 - 128 * j
                            if j == 0:
                                bs = sls[e][:kr, D_W:D_W + qn]
                            else:
                                bs = sls[e][:kr, o + D_C:o + D_C + qn]
                            tl.append(dict(rows=kr, j=j, e=e, bias=("s", bs, [slks[e]]),
                                           mms=[(kT2[:, kv, 128 * j:128 * j + kr], qT[:, h, q0:q0 + qn])]))

                    def fin1(ops_, ok, dps, dk, p=p, q0=q0, qn=qn, qb=qb):
                        rd_ = rd[qb % 2]
                        rk = ("d_rd", qb % 2)
                        self.ts("dve", rd_[:, :qn], dps[:, :qn], es2[:, p:p + 1], None, ALU.add, None, [dk, "d_es2"], [rk])
                        self.recip(rd_[:, :qn], rd_[:, :qn], [rk], [rk])
                        self.tt("dve", self.oall[:, 12 + p, q0:q0 + qn], ops_[:, :qn], rd_[:, :qn], ALU.mult, [ok, rk], ["oall"])
                    jobs.append(dict(qn=qn, tiles=tl, o_M=128, scale=0.125, fin1=fin1, fin2=None,
                                     v_fn=lambda t, kv=kv: (vpad[:t["rows"], t["j"] * 4 + kv * 2 + t["e"], :], ["d_v"]),
                                     ones_fn=lambda t: oh[:t["rows"], t["e"], :], okeys=["d_oh"], rkeys=["d_q", "d_k"],
                                     pre=(pre_p if qb == 0 else None)))
            self.attn_stream("d", jobs, ptbuf)

    def mix_b(self, l):
        I = self.I
        with ExitStack() as es:
            wb = self.sb(es, "b_w", [128, 8, 1568], BF16)
            wgu = self.sb(es, "b_wgu", [16, 2, 256], F32)
            gb = self.sb(es, "b_gb", [128, 512], F32)
            msk = self.sb(es, "b_msk", [128, 4, 128], F32)
            obw = self.sb(es, "b_obw", [128, 4, T], F32)
            S = self.sb(es, "b_S", [64, 4, 128], F32)
            Sbf = self.sb(es, "b_Sbf", [64, 4, 128], BF16)
            self.nsq = self.sb(es, "b_sq", [128, 2, 64], F32)
            self.nrs = self.sb(es, "b_rs", [128, 64], F32)
            NBUF = 2
            bufs = {}

            def tb(name, shape, dt, i):
                k = (name, i % NBUF)
                if k not in bufs:
                    bufs[k] = self.sb(es, "b_%s%d" % (name, i % NBUF), shape, dt)
                return bufs[k], ("b_" + name, i % NBUF)

            win = I["w_in"][l].rearrange("(kc p) n -> p kc n", p=128)
            self.loadw(wb[:], win[:, :, OB_Q:OB_Q + 1568], "b_w")
            self.dma("sp", wgu[:], I["gla_gate_up"][l].rearrange("g r c -> r g c"), (), ["b_wgu"])
            self.dma("sp", gb[:], I["gbias"][:, l * 512:(l + 1) * 512], (), ["b_gb"])
            self.dma("sp", msk[:], I["glam"].rearrange("p (m t) -> p m t", m=4), (), ["b_msk"])
            ukeys = [("uT", kc) for kc in range(8)]
            gch = [(0, 16)] + [(16 + 64 * (c - 1), 64) for c in range(1, 33)]
            seq = [(1, ci) for ci in range(32, -1, -1)] + [(0, ci) for ci in range(33)]
            NS = len(seq)
            cxs = {}

            def ctx(i):
                if i not in cxs:
                    dr, ci = seq[i]
                    t0, n = gch[ci]
                    cxs[i] = dict(i=i, dr=dr, ci=ci, t0=t0, n=n, mi_c=(0 if dr == 0 else 1), mi_r=(2 if dr == 0 else 3))
                return cxs[i]

            def P1(cx):
                i, dr, t0, n = cx["i"], cx["dr"], cx["t0"], cx["n"]
                ut = lambda kc: self.U(kc, t0, t0 + n)
                pgl, pglk = self.psb("x")
                for kc in range(8):
                    self.mm(pgl[:16, :n], wb[:, kc, 1536 + 16 * dr:1552 + 16 * dr], ut(kc), kc == 0, kc == 7, ["b_w"] + ukeys, [pglk])
                cx["glT"], cx["glk"] = tb3("glT", [16, 64], F32, i)
                self.cp("act", cx["glT"][:, :n], pgl[:16, :n], [pglk], [cx["glk"]])
                pvt, pvtk = self.psb("x")
                for kc in range(8):
                    self.mm(pvt[:n, :512], ut(kc), wb[:, kc, 512:1024], kc == 0, kc == 7, ["b_w"] + ukeys, [pvtk])
                cx["vt"], cx["vtk"] = tb3("vt", [64, 512], BF16, i)
                self.cp("act", cx["vt"][:n, :], pvt[:n, :512], [pvtk], [cx["vtk"]])
                if dr == 0:
                    prr, prrk = self.psb("x")
                    for h in range(4):
                        for kc in range(8):
                            self.mm(prr[:, h * 64:h * 64 + n], wb[:, kc, 1024 + h * 128:1024 + (h + 1) * 128], ut(kc), kc == 0, kc == 7, ["b_w"] + ukeys, [prrk])
                    k4 = ("sr", i % 4)
                    if k4 not in bufs:
                        bufs[k4] = self.sb(es, "b_sr%d" % (i % 4), [128, 4, 64], F32)
                    cx["sr"], cx["srk"] = bufs[k4], ("b_sr", i % 4)
                    r3 = prr[:, 0:256].rearrange("p (h t) -> p h t", h=4)[:, :, :n]
                    sr_ = cx["sr"]
                    self.act(sr_[:, :, :n], r3, AF.Exp, [prrk], [cx["srk"]], scale=-1.0)
                    self.ts("dve", sr_[:, :, :n], sr_[:, :, :n], 1.0, None, ALU.add, None, [cx["srk"]], [cx["srk"]])
                    self.recip(sr_[:, :, :n], sr_[:, :, :n], [cx["srk"]], [cx["srk"]])
                    self.tt("dve", sr_[:, :, :n], sr_[:, :, :n], r3, ALU.mult, [cx["srk"], prrk], [cx["srk"]])

            def P2a(cx):
                i, dr, n = cx["i"], cx["dr"], cx["n"]
                ppre, pprek = self.psb("x")
                self.mm(ppre[:n, :256], cx["glT"][:, :n], wgu[:, dr, :], True, True, [cx["glk"], "b_wgu"], [pprek])
                xla, xlk = tb("xla", [64, 256], F32, i)
                self.tt("dve", xla[:n, :], ppre[:n, :256], gb[:n, dr * 256:(dr + 1) * 256], ALU.add, [pprek, "b_gb"], [xlk])
                self.act(xla[:n, :], xla[:n, :], AF.Exp, [xlk], [xlk], scale=-1.0)
                cx["sp"], cx["spk"] = tb("sp", [64, 256], F32, i)
                self.act(cx["sp"][:n, :], xla[:n, :], AF.Ln, [xlk], [cx["spk"]], bias=1.0)

            def P2b(cx):
                i, dr, t0, n = cx["i"], cx["dr"], cx["t0"], cx["n"]
                sp_, spk = cx["sp"], cx["spk"]
                ut = lambda kc: self.U(kc, t0, t0 + n)
                pqk, pqkk = self.psb("x")
                for qi in range(8):
                    for kc in range(8):
                        self.mm(pqk[:64, qi * 64:qi * 64 + n], wb[:, kc, qi * 64:(qi + 1) * 64], ut(kc), kc == 0, kc == 7, ["b_w"] + ukeys, [pqkk])
                pkt, pktk = self.psb("x")
                for kc in range(8):
                    self.mm(pkt[:n, :256], ut(kc), wb[:, kc, 256:512], kc == 0, kc == 7, ["b_w"] + ukeys, [pktk])
                pc, pck = self.psb("x")
                for h in range(4):
                    self.mm(pc[:64, h * 64:h * 64 + n], sp_[:n, h * 64:(h + 1) * 64], msk[:n, cx["mi_c"], :n], True, True, [spk, "b_msk"], [pck])
                pr_, prk = self.psb("x")
                self.mm(pr_[:n, :256], msk[:n, cx["mi_r"], :n], sp_[:n, :], True, True, [spk, "b_msk"], [prk])
                cx["eb"], cx["ebk"] = tb("eb", [64, 4, 64], F32, i)
                einv, eik = tb("einv", [64, 4, 64], F32, i)
                pc3 = pc[:64, 0:256].rearrange("p (h t) -> p h t", h=4)[:, :, :n]
                self.act(cx["eb"][:, :, :n], pc3, AF.Exp, [pck], [cx["ebk"]], scale=-1.0 / 16)
                self.act(einv[:, :, :n], pc3, AF.Exp, [pck], [eik], scale=1.0 / 16)
                eo, eok = tb("eo", [64, 256], F32, i)
                self.act(eo[:n, :], pr_[:n, :256], AF.Exp, [prk], [eok], scale=-1.0 / 16)
                cx["qd"], cx["qdk"] = tb("qd", [64, 4, 64], BF16, i)
                cx["ki"], cx["kik"] = tb("ki", [64, 4, 64], BF16, i)
                q3 = pqk[:64, 0:256].rearrange("p (h t) -> p h t", h=4)[:, :, :n]
                k3 = pqk[:64, 256:512].rearrange("p (h t) -> p h t", h=4)[:, :, :n]
                self.stt("dve", cx["qd"][:, :, :n], q3, 0.125, cx["eb"][:, :, :n], ALU.mult, ALU.mult, [pqkk, cx["ebk"]], [cx["qdk"]])
                self.tt("dve", cx["ki"][:, :, :n], k3, einv[:, :, :n], ALU.mult, [pqkk, eik], [cx["kik"]])
                cx["ko"], cx["kok"] = tb("ko", [64, 256], BF16, i)
                self.tt("dve", cx["ko"][:n, :], pkt[:n, :256], eo[:n, :], ALU.mult, [pktk, eok], [cx["kok"]])

            def P3a(cx):
                i, n = cx["i"], cx["n"]
                pat, patk = self.psb("x")
                for h in range(4):
                    self.mm(pat[:n, h * 64:h * 64 + n], cx["ki"][:, h, :n], cx["qd"][:, h, :n], True, True, [cx["kik"], cx["qdk"]], [patk])
                cx["att"], cx["atk"] = tb("att", [64, 4, 64], BF16, i)
                for h in range(4):
                    self.tt("dve", cx["att"][:n, h, :n], pat[:n, h * 64:h * 64 + n], msk[:n, cx["mi_c"], :n], ALU.mult, [patk, "b_msk"], [cx["atk"]])
                pds, pdsk = self.psb("x")
                for h in range(4):
                    self.mm(pds[:64, h * 128:(h + 1) * 128], cx["ko"][:n, h * 64:(h + 1) * 64], cx["vt"][:n, h * 128:(h + 1) * 128], True, True, [cx["kok"], cx["vtk"]], [pdsk])
                cx["pds"], cx["pdsk"] = pds, pdsk

            def P3b(cx):
                i, dr, t0, n = cx["i"], cx["dr"], cx["t0"], cx["n"]
                vt, vtk, att, atk, qd, qdk = cx["vt"], cx["vtk"], cx["att"], cx["atk"], cx["qd"], cx["qdk"]
                if i == 0 or seq[i][0] != seq[i - 1][0]:
                    self.memset("dve", S[:], 0.0, ["b_S"])
                    self.memset("dve", Sbf[:], 0.0, ["b_Sbf"])
                po, pok = self.psb("x")
                for h in range(4):
                    self.mm(po[:, h * 64:h * 64 + n], vt[:n, h * 128:(h + 1) * 128], att[:n, h, :n], True, False, [vtk, atk], [pok])
                    self.mm(po[:, h * 64:h * 64 + n], Sbf[:, h, :], qd[:, h, :n], False, True, ["b_Sbf", qdk], [pok])
                dcol = (n - 1) if dr == 0 else 0
                pds, pdsk = cx["pds"], cx["pdsk"]
                for h in range(4):
                    self.stt("dve", S[:, h, :], S[:, h, :], cx["eb"][:, h, dcol:dcol + 1], pds[:64, h * 128:(h + 1) * 128], ALU.mult, ALU.add, ["b_S", cx["ebk"], pdsk], ["b_S"])
                self.cp("act", Sbf[:], S[:], ["b_S"], ["b_Sbf"])
                if dr == 1:
                    self.cp("act", obw[:, :, t0:t0 + n], po[:, 0:256].rearrange("p (h t) -> p h t", h=4)[:, :, :n], [pok], ["b_obw"])
                else:
                    of, ofk = tb("of", [128, 4, 64], F32, i)
                    sq4, sqk = tb("sq4", [128, 4, 64], F32, i)
                    if ("of", i % NBUF) not in init_done:
                        init_done.add(("of", i % NBUF))
                        self.memset("dve", of[:], 0.0, [ofk])
                    self.tt("dve", of[:, :, :n], po[:, 0:256].rearrange("p (h t) -> p h t", h=4)[:, :, :n], obw[:, :, t0:t0 + n], ALU.add, [pok, "b_obw"], [ofk])
                    self.tt("pool", sq4[:], of[:], of[:], ALU.mult, [ofk], [sqk])
                    cx["of"], cx["ofk"], cx["sq4"], cx["sqk"] = of, ofk, sq4, sqk
                    pend3.append(cx)
                del cxs[i]

            def P3c(cx):
                i, t0, n = cx["i"], cx["t0"], cx["n"]
                of, ofk = cx["of"], cx["ofk"]
                pn, pnk = self.psb("x")
                self.mm(pn[:, 0:256], self.onesf[:], cx["sq4"][:].rearrange("p h t -> p (h t)"), True, True, [cx["sqk"], "onesf"], [pnk])
                rs4, rsk = tb("rs4", [128, 4, 64], F32, i)
                rs2 = rs4[:].rearrange("p h t -> p (h t)")
                self.ts("dve", rs2, pn[:, 0:256], 1.0 / 128, EPS, ALU.mult, ALU.add, [pnk], [rsk])
                self.act(rs2, rs2, AF.Ln, [rsk], [rsk])
                self.act(rs2, rs2, AF.Exp, [rsk], [rsk], scale=-0.5)
                self.stt("dve", of[:, :, :n], of[:, :, :n], self.smallc[:, l * 8 + 1:l * 8 + 2], rs4[:, :, :n], ALU.mult, ALU.mult, [ofk, rsk, "smallc"], [ofk])
                self.tt("dve", self.oall[:, 4:8, t0:t0 + n], of[:, :, :n], cx["sr"][:, :, :n], ALU.mult, [ofk, cx["srk"]], ["oall"])

            init_done = set()
            pend3 = []

            def tb3(name, shape, dt, i):
                k = (name, i % 3)
                if k not in bufs:
                    bufs[k] = self.sb(es, "b_%s%d" % (name, i % 3), shape, dt)
                return bufs[k], ("b_" + name, i % 3)

            P1(ctx(0))
            P1(ctx(1))
            P2a(ctx(0))
            P2b(ctx(0))
            for i in range(NS):
                if i + 2 < NS:
                    P1(ctx(i + 2))
                if i + 1 < NS:
                    P2a(ctx(i + 1))
                P3a(ctx(i))
                while pend3:
                    P3c(pend3.pop(0))
                if i + 1 < NS:
                    P2b(ctx(i + 1))
                P3b(ctx(i))
            while pend3:
                P3c(pend3.pop(0))

    def resid_block(self, y, ykeys, b, l, gw, hsrc, hdst, bufs, nextnorm=None, final=False):
        hb, hk = bufs
        hv_s = hsrc.rearrange("(c p) t -> p c t", p=128)
        hv_d = hdst.rearrange("(c p) t -> p c t", p=128)
        sl = slice(b * TB, (b + 1) * TB)
        self.dma("sp", hb[:], hv_s[:, :, sl], (), [hk])
        self.rstd_from([y[:, kc, :] for kc in range(8)], TB, D, ykeys, self.nrs[:, :], "nrs", self.nsq, "nsq")
        for kc in range(8):
            self.stt("dve", y[:, kc, :], y[:, kc, :], self.gcol(l, gw, kc), self.nrs[:, :], ALU.mult, ALU.mult,
                     ykeys + ["nrs", "gains"], ykeys)
        self.tt("dve", hb[:], hb[:], y[:], ALU.add, [hk] + ykeys, [hk])
        st = self.dma("sp", hv_d[:, :, sl], hb[:], [hk], [("hdram", b)])
        if nextnorm is not None:
            self.norm_block(hb, hk, b, nextnorm[0], nextnorm[1], "nn")
        return st

    def merge_out(self, l, hsrc, hT, les):
        I = self.I
        with ExitStack() as es:
            merged = self.sb(es, "m_merged", [128, 8, T], BF16)
            with ExitStack() as es2:
                wbr = [self.sb(es2, "m_wbr%d" % i, [128, 16, 128], BF16) for i in range(2)]
                wg = [self.sb(es2, "m_wg%d" % i, [128, 8, 4, 128], BF16) for i in range(2)]
                sig = [self.sb(es2, "m_sig%d" % i, [128, TB], F32) for i in range(2)]
                acc = self.sb(es2, "m_acc", [128, TB], F32)
                prod = self.sb(es2, "m_prod", [128, TB], F32)
                ukeys = [("uT", kc) for kc in range(8)]
                def ldm(dc):
                    wb_, wbk = wbr[dc % 2], ("m_wbr", dc % 2)
                    wg_, wgk = wg[dc % 2], ("m_wg", dc % 2)
                    self.loadw(wb_[:], I["w_branch"][l].rearrange("n (ec p) d -> p (n ec) d", p=128)[:, :, dc * 128:(dc + 1) * 128], wbk)
                    for br in range(4):
                        self.loadw(wg_[:, :, br, :], I["w_in"][l].rearrange("(kc p) n -> p kc n", p=128)
                                   [:, :, O_GATE + br * 1024 + dc * 128:O_GATE + br * 1024 + (dc + 1) * 128], wgk)
                ldm(0)
                for dc in range(8):
                    wb_, wbk = wbr[dc % 2], ("m_wbr", dc % 2)
                    wg_, wgk = wg[dc % 2], ("m_wg", dc % 2)
                    if dc + 1 < 8:
                        ldm(dc + 1)
                    for b in range(NB):
                        sl = slice(b * TB, (b + 1) * TB)
                        for br in range(4):
                            pg, pgk = self.psb("x")
                            for kc in range(8):
                                self.mm(pg[:, :TB], wg_[:, kc, br, :], self.U(kc, b * TB, (b + 1) * TB), kc == 0, kc == 7, [wgk] + ukeys, [pgk])
                            pp, ppk = self.psb("x")
                            for ec in range(4):
                                self.mm(pp[:, :TB], wb_[:, br * 4 + ec, :], self.oall[:, br * 4 + ec, sl], ec == 0, ec == 3, [wbk, "oall"], [ppk])
                            sg, sgk = sig[br % 2], ("m_sig", br % 2)
                            self.act(sg[:, :], pg[:, :TB], AF.Sigmoid, [pgk], [sgk])
                            if br == 0:
                                self.tt("dve", acc[:, :], pp[:, :TB], sg[:, :], ALU.mult, [ppk, sgk], ["m_acc"])
                            else:
                                self.tt("dve", prod[:, :], pp[:, :TB], sg[:, :], ALU.mult, [ppk, sgk], ["m_prod"])
                                if br < 3:
                                    self.tt("dve", acc[:, :], acc[:, :], prod[:, :], ALU.add, ["m_acc", "m_prod"], ["m_acc"])
                                else:
                                    self.tt("dve", merged[:, dc, sl], acc[:, :], prod[:, :], ALU.add, ["m_acc", "m_prod"], ["m_merged"])
                self.P.barrier()
            if "merged" in self.dbg_out and l == 0:
                with ExitStack() as es3:
                    tmp = self.sb(es3, "dbgtmp2", [128, T], F32)
                    for c in range(8):
                        self.cp("dve", tmp[:], merged[:, c, :], ["m_merged"], ["dbgtmp2"])
                        self.dma("sp", self.dbg_out["merged"][c * 128:(c + 1) * 128, :], tmp[:], ["dbgtmp2"], ())
                    self.P.barrier()
            with ExitStack() as es2:
                wo = self.sb(es2, "o_w", [128, 8, D], BF16)
                y2 = [self.sb(es2, "o_y%d" % i, [128, 8, TB], F32) for i in range(2)]
                hb2 = [self.sb(es2, "o_h%d" % i, [128, 8, TB], F32) for i in range(2)]
                self.nsq = self.sb(es2, "o_sq", [128, 2, TB], F32)
                self.nrs = self.sb(es2, "o_rs", [128, TB], F32)
                self.loadw(wo[:], I["w_out"][l].rearrange("(kc p) n -> p kc n", p=128), "o_w")
                for b in range(NB):
                    y, yk = y2[b % 2], ("o_y", b % 2)
                    sl = slice(b * TB, (b + 1) * TB)
                    for dc in range(8):
                        pst, pk = self.psb("x")
                        for kc in range(8):
                            self.mm(pst[:, :TB], wo[:, kc, dc * 128:(dc + 1) * 128], merged[:, kc, sl], kc == 0, kc == 7, ["o_w", "m_merged"], [pk])
                        self.cp("act", y[:, dc, :], pst[:, :TB], [pk], [yk])
                    self.resid_block(y, [yk], b, l, 1, hsrc, hT, (hb2[b % 2], ("o_h", b % 2)), nextnorm=(l, 2))
                self.P.barrier()

    def ffn(self, l, hT, hdst):
        I = self.I
        finals = []
        NJ = DFF // 128
        for half in range(2):
            with ExitStack() as es:
                actT = self.sb(es, "f_act", [128, NJ, 3 * TB], BF16)
                with ExitStack() as es2:
                    wu = [self.sb(es2, "f_wu%d" % i, [128, 8, 2, 128], BF16) for i in range(2)]
                    cgs = [self.sb(es2, "f_cg%d" % i, [128, TB], F32) for i in range(2)]
                    cvs = [self.sb(es2, "f_cv%d" % i, [128, TB], F32) for i in range(2)]
                    t1s = [self.sb(es2, "f_t1%d" % i, [128, TB], F32) for i in range(2)]
                    wup = I["ffn_w_up"][l].rearrange("(kc p) n -> p kc n", p=128)
                    ukeys = [("uT", kc) for kc in range(8)]
                    cw = self.convw
                    it = 0
                    def ldw(j):
                        w_, wk = wu[j % 2], ("f_wu", j % 2)
                        self.loadw(w_[:, :, 0, :], wup[:, :, j * 128:(j + 1) * 128], wk)
                        self.loadw(w_[:, :, 1, :], wup[:, :, DFF + j * 128:DFF + (j + 1) * 128], wk)
                    ldw(0)
                    for j in range(NJ):
                        w_, wk = wu[j % 2], ("f_wu", j % 2)
                        if j + 1 < NJ:
                            ldw(j + 1)
                        for b in range(3 * half, 3 * half + 3):
                            it += 1
                            cg, cv, t1 = cgs[it % 2], cvs[it % 2], t1s[it % 2]
                            cgk, cvk, t1k = ("f_cg", it % 2), ("f_cv", it % 2), ("f_t1", it % 2)
                            taps = []
                            for gv in range(2):
                                pst, pk = self.psb("x")
                                for kc in range(8):
                                    self.mm(pst[:, :TB + 2], w_[:, kc, gv, :], self.uT[:, kc, b * TB:b * TB + TB + 2], kc == 0, kc == 7, [wk] + ukeys, [pk])
                                ch = gv * NJ + j
                                base = (l * 4) * 44
                                c0 = cw[:, base + ch:base + ch + 1]
                                c1 = cw[:, base + 44 + ch:base + 44 + ch + 1]
                                c2 = cw[:, base + 88 + ch:base + 88 + ch + 1]
                                cb = cw[:, base + 132 + ch:base + 132 + ch + 1]
                                dst, dk = (cg, cgk) if gv == 0 else (cv, cvk)
                                self.act(dst[:, :], pst[:, 0:TB], AF.Identity, [pk, "convw"], [dk], bias=cb, scale=c0)
                                taps.append((dst, dk, pst, pk, c1, c2))
                            for (dst, dk, pst, pk, c1, c2) in taps:
                                self.stt("dve", dst[:, :], pst[:, 1:TB + 1], c1, dst[:, :], ALU.mult, ALU.add, [pk, dk, "convw"], [dk])
                            for (dst, dk, pst, pk, c1, c2) in taps:
                                self.stt("dve", dst[:, :], pst[:, 2:TB + 2], c2, dst[:, :], ALU.mult, ALU.add, [pk, dk, "convw"], [dk])
                            self.act(t1[:, :], cg[:, :], AF.Gelu_apprx_tanh, [cgk], [t1k])
                            self.tt("pool", actT[:, j, (b - 3 * half) * TB:(b - 3 * half + 1) * TB], t1[:, :], cv[:, :], ALU.mult, [t1k, cvk], ["f_act"])
                    self.P.barrier()
                with ExitStack() as es2:
                    wd = self.sb(es2, "f_wd", [128, NJ, D], BF16)
                    y2 = [self.sb(es2, "f_y%d" % i, [128, 8, TB], F32) for i in range(2)]
                    hb2 = [self.sb(es2, "f_h%d" % i, [128, 8, TB], F32) for i in range(2)]
                    self.nsq = self.sb(es2, "f_sq", [128, 2, TB], F32)
                    self.nrs = self.sb(es2, "f_rs", [128, TB], F32)
                    self.loadw(wd[:], I["ffn_w_down"][l].rearrange("(j p) n -> p j n", p=128), "f_wd")
                    for b in range(3 * half, 3 * half + 3):
                        y, yk = y2[b % 2], ("f_y", b % 2)
                        sl = slice(b * TB, (b + 1) * TB)
                        for dc in range(8):
                            pst, pk = self.psb("x")
                            for j in range(NJ):
                                self.mm(pst[:, :TB], wd[:, j, dc * 128:(dc + 1) * 128], actT[:, j, (b - 3 * half) * TB:(b - 3 * half + 1) * TB], j == 0, j == NJ - 1, ["f_wd", "f_act"], [pk])
                            self.cp("act", y[:, dc, :], pst[:, :TB], [pk], [yk])
                        st = self.resid_block(y, [yk], b, l, 3, hT, hdst, (hb2[b % 2], ("f_h", b % 2)))
                        finals.append(st)
                    self.P.barrier()
        return finals


def host_consts(inp):
    f32 = np.float32
    c = {}
    tab = np.asarray(inp["rel_bias_table"], f32)
    kk = np.arange(128)[:, None]
    jj = np.arange(A_W)[None, :]
    bk = rel_bucket_jax(kk - jj + A_C)
    c["slabA"] = np.ascontiguousarray(np.transpose(tab[bk][:, :, 0:8], (2, 0, 1))).astype(f32)
    ac = np.concatenate([tab[15, 0:8], tab[31, 0:8]])
    c["aconst"] = np.ascontiguousarray(np.broadcast_to(ac[None, :], (128, 16))).astype(f32)
    tabd = tab[:, 8:16]
    jj = np.arange(D_W)[None, :]
    rel = kk - jj + D_C
    tz = np.where((np.abs(rel) <= 128)[:, :, None], tabd[rel_bucket_jax(rel)], f32(NEG))
    qq = np.arange(256)[None, :]
    rel0 = kk - qq
    t0 = np.where(((np.abs(rel0) <= 128) & (kk >= NMETA))[:, :, None], tabd[rel_bucket_jax(rel0)], f32(NEG))
    tm = np.where((kk < NMETA)[:, :, None], tabd[rel_bucket_jax(rel0)], f32(NEG))
    c["slabD"] = np.ascontiguousarray(np.transpose(np.concatenate([tz, t0, tm], axis=1), (2, 0, 1))).astype(f32)
    c["dconst"] = np.ascontiguousarray(np.broadcast_to(tabd[15][None, :], (128, 8))).astype(f32)
    half = 32
    inv = (10000.0 ** (-np.arange(half, dtype=np.float32) / half)).astype(f32)
    ang = np.arange(T, dtype=f32)[None, :] * inv[:, None]
    cos, sin = np.cos(ang).astype(f32), np.sin(ang).astype(f32)
    c["rope"] = np.ascontiguousarray(np.concatenate([np.concatenate([cos, cos], 0), np.concatenate([-sin, sin], 0)], 1)).astype(f32)
    s = np.arange(128)[:, None]
    t = np.arange(128)[None, :]
    same = (s // 64) == (t // 64)
    LT = (same & (s <= t)).astype(f32)
    L = (same & (s >= t)).astype(f32)
    SU = (same & (s > t)).astype(f32)
    SL = (same & (s < t)).astype(f32)
    c["glam"] = np.ascontiguousarray(np.concatenate([LT, L, SU, SL], 1))
    return c


def host_layout(inp):
    f32 = np.float32
    g = {}
    sw = np.concatenate([np.arange(32, 64), np.arange(0, 32)])
    wq = np.asarray(inp["mla_w_q_up"], f32).reshape(DEPTH, 256, 4, 192)
    g["mla_w_q_up_sw"] = np.ascontiguousarray(wq[:, :, :, 128:][:, :, :, sw].reshape(DEPTH, 256, 256))
    g["w_in_kr_sw"] = np.ascontiguousarray(np.asarray(inp["w_in"])[:, :, OC_KR:OC_KR + 64][:, :, sw])
    gains = np.stack([inp["norm_mix_pre"], inp["norm_mix_post"], inp["norm_ffn_pre"], inp["norm_ffn_post"]], 1)
    g["gains"] = np.ascontiguousarray(gains.reshape(DEPTH * 4 * 8, 128).T).astype(f32)
    cw = np.concatenate([np.asarray(inp["ffn_conv_w"], f32), np.asarray(inp["ffn_conv_b"], f32)[:, None, :]], 1)
    g["convw"] = np.ascontiguousarray(cw.reshape(DEPTH * 4 * 44, 128).T).astype(f32)
    sc = np.zeros((DEPTH, 8, 128), f32)
    sc[:, 0] = inp["diff_subln"]
    sc[:, 1] = inp["gla_norm"]
    sc[:, 2] = inp["mla_kv_norm"]
    sc[:, 3:5] = np.asarray(inp["mla_q_norm"]).reshape(DEPTH, 2, 128)
    g["smallc"] = np.ascontiguousarray(sc.reshape(DEPTH * 8, 128).T)
    g["lamrep"] = np.ascontiguousarray(np.broadcast_to(np.asarray(inp["diff_lambda"], f32).reshape(1, DEPTH * 256), (128, DEPTH * 256)))
    g["gbias"] = np.ascontiguousarray(np.broadcast_to(np.asarray(inp["gla_gate_bias"], f32).reshape(1, DEPTH * 512), (128, DEPTH * 512)))
    g["sinkrep"] = np.ascontiguousarray(np.broadcast_to(np.asarray(inp["swa_sinks"], f32).reshape(1, DEPTH * 8), (128, DEPTH * 8)))
    return g


_NC_CACHE = {}


def get_nc(layers=(0, 1), dbg=None):
    key = (tuple(layers), tuple(sorted((dbg or {}).items())))
    if key not in _NC_CACHE:
        nc = bass.Bass("TRN2", target_bir_lowering=False)
        KB(nc, dbg).build(layers)
        _NC_CACHE[key] = nc
    return _NC_CACHE[key]


def make_in_maps(inp, cores):
    shared = {}
    for k in ("w_in", "w_branch", "w_out", "ffn_w_up", "ffn_w_down", "mla_w_q_up", "mla_w_kv_up", "gla_gate_up"):
        shared[k] = np.ascontiguousarray(np.asarray(inp[k], np.float32))
    shared.update(host_layout(inp))
    shared.update(host_consts(inp))
    meta = np.asarray(inp["meta_tokens"], np.float32)
    x = np.asarray(inp["x"], np.float32)
    maps = []
    for b in cores:
        h0 = np.concatenate([meta, x[b]], axis=0)
        m = dict(shared)
        m["h0T"] = np.ascontiguousarray(h0.T)
        maps.append(m)
    return maps


def kernel(**inputs):
    nc = get_nc()
    maps = make_in_maps(inputs, list(range(8)))
    res = run_bass_kernel_spmd(nc, maps, core_ids=list(range(8)))
    out = np.stack([np.ascontiguousarray(r["outT"][:, NMETA:].T) for r in res.results], axis=0)
    return out.astype(np.float32)
```

```python
import math
import os
import numpy as np
from contextlib import ExitStack
import concourse.bass as bass
import concourse.mybir as mybir
from concourse.bass_utils import run_bass_kernel_spmd

F32 = mybir.dt.float32
BF16 = mybir.dt.bfloat16
AF = mybir.ActivationFunctionType
ALU = mybir.AluOpType

DEPTH = 2
D = 1024
SEQ = 2048
NMETA = 16
T = SEQ + NMETA
TB = 344
NB = 6
NT = 17
EPS = 1e-6
DFF = 2816
NIN = 8416
OA_Q, OA_K, OA_V = 0, 512, 1024
OB_Q, OB_K, OB_V, OB_R, OB_G = 1536, 1792, 2048, 2560, 3072
OC_QA, OC_KVA, OC_KR = 3104, 3360, 3488
OD_Q, OD_K, OD_V = 3552, 4064, 4192
O_GATE = 4320
NEG = -30000.0


def trows(j):
    return 128 if j < 16 else 16


class Dep:
    __slots__ = ("w", "r")

    def __init__(self):
        self.w = None
        self.r = []


class Op:
    __slots__ = ("eng", "fn", "deps", "ms", "val", "sem", "is_dma")

    def __init__(self, eng, fn, is_dma):
        self.eng = eng
        self.fn = fn
        self.deps = []
        self.ms = False
        self.val = 0
        self.sem = None
        self.is_dma = is_dma


ENGS = ("pe", "act", "dve", "pool", "sp")
NDMASEM = 8


class Prog:
    def __init__(self, nc, es):
        self.nc = nc
        self.es = es
        self.ops = {e: [] for e in ENGS}
        self.dma_hist = {e: [] for e in ENGS}
        self.dd = {}

    def D(self, key):
        d = self.dd.get(key)
        if d is None:
            d = self.dd[key] = Dep()
        return d

    def op(self, eng, fn, r=(), w=(), dma=False, extra=()):
        o = Op(eng, fn, dma)
        need = list(extra)
        for k in r:
            d = self.D(k)
            if d.w is not None:
                need.append(d.w)
        for k in w:
            d = self.D(k)
            if d.w is not None:
                need.append(d.w)
            for q in d.r:
                need.append(q)
        if dma:
            h = self.dma_hist[eng]
            if len(h) >= NDMASEM:
                need.append(h[-NDMASEM])
            h.append(o)
        seen = set()
        for p in need:
            if p is o or id(p) in seen:
                continue
            seen.add(id(p))
            if (not dma) and eng == "pe" and p.eng == "pe" and not p.is_dma:
                continue
            o.deps.append(p)
        for k in r:
            lst = self.D(k).r
            if not dma:
                lst[:] = [q for q in lst if q.is_dma or q.eng != eng]
            lst.append(o)
        for k in w:
            d = self.D(k)
            d.w = o
            d.r = []
        self.ops[eng].append(o)
        return o

    def barrier(self):
        lasts = []
        for e in ENGS:
            cl = [o for o in self.ops[e] if not o.is_dma]
            if cl:
                lasts.append(cl[-1])
            lasts.extend(self.dma_hist[e][-NDMASEM:])
        for e in ENGS:
            if self.ops[e]:
                self.op(e, lambda eng: eng.nop(), extra=lasts)

    def finalize(self, final_ops=()):
        nc, es = self.nc, self.es
        for e in ENGS:
            for o in self.ops[e]:
                for p in o.deps:
                    p.ms = True
        esem = {e: es.enter_context(nc.semaphore("s_" + e)) for e in ENGS}
        dsem = {e: [es.enter_context(nc.semaphore("d_%s%d" % (e, i))) for i in range(NDMASEM)]
                for e in ENGS if self.dma_hist[e]}
        for e in ENGS:
            cnt = 0
            dcnt = [0] * NDMASEM
            k = 0
            for o in self.ops[e]:
                if o.is_dma:
                    s = k % NDMASEM
                    k += 1
                    dcnt[s] += 16
                    o.sem = dsem[e][s]
                    o.val = dcnt[s]
                elif o.ms:
                    cnt += 1
                    o.sem = esem[e]
                    o.val = cnt
        engobj = {"pe": "tensor", "act": "scalar", "dve": "vector", "pool": "gpsimd", "sp": "sync"}
        block = es.enter_context(nc.Block())

        def emit(e):
            def body(eng):
                known = {}
                for o in self.ops[e]:
                    wl = {}
                    for p in o.deps:
                        key = id(p.sem)
                        if known.get(key, 0) >= p.val:
                            continue
                        if key not in wl or wl[key][1] < p.val:
                            wl[key] = (p.sem, p.val)
                    for key, (s, v) in wl.items():
                        eng.wait_ge(s, v)
                        known[key] = v
                    ins = o.fn(eng)
                    if o.is_dma:
                        ins.then_inc(o.sem, 16)
                    elif o.ms:
                        ins.then_inc(o.sem, 1)
                if e == "sp":
                    for o in final_ops:
                        eng.wait_ge(o.sem, o.val)
            return body

        for e in ENGS:
            if self.ops[e] or e == "sp":
                getattr(block, engobj[e])(emit(e))


def rel_bucket(rel):
    rel = np.asarray(rel, dtype=np.int64)
    half, max_exact = 16, 8
    ret = np.where(rel > 0, half, 0)
    n = np.abs(rel)
    nf = np.maximum(n, 1).astype(np.float32)
    large = max_exact + (np.log(nf / np.float32(max_exact)) / np.float32(math.log(128 / max_exact))
                         * (half - max_exact)).astype(np.int32)
    large = np.minimum(large, half - 1)
    return ret + np.where(n < max_exact, n, large)


def rel_bucket_jax(rel):
    import jax
    import jax.numpy as jnp
    with jax.default_device(jax.devices("cpu")[0]):
        rel = jnp.asarray(np.asarray(rel, dtype=np.int32))
        half, max_exact = 16, 8
        ret = jnp.where(rel > 0, half, 0)
        n = jnp.abs(rel)
        nf = jnp.maximum(n, 1).astype(jnp.float32)
        large = max_exact + (jnp.log(nf / max_exact) / math.log(128 / max_exact) * (half - max_exact)).astype(jnp.int32)
        large = jnp.minimum(large, half - 1)
        return np.asarray(ret + jnp.where(n < max_exact, n, large))


A_OS = sorted(set(TB * b - 128 * j for b in range(NB) for j in range(NT)))
A_NEAR = [o for o in A_OS if not (127 - o <= -91 or -o - (TB - 1) >= 91)]
A_C = -min(A_NEAR)
A_W = TB + max(A_NEAR) + A_C
D_QB = [(256 * i, 256) for i in range(8)] + [(2048, 16)]
D_C = 256
D_W = 256 + 384
D_SLABW = D_W + 256 + 256


class KB:
    def __init__(self, nc, dbg=None):
        self.nc = nc
        self.dbg = dbg or {}

    def mm(self, out, lhsT, rhs, start, stop, r, w):
        return self.P.op("pe", lambda e: e.matmul(out, lhsT=lhsT, rhs=rhs, start=start, stop=stop), r, w)

    def act(self, out, in_, func, r, w, bias=None, scale=1.0, accum=None):
        def f(e):
            kw = {}
            if bias is not None:
                kw["bias"] = bias
            if accum is not None:
                kw["accum_out"] = accum
            return e.activation(out=out, in_=in_, func=func, scale=scale, **kw)
        return self.P.op("act", f, r, w)

    def stt(self, eng, out, in0, scalar, in1, op0, op1, r, w):
        nm = {"dve": "vector", "pool": "gpsimd"}[eng]
        return self.P.op(eng, lambda e: e.scalar_tensor_tensor(out=out, in0=in0, scalar=scalar, in1=in1, op0=op0, op1=op1), r, w)

    def ts(self, eng, out, in0, s1, s2, op0, op1, r, w):
        if s2 is None:
            return self.P.op(eng, lambda e: e.tensor_scalar(out=out, in0=in0, scalar1=s1, scalar2=None, op0=op0), r, w)
        return self.P.op(eng, lambda e: e.tensor_scalar(out=out, in0=in0, scalar1=s1, scalar2=s2, op0=op0, op1=op1), r, w)

    def tt(self, eng, out, in0, in1, op, r, w):
        return self.P.op(eng, lambda e: e.tensor_tensor(out=out, in0=in0, in1=in1, op=op), r, w)

    def cp(self, eng, out, in_, r, w):
        if eng == "act":
            return self.P.op("act", lambda e: e.copy(out=out, in_=in_), r, w)
        return self.P.op(eng, lambda e: e.tensor_copy(out=out, in_=in_), r, w)

    def recip(self, out, in_, r, w):
        return self.P.op("dve", lambda e: e.reciprocal(out=out, in_=in_), r, w)

    def memset(self, eng, ap, val, w):
        return self.P.op(eng, lambda e: e.memset(ap, val), (), w)

    def dma(self, q, out, in_, r, w):
        return self.P.op(q, lambda e: e.dma_start(out=out, in_=in_), r, w, dma=True)

    def sb(self, es, name, shape, dt):
        self.sbcnt = getattr(self, "sbcnt", 0) + 1
        return es.enter_context(self.nc.sbuf_tensor("sb%d_%s" % (self.sbcnt, name), shape, dt))

    def U(self, kc, t0, t1):
        return self.uT[:, kc, t0 + 1:t1 + 1]

    def psb(self, group):
        lst = self.psgroups[group]
        i = self.psidx.get(group, 0)
        self.psidx[group] = i + 1
        b = lst[i % len(lst)]
        return self.ps[b], ("ps", b)

    def rstd_from(self, srcs, n, Dn, rkeys, out_ap, out_key, sq_ap, sq_key, sq_eng="pool"):
        pst, pk = self.ps[7], ("ps", 7)
        for i, s in enumerate(srcs):
            eng_ = sq_eng if (len(srcs) < 4 or i % 3 == 2) else "dve"
            self.tt(eng_, sq_ap[:, i % 2, :n], s, s, ALU.mult, rkeys, [(sq_key, i % 2)])
            self.mm(pst[:, :n], self.onesf[:], sq_ap[:, i % 2, :n], i == 0, i == len(srcs) - 1, [(sq_key, i % 2), "onesf"], [pk])
        self.ts("dve", out_ap, pst[:, :n], 1.0 / Dn, EPS, ALU.mult, ALU.add, [pk], [out_key])
        self.act(out_ap, out_ap, AF.Ln, [out_key], [out_key])
        self.act(out_ap, out_ap, AF.Exp, [out_key], [out_key], scale=-0.5)

    def loadw(self, dst, src, wkey):
        return self.dma("pool", dst, src, (), [wkey])

    def build(self, layers=(0, 1)):
        nc = self.nc
        I = {}

        def din(name, shape):
            I[name] = nc.dram_tensor(name, list(shape), F32, kind="ExternalInput").ap()
            return I[name]

        din("h0T", [D, T])
        din("w_in", [DEPTH, D, NIN])
        din("w_branch", [DEPTH, 4, 512, D])
        din("w_out", [DEPTH, D, D])
        din("ffn_w_up", [DEPTH, D, 2 * DFF])
        din("ffn_w_down", [DEPTH, DFF, D])
        din("mla_w_q_up", [DEPTH, 256, 768])
        din("mla_w_q_up_sw", [DEPTH, 256, 256])
        din("mla_w_kv_up", [DEPTH, 128, 1024])
        din("w_in_kr_sw", [DEPTH, D, 64])
        din("gla_gate_up", [DEPTH, 2, 16, 256])
        din("gains", [128, DEPTH * 4 * 8])
        din("convw", [128, DEPTH * 4 * 44])
        din("smallc", [128, DEPTH * 8])
        din("lamrep", [128, DEPTH * 256])
        din("gbias", [128, DEPTH * 512])
        din("sinkrep", [128, DEPTH * 8])
        din("aconst", [128, 16])
        din("dconst", [128, 8])
        din("slabA", [8, 128, A_W])
        din("slabD", [8, 128, D_SLABW])
        din("rope", [64, 2 * T])
        din("glam", [128, 4 * 128])
        outT = nc.dram_tensor("outT", [D, T], F32, kind="ExternalOutput").ap()
        hT = nc.dram_tensor("hT_scr", [D, T], F32, kind="Internal").ap()
        dbg_out = {}
        for k, shp in self.dbg.items():
            dbg_out[k] = nc.dram_tensor("dbg_" + k, list(shp), F32, kind="ExternalOutput").ap()
        self.dbg_out = dbg_out
        self.I = I

        with ExitStack() as es:
            P = self.P = Prog(nc, es)
            self.ps = [es.enter_context(nc.psum_tensor("ps%d" % i, [128, 512], F32)) for i in range(8)]
            self.psgroups = {"s": [0, 1, 2], "o": [3, 4], "d": [5, 6], "x": [0, 1, 2, 3, 4, 5, 6]}
            self.psidx = {}
            self.uT = self.sb(es, "uT", [128, 8, T + 2], BF16)
            self.onesf = self.sb(es, "onesf", [128, 128], F32)
            self.onesb = self.sb(es, "onesb", [128, 128], BF16)
            self.gains = self.sb(es, "gains", [128, DEPTH * 32], F32)
            self.convw = self.sb(es, "convw", [128, DEPTH * 4 * 44], F32)
            self.smallc = self.sb(es, "smallc", [128, DEPTH * 8], F32)
            self.aconst = self.sb(es, "aconst", [128, 16], F32)
            self.dconst = self.sb(es, "dconst", [128, 8], F32)
            self.memset("dve", self.onesf[:], 1.0, ["onesf"])
            self.memset("dve", self.onesb[:], 1.0, ["onesb"])
            self.ones16 = self.sb(es, "ones16", [128, 128], BF16)
            self.memset("dve", self.ones16[:], 0.0, ["onesb"])
            self.memset("dve", self.ones16[0:16, :], 1.0, ["onesb"])
            self.memset("dve", self.uT[:, :, 0:1], 0.0, ["uT"])
            self.memset("dve", self.uT[:, :, T + 1:T + 2], 0.0, ["uT"])
            self.dma("sp", self.gains[:], I["gains"], (), ["gains"])
            self.dma("sp", self.convw[:], I["convw"], (), ["convw"])
            self.dma("sp", self.smallc[:], I["smallc"], (), ["smallc"])
            self.dma("sp", self.aconst[:], I["aconst"], (), ["aconst"])
            self.dma("sp", self.dconst[:], I["dconst"], (), ["dconst"])

            finals = []
            nl = len(layers)
            for li, l in enumerate(layers):
                hsrc = I["h0T"] if li == 0 else hT
                last = (li == nl - 1)
                self.norm1(l, hsrc)
                with ExitStack() as les:
                    self.oall = self.sb(les, "oall", [128, 16, T], BF16)
                    self.mixers(l)
                    if not os.environ.get("ONLYMIX"):
                        self.merge_out(l, hsrc, hT, les)
                P.barrier()
                if not os.environ.get("ONLYMIX"):
                    finals += self.ffn(l, hT, outT if last else hT)
                P.barrier()
            P.finalize(finals)
        return nc

    def gcol(self, l, which, kc):
        i = (l * 4 + which) * 8 + kc
        return self.gains[:, i:i + 1]

    def norm_block(self, hb, hkey, b, gl, gw, tag):
        sq, rs = self.nsq, self.nrs
        self.rstd_from([hb[:, kc, :] for kc in range(8)], TB, D, [hkey], rs[:, :], "nrs", sq, "nsq")
        for kc in range(8):
            self.stt("dve", self.U(kc, b * TB, (b + 1) * TB), hb[:, kc, :], self.gcol(gl, gw, kc), rs[:, :],
                     ALU.mult, ALU.mult, [hkey, "nrs", "gains"], [("uT", kc)])

    def norm1(self, l, hsrc):
        with ExitStack() as es:
            hb2 = [self.sb(es, "n1h%d" % i, [128, 8, TB], F32) for i in range(2)]
            self.nsq = self.sb(es, "n1sq", [128, 2, TB], F32)
            self.nrs = self.sb(es, "n1rs", [128, TB], F32)
            hv = hsrc.rearrange("(c p) t -> p c t", p=128)
            for b in range(NB):
                hb = hb2[b % 2]
                hk = ("n1h", b % 2)
                self.dma("sp", hb[:], hv[:, :, b * TB:(b + 1) * TB], (), [hk])
                self.norm_block(hb, hk, b, l, 0, "n1")
            self.P.barrier()

    def mixers(self, l):
        import os
        sel = os.environ.get("MIX", "cadb")
        for nm, fn, c0 in (("c", self.mix_c, 8), ("a", self.mix_a, 0), ("d", self.mix_d, 12), ("b", self.mix_b, 4)):
            if nm in sel:
                fn(l)
            else:
                self.memset("dve", self.oall[:, c0:c0 + 4, :], 0.0, ["oall"])
            self.P.barrier()
        if "oall" in self.dbg_out and l == 0:
            self.dbgdump_oall()

    def dbgdump_oall(self):
        with ExitStack() as es:
            tmp = self.sb(es, "dbgtmp", [128, T], F32)
            for c in range(16):
                self.cp("dve", tmp[:], self.oall[:, c, :], ["oall"], ["dbgtmp"])
                self.dma("sp", self.dbg_out["oall"][c * 128:(c + 1) * 128, :], tmp[:], ["dbgtmp"], ())
            self.P.barrier()

    def proj_fm(self, w, wkey, ncontr, col0, M, rhs_fn, rkeys, evac):
        for b in range(NB):
            pst, pk = self.psb("x")
            for kc in range(ncontr):
                self.mm(pst[:M, :TB], w[:, kc, col0:col0 + M], rhs_fn(kc, b), kc == 0, kc == ncontr - 1,
                        [wkey] + rkeys, [pk])
            evac(b, pst[:M, :TB], pk)

    def proj_tm(self, w, wkey, ncontr, col0, N, lhs_fn, rkeys, tiles, evac):
        for j, (t0, n) in enumerate(tiles):
            pst, pk = self.psb("x")
            for kc in range(ncontr):
                self.mm(pst[:n, :N], lhs_fn(kc, t0, n), w[:, kc, col0:col0 + N], kc == 0, kc == ncontr - 1,
                        [wkey] + rkeys, [pk])
            evac(j, n, pst[:n, :N], pk)

    def attn_stream(self, tag, jobs, ptbuf, LOOK=2, DEFER=5):
        flat = []
        for ji, jb in enumerate(jobs):
            n = len(jb["tiles"])
            for i, tl in enumerate(jb["tiles"]):
                flat.append((ji, i, n, tl))
        pend = []
        state = {}
        pts = {}

        def issue_s(idx):
            ji, i, n, tl = flat[idx]
            jb = jobs[ji]
            if i == 0 and jb.get("pre") is not None:
                jb["pre"]()
            qn, scale = jb["qn"], jb["scale"]
            kr = tl["rows"]
            pst, pk = self.psb("s")
            nm = len(tl["mms"])
            for mi, (lt, rh) in enumerate(tl["mms"]):
                self.mm(pst[:kr, :qn], lt, rh, mi == 0, mi == nm - 1, jb["rkeys"], [pk])
            pi = self.ptidx
            self.ptidx += 1
            pt = ptbuf[pi % len(ptbuf)]
            ptk = (tag + "pt", pi % len(ptbuf))
            bias = tl["bias"]
            if bias is None:
                self.act(pt[:kr, :qn], pst[:kr, :qn], AF.Exp, [pk], [ptk], scale=scale)
            elif bias[0] == "c":
                self.act(pt[:kr, :qn], pst[:kr, :qn], AF.Exp, [pk] + bias[2], [ptk], bias=bias[1][:kr, :], scale=scale)
            else:
                tmp = self.sbias[pi % len(self.sbias)]
                tk = (tag + "sb", pi % len(self.sbias))
                self.stt("dve", tmp[:kr, :qn], pst[:kr, :qn], scale, bias[1], ALU.mult, ALU.add, [pk] + bias[2], [tk])
                self.act(pt[:kr, :qn], tmp[:kr, :qn], AF.Exp, [tk], [ptk])
            pts[idx] = (pt, ptk)

        def issue_pv(idx):
            ji, i, n, tl = flat[idx]
            jb = jobs[ji]
            qn, o_M = jb["qn"], jb["o_M"]
            kr = tl["rows"]
            if i == 0:
                state[ji] = self.psb("o") + self.psb("d")
            ops_, ok, dps, dk = state[ji]
            pt, ptk = pts.pop(idx)
            vl, vkeys = jb["v_fn"](tl)
            self.mm(ops_[:o_M, :qn], vl, pt[:kr, :qn], i == 0, i == n - 1, [ptk] + vkeys, [ok])
            self.mm(dps[:o_M, :qn], jb["ones_fn"](tl), pt[:kr, :qn], i == 0, i == n - 1, [ptk, "onesb"] + jb.get("okeys", []), [dk])
            if i == n - 1:
                jb["fin1"](ops_, ok, dps, dk)
                if jb.get("fin2") is not None:
                    pend.append((idx + DEFER, jb["fin2"]))
                del state[ji]

        N = len(flat)
        for idx in range(N + LOOK):
            if idx < N:
                issue_s(idx)
            if idx - LOOK >= 0:
                issue_pv(idx - LOOK)
            while pend and pend[0][0] <= idx - LOOK:
                pend.pop(0)[1]()
        for _, fn in pend:
            fn()

    def mix_c(self, l):
        I = self.I
        with ExitStack() as es:
            wc = self.sb(es, "c_w", [128, 8, 512], BF16)
            wq = self.sb(es, "c_wq", [128, 2, 768 + 256], BF16)
            wkv = self.sb(es, "c_wkv", [128, 1, 1024], BF16)
            wvv = self.sb(es, "c_wvv", [128, 1, 512], BF16)
            lat = self.sb(es, "c_lat", [128, 3, TB], F32)
            qn = self.sb(es, "c_qn", [128, 2, T], BF16)
            kvn = self.sb(es, "c_kvn", [128, T], BF16)
            kpe = self.sb(es, "c_kpe", [128, 17 * 128], BF16)
            rope = self.sb(es, "c_rope", [64, 2 * T], F32)
            vtok = self.sb(es, "c_v", [128, NT, 512], BF16)
            qno = self.sb(es, "c_qno", [128, 2, T], BF16)
            qpe = self.sb(es, "c_qpe", [128, 2, T], BF16)
            kno = self.sb(es, "c_kno", [128, 2, 17 * 128], BF16)
            ptbuf = [self.sb(es, "c_pt%d" % i, [128, TB], BF16) for i in range(3)]
            self.nsq = self.sb(es, "c_sq", [128, 2, TB], F32)
            self.nrs = self.sb(es, "c_rs", [128, TB], F32)
            t1 = self.sb(es, "c_t1", [64, TB], F32)
            t2 = self.sb(es, "c_t2", [64, TB], F32)
            rd = [self.sb(es, "c_rd%d" % i, [128, TB], F32) for i in range(2)]
            win = I["w_in"][l].rearrange("(kc p) n -> p kc n", p=128)
            self.loadw(wc[:, :, 0:448], win[:, :, OC_QA:OC_QA + 448], "c_w")
            self.loadw(wc[:, :, 448:512], I["w_in_kr_sw"][l].rearrange("(kc p) n -> p kc n", p=128), "c_w")
            self.loadw(wq[:, :, 0:768], I["mla_w_q_up"][l].rearrange("(kc p) n -> p kc n", p=128), "c_wq")
            self.loadw(wq[:, :, 768:1024], I["mla_w_q_up_sw"][l].rearrange("(kc p) n -> p kc n", p=128), "c_wq")
            self.loadw(wkv[:, 0, :], I["mla_w_kv_up"][l], "c_wkv")
            self.loadw(wvv[:, 0, :].rearrange("p (h e) -> p h e", h=4),
                       I["mla_w_kv_up"][l].rearrange("p (h e) -> p h e", h=4)[:, :, 128:256], "c_wvv")
            self.dma("sp", rope[:], I["rope"], (), ["c_rope"])
            self.memset("dve", kpe[:], 0.0, ["c_kpe"])
            for i_ in range(2):
                self.memset("dve", kno[:, i_, T:17 * 128], 0.0, [("c_kno", i_)])
            self.memset("dve", vtok[:, NT - 1, :], 0.0, ["c_v"])
            for i_ in range(2):
                self.memset("dve", qpe[64:128, i_, :], 0.0, [("c_qpe", i_)])
            ukeys = [("uT", kc) for kc in range(8)]
            urhs = lambda kc, b: self.U(kc, b * TB, (b + 1) * TB)
            for b in range(NB):
                sl = slice(b * TB, (b + 1) * TB)
                pst, pk = self.psb("x")
                pst2, pk2 = self.psb("x")
                for kc in range(8):
                    self.mm(pst[:64, :TB], wc[:, kc, 384:448], urhs(kc, b), kc == 0, kc == 7, ["c_w"] + ukeys, [pk])
                for kc in range(8):
                    self.mm(pst2[:64, :TB], wc[:, kc, 448:512], urhs(kc, b), kc == 0, kc == 7, ["c_w"] + ukeys, [pk2])
                self.tt("dve", t1[:, :], pst[:64, :TB], rope[:, sl], ALU.mult, [pk, "c_rope"], ["c_t1"])
                self.tt("dve", t2[:, :], pst2[:64, :TB], rope[:, T + b * TB:T + (b + 1) * TB], ALU.mult, [pk2, "c_rope"], ["c_t2"])
                self.tt("dve", kpe[0:64, sl], t1[:, :], t2[:, :], ALU.add, ["c_t1", "c_t2"], ["c_kpe"])
            sc = self.smallc
            for b in range(NB):
                sl = slice(b * TB, (b + 1) * TB)
                for ci in range(3):
                    pst, pk = self.psb("x")
                    for kc in range(8):
                        self.mm(pst[:, :TB], wc[:, kc, ci * 128:(ci + 1) * 128], urhs(kc, b), kc == 0, kc == 7, ["c_w"] + ukeys, [pk])
                    self.cp("act", lat[:, ci, :], pst[:, :TB], [pk], [("c_lat", ci)])
                self.rstd_from([lat[:, 0, :], lat[:, 1, :]], TB, 256, [("c_lat", 0), ("c_lat", 1)], self.nrs[:, :], "nrs", self.nsq, "nsq")
                for ci in range(2):
                    self.stt("dve", qn[:, ci, sl], lat[:, ci, :], sc[:, l * 8 + 3 + ci:l * 8 + 4 + ci], self.nrs[:, :],
                             ALU.mult, ALU.mult, [("c_lat", ci), "nrs", "smallc"], ["c_qn"])
                self.rstd_from([lat[:, 2, :]], TB, 128, [("c_lat", 2)], self.nrs[:, :], "nrs", self.nsq, "nsq")
                self.stt("dve", kvn[:, sl], lat[:, 2, :], sc[:, l * 8 + 2:l * 8 + 3], self.nrs[:, :],
                         ALU.mult, ALU.mult, [("c_lat", 2), "nrs", "smallc"], ["c_kvn"])
            tiles = [(128 * j, trows(j)) for j in range(NT)]
            self.proj_tm(wvv, "c_wvv", 1, 0, 512, lambda kc, t0, n: kvn[:, t0:t0 + n], ["c_kvn"], tiles,
                         lambda j, n, ps, pk: self.cp("act", vtok[:n, j, :], ps, [pk], ["c_v"]))
            scale = (128 + 64) ** -0.5
            self.ptidx = 0
            qrhs = lambda kc, b: qn[:, kc, b * TB:(b + 1) * TB]

            def cproj(h):
                hb_ = h % 2
                self.proj_fm(wq, "c_wq", 2, h * 192, 128, qrhs, ["c_qn"],
                             lambda b, ps, pk: self.cp("act", qno[:, hb_, b * TB:(b + 1) * TB], ps, [pk], [("c_qno", hb_)]))
                for b in range(NB):
                    sl = slice(b * TB, (b + 1) * TB)
                    pst, pk = self.psb("x")
                    pst2, pk2 = self.psb("x")
                    for kc in range(2):
                        self.mm(pst[:64, :TB], wq[:, kc, h * 192 + 128:h * 192 + 192], qrhs(kc, b), kc == 0, kc == 1, ["c_wq", "c_qn"], [pk])
                    for kc in range(2):
                        self.mm(pst2[:64, :TB], wq[:, kc, 768 + h * 64:768 + h * 64 + 64], qrhs(kc, b), kc == 0, kc == 1, ["c_wq", "c_qn"], [pk2])
                    self.tt("dve", t1[:, :], pst[:64, :TB], rope[:, sl], ALU.mult, [pk, "c_rope"], ["c_t1"])
                    self.tt("dve", t2[:, :], pst2[:64, :TB], rope[:, T + b * TB:T + (b + 1) * TB], ALU.mult, [pk2, "c_rope"], ["c_t2"])
                    self.tt("dve", qpe[0:64, hb_, sl], t1[:, :], t2[:, :], ALU.add, ["c_t1", "c_t2"], [("c_qpe", hb_)])
                self.proj_fm(wkv, "c_wkv", 1, h * 256, 128, lambda kc, b: kvn[:, b * TB:(b + 1) * TB], ["c_kvn"],
                             lambda b, ps, pk: self.cp("act", kno[:, hb_, b * TB:(b + 1) * TB], ps, [pk], [("c_kno", hb_)]))

            cproj(0)
            for h in range(4):
                if h + 1 < 4:
                    cproj(h + 1)
                hb_ = h % 2
                jobs = []
                for b in range(NB):
                    q0 = b * TB
                    tl = []
                    for j in range(NT):
                        kr = 128
                        tl.append(dict(rows=kr, j=j, bias=None,
                                       mms=[(kno[:, hb_, 128 * j:128 * j + kr], qno[:, hb_, q0:q0 + TB]),
                                            (kpe[:, 128 * j:128 * j + kr], qpe[:, hb_, q0:q0 + TB])]))

                    def fin1(ops_, ok, dps, dk, q0=q0, h=h, b=b):
                        rd_ = rd[b % 2]
                        self.recip(rd_[:, :], dps[:, :TB], [dk], [("c_rd", b % 2)])
                        self.tt("dve", self.oall[:, 8 + h, q0:q0 + TB], ops_[:, :TB], rd_[:, :], ALU.mult, [ok, ("c_rd", b % 2)], ["oall"])
                    jobs.append(dict(qn=TB, tiles=tl, o_M=128, scale=scale, fin1=fin1, fin2=None,
                                     v_fn=lambda t, h=h: (vtok[:t["rows"], t["j"], h * 128:(h + 1) * 128], ["c_v"]),
                                     ones_fn=lambda t: (self.ones16 if t["j"] == NT - 1 else self.onesb)[:, :],
                                     rkeys=[("c_kno", hb_), ("c_qno", hb_), "c_kpe", ("c_qpe", hb_)]))
                self.attn_stream("c", jobs, ptbuf)

    def mix_a(self, l):
        I = self.I
        lam_init = 0.8 - 0.6 * math.exp(-0.3 * l)
        with ExitStack() as es:
            vtok = self.sb(es, "a_v", [128, NT, 512], BF16)
            qT = self.sb(es, "a_q", [128, 8, T], BF16)
            kT = self.sb(es, "a_k", [128, 4, 17 * 128], BF16)
            lam = self.sb(es, "a_lam", [128, 256], F32)
            lt = self.sb(es, "a_lt", [128, 128], F32)
            lv = self.sb(es, "a_lv", [128, 4], F32)
            gsub = self.sb(es, "a_gs", [128, 1], F32)
            win = I["w_in"][l].rearrange("(kc p) n -> p kc n", p=128)
            self.dma("sp", lam[:], I["lamrep"][:, l * 256:(l + 1) * 256], (), ["a_lam"])
            self.tt("dve", lt[:, 0:64], lam[:, 0:64], lam[:, 64:128], ALU.mult, ["a_lam"], ["a_lt"])
            self.tt("dve", lt[:, 64:128], lam[:, 128:192], lam[:, 192:256], ALU.mult, ["a_lam"], ["a_lt"])
            self.P.op("dve", lambda e: e.reduce_sum(out=lv[:, 0:1], in_=lt[:, 0:64], axis=mybir.AxisListType.X), ["a_lt"], ["a_lv"])
            self.P.op("dve", lambda e: e.reduce_sum(out=lv[:, 1:2], in_=lt[:, 64:128], axis=mybir.AxisListType.X), ["a_lt"], ["a_lv"])
            self.act(lv[:, 0:2], lv[:, 0:2], AF.Exp, ["a_lv"], ["a_lv"])
            self.tt("dve", lv[:, 2:3], lv[:, 1:2], lv[:, 0:1], ALU.subtract, ["a_lv"], ["a_lv"])
            self.ts("dve", lv[:, 3:4], lv[:, 2:3], -lam_init, None, ALU.add, None, ["a_lv"], ["a_lv"])
            self.ts("dve", gsub[:, :], self.smallc[:, l * 8:l * 8 + 1], 1.0 - lam_init, None, ALU.mult, None, ["smallc"], ["a_gs"])
            self.memset("dve", qT[:], 0.0, ["a_q"])
            self.memset("dve", kT[:], 0.0, ["a_k"])
            self.memset("dve", vtok[:, NT - 1, :], 0.0, ["a_v"])
            ukeys = [("uT", kc) for kc in range(8)]
            urhs = lambda kc, b: self.U(kc, b * TB, (b + 1) * TB)
            tiles = [(128 * j, trows(j)) for j in range(NT)]
            with ExitStack() as es1:
                wqk = [self.sb(es1, "a_wqk%d" % i, [128, 8, 256], BF16) for i in range(2)]
                wv = self.sb(es1, "a_wv", [128, 8, 512], BF16)
                self.loadw(wv[:], win[:, :, OA_V:OA_V + 512], "a_wv")
                for h in range(4):
                    wq_, wqkk = wqk[h % 2], ("a_wqk", h % 2)
                    self.loadw(wq_[:, :, 0:128], win[:, :, OA_Q + h * 128:OA_Q + (h + 1) * 128], wqkk)
                    self.loadw(wq_[:, :, 128:256], win[:, :, OA_K + h * 128:OA_K + (h + 1) * 128], wqkk)
                    if h == 0:
                        self.proj_tm(wv, "a_wv", 8, 0, 512, lambda kc, t0, n: self.U(kc, t0, t0 + n), ukeys, tiles,
                                     lambda j, n, ps, pk: self.cp("act", vtok[:n, j, :], ps, [pk], ["a_v"]))

                    def qev(b, ps, pk, h=h):
                        self.cp("act", qT[0:64, 2 * h, b * TB:(b + 1) * TB], ps[0:64, :], [pk], ["a_q"])
                        self.cp("dve", qT[64:128, 2 * h + 1, b * TB:(b + 1) * TB], ps[64:128, :], [pk], ["a_q"])
                    self.proj_fm(wq_, wqkk, 8, 0, 128, urhs, ukeys, qev)
                    self.proj_fm(wq_, wqkk, 8, 128, 128, urhs, ukeys,
                                 lambda b, ps, pk, h=h: self.cp("act", kT[:, h, b * TB:(b + 1) * TB], ps, [pk], ["a_k"]))
                self.P.barrier()
            slab = [self.sb(es, "a_slab%d" % i, [128, 2, A_W], F32) for i in range(2)]
            ptbuf = [self.sb(es, "a_pt%d" % i, [128, TB], BF16) for i in range(3)]
            self.sbias = [self.sb(es, "a_sb%d" % i, [128, TB], F32) for i in range(2)]
            self.nsq = self.sb(es, "a_sq", [128, 2, TB], F32)
            self.nrs = self.sb(es, "a_rs", [128, TB], F32)
            rd = [self.sb(es, "a_rd%d" % i, [128, TB], F32) for i in range(2)]
            on = [self.sb(es, "a_on%d" % i, [128, TB], F32) for i in range(4)]
            self.ptidx = 0
            jobs = []
            for h in range(4):
                sl_ = slab[h % 2]
                slk = ("a_slab", h % 2)
                slab_loaded = [False]
                for b in range(NB):
                    q0 = b * TB
                    for m in range(2):
                        mh = m * 4 + h
                        tl = []
                        for j in range(NT):
                            kr = 128
                            o = TB * b - 128 * j
                            if 127 - o <= -91:
                                bias = ("c", self.aconst[:, mh:mh + 1], ["aconst"])
                            elif -o - (TB - 1) >= 91:
                                bias = ("c", self.aconst[:, 8 + mh:9 + mh], ["aconst"])
                            else:
                                bias = ("s", sl_[:kr, m, o + A_C:o + A_C + TB], [slk])
                            tl.append(dict(rows=kr, j=j, bias=bias,
                                           mms=[(kT[:, h, 128 * j:128 * j + kr], qT[:, 2 * h + m, q0:q0 + TB])]))
                        oi = (b % 2) * 2 + m

                        def fin1(ops_, ok, dps, dk, oi=oi):
                            rd_ = rd[oi % 2]
                            self.recip(rd_[:, :], dps[:, :TB], [dk], [("a_rd", oi % 2)])
                            self.tt("dve", on[oi][:, :], ops_[:, :TB], rd_[:, :], ALU.mult, [ok, ("a_rd", oi % 2)], [("a_on", oi)])

                        def fin2(b=b, h=h, q0=q0):
                            o0, o1 = (b % 2) * 2, (b % 2) * 2 + 1
                            self.stt("dve", on[o0][:, :], on[o1][:, :], lv[:, 3:4], on[o0][:, :], ALU.mult, ALU.add,
                                     [("a_on", o0), ("a_on", o1), "a_lv"], [("a_on", o0)])
                            self.rstd_from([on[o0][:, :]], TB, 128, [("a_on", o0)], self.nrs[:, :], "nrs", self.nsq, "nsq")
                            self.stt("dve", self.oall[:, h, q0:q0 + TB], on[o0][:, :], gsub[:, 0:1], self.nrs[:, :], ALU.mult, ALU.mult,
                                     [("a_on", o0), "nrs", "a_gs"], ["oall"])
                        jobs.append(dict(qn=TB, tiles=tl, o_M=128, scale=0.125, fin1=fin1, fin2=(fin2 if m == 1 else None),
                                         v_fn=lambda t, h=h: (vtok[:t["rows"], t["j"], h * 128:(h + 1) * 128], ["a_v"]),
                                         ones_fn=lambda t: (self.ones16 if t["j"] == NT - 1 else self.onesb)[:, :], rkeys=["a_q", "a_k"],
                                         pre=((lambda h=h: [self.dma("sp", slab[h % 2][:, mm_, :], self.I["slabA"][mm_ * 4 + h], (), [("a_slab", h % 2)]) for mm_ in range(2)])
                                              if (b == 0 and m == 0) else None)))
            self.attn_stream("a", jobs, ptbuf)

    def mix_d(self, l):
        I = self.I
        with ExitStack() as es:
            wq = self.sb(es, "d_wq", [128, 8, 512], BF16)
            wkk = self.sb(es, "d_wkk", [128, 8, 2, 128], BF16)
            wv = self.sb(es, "d_wv", [128, 8, 128], BF16)
            qT = self.sb(es, "d_q", [128, 8, T], BF16)
            kT2 = self.sb(es, "d_k", [128, 2, T], BF16)
            vpad = self.sb(es, "d_v", [128, NT * 4, 128], BF16)
            slab = [self.sb(es, "d_slab%d" % i, [128, D_SLABW], F32) for i in range(4)]
            ptbuf = [self.sb(es, "d_pt%d" % i, [128, 256], BF16) for i in range(3)]
            self.sbias = [self.sb(es, "d_sb%d" % i, [128, 256], F32) for i in range(2)]
            oh = self.sb(es, "d_oh", [128, 2, 128], BF16)
            es8 = self.sb(es, "d_es8", [128, 8], F32)
            es2 = self.sb(es, "d_es2", [128, 4], F32)
            rd = [self.sb(es, "d_rd%d" % i, [128, 256], F32) for i in range(2)]
            win = I["w_in"][l].rearrange("(kc p) n -> p kc n", p=128)
            self.loadw(wq[:], win[:, :, OD_Q:OD_Q + 512], "d_wq")
            for kv in range(2):
                for e in range(2):
                    self.loadw(wkk[:, :, kv, e * 64:(e + 1) * 64], win[:, :, OD_K + kv * 64:OD_K + (kv + 1) * 64], "d_wkk")
            self.loadw(wv[:], win[:, :, OD_V:OD_V + 128], "d_wv")
            self.memset("dve", vpad[:], 0.0, ["d_v"])
            self.memset("dve", oh[:], 0.0, ["d_oh"])
            self.memset("dve", oh[:, 0, 0:64], 1.0, ["d_oh"])
            self.memset("dve", oh[:, 1, 64:128], 1.0, ["d_oh"])
            self.dma("sp", es8[:], I["sinkrep"][:, l * 8:(l + 1) * 8], (), ["d_es8"])
            self.act(es8[:], es8[:], AF.Exp, ["d_es8"], ["d_es8"])
            for p in range(4):
                self.cp("dve", es2[0:64, p:p + 1], es8[0:64, 2 * p:2 * p + 1], ["d_es8"], ["d_es2"])
                self.cp("dve", es2[64:128, p:p + 1], es8[64:128, 2 * p + 1:2 * p + 2], ["d_es8"], ["d_es2"])
            ukeys = [("uT", kc) for kc in range(8)]
            urhs = lambda kc, b: self.U(kc, b * TB, (b + 1) * TB)
            self.memset("dve", qT[:], 0.0, ["d_q"])

            def qev(b, ps, pk, p):
                self.cp("act", qT[0:64, 2 * p, b * TB:(b + 1) * TB], ps[0:64, :], [pk], ["d_q"])
                self.cp("act", qT[64:128, 2 * p + 1, b * TB:(b + 1) * TB], ps[64:128, :], [pk], ["d_q"])
            for p in range(4):
                self.proj_fm(wq, "d_wq", 8, p * 128, 128, urhs, ukeys, lambda b, ps, pk, p=p: qev(b, ps, pk, p))
            for kv in range(2):
                self.proj_fm(wkk[:, :, kv, :], "d_wkk", 8, 0, 128, urhs, ukeys,
                             lambda b, ps, pk, kv=kv: self.cp("act", kT2[:, kv, b * TB:(b + 1) * TB], ps, [pk], ["d_k"]))
            tiles = [(128 * j, trows(j)) for j in range(NT)]

            def vev(j, n, ps, pk):
                for kv in range(2):
                    self.cp("act", vpad[:n, j * 4 + kv * 2, 0:64], ps[:, kv * 64:(kv + 1) * 64], [pk], ["d_v"])
                    self.cp("dve", vpad[:n, j * 4 + kv * 2 + 1, 64:128], ps[:, kv * 64:(kv + 1) * 64], [pk], ["d_v"])
            self.proj_tm(wv, "d_wv", 8, 0, 128, lambda kc, t0, n: self.U(kc, t0, t0 + n), ukeys, tiles, vev)
            self.ptidx = 0
            jobs = []
            for p in range(4):
                kv = p // 2
                sls = [slab[(p % 2) * 2 + e] for e in range(2)]
                slks = [("d_slab", (p % 2) * 2 + e) for e in range(2)]
                pre_p = (lambda p=p, sls=sls, slks=slks: [self.dma("sp", sls[e][:], I["slabD"][2 * p + e], (), [slks[e]]) for e in range(2)])
                for qb, (q0, qn) in enumerate(D_QB):
                    tl = []
                    for e in range(2):
                        h = 2 * p + e
                        pr = slice(64 * e, 64 * e + 64)
                        if qb == 0:
                            mb = ("s", sls[e][:16, D_W + 256:D_W + 256 + qn], [slks[e]])
                        else:
                            mb = ("c", self.dconst[:, h:h + 1], ["dconst"])
                        tl.append(dict(rows=16, j=0, e=e, bias=mb, mms=[(kT2[:, kv, 0:16], qT[:, h, q0:q0 + qn])]))
                        for j in range(max(0, q0 // 128 - 1), min(16, (q0 + qn + 127) // 128) + 1):
                            kr = trows(j)
                            o = q0 - 128 * j
                            if j == 0:
                                bs = sls[e][:kr, D_W:D_W + qn]
                            else:
                                bs = sls[e][:kr, o + D_C:o + D_C + qn]
                            tl.append(dict(rows=kr, j=j, e=e, bias=("s", bs, [slks[e]]),
                                           mms=[(kT2[:, kv, 128 * j:128 * j + kr], qT[:, h, q0:q0 + qn])]))

                    def fin1(ops_, ok, dps, dk, p=p, q0=q0, qn=qn, qb=qb):
                        rd_ = rd[qb % 2]
                        rk = ("d_rd", qb % 2)
                        self.ts("dve", rd_[:, :qn], dps[:, :qn], es2[:, p:p + 1], None, ALU.add, None, [dk, "d_es2"], [rk])
                        self.recip(rd_[:, :qn], rd_[:, :qn], [rk], [rk])
                        self.tt("dve", self.oall[:, 12 + p, q0:q0 + qn], ops_[:, :qn], rd_[:, :qn], ALU.mult, [ok, rk], ["oall"])
                    jobs.append(dict(qn=qn, tiles=tl, o_M=128, scale=0.125, fin1=fin1, fin2=None,
                                     v_fn=lambda t, kv=kv: (vpad[:t["rows"], t["j"] * 4 + kv * 2 + t["e"], :], ["d_v"]),
                                     ones_fn=lambda t: oh[:t["rows"], t["e"], :], okeys=["d_oh"], rkeys=["d_q", "d_k"],
                                     pre=(pre_p if qb == 0 else None)))
            self.attn_stream("d", jobs, ptbuf)

    def mix_b(self, l):
        I = self.I
        with ExitStack() as es:
            wb = self.sb(es, "b_w", [128, 8, 1568], BF16)
            wgu = self.sb(es, "b_wgu", [16, 2, 256], F32)
            gb = self.sb(es, "b_gb", [128, 512], F32)
            msk = self.sb(es, "b_msk", [128, 4, 128], F32)
            obw = self.sb(es, "b_obw", [128, 4, T], F32)
            S = self.sb(es, "b_S", [64, 4, 128], F32)
            Sbf = self.sb(es, "b_Sbf", [64, 4, 128], BF16)
            self.nsq = self.sb(es, "b_sq", [128, 2, 64], F32)
            self.nrs = self.sb(es, "b_rs", [128, 64], F32)
            NBUF = 2
            bufs = {}

            def tb(name, shape, dt, i):
                k = (name, i % NBUF)
                if k not in bufs:
                    bufs[k] = self.sb(es, "b_%s%d" % (name, i % NBUF), shape, dt)
                return bufs[k], ("b_" + name, i % NBUF)

            win = I["w_in"][l].rearrange("(kc p) n -> p kc n", p=128)
            self.loadw(wb[:], win[:, :, OB_Q:OB_Q + 1568], "b_w")
            self.dma("sp", wgu[:], I["gla_gate_up"][l].rearrange("g r c -> r g c"), (), ["b_wgu"])
            self.dma("sp", gb[:], I["gbias"][:, l * 512:(l + 1) * 512], (), ["b_gb"])
            self.dma("sp", msk[:], I["glam"].rearrange("p (m t) -> p m t", m=4), (), ["b_msk"])
            ukeys = [("uT", kc) for kc in range(8)]
            gch = [(0, 16)] + [(16 + 64 * (c - 1), 64) for c in range(1, 33)]
            seq = [(1, ci) for ci in range(32, -1, -1)] + [(0, ci) for ci in range(33)]
            NS = len(seq)
            cxs = {}

            def ctx(i):
                if i not in cxs:
                    dr, ci = seq[i]
                    t0, n = gch[ci]
                    cxs[i] = dict(i=i, dr=dr, ci=ci, t0=t0, n=n, mi_c=(0 if dr == 0 else 1), mi_r=(2 if dr == 0 else 3))
                return cxs[i]

            def P1(cx):
                i, dr, t0, n = cx["i"], cx["dr"], cx["t0"], cx["n"]
                ut = lambda kc: self.U(kc, t0, t0 + n)
                pgl, pglk = self.psb("x")
                for kc in range(8):
                    self.mm(pgl[:16, :n], wb[:, kc, 1536 + 16 * dr:1552 + 16 * dr], ut(kc), kc == 0, kc == 7, ["b_w"] + ukeys, [pglk])
                cx["glT"], cx["glk"] = tb3("glT", [16, 64], F32, i)
                self.cp("act", cx["glT"][:, :n], pgl[:16, :n], [pglk], [cx["glk"]])
                pvt, pvtk = self.psb("x")
                for kc in range(8):
                    self.mm(pvt[:n, :512], ut(kc), wb[:, kc, 512:1024], kc == 0, kc == 7, ["b_w"] + ukeys, [pvtk])
                cx["vt"], cx["vtk"] = tb3("vt", [64, 512], BF16, i)
                self.cp("act", cx["vt"][:n, :], pvt[:n, :512], [pvtk], [cx["vtk"]])
                if dr == 0:
                    prr, prrk = self.psb("x")
                    for h in range(4):
                        for kc in range(8):
                            self.mm(prr[:, h * 64:h * 64 + n], wb[:, kc, 1024 + h * 128:1024 + (h + 1) * 128], ut(kc), kc == 0, kc == 7, ["b_w"] + ukeys, [prrk])
                    k4 = ("sr", i % 4)
                    if k4 not in bufs:
                        bufs[k4] = self.sb(es, "b_sr%d" % (i % 4), [128, 4, 64], F32)
                    cx["sr"], cx["srk"] = bufs[k4], ("b_sr", i % 4)
                    r3 = prr[:, 0:256].rearrange("p (h t) -> p h t", h=4)[:, :, :n]
                    sr_ = cx["sr"]
                    self.act(sr_[:, :, :n], r3, AF.Exp, [prrk], [cx["srk"]], scale=-1.0)
                    self.ts("dve", sr_[:, :, :n], sr_[:, :, :n], 1.0, None, ALU.add, None, [cx["srk"]], [cx["srk"]])
                    self.recip(sr_[:, :, :n], sr_[:, :, :n], [cx["srk"]], [cx["srk"]])
                    self.tt("dve", sr_[:, :, :n], sr_[:, :, :n], r3, ALU.mult, [cx["srk"], prrk], [cx["srk"]])

            def P2a(cx):
                i, dr, n = cx["i"], cx["dr"], cx["n"]
                ppre, pprek = self.psb("x")
                self.mm(ppre[:n, :256], cx["glT"][:, :n], wgu[:, dr, :], True, True, [cx["glk"], "b_wgu"], [pprek])
                xla, xlk = tb("xla", [64, 256], F32, i)
                self.tt("dve", xla[:n, :], ppre[:n, :256], gb[:n, dr * 256:(dr + 1) * 256], ALU.add, [pprek, "b_gb"], [xlk])
                self.act(xla[:n, :], xla[:n, :], AF.Exp, [xlk], [xlk], scale=-1.0)
                cx["sp"], cx["spk"] = tb("sp", [64, 256], F32, i)
                self.act(cx["sp"][:n, :], xla[:n, :], AF.Ln, [xlk], [cx["spk"]], bias=1.0)

            def P2b(cx):
                i, dr, t0, n = cx["i"], cx["dr"], cx["t0"], cx["n"]
                sp_, spk = cx["sp"], cx["spk"]
                ut = lambda kc: self.U(kc, t0, t0 + n)
                pqk, pqkk = self.psb("x")
                for qi in range(8):
                    for kc in range(8):
                        self.mm(pqk[:64, qi * 64:qi * 64 + n], wb[:, kc, qi * 64:(qi + 1) * 64], ut(kc), kc == 0, kc == 7, ["b_w"] + ukeys, [pqkk])
                pkt, pktk = self.psb("x")
                for kc in range(8):
                    self.mm(pkt[:n, :256], ut(kc), wb[:, kc, 256:512], kc == 0, kc == 7, ["b_w"] + ukeys, [pktk])
                pc, pck = self.psb("x")
                for h in range(4):
                    self.mm(pc[:64, h * 64:h * 64 + n], sp_[:n, h * 64:(h + 1) * 64], msk[:n, cx["mi_c"], :n], True, True, [spk, "b_msk"], [pck])
                pr_, prk = self.psb("x")
                self.mm(pr_[:n, :256], msk[:n, cx["mi_r"], :n], sp_[:n, :], True, True, [spk, "b_msk"], [prk])
                cx["eb"], cx["ebk"] = tb("eb", [64, 4, 64], F32, i)
                einv, eik = tb("einv", [64, 4, 64], F32, i)
                pc3 = pc[:64, 0:256].rearrange("p (h t) -> p h t", h=4)[:, :, :n]
                self.act(cx["eb"][:, :, :n], pc3, AF.Exp, [pck], [cx["ebk"]], scale=-1.0 / 16)
                self.act(einv[:, :, :n], pc3, AF.Exp, [pck], [eik], scale=1.0 / 16)
                eo, eok = tb("eo", [64, 256], F32, i)
                self.act(eo[:n, :], pr_[:n, :256], AF.Exp, [prk], [eok], scale=-1.0 / 16)
                cx["qd"], cx["qdk"] = tb("qd", [64, 4, 64], BF16, i)
                cx["ki"], cx["kik"] = tb("ki", [64, 4, 64], BF16, i)
                q3 = pqk[:64, 0:256].rearrange("p (h t) -> p h t", h=4)[:, :, :n]
                k3 = pqk[:64, 256:512].rearrange("p (h t) -> p h t", h=4)[:, :, :n]
                self.stt("dve", cx["qd"][:, :, :n], q3, 0.125, cx["eb"][:, :, :n], ALU.mult, ALU.mult, [pqkk, cx["ebk"]], [cx["qdk"]])
                self.tt("dve", cx["ki"][:, :, :n], k3, einv[:, :, :n], ALU.mult, [pqkk, eik], [cx["kik"]])
                cx["ko"], cx["kok"] = tb("ko", [64, 256], BF16, i)
                self.tt("dve", cx["ko"][:n, :], pkt[:n, :256], eo[:n, :], ALU.mult, [pktk, eok], [cx["kok"]])

            def P3a(cx):
                i, n = cx["i"], cx["n"]
                pat, patk = self.psb("x")
                for h in range(4):
                    self.mm(pat[:n, h * 64:h * 64 + n], cx["ki"][:, h, :n], cx["qd"][:, h, :n], True, True, [cx["kik"], cx["qdk"]], [patk])
                cx["att"], cx["atk"] = tb("att", [64, 4, 64], BF16, i)
                for h in range(4):
                    self.tt("dve", cx["att"][:n, h, :n], pat[:n, h * 64:h * 64 + n], msk[:n, cx["mi_c"], :n], ALU.mult, [patk, "b_msk"], [cx["atk"]])
                pds, pdsk = self.psb("x")
                for h in range(4):
                    self.mm(pds[:64, h * 128:(h + 1) * 128], cx["ko"][:n, h * 64:(h + 1) * 64], cx["vt"][:n, h * 128:(h + 1) * 128], True, True, [cx["kok"], cx["vtk"]], [pdsk])
                cx["pds"], cx["pdsk"] = pds, pdsk

            def P3b(cx):
                i, dr, t0, n = cx["i"], cx["dr"], cx["t0"], cx["n"]
                vt, vtk, att, atk, qd, qdk = cx["vt"], cx["vtk"], cx["att"], cx["atk"], cx["qd"], cx["qdk"]
                if i == 0 or seq[i][0] != seq[i - 1][0]:
                    self.memset("dve", S[:], 0.0, ["b_S"])
                    self.memset("dve", Sbf[:], 0.0, ["b_Sbf"])
                po, pok = self.psb("x")
                for h in range(4):
                    self.mm(po[:, h * 64:h * 64 + n], vt[:n, h * 128:(h + 1) * 128], att[:n, h, :n], True, False, [vtk, atk], [pok])
                    self.mm(po[:, h * 64:h * 64 + n], Sbf[:, h, :], qd[:, h, :n], False, True, ["b_Sbf", qdk], [pok])
                dcol = (n - 1) if dr == 0 else 0
                pds, pdsk = cx["pds"], cx["pdsk"]
                for h in range(4):
                    self.stt("dve", S[:, h, :], S[:, h, :], cx["eb"][:, h, dcol:dcol + 1], pds[:64, h * 128:(h + 1) * 128], ALU.mult, ALU.add, ["b_S", cx["ebk"], pdsk], ["b_S"])
                self.cp("act", Sbf[:], S[:], ["b_S"], ["b_Sbf"])
                if dr == 1:
                    self.cp("act", obw[:, :, t0:t0 + n], po[:, 0:256].rearrange("p (h t) -> p h t", h=4)[:, :, :n], [pok], ["b_obw"])
                else:
                    of, ofk = tb("of", [128, 4, 64], F32, i)
                    sq4, sqk = tb("sq4", [128, 4, 64], F32, i)
                    if ("of", i % NBUF) not in init_done:
                        init_done.add(("of", i % NBUF))
                        self.memset("dve", of[:], 0.0, [ofk])
                    self.tt("dve", of[:, :, :n], po[:, 0:256].rearrange("p (h t) -> p h t", h=4)[:, :, :n], obw[:, :, t0:t0 + n], ALU.add, [pok, "b_obw"], [ofk])
                    self.tt("pool", sq4[:], of[:], of[:], ALU.mult, [ofk], [sqk])
                    cx["of"], cx["ofk"], cx["sq4"], cx["sqk"] = of, ofk, sq4, sqk
                    pend3.append(cx)
                del cxs[i]

            def P3c(cx):
                i, t0, n = cx["i"], cx["t0"], cx["n"]
                of, ofk = cx["of"], cx["ofk"]
                pn, pnk = self.psb("x")
                self.mm(pn[:, 0:256], self.onesf[:], cx["sq4"][:].rearrange("p h t -> p (h t)"), True, True, [cx["sqk"], "onesf"], [pnk])
                rs4, rsk = tb("rs4", [128, 4, 64], F32, i)
                rs2 = rs4[:].rearrange("p h t -> p (h t)")
                self.ts("dve", rs2, pn[:, 0:256], 1.0 / 128, EPS, ALU.mult, ALU.add, [pnk], [rsk])
                self.act(rs2, rs2, AF.Ln, [rsk], [rsk])
                self.act(rs2, rs2, AF.Exp, [rsk], [rsk], scale=-0.5)
                self.stt("dve", of[:, :, :n], of[:, :, :n], self.smallc[:, l * 8 + 1:l * 8 + 2], rs4[:, :, :n], ALU.mult, ALU.mult, [ofk, rsk, "smallc"], [ofk])
                self.tt("dve", self.oall[:, 4:8, t0:t0 + n], of[:, :, :n], cx["sr"][:, :, :n], ALU.mult, [ofk, cx["srk"]], ["oall"])

            init_done = set()
            pend3 = []

            def tb3(name, shape, dt, i):
                k = (name, i % 3)
                if k not in bufs:
                    bufs[k] = self.sb(es, "b_%s%d" % (name, i % 3), shape, dt)
                return bufs[k], ("b_" + name, i % 3)

            P1(ctx(0))
            P1(ctx(1))
            P2a(ctx(0))
            P2b(ctx(0))
            for i in range(NS):
                if i + 2 < NS:
                    P1(ctx(i + 2))
                if i + 1 < NS:
                    P2a(ctx(i + 1))
                P3a(ctx(i))
                while pend3:
                    P3c(pend3.pop(0))
                if i + 1 < NS:
                    P2b(ctx(i + 1))
                P3b(ctx(i))
            while pend3:
                P3c(pend3.pop(0))

    def resid_block(self, y, ykeys, b, l, gw, hsrc, hdst, bufs, nextnorm=None, final=False):
        hb, hk = bufs
        hv_s = hsrc.rearrange("(c p) t -> p c t", p=128)
        hv_d = hdst.rearrange("(c p) t -> p c t", p=128)
        sl = slice(b * TB, (b + 1) * TB)
        self.dma("sp", hb[:], hv_s[:, :, sl], (), [hk])
        self.rstd_from([y[:, kc, :] for kc in range(8)], TB, D, ykeys, self.nrs[:, :], "nrs", self.nsq, "nsq")
        for kc in range(8):
            self.stt("dve", y[:, kc, :], y[:, kc, :], self.gcol(l, gw, kc), self.nrs[:, :], ALU.mult, ALU.mult,
                     ykeys + ["nrs", "gains"], ykeys)
        self.tt("dve", hb[:], hb[:], y[:], ALU.add, [hk] + ykeys, [hk])
        st = self.dma("sp", hv_d[:, :, sl], hb[:], [hk], [("hdram", b)])
        if nextnorm is not None:
            self.norm_block(hb, hk, b, nextnorm[0], nextnorm[1], "nn")
        return st

    def merge_out(self, l, hsrc, hT, les):
        I = self.I
        with ExitStack() as es:
            merged = self.sb(es, "m_merged", [128, 8, T], BF16)
            wo = self.sb(es, "o_w", [128, 8, D], BF16)
            with ExitStack() as es2:
                wbr = [self.sb(es2, "m_wbr%d" % i, [128, 16, 128], BF16) for i in range(2)]
                wg = [self.sb(es2, "m_wg%d" % i, [128, 8, 4, 128], BF16) for i in range(2)]
                sig = [self.sb(es2, "m_sig%d" % i, [128, TB], F32) for i in range(2)]
                acc = self.sb(es2, "m_acc", [128, TB], F32)
                prod = self.sb(es2, "m_prod", [128, TB], F32)
                ukeys = [("uT", kc) for kc in range(8)]
                def ldm(dc):
                    wb_, wbk = wbr[dc % 2], ("m_wbr", dc % 2)
                    wg_, wgk = wg[dc % 2], ("m_wg", dc % 2)
                    self.loadw(wb_[:], I["w_branch"][l].rearrange("n (ec p) d -> p (n ec) d", p=128)[:, :, dc * 128:(dc + 1) * 128], wbk)
                    for br in range(4):
                        self.loadw(wg_[:, :, br, :], I["w_in"][l].rearrange("(kc p) n -> p kc n", p=128)
                                   [:, :, O_GATE + br * 1024 + dc * 128:O_GATE + br * 1024 + (dc + 1) * 128], wgk)
                ldm(0)
                self.loadw(wo[:], I["w_out"][l].rearrange("(kc p) n -> p kc n", p=128), "o_w")
                for dc in range(8):
                    wb_, wbk = wbr[dc % 2], ("m_wbr", dc % 2)
                    wg_, wgk = wg[dc % 2], ("m_wg", dc % 2)
                    if dc + 1 < 8:
                        ldm(dc + 1)
                    for b in range(NB):
                        sl = slice(b * TB, (b + 1) * TB)
                        for br in range(4):
                            pg, pgk = self.psb("x")
                            for kc in range(8):
                                self.mm(pg[:, :TB], wg_[:, kc, br, :], self.U(kc, b * TB, (b + 1) * TB), kc == 0, kc == 7, [wgk] + ukeys, [pgk])
                            pp, ppk = self.psb("x")
                            for ec in range(4):
                                self.mm(pp[:, :TB], wb_[:, br * 4 + ec, :], self.oall[:, br * 4 + ec, sl], ec == 0, ec == 3, [wbk, "oall"], [ppk])
                            sg, sgk = sig[br % 2], ("m_sig", br % 2)
                            self.act(sg[:, :], pg[:, :TB], AF.Sigmoid, [pgk], [sgk])
                            if br == 0:
                                self.tt("dve", acc[:, :], pp[:, :TB], sg[:, :], ALU.mult, [ppk, sgk], ["m_acc"])
                            else:
                                self.tt("dve", prod[:, :], pp[:, :TB], sg[:, :], ALU.mult, [ppk, sgk], ["m_prod"])
                                if br < 3:
                                    self.tt("dve", acc[:, :], acc[:, :], prod[:, :], ALU.add, ["m_acc", "m_prod"], ["m_acc"])
                                else:
                                    self.tt("dve", merged[:, dc, sl], acc[:, :], prod[:, :], ALU.add, ["m_acc", "m_prod"], ["m_merged"])
                self.P.barrier()
            if "merged" in self.dbg_out and l == 0:
                with ExitStack() as es3:
                    tmp = self.sb(es3, "dbgtmp2", [128, T], F32)
                    for c in range(8):
                        self.cp("dve", tmp[:], merged[:, c, :], ["m_merged"], ["dbgtmp2"])
                        self.dma("sp", self.dbg_out["merged"][c * 128:(c + 1) * 128, :], tmp[:], ["dbgtmp2"], ())
                    self.P.barrier()
            with ExitStack() as es2:
                y2 = [self.sb(es2, "o_y%d" % i, [128, 8, TB], F32) for i in range(2)]
                hb2 = [self.sb(es2, "o_h%d" % i, [128, 8, TB], F32) for i in range(2)]
                self.nsq = self.sb(es2, "o_sq", [128, 2, TB], F32)
                self.nrs = self.sb(es2, "o_rs", [128, TB], F32)
                for b in range(NB):
                    y, yk = y2[b % 2], ("o_y", b % 2)
                    sl = slice(b * TB, (b + 1) * TB)
                    for dc in range(8):
                        pst, pk = self.psb("x")
                        for kc in range(8):
                            self.mm(pst[:, :TB], wo[:, kc, dc * 128:(dc + 1) * 128], merged[:, kc, sl], kc == 0, kc == 7, ["o_w", "m_merged"], [pk])
                        self.cp("act", y[:, dc, :], pst[:, :TB], [pk], [yk])
                    self.resid_block(y, [yk], b, l, 1, hsrc, hT, (hb2[b % 2], ("o_h", b % 2)), nextnorm=(l, 2))
                self.P.barrier()

    def ffn(self, l, hT, hdst):
        I = self.I
        finals = []
        NJ = DFF // 128
        wd_es = ExitStack()
        wd = self.sb(wd_es, "f_wd", [128, NJ, D], BF16)
        wd_loaded = [False]
        for half in range(2):
            with ExitStack() as es:
                actT = self.sb(es, "f_act", [128, NJ, 3 * TB], BF16)
                with ExitStack() as es2:
                    wu = [self.sb(es2, "f_wu%d" % i, [128, 8, 2, 128], BF16) for i in range(2)]
                    cgs = [self.sb(es2, "f_cg%d" % i, [128, TB], F32) for i in range(2)]
                    cvs = [self.sb(es2, "f_cv%d" % i, [128, TB], F32) for i in range(2)]
                    t1s = [self.sb(es2, "f_t1%d" % i, [128, TB], F32) for i in range(2)]
                    wup = I["ffn_w_up"][l].rearrange("(kc p) n -> p kc n", p=128)
                    ukeys = [("uT", kc) for kc in range(8)]
                    cw = self.convw
                    it = 0
                    def ldw(j):
                        w_, wk = wu[j % 2], ("f_wu", j % 2)
                        self.loadw(w_[:, :, 0, :], wup[:, :, j * 128:(j + 1) * 128], wk)
                        self.loadw(w_[:, :, 1, :], wup[:, :, DFF + j * 128:DFF + (j + 1) * 128], wk)
                    ldw(0)
                    if not wd_loaded[0]:
                        wd_loaded[0] = True
                        self.loadw(wd[:], I["ffn_w_down"][l].rearrange("(j p) n -> p j n", p=128), "f_wd")
                    for j in range(NJ):
                        w_, wk = wu[j % 2], ("f_wu", j % 2)
                        if j + 1 < NJ:
                            ldw(j + 1)
                        for b in range(3 * half, 3 * half + 3):
                            it += 1
                            cg, cv, t1 = cgs[it % 2], cvs[it % 2], t1s[it % 2]
                            cgk, cvk, t1k = ("f_cg", it % 2), ("f_cv", it % 2), ("f_t1", it % 2)
                            taps = []
                            for gv in range(2):
                                pst, pk = self.psb("x")
                                for kc in range(8):
                                    self.mm(pst[:, :TB + 2], w_[:, kc, gv, :], self.uT[:, kc, b * TB:b * TB + TB + 2], kc == 0, kc == 7, [wk] + ukeys, [pk])
                                ch = gv * NJ + j
                                base = (l * 4) * 44
                                c0 = cw[:, base + ch:base + ch + 1]
                                c1 = cw[:, base + 44 + ch:base + 44 + ch + 1]
                                c2 = cw[:, base + 88 + ch:base + 88 + ch + 1]
                                cb = cw[:, base + 132 + ch:base + 132 + ch + 1]
                                dst, dk = (cg, cgk) if gv == 0 else (cv, cvk)
                                self.act(dst[:, :], pst[:, 0:TB], AF.Identity, [pk, "convw"], [dk], bias=cb, scale=c0)
                                taps.append((dst, dk, pst, pk, c1, c2))
                            for (dst, dk, pst, pk, c1, c2) in taps:
                                self.stt("dve", dst[:, :], pst[:, 1:TB + 1], c1, dst[:, :], ALU.mult, ALU.add, [pk, dk, "convw"], [dk])
                            for (dst, dk, pst, pk, c1, c2) in taps:
                                self.stt("dve", dst[:, :], pst[:, 2:TB + 2], c2, dst[:, :], ALU.mult, ALU.add, [pk, dk, "convw"], [dk])
                            self.act(t1[:, :], cg[:, :], AF.Gelu_apprx_tanh, [cgk], [t1k])
                            self.tt("pool", actT[:, j, (b - 3 * half) * TB:(b - 3 * half + 1) * TB], t1[:, :], cv[:, :], ALU.mult, [t1k, cvk], ["f_act"])
                    self.P.barrier()
                with ExitStack() as es2:
                    y2 = [self.sb(es2, "f_y%d" % i, [128, 8, TB], F32) for i in range(2)]
                    hb2 = [self.sb(es2, "f_h%d" % i, [128, 8, TB], F32) for i in range(2)]
                    self.nsq = self.sb(es2, "f_sq", [128, 2, TB], F32)
                    self.nrs = self.sb(es2, "f_rs", [128, TB], F32)
                    for b in range(3 * half, 3 * half + 3):
                        y, yk = y2[b % 2], ("f_y", b % 2)
                        sl = slice(b * TB, (b + 1) * TB)
                        for dc in range(8):
                            pst, pk = self.psb("x")
                            for j in range(NJ):
                                self.mm(pst[:, :TB], wd[:, j, dc * 128:(dc + 1) * 128], actT[:, j, (b - 3 * half) * TB:(b - 3 * half + 1) * TB], j == 0, j == NJ - 1, ["f_wd", "f_act"], [pk])
                            self.cp("act", y[:, dc, :], pst[:, :TB], [pk], [yk])
                        st = self.resid_block(y, [yk], b, l, 3, hT, hdst, (hb2[b % 2], ("f_h", b % 2)))
                        finals.append(st)
                    self.P.barrier()
        wd_es.close()
        return finals


def host_consts(inp):
    f32 = np.float32
    c = {}
    tab = np.asarray(inp["rel_bias_table"], f32)
    kk = np.arange(128)[:, None]
    jj = np.arange(A_W)[None, :]
    bk = rel_bucket_jax(kk - jj + A_C)
    c["slabA"] = np.ascontiguousarray(np.transpose(tab[bk][:, :, 0:8], (2, 0, 1))).astype(f32)
    ac = np.concatenate([tab[15, 0:8], tab[31, 0:8]])
    c["aconst"] = np.ascontiguousarray(np.broadcast_to(ac[None, :], (128, 16))).astype(f32)
    tabd = tab[:, 8:16]
    jj = np.arange(D_W)[None, :]
    rel = kk - jj + D_C
    tz = np.where((np.abs(rel) <= 128)[:, :, None], tabd[rel_bucket_jax(rel)], f32(NEG))
    qq = np.arange(256)[None, :]
    rel0 = kk - qq
    t0 = np.where(((np.abs(rel0) <= 128) & (kk >= NMETA))[:, :, None], tabd[rel_bucket_jax(rel0)], f32(NEG))
    tm = np.where((kk < NMETA)[:, :, None], tabd[rel_bucket_jax(rel0)], f32(NEG))
    c["slabD"] = np.ascontiguousarray(np.transpose(np.concatenate([tz, t0, tm], axis=1), (2, 0, 1))).astype(f32)
    c["dconst"] = np.ascontiguousarray(np.broadcast_to(tabd[15][None, :], (128, 8))).astype(f32)
    half = 32
    inv = (10000.0 ** (-np.arange(half, dtype=np.float32) / half)).astype(f32)
    ang = np.arange(T, dtype=f32)[None, :] * inv[:, None]
    cos, sin = np.cos(ang).astype(f32), np.sin(ang).astype(f32)
    c["rope"] = np.ascontiguousarray(np.concatenate([np.concatenate([cos, cos], 0), np.concatenate([-sin, sin], 0)], 1)).astype(f32)
    s = np.arange(128)[:, None]
    t = np.arange(128)[None, :]
    same = (s // 64) == (t // 64)
    LT = (same & (s <= t)).astype(f32)
    L = (same & (s >= t)).astype(f32)
    SU = (same & (s > t)).astype(f32)
    SL = (same & (s < t)).astype(f32)
    c["glam"] = np.ascontiguousarray(np.concatenate([LT, L, SU, SL], 1))
    return c


def host_layout(inp):
    f32 = np.float32
    g = {}
    sw = np.concatenate([np.arange(32, 64), np.arange(0, 32)])
    wq = np.asarray(inp["mla_w_q_up"], f32).reshape(DEPTH, 256, 4, 192)
    g["mla_w_q_up_sw"] = np.ascontiguousarray(wq[:, :, :, 128:][:, :, :, sw].reshape(DEPTH, 256, 256))
    g["w_in_kr_sw"] = np.ascontiguousarray(np.asarray(inp["w_in"])[:, :, OC_KR:OC_KR + 64][:, :, sw])
    gains = np.stack([inp["norm_mix_pre"], inp["norm_mix_post"], inp["norm_ffn_pre"], inp["norm_ffn_post"]], 1)
    g["gains"] = np.ascontiguousarray(gains.reshape(DEPTH * 4 * 8, 128).T).astype(f32)
    cw = np.concatenate([np.asarray(inp["ffn_conv_w"], f32), np.asarray(inp["ffn_conv_b"], f32)[:, None, :]], 1)
    g["convw"] = np.ascontiguousarray(cw.reshape(DEPTH * 4 * 44, 128).T).astype(f32)
    sc = np.zeros((DEPTH, 8, 128), f32)
    sc[:, 0] = inp["diff_subln"]
    sc[:, 1] = inp["gla_norm"]
    sc[:, 2] = inp["mla_kv_norm"]
    sc[:, 3:5] = np.asarray(inp["mla_q_norm"]).reshape(DEPTH, 2, 128)
    g["smallc"] = np.ascontiguousarray(sc.reshape(DEPTH * 8, 128).T)
    g["lamrep"] = np.ascontiguousarray(np.broadcast_to(np.asarray(inp["diff_lambda"], f32).reshape(1, DEPTH * 256), (128, DEPTH * 256)))
    g["gbias"] = np.ascontiguousarray(np.broadcast_to(np.asarray(inp["gla_gate_bias"], f32).reshape(1, DEPTH * 512), (128, DEPTH * 512)))
    g["sinkrep"] = np.ascontiguousarray(np.broadcast_to(np.asarray(inp["swa_sinks"], f32).reshape(1, DEPTH * 8), (128, DEPTH * 8)))
    return g


_NC_CACHE = {}


def get_nc(layers=(0, 1), dbg=None):
    key = (tuple(layers), tuple(sorted((dbg or {}).items())))
    if key not in _NC_CACHE:
        nc = bass.Bass("TRN2", target_bir_lowering=False)
        KB(nc, dbg).build(layers)
        _NC_CACHE[key] = nc
    return _NC_CACHE[key]


def make_in_maps(inp, cores):
    shared = {}
    for k in ("w_in", "w_branch", "w_out", "ffn_w_up", "ffn_w_down", "mla_w_q_up", "mla_w_kv_up", "gla_gate_up"):
        shared[k] = np.ascontiguousarray(np.asarray(inp[k], np.float32))
    shared.update(host_layout(inp))
    shared.update(host_consts(inp))
    meta = np.asarray(inp["meta_tokens"], np.float32)
    x = np.asarray(inp["x"], np.float32)
    maps = []
    for b in cores:
        h0 = np.concatenate([meta, x[b]], axis=0)
        m = dict(shared)
        m["h0T"] = np.ascontiguousarray(h0.T)
        maps.append(m)
    return maps


def kernel(**inputs):
    nc = get_nc()
    maps = make_in_maps(inputs, list(range(8)))
    res = run_bass_kernel_spmd(nc, maps, core_ids=list(range(8)))
    out = np.stack([np.ascontiguousarray(r["outT"][:, NMETA:].T) for r in res.results], axis=0)
    return out.astype(np.float32)
```

```python
import math
import os
import numpy as np
from contextlib import ExitStack
import concourse.bass as bass
import concourse.mybir as mybir
from concourse.bass_utils import run_bass_kernel_spmd

F32 = mybir.dt.float32
BF16 = mybir.dt.bfloat16
AF = mybir.ActivationFunctionType
ALU = mybir.AluOpType

DEPTH = 2
D = 1024
SEQ = 2048
NMETA = 16
T = SEQ + NMETA
TB = 344
NB = 6
NT = 17
EPS = 1e-6
DFF = 2816
NIN = 8416
OA_Q, OA_K, OA_V = 0, 512, 1024
OB_Q, OB_K, OB_V, OB_R, OB_G = 1536, 1792, 2048, 2560, 3072
OC_QA, OC_KVA, OC_KR = 3104, 3360, 3488
OD_Q, OD_K, OD_V = 3552, 4064, 4192
O_GATE = 4320
NEG = -30000.0


def trows(j):
    return 128 if j < 16 else 16


class Dep:
    __slots__ = ("w", "r")

    def __init__(self):
        self.w = None
        self.r = []


class Op:
    __slots__ = ("eng", "fn", "deps", "ms", "val", "sem", "is_dma")

    def __init__(self, eng, fn, is_dma):
        self.eng = eng
        self.fn = fn
        self.deps = []
        self.ms = False
        self.val = 0
        self.sem = None
        self.is_dma = is_dma


ENGS = ("pe", "act", "dve", "pool", "sp")
NDMASEM = 8


class Prog:
    def __init__(self, nc, es):
        self.nc = nc
        self.es = es
        self.ops = {e: [] for e in ENGS}
        self.dma_hist = {e: [] for e in ENGS}
        self.dd = {}

    def D(self, key):
        d = self.dd.get(key)
        if d is None:
            d = self.dd[key] = Dep()
        return d

    def op(self, eng, fn, r=(), w=(), dma=False, extra=()):
        o = Op(eng, fn, dma)
        need = list(extra)
        for k in r:
            d = self.D(k)
            if d.w is not None:
                need.append(d.w)
        for k in w:
            d = self.D(k)
            if d.w is not None:
                need.append(d.w)
            for q in d.r:
                need.append(q)
        if dma:
            h = self.dma_hist[eng]
            if len(h) >= NDMASEM:
                need.append(h[-NDMASEM])
            h.append(o)
        seen = set()
        for p in need:
            if p is o or id(p) in seen:
                continue
            seen.add(id(p))
            if (not dma) and eng == "pe" and p.eng == "pe" and not p.is_dma:
                continue
            o.deps.append(p)
        for k in r:
            lst = self.D(k).r
            if not dma:
                lst[:] = [q for q in lst if q.is_dma or q.eng != eng]
            lst.append(o)
        for k in w:
            d = self.D(k)
            d.w = o
            d.r = []
        self.ops[eng].append(o)
        return o

    def barrier(self):
        lasts = []
        for e in ENGS:
            cl = [o for o in self.ops[e] if not o.is_dma]
            if cl:
                lasts.append(cl[-1])
            lasts.extend(self.dma_hist[e][-NDMASEM:])
        for e in ENGS:
            if self.ops[e]:
                self.op(e, lambda eng: eng.nop(), extra=lasts)

    def finalize(self, final_ops=()):
        nc, es = self.nc, self.es
        for e in ENGS:
            for o in self.ops[e]:
                for p in o.deps:
                    p.ms = True
        esem = {e: es.enter_context(nc.semaphore("s_" + e)) for e in ENGS}
        dsem = {e: [es.enter_context(nc.semaphore("d_%s%d" % (e, i))) for i in range(NDMASEM)]
                for e in ENGS if self.dma_hist[e]}
        for e in ENGS:
            cnt = 0
            dcnt = [0] * NDMASEM
            k = 0
            for o in self.ops[e]:
                if o.is_dma:
                    s = k % NDMASEM
                    k += 1
                    dcnt[s] += 16
                    o.sem = dsem[e][s]
                    o.val = dcnt[s]
                elif o.ms:
                    cnt += 1
                    o.sem = esem[e]
                    o.val = cnt
        engobj = {"pe": "tensor", "act": "scalar", "dve": "vector", "pool": "gpsimd", "sp": "sync"}
        block = es.enter_context(nc.Block())

        def emit(e):
            def body(eng):
                known = {}
                for o in self.ops[e]:
                    wl = {}
                    for p in o.deps:
                        key = id(p.sem)
                        if known.get(key, 0) >= p.val:
                            continue
                        if key not in wl or wl[key][1] < p.val:
                            wl[key] = (p.sem, p.val)
                    for key, (s, v) in wl.items():
                        eng.wait_ge(s, v)
                        known[key] = v
                    ins = o.fn(eng)
                    if o.is_dma:
                        ins.then_inc(o.sem, 16)
                    elif o.ms:
                        ins.then_inc(o.sem, 1)
                if e == "sp":
                    for o in final_ops:
                        eng.wait_ge(o.sem, o.val)
            return body

        for e in ENGS:
            if self.ops[e] or e == "sp":
                getattr(block, engobj[e])(emit(e))


def rel_bucket(rel):
    rel = np.asarray(rel, dtype=np.int64)
    half, max_exact = 16, 8
    ret = np.where(rel > 0, half, 0)
    n = np.abs(rel)
    nf = np.maximum(n, 1).astype(np.float32)
    large = max_exact + (np.log(nf / np.float32(max_exact)) / np.float32(math.log(128 / max_exact))
                         * (half - max_exact)).astype(np.int32)
    large = np.minimum(large, half - 1)
    return ret + np.where(n < max_exact, n, large)


def rel_bucket_jax(rel):
    import jax
    import jax.numpy as jnp
    with jax.default_device(jax.devices("cpu")[0]):
        rel = jnp.asarray(np.asarray(rel, dtype=np.int32))
        half, max_exact = 16, 8
        ret = jnp.where(rel > 0, half, 0)
        n = jnp.abs(rel)
        nf = jnp.maximum(n, 1).astype(jnp.float32)
        large = max_exact + (jnp.log(nf / max_exact) / math.log(128 / max_exact) * (half - max_exact)).astype(jnp.int32)
        large = jnp.minimum(large, half - 1)
        return np.asarray(ret + jnp.where(n < max_exact, n, large))


A_OS = sorted(set(TB * b - 128 * j for b in range(NB) for j in range(NT)))
A_NEAR = [o for o in A_OS if not (127 - o <= -91 or -o - (TB - 1) >= 91)]
A_C = -min(A_NEAR)
A_W = TB + max(A_NEAR) + A_C
D_QB = [(256 * i, 256) for i in range(8)] + [(2048, 16)]
D_C = 256
D_W = 256 + 384
D_SLABW = D_W + 256 + 256


class KB:
    def __init__(self, nc, dbg=None):
        self.nc = nc
        self.dbg = dbg or {}

    def mm(self, out, lhsT, rhs, start, stop, r, w):
        return self.P.op("pe", lambda e: e.matmul(out, lhsT=lhsT, rhs=rhs, start=start, stop=stop), r, w)

    def act(self, out, in_, func, r, w, bias=None, scale=1.0, accum=None):
        def f(e):
            kw = {}
            if bias is not None:
                kw["bias"] = bias
            if accum is not None:
                kw["accum_out"] = accum
            return e.activation(out=out, in_=in_, func=func, scale=scale, **kw)
        return self.P.op("act", f, r, w)

    def stt(self, eng, out, in0, scalar, in1, op0, op1, r, w):
        nm = {"dve": "vector", "pool": "gpsimd"}[eng]
        return self.P.op(eng, lambda e: e.scalar_tensor_tensor(out=out, in0=in0, scalar=scalar, in1=in1, op0=op0, op1=op1), r, w)

    def ts(self, eng, out, in0, s1, s2, op0, op1, r, w):
        if s2 is None:
            return self.P.op(eng, lambda e: e.tensor_scalar(out=out, in0=in0, scalar1=s1, scalar2=None, op0=op0), r, w)
        return self.P.op(eng, lambda e: e.tensor_scalar(out=out, in0=in0, scalar1=s1, scalar2=s2, op0=op0, op1=op1), r, w)

    def tt(self, eng, out, in0, in1, op, r, w):
        return self.P.op(eng, lambda e: e.tensor_tensor(out=out, in0=in0, in1=in1, op=op), r, w)

    def cp(self, eng, out, in_, r, w):
        if eng == "act":
            return self.P.op("act", lambda e: e.copy(out=out, in_=in_), r, w)
        return self.P.op(eng, lambda e: e.tensor_copy(out=out, in_=in_), r, w)

    def recip(self, out, in_, r, w):
        return self.P.op("dve", lambda e: e.reciprocal(out=out, in_=in_), r, w)

    def memset(self, eng, ap, val, w):
        return self.P.op(eng, lambda e: e.memset(ap, val), (), w)

    def dma(self, q, out, in_, r, w):
        return self.P.op(q, lambda e: e.dma_start(out=out, in_=in_), r, w, dma=True)

    def sb(self, es, name, shape, dt):
        self.sbcnt = getattr(self, "sbcnt", 0) + 1
        return es.enter_context(self.nc.sbuf_tensor("sb%d_%s" % (self.sbcnt, name), shape, dt))

    def U(self, kc, t0, t1):
        return self.uT[:, kc, t0 + 1:t1 + 1]

    def psb(self, group):
        lst = self.psgroups[group]
        i = self.psidx.get(group, 0)
        self.psidx[group] = i + 1
        b = lst[i % len(lst)]
        return self.ps[b], ("ps", b)

    def rstd_from(self, srcs, n, Dn, rkeys, out_ap, out_key, sq_ap, sq_key, sq_eng="pool"):
        pst, pk = self.ps[7], ("ps", 7)
        for i, s in enumerate(srcs):
            eng_ = sq_eng if (len(srcs) < 4 or i % 3 == 2) else "dve"
            self.tt(eng_, sq_ap[:, i % 2, :n], s, s, ALU.mult, rkeys, [(sq_key, i % 2)])
            self.mm(pst[:, :n], self.onesf[:], sq_ap[:, i % 2, :n], i == 0, i == len(srcs) - 1, [(sq_key, i % 2), "onesf"], [pk])
        self.ts("dve", out_ap, pst[:, :n], 1.0 / Dn, EPS, ALU.mult, ALU.add, [pk], [out_key])
        self.act(out_ap, out_ap, AF.Ln, [out_key], [out_key])
        self.act(out_ap, out_ap, AF.Exp, [out_key], [out_key], scale=-0.5)

    def loadw(self, dst, src, wkey):
        return self.dma("pool", dst, src, (), [wkey])

    def build(self, layers=(0, 1)):
        nc = self.nc
        I = {}

        def din(name, shape):
            I[name] = nc.dram_tensor(name, list(shape), F32, kind="ExternalInput").ap()
            return I[name]

        din("h0T", [D, T])
        din("w_in", [DEPTH, D, NIN])
        din("w_branch", [DEPTH, 4, 512, D])
        din("w_out", [DEPTH, D, D])
        din("ffn_w_up", [DEPTH, D, 2 * DFF])
        din("ffn_w_down", [DEPTH, DFF, D])
        din("mla_w_q_up", [DEPTH, 256, 768])
        din("mla_w_q_up_sw", [DEPTH, 256, 256])
        din("mla_w_kv_up", [DEPTH, 128, 1024])
        din("w_in_kr_sw", [DEPTH, D, 64])
        din("gla_gate_up", [DEPTH, 2, 16, 256])
        din("gains", [128, DEPTH * 4 * 8])
        din("convw", [128, DEPTH * 4 * 44])
        din("smallc", [128, DEPTH * 8])
        din("lamrep", [128, DEPTH * 256])
        din("gbias", [128, DEPTH * 512])
        din("sinkrep", [128, DEPTH * 8])
        din("aconst", [128, 16])
        din("dconst", [128, 8])
        din("slabA", [8, 128, A_W])
        din("slabD", [8, 128, D_SLABW])
        din("rope", [64, 2 * T])
        din("glam", [128, 4 * 128])
        outT = nc.dram_tensor("outT", [D, T], F32, kind="ExternalOutput").ap()
        hT = nc.dram_tensor("hT_scr", [D, T], F32, kind="Internal").ap()
        dbg_out = {}
        for k, shp in self.dbg.items():
            dbg_out[k] = nc.dram_tensor("dbg_" + k, list(shp), F32, kind="ExternalOutput").ap()
        self.dbg_out = dbg_out
        self.I = I

        with ExitStack() as es:
            P = self.P = Prog(nc, es)
            self.ps = [es.enter_context(nc.psum_tensor("ps%d" % i, [128, 512], F32)) for i in range(8)]
            self.psgroups = {"s": [0, 1, 2], "o": [3, 4], "d": [5, 6], "x": [0, 1, 2, 3, 4, 5, 6]}
            self.psidx = {}
            self.uT = self.sb(es, "uT", [128, 8, T + 2], BF16)
            self.onesf = self.sb(es, "onesf", [128, 128], F32)
            self.onesb = self.sb(es, "onesb", [128, 128], BF16)
            self.gains = self.sb(es, "gains", [128, DEPTH * 32], F32)
            self.convw = self.sb(es, "convw", [128, DEPTH * 4 * 44], F32)
            self.smallc = self.sb(es, "smallc", [128, DEPTH * 8], F32)
            self.aconst = self.sb(es, "aconst", [128, 16], F32)
            self.dconst = self.sb(es, "dconst", [128, 8], F32)
            self.memset("dve", self.onesf[:], 1.0, ["onesf"])
            self.memset("dve", self.onesb[:], 1.0, ["onesb"])
            self.ones16 = self.sb(es, "ones16", [128, 128], BF16)
            self.memset("dve", self.ones16[:], 0.0, ["onesb"])
            self.memset("dve", self.ones16[0:16, :], 1.0, ["onesb"])
            self.memset("dve", self.uT[:, :, 0:1], 0.0, ["uT"])
            self.memset("dve", self.uT[:, :, T + 1:T + 2], 0.0, ["uT"])
            self.dma("sp", self.gains[:], I["gains"], (), ["gains"])
            self.dma("sp", self.convw[:], I["convw"], (), ["convw"])
            self.dma("sp", self.smallc[:], I["smallc"], (), ["smallc"])
            self.dma("sp", self.aconst[:], I["aconst"], (), ["aconst"])
            self.dma("sp", self.dconst[:], I["dconst"], (), ["dconst"])

            finals = []
            nl = len(layers)
            for li, l in enumerate(layers):
                hsrc = I["h0T"] if li == 0 else hT
                last = (li == nl - 1)
                self.norm1(l, hsrc)
                with ExitStack() as les:
                    self.oall = self.sb(les, "oall", [128, 16, T], BF16)
                    self.mixers(l)
                    if not os.environ.get("ONLYMIX"):
                        self.merge_out(l, hsrc, hT, les)
                P.barrier()
                if not os.environ.get("ONLYMIX"):
                    finals += self.ffn(l, hT, outT if last else hT)
                P.barrier()
            P.finalize(finals)
        return nc

    def gcol(self, l, which, kc):
        i = (l * 4 + which) * 8 + kc
        return self.gains[:, i:i + 1]

    def norm_block(self, hb, hkey, b, gl, gw, tag):
        sq, rs = self.nsq, self.nrs
        self.rstd_from([hb[:, kc, :] for kc in range(8)], TB, D, [hkey], rs[:, :], "nrs", sq, "nsq")
        for kc in range(8):
            self.stt("dve", self.U(kc, b * TB, (b + 1) * TB), hb[:, kc, :], self.gcol(gl, gw, kc), rs[:, :],
                     ALU.mult, ALU.mult, [hkey, "nrs", "gains"], [("uT", kc)])

    def norm1(self, l, hsrc):
        with ExitStack() as es:
            hb2 = [self.sb(es, "n1h%d" % i, [128, 8, TB], F32) for i in range(2)]
            self.nsq = self.sb(es, "n1sq", [128, 2, TB], F32)
            self.nrs = self.sb(es, "n1rs", [128, TB], F32)
            hv = hsrc.rearrange("(c p) t -> p c t", p=128)
            for b in range(NB):
                hb = hb2[b % 2]
                hk = ("n1h", b % 2)
                self.dma("sp", hb[:], hv[:, :, b * TB:(b + 1) * TB], (), [hk])
                self.norm_block(hb, hk, b, l, 0, "n1")
            self.P.barrier()

    def mixers(self, l):
        import os
        sel = os.environ.get("MIX", "cadb")
        for nm, fn, c0 in (("c", self.mix_c, 8), ("a", self.mix_a, 0), ("d", self.mix_d, 12), ("b", self.mix_b, 4)):
            if nm in sel:
                fn(l)
            else:
                self.memset("dve", self.oall[:, c0:c0 + 4, :], 0.0, ["oall"])
            self.P.barrier()
        if "oall" in self.dbg_out and l == 0:
            self.dbgdump_oall()

    def dbgdump_oall(self):
        with ExitStack() as es:
            tmp = self.sb(es, "dbgtmp", [128, T], F32)
            for c in range(16):
                self.cp("dve", tmp[:], self.oall[:, c, :], ["oall"], ["dbgtmp"])
                self.dma("sp", self.dbg_out["oall"][c * 128:(c + 1) * 128, :], tmp[:], ["dbgtmp"], ())
            self.P.barrier()

    def proj_fm(self, w, wkey, ncontr, col0, M, rhs_fn, rkeys, evac):
        for b in range(NB):
            pst, pk = self.psb("x")
            for kc in range(ncontr):
                self.mm(pst[:M, :TB], w[:, kc, col0:col0 + M], rhs_fn(kc, b), kc == 0, kc == ncontr - 1,
                        [wkey] + rkeys, [pk])
            evac(b, pst[:M, :TB], pk)

    def proj_tm(self, w, wkey, ncontr, col0, N, lhs_fn, rkeys, tiles, evac):
        for j, (t0, n) in enumerate(tiles):
            pst, pk = self.psb("x")
            for kc in range(ncontr):
                self.mm(pst[:n, :N], lhs_fn(kc, t0, n), w[:, kc, col0:col0 + N], kc == 0, kc == ncontr - 1,
                        [wkey] + rkeys, [pk])
            evac(j, n, pst[:n, :N], pk)

    def attn_stream(self, tag, jobs, ptbuf, LOOK=2, DEFER=5):
        flat = []
        for ji, jb in enumerate(jobs):
            n = len(jb["tiles"])
            for i, tl in enumerate(jb["tiles"]):
                flat.append((ji, i, n, tl))
        pend = []
        state = {}
        pts = {}

        def issue_s(idx):
            ji, i, n, tl = flat[idx]
            jb = jobs[ji]
            if i == 0 and jb.get("pre") is not None:
                jb["pre"]()
            qn, scale = jb["qn"], jb["scale"]
            kr = tl["rows"]
            pst, pk = self.psb("s")
            nm = len(tl["mms"])
            for mi, (lt, rh) in enumerate(tl["mms"]):
                self.mm(pst[:kr, :qn], lt, rh, mi == 0, mi == nm - 1, jb["rkeys"], [pk])
            pi = self.ptidx
            self.ptidx += 1
            pt = ptbuf[pi % len(ptbuf)]
            ptk = (tag + "pt", pi % len(ptbuf))
            bias = tl["bias"]
            if bias is None:
                self.act(pt[:kr, :qn], pst[:kr, :qn], AF.Exp, [pk], [ptk], scale=scale)
            elif bias[0] == "c":
                self.act(pt[:kr, :qn], pst[:kr, :qn], AF.Exp, [pk] + bias[2], [ptk], bias=bias[1][:kr, :], scale=scale)
            else:
                tmp = self.sbias[pi % len(self.sbias)]
                tk = (tag + "sb", pi % len(self.sbias))
                self.stt("dve", tmp[:kr, :qn], pst[:kr, :qn], scale, bias[1], ALU.mult, ALU.add, [pk] + bias[2], [tk])
                self.act(pt[:kr, :qn], tmp[:kr, :qn], AF.Exp, [tk], [ptk])
            pts[idx] = (pt, ptk)

        def issue_pv(idx):
            ji, i, n, tl = flat[idx]
            jb = jobs[ji]
            qn, o_M = jb["qn"], jb["o_M"]
            kr = tl["rows"]
            if i == 0:
                state[ji] = self.psb("o") + self.psb("d")
            ops_, ok, dps, dk = state[ji]
            pt, ptk = pts.pop(idx)
            vl, vkeys = jb["v_fn"](tl)
            self.mm(ops_[:o_M, :qn], vl, pt[:kr, :qn], i == 0, i == n - 1, [ptk] + vkeys, [ok])
            self.mm(dps[:o_M, :qn], jb["ones_fn"](tl), pt[:kr, :qn], i == 0, i == n - 1, [ptk, "onesb"] + jb.get("okeys", []), [dk])
            if i == n - 1:
                jb["fin1"](ops_, ok, dps, dk)
                if jb.get("fin2") is not None:
                    pend.append((idx + DEFER, jb["fin2"]))
                del state[ji]

        N = len(flat)
        for idx in range(N + LOOK):
            if idx < N:
                issue_s(idx)
            if idx - LOOK >= 0:
                issue_pv(idx - LOOK)
            while pend and pend[0][0] <= idx - LOOK:
                pend.pop(0)[1]()
        for _, fn in pend:
            fn()

    def mix_c(self, l):
        I = self.I
        with ExitStack() as es:
            wc = self.sb(es, "c_w", [128, 8, 512], BF16)
            wq = self.sb(es, "c_wq", [128, 2, 768 + 256], BF16)
            wkv = self.sb(es, "c_wkv", [128, 1, 1024], BF16)
            wvv = self.sb(es, "c_wvv", [128, 1, 512], BF16)
            lat = self.sb(es, "c_lat", [128, 3, TB], F32)
            qn = self.sb(es, "c_qn", [128, 2, T], BF16)
            kvn = self.sb(es, "c_kvn", [128, T], BF16)
            kpe = self.sb(es, "c_kpe", [128, 17 * 128], BF16)
            rope = self.sb(es, "c_rope", [64, 2 * T], F32)
            vtok = self.sb(es, "c_v", [128, NT, 512], BF16)
            qno = self.sb(es, "c_qno", [128, 2, T], BF16)
            qpe = self.sb(es, "c_qpe", [128, 2, T], BF16)
            kno = self.sb(es, "c_kno", [128, 2, 17 * 128], BF16)
            ptbuf = [self.sb(es, "c_pt%d" % i, [128, TB], BF16) for i in range(3)]
            self.nsq = self.sb(es, "c_sq", [128, 2, TB], F32)
            self.nrs = self.sb(es, "c_rs", [128, TB], F32)
            t1 = self.sb(es, "c_t1", [64, TB], F32)
            t2 = self.sb(es, "c_t2", [64, TB], F32)
            rd = [self.sb(es, "c_rd%d" % i, [128, TB], F32) for i in range(2)]
            win = I["w_in"][l].rearrange("(kc p) n -> p kc n", p=128)
            self.loadw(wc[:, :, 0:448], win[:, :, OC_QA:OC_QA + 448], "c_w")
            self.loadw(wc[:, :, 448:512], I["w_in_kr_sw"][l].rearrange("(kc p) n -> p kc n", p=128), "c_w")
            self.loadw(wq[:, :, 0:768], I["mla_w_q_up"][l].rearrange("(kc p) n -> p kc n", p=128), "c_wq")
            self.loadw(wq[:, :, 768:1024], I["mla_w_q_up_sw"][l].rearrange("(kc p) n -> p kc n", p=128), "c_wq")
            self.loadw(wkv[:, 0, :], I["mla_w_kv_up"][l], "c_wkv")
            self.loadw(wvv[:, 0, :].rearrange("p (h e) -> p h e", h=4),
                       I["mla_w_kv_up"][l].rearrange("p (h e) -> p h e", h=4)[:, :, 128:256], "c_wvv")
            self.dma("sp", rope[:], I["rope"], (), ["c_rope"])
            self.memset("dve", kpe[:], 0.0, ["c_kpe"])
            for i_ in range(2):
                self.memset("dve", kno[:, i_, T:17 * 128], 0.0, [("c_kno", i_)])
            self.memset("dve", vtok[:, NT - 1, :], 0.0, ["c_v"])
            for i_ in range(2):
                self.memset("dve", qpe[64:128, i_, :], 0.0, [("c_qpe", i_)])
            ukeys = [("uT", kc) for kc in range(8)]
            urhs = lambda kc, b: self.U(kc, b * TB, (b + 1) * TB)
            for b in range(NB):
                sl = slice(b * TB, (b + 1) * TB)
                pst, pk = self.psb("x")
                pst2, pk2 = self.psb("x")
                for kc in range(8):
                    self.mm(pst[:64, :TB], wc[:, kc, 384:448], urhs(kc, b), kc == 0, kc == 7, ["c_w"] + ukeys, [pk])
                for kc in range(8):
                    self.mm(pst2[:64, :TB], wc[:, kc, 448:512], urhs(kc, b), kc == 0, kc == 7, ["c_w"] + ukeys, [pk2])
                self.tt("dve", t1[:, :], pst[:64, :TB], rope[:, sl], ALU.mult, [pk, "c_rope"], ["c_t1"])
                self.tt("dve", t2[:, :], pst2[:64, :TB], rope[:, T + b * TB:T + (b + 1) * TB], ALU.mult, [pk2, "c_rope"], ["c_t2"])
                self.tt("dve", kpe[0:64, sl], t1[:, :], t2[:, :], ALU.add, ["c_t1", "c_t2"], ["c_kpe"])
            sc = self.smallc
            for b in range(NB):
                sl = slice(b * TB, (b + 1) * TB)
                for ci in range(3):
                    pst, pk = self.psb("x")
                    for kc in range(8):
                        self.mm(pst[:, :TB], wc[:, kc, ci * 128:(ci + 1) * 128], urhs(kc, b), kc == 0, kc == 7, ["c_w"] + ukeys, [pk])
                    self.cp("act", lat[:, ci, :], pst[:, :TB], [pk], [("c_lat", ci)])
                self.rstd_from([lat[:, 0, :], lat[:, 1, :]], TB, 256, [("c_lat", 0), ("c_lat", 1)], self.nrs[:, :], "nrs", self.nsq, "nsq")
                for ci in range(2):
                    self.stt("dve", qn[:, ci, sl], lat[:, ci, :], sc[:, l * 8 + 3 + ci:l * 8 + 4 + ci], self.nrs[:, :],
                             ALU.mult, ALU.mult, [("c_lat", ci), "nrs", "smallc"], ["c_qn"])
                self.rstd_from([lat[:, 2, :]], TB, 128, [("c_lat", 2)], self.nrs[:, :], "nrs", self.nsq, "nsq")
                self.stt("dve", kvn[:, sl], lat[:, 2, :], sc[:, l * 8 + 2:l * 8 + 3], self.nrs[:, :],
                         ALU.mult, ALU.mult, [("c_lat", 2), "nrs", "smallc"], ["c_kvn"])
            tiles = [(128 * j, trows(j)) for j in range(NT)]
            self.proj_tm(wvv, "c_wvv", 1, 0, 512, lambda kc, t0, n: kvn[:, t0:t0 + n], ["c_kvn"], tiles,
                         lambda j, n, ps, pk: self.cp("act", vtok[:n, j, :], ps, [pk], ["c_v"]))
            scale = (128 + 64) ** -0.5
            self.ptidx = 0
            qrhs = lambda kc, b: qn[:, kc, b * TB:(b + 1) * TB]

            def cproj(h):
                hb_ = h % 2
                self.proj_fm(wq, "c_wq", 2, h * 192, 128, qrhs, ["c_qn"],
                             lambda b, ps, pk: self.cp("act", qno[:, hb_, b * TB:(b + 1) * TB], ps, [pk], [("c_qno", hb_)]))
                for b in range(NB):
                    sl = slice(b * TB, (b + 1) * TB)
                    pst, pk = self.psb("x")
                    pst2, pk2 = self.psb("x")
                    for kc in range(2):
                        self.mm(pst[:64, :TB], wq[:, kc, h * 192 + 128:h * 192 + 192], qrhs(kc, b), kc == 0, kc == 1, ["c_wq", "c_qn"], [pk])
                    for kc in range(2):
                        self.mm(pst2[:64, :TB], wq[:, kc, 768 + h * 64:768 + h * 64 + 64], qrhs(kc, b), kc == 0, kc == 1, ["c_wq", "c_qn"], [pk2])
                    self.tt("dve", t1[:, :], pst[:64, :TB], rope[:, sl], ALU.mult, [pk, "c_rope"], ["c_t1"])
                    self.tt("dve", t2[:, :], pst2[:64, :TB], rope[:, T + b * TB:T + (b + 1) * TB], ALU.mult, [pk2, "c_rope"], ["c_t2"])
                    self.tt("dve", qpe[0:64, hb_, sl], t1[:, :], t2[:, :], ALU.add, ["c_t1", "c_t2"], [("c_qpe", hb_)])
                self.proj_fm(wkv, "c_wkv", 1, h * 256, 128, lambda kc, b: kvn[:, b * TB:(b + 1) * TB], ["c_kvn"],
                             lambda b, ps, pk: self.cp("act", kno[:, hb_, b * TB:(b + 1) * TB], ps, [pk], [("c_kno", hb_)]))

            cproj(0)
            for h in range(4):
                if h + 1 < 4:
                    cproj(h + 1)
                hb_ = h % 2
                jobs = []
                for b in range(NB):
                    q0 = b * TB
                    tl = []
                    for j in range(NT):
                        kr = 128
                        tl.append(dict(rows=kr, j=j, bias=None,
                                       mms=[(kno[:, hb_, 128 * j:128 * j + kr], qno[:, hb_, q0:q0 + TB]),
                                            (kpe[:, 128 * j:128 * j + kr], qpe[:, hb_, q0:q0 + TB])]))

                    def fin1(ops_, ok, dps, dk, q0=q0, h=h, b=b):
                        rd_ = rd[b % 2]
                        self.recip(rd_[:, :], dps[:, :TB], [dk], [("c_rd", b % 2)])
                        self.tt("dve", self.oall[:, 8 + h, q0:q0 + TB], ops_[:, :TB], rd_[:, :], ALU.mult, [ok, ("c_rd", b % 2)], ["oall"])
                    jobs.append(dict(qn=TB, tiles=tl, o_M=128, scale=scale, fin1=fin1, fin2=None,
                                     v_fn=lambda t, h=h: (vtok[:t["rows"], t["j"], h * 128:(h + 1) * 128], ["c_v"]),
                                     ones_fn=lambda t: (self.ones16 if t["j"] == NT - 1 else self.onesb)[:, :],
                                     rkeys=[("c_kno", hb_), ("c_qno", hb_), "c_kpe", ("c_qpe", hb_)]))
                self.attn_stream("c", jobs, ptbuf)

    def mix_a(self, l):
        I = self.I
        lam_init = 0.8 - 0.6 * math.exp(-0.3 * l)
        with ExitStack() as es:
            vtok = self.sb(es, "a_v", [128, NT, 512], BF16)
            qT = self.sb(es, "a_q", [128, 8, T], BF16)
            kT = self.sb(es, "a_k", [128, 4, 17 * 128], BF16)
            lam = self.sb(es, "a_lam", [128, 256], F32)
            lt = self.sb(es, "a_lt", [128, 128], F32)
            lv = self.sb(es, "a_lv", [128, 4], F32)
            gsub = self.sb(es, "a_gs", [128, 1], F32)
            win = I["w_in"][l].rearrange("(kc p) n -> p kc n", p=128)
            self.dma("sp", lam[:], I["lamrep"][:, l * 256:(l + 1) * 256], (), ["a_lam"])
            self.tt("dve", lt[:, 0:64], lam[:, 0:64], lam[:, 64:128], ALU.mult, ["a_lam"], ["a_lt"])
            self.tt("dve", lt[:, 64:128], lam[:, 128:192], lam[:, 192:256], ALU.mult, ["a_lam"], ["a_lt"])
            self.P.op("dve", lambda e: e.reduce_sum(out=lv[:, 0:1], in_=lt[:, 0:64], axis=mybir.AxisListType.X), ["a_lt"], ["a_lv"])
            self.P.op("dve", lambda e: e.reduce_sum(out=lv[:, 1:2], in_=lt[:, 64:128], axis=mybir.AxisListType.X), ["a_lt"], ["a_lv"])
            self.act(lv[:, 0:2], lv[:, 0:2], AF.Exp, ["a_lv"], ["a_lv"])
            self.tt("dve", lv[:, 2:3], lv[:, 1:2], lv[:, 0:1], ALU.subtract, ["a_lv"], ["a_lv"])
            self.ts("dve", lv[:, 3:4], lv[:, 2:3], -lam_init, None, ALU.add, None, ["a_lv"], ["a_lv"])
            self.ts("dve", gsub[:, :], self.smallc[:, l * 8:l * 8 + 1], 1.0 - lam_init, None, ALU.mult, None, ["smallc"], ["a_gs"])
            self.memset("dve", qT[:], 0.0, ["a_q"])
            self.memset("dve", kT[:], 0.0, ["a_k"])
            self.memset("dve", vtok[:, NT - 1, :], 0.0, ["a_v"])
            ukeys = [("uT", kc) for kc in range(8)]
            urhs = lambda kc, b: self.U(kc, b * TB, (b + 1) * TB)
            tiles = [(128 * j, trows(j)) for j in range(NT)]
            with ExitStack() as es1:
                wqk = [self.sb(es1, "a_wqk%d" % i, [128, 8, 256], BF16) for i in range(2)]
                wv = self.sb(es1, "a_wv", [128, 8, 512], BF16)
                self.loadw(wv[:], win[:, :, OA_V:OA_V + 512], "a_wv")
                for h in range(4):
                    wq_, wqkk = wqk[h % 2], ("a_wqk", h % 2)
                    self.loadw(wq_[:, :, 0:128], win[:, :, OA_Q + h * 128:OA_Q + (h + 1) * 128], wqkk)
                    self.loadw(wq_[:, :, 128:256], win[:, :, OA_K + h * 128:OA_K + (h + 1) * 128], wqkk)
                    if h == 0:
                        self.proj_tm(wv, "a_wv", 8, 0, 512, lambda kc, t0, n: self.U(kc, t0, t0 + n), ukeys, tiles,
                                     lambda j, n, ps, pk: self.cp("act", vtok[:n, j, :], ps, [pk], ["a_v"]))

                    def qev(b, ps, pk, h=h):
                        self.cp("act", qT[0:64, 2 * h, b * TB:(b + 1) * TB], ps[0:64, :], [pk], ["a_q"])
                        self.cp("dve", qT[64:128, 2 * h + 1, b * TB:(b + 1) * TB], ps[64:128, :], [pk], ["a_q"])
                    self.proj_fm(wq_, wqkk, 8, 0, 128, urhs, ukeys, qev)
                    self.proj_fm(wq_, wqkk, 8, 128, 128, urhs, ukeys,
                                 lambda b, ps, pk, h=h: self.cp("act", kT[:, h, b * TB:(b + 1) * TB], ps, [pk], ["a_k"]))
                self.P.barrier()
            slab = [self.sb(es, "a_slab%d" % i, [128, 2, A_W], F32) for i in range(2)]
            ptbuf = [self.sb(es, "a_pt%d" % i, [128, TB], BF16) for i in range(3)]
            self.sbias = [self.sb(es, "a_sb%d" % i, [128, TB], F32) for i in range(2)]
            self.nsq = self.sb(es, "a_sq", [128, 2, TB], F32)
            self.nrs = self.sb(es, "a_rs", [128, TB], F32)
            rd = [self.sb(es, "a_rd%d" % i, [128, TB], F32) for i in range(2)]
            on = [self.sb(es, "a_on%d" % i, [128, TB], F32) for i in range(4)]
            self.ptidx = 0
            jobs = []
            for h in range(4):
                sl_ = slab[h % 2]
                slk = ("a_slab", h % 2)
                slab_loaded = [False]
                for b in range(NB):
                    q0 = b * TB
                    for m in range(2):
                        mh = m * 4 + h
                        tl = []
                        for j in range(NT):
                            kr = 128
                            o = TB * b - 128 * j
                            if 127 - o <= -91:
                                bias = ("c", self.aconst[:, mh:mh + 1], ["aconst"])
                            elif -o - (TB - 1) >= 91:
                                bias = ("c", self.aconst[:, 8 + mh:9 + mh], ["aconst"])
                            else:
                                bias = ("s", sl_[:kr, m, o + A_C:o + A_C + TB], [slk])
                            tl.append(dict(rows=kr, j=j, bias=bias,
                                           mms=[(kT[:, h, 128 * j:128 * j + kr], qT[:, 2 * h + m, q0:q0 + TB])]))
                        oi = (b % 2) * 2 + m

                        def fin1(ops_, ok, dps, dk, oi=oi):
                            rd_ = rd[oi % 2]
                            self.recip(rd_[:, :], dps[:, :TB], [dk], [("a_rd", oi % 2)])
                            self.tt("dve", on[oi][:, :], ops_[:, :TB], rd_[:, :], ALU.mult, [ok, ("a_rd", oi % 2)], [("a_on", oi)])

                        def fin2(b=b, h=h, q0=q0):
                            o0, o1 = (b % 2) * 2, (b % 2) * 2 + 1
                            self.stt("dve", on[o0][:, :], on[o1][:, :], lv[:, 3:4], on[o0][:, :], ALU.mult, ALU.add,
                                     [("a_on", o0), ("a_on", o1), "a_lv"], [("a_on", o0)])
                            self.rstd_from([on[o0][:, :]], TB, 128, [("a_on", o0)], self.nrs[:, :], "nrs", self.nsq, "nsq")
                            self.stt("dve", self.oall[:, h, q0:q0 + TB], on[o0][:, :], gsub[:, 0:1], self.nrs[:, :], ALU.mult, ALU.mult,
                                     [("a_on", o0), "nrs", "a_gs"], ["oall"])
                        jobs.append(dict(qn=TB, tiles=tl, o_M=128, scale=0.125, fin1=fin1, fin2=(fin2 if m == 1 else None),
                                         v_fn=lambda t, h=h: (vtok[:t["rows"], t["j"], h * 128:(h + 1) * 128], ["a_v"]),
                                         ones_fn=lambda t: (self.ones16 if t["j"] == NT - 1 else self.onesb)[:, :], rkeys=["a_q", "a_k"],
                                         pre=((lambda h=h: [self.dma("sp", slab[h % 2][:, mm_, :], self.I["slabA"][mm_ * 4 + h], (), [("a_slab", h % 2)]) for mm_ in range(2)])
                                              if (b == 0 and m == 0) else None)))
            self.attn_stream("a", jobs, ptbuf)

    def mix_d(self, l):
        I = self.I
        with ExitStack() as es:
            wq = self.sb(es, "d_wq", [128, 8, 512], BF16)
            wkk = self.sb(es, "d_wkk", [128, 8, 2, 128], BF16)
            wv = self.sb(es, "d_wv", [128, 8, 128], BF16)
            qT = self.sb(es, "d_q", [128, 8, T], BF16)
            kT2 = self.sb(es, "d_k", [128, 2, T], BF16)
            vpad = self.sb(es, "d_v", [128, NT * 4, 128], BF16)
            slab = [self.sb(es, "d_slab%d" % i, [128, D_SLABW], F32) for i in range(4)]
            ptbuf = [self.sb(es, "d_pt%d" % i, [128, 256], BF16) for i in range(3)]
            self.sbias = [self.sb(es, "d_sb%d" % i, [128, 256], F32) for i in range(2)]
            oh = self.sb(es, "d_oh", [128, 2, 128], BF16)
            es8 = self.sb(es, "d_es8", [128, 8], F32)
            es2 = self.sb(es, "d_es2", [128, 4], F32)
            rd = [self.sb(es, "d_rd%d" % i, [128, 256], F32) for i in range(2)]
            win = I["w_in"][l].rearrange("(kc p) n -> p kc n", p=128)
            self.loadw(wq[:], win[:, :, OD_Q:OD_Q + 512], "d_wq")
            for kv in range(2):
                for e in range(2):
                    self.loadw(wkk[:, :, kv, e * 64:(e + 1) * 64], win[:, :, OD_K + kv * 64:OD_K + (kv + 1) * 64], "d_wkk")
            self.loadw(wv[:], win[:, :, OD_V:OD_V + 128], "d_wv")
            self.memset("dve", vpad[:], 0.0, ["d_v"])
            self.memset("dve", oh[:], 0.0, ["d_oh"])
            self.memset("dve", oh[:, 0, 0:64], 1.0, ["d_oh"])
            self.memset("dve", oh[:, 1, 64:128], 1.0, ["d_oh"])
            self.dma("sp", es8[:], I["sinkrep"][:, l * 8:(l + 1) * 8], (), ["d_es8"])
            self.act(es8[:], es8[:], AF.Exp, ["d_es8"], ["d_es8"])
            for p in range(4):
                self.cp("dve", es2[0:64, p:p + 1], es8[0:64, 2 * p:2 * p + 1], ["d_es8"], ["d_es2"])
                self.cp("dve", es2[64:128, p:p + 1], es8[64:128, 2 * p + 1:2 * p + 2], ["d_es8"], ["d_es2"])
            ukeys = [("uT", kc) for kc in range(8)]
            urhs = lambda kc, b: self.U(kc, b * TB, (b + 1) * TB)
            self.memset("dve", qT[:], 0.0, ["d_q"])

            def qev(b, ps, pk, p):
                self.cp("act", qT[0:64, 2 * p, b * TB:(b + 1) * TB], ps[0:64, :], [pk], ["d_q"])
                self.cp("act", qT[64:128, 2 * p + 1, b * TB:(b + 1) * TB], ps[64:128, :], [pk], ["d_q"])
            for p in range(4):
                self.proj_fm(wq, "d_wq", 8, p * 128, 128, urhs, ukeys, lambda b, ps, pk, p=p: qev(b, ps, pk, p))
            for kv in range(2):
                self.proj_fm(wkk[:, :, kv, :], "d_wkk", 8, 0, 128, urhs, ukeys,
                             lambda b, ps, pk, kv=kv: self.cp("act", kT2[:, kv, b * TB:(b + 1) * TB], ps, [pk], ["d_k"]))
            tiles = [(128 * j, trows(j)) for j in range(NT)]

            def vev(j, n, ps, pk):
                for kv in range(2):
                    self.cp("act", vpad[:n, j * 4 + kv * 2, 0:64], ps[:, kv * 64:(kv + 1) * 64], [pk], ["d_v"])
                    self.cp("dve", vpad[:n, j * 4 + kv * 2 + 1, 64:128], ps[:, kv * 64:(kv + 1) * 64], [pk], ["d_v"])
            self.proj_tm(wv, "d_wv", 8, 0, 128, lambda kc, t0, n: self.U(kc, t0, t0 + n), ukeys, tiles, vev)
            self.ptidx = 0
            jobs = []
            for p in range(4):
                kv = p // 2
                sls = [slab[(p % 2) * 2 + e] for e in range(2)]
                slks = [("d_slab", (p % 2) * 2 + e) for e in range(2)]
                pre_p = (lambda p=p, sls=sls, slks=slks: [self.dma("sp", sls[e][:], I["slabD"][2 * p + e], (), [slks[e]]) for e in range(2)])
                for qb, (q0, qn) in enumerate(D_QB):
                    tl = []
                    for e in range(2):
                        h = 2 * p + e
                        pr = slice(64 * e, 64 * e + 64)
                        if qb == 0:
                            mb = ("s", sls[e][:16, D_W + 256:D_W + 256 + qn], [slks[e]])
                        else:
                            mb = ("c", self.dconst[:, h:h + 1], ["dconst"])
                        tl.append(dict(rows=16, j=0, e=e, bias=mb, mms=[(kT2[:, kv, 0:16], qT[:, h, q0:q0 + qn])]))
                        for j in range(max(0, q0 // 128 - 1), min(16, (q0 + qn + 127) // 128) + 1):
                            kr = trows(j)
                            o = q0 - 128 * j
                            if j == 0:
                                bs = sls[e][:kr, D_W:D_W + qn]
                            else:
                                bs = sls[e][:kr, o + D_C:o + D_C + qn]
                            tl.append(dict(rows=kr, j=j, e=e, bias=("s", bs, [slks[e]]),
                                           mms=[(kT2[:, kv, 128 * j:128 * j + kr], qT[:, h, q0:q0 + qn])]))

                    def fin1(ops_, ok, dps, dk, p=p, q0=q0, qn=qn, qb=qb):
                        rd_ = rd[qb % 2]
                        rk = ("d_rd", qb % 2)
                        self.ts("dve", rd_[:, :qn], dps[:, :qn], es2[:, p:p + 1], None, ALU.add, None, [dk, "d_es2"], [rk])
                        self.recip(rd_[:, :qn], rd_[:, :qn], [rk], [rk])
                        self.tt("dve", self.oall[:, 12 + p, q0:q0 + qn], ops_[:, :qn], rd_[:, :qn], ALU.mult, [ok, rk], ["oall"])
                    jobs.append(dict(qn=qn, tiles=tl, o_M=128, scale=0.125, fin1=fin1, fin2=None,
                                     v_fn=lambda t, kv=kv: (vpad[:t["rows"], t["j"] * 4 + kv * 2 + t["e"], :], ["d_v"]),
                                     ones_fn=lambda t: oh[:t["rows"], t["e"], :], okeys=["d_oh"], rkeys=["d_q", "d_k"],
                                     pre=(pre_p if qb == 0 else None)))
            self.attn_stream("d", jobs, ptbuf)

    def mix_b(self, l):
        I = self.I
        with ExitStack() as es:
            wb = self.sb(es, "b_w", [128, 8, 1568], BF16)
            wgu = self.sb(es, "b_wgu", [16, 2, 256], F32)
            gb = self.sb(es, "b_gb", [128, 512], F32)
            msk = self.sb(es, "b_msk", [128, 4, 128], F32)
            obw = self.sb(es, "b_obw", [128, 4, T], F32)
            S = self.sb(es, "b_S", [64, 4, 128], F32)
            Sbf = self.sb(es, "b_Sbf", [64, 4, 128], BF16)
            self.nsq = self.sb(es, "b_sq", [128, 2, 64], F32)
            self.nrs = self.sb(es, "b_rs", [128, 64], F32)
            NBUF = 2
            bufs = {}

            def tb(name, shape, dt, i):
                k = (name, i % NBUF)
                if k not in bufs:
                    bufs[k] = self.sb(es, "b_%s%d" % (name, i % NBUF), shape, dt)
                return bufs[k], ("b_" + name, i % NBUF)

            win = I["w_in"][l].rearrange("(kc p) n -> p kc n", p=128)
            self.loadw(wb[:], win[:, :, OB_Q:OB_Q + 1568], "b_w")
            self.dma("sp", wgu[:], I["gla_gate_up"][l].rearrange("g r c -> r g c"), (), ["b_wgu"])
            self.dma("sp", gb[:], I["gbias"][:, l * 512:(l + 1) * 512], (), ["b_gb"])
            self.dma("sp", msk[:], I["glam"].rearrange("p (m t) -> p m t", m=4), (), ["b_msk"])
            ukeys = [("uT", kc) for kc in range(8)]
            gch = [(0, 16)] + [(16 + 64 * (c - 1), 64) for c in range(1, 33)]
            seq = [(1, ci) for ci in range(32, -1, -1)] + [(0, ci) for ci in range(33)]
            NS = len(seq)
            cxs = {}

            def ctx(i):
                if i not in cxs:
                    dr, ci = seq[i]
                    t0, n = gch[ci]
                    cxs[i] = dict(i=i, dr=dr, ci=ci, t0=t0, n=n, mi_c=(0 if dr == 0 else 1), mi_r=(2 if dr == 0 else 3))
                return cxs[i]

            def P1(cx):
                i, dr, t0, n = cx["i"], cx["dr"], cx["t0"], cx["n"]
                ut = lambda kc: self.U(kc, t0, t0 + n)
                pgl, pglk = self.psb("x")
                for kc in range(8):
                    self.mm(pgl[:16, :n], wb[:, kc, 1536 + 16 * dr:1552 + 16 * dr], ut(kc), kc == 0, kc == 7, ["b_w"] + ukeys, [pglk])
                cx["glT"], cx["glk"] = tb3("glT", [16, 64], F32, i)
                self.cp("act", cx["glT"][:, :n], pgl[:16, :n], [pglk], [cx["glk"]])
                pvt, pvtk = self.psb("x")
                for kc in range(8):
                    self.mm(pvt[:n, :512], ut(kc), wb[:, kc, 512:1024], kc == 0, kc == 7, ["b_w"] + ukeys, [pvtk])
                cx["vt"], cx["vtk"] = tb3("vt", [64, 512], BF16, i)
                self.cp("act", cx["vt"][:n, :], pvt[:n, :512], [pvtk], [cx["vtk"]])
                if dr == 0:
                    prr, prrk = self.psb("x")
                    for h in range(4):
                        for kc in range(8):
                            self.mm(prr[:, h * 64:h * 64 + n], wb[:, kc, 1024 + h * 128:1024 + (h + 1) * 128], ut(kc), kc == 0, kc == 7, ["b_w"] + ukeys, [prrk])
                    k4 = ("sr", i % 4)
                    if k4 not in bufs:
                        bufs[k4] = self.sb(es, "b_sr%d" % (i % 4), [128, 4, 64], F32)
                    cx["sr"], cx["srk"] = bufs[k4], ("b_sr", i % 4)
                    r3 = prr[:, 0:256].rearrange("p (h t) -> p h t", h=4)[:, :, :n]
                    sr_ = cx["sr"]
                    self.act(sr_[:, :, :n], r3, AF.Exp, [prrk], [cx["srk"]], scale=-1.0)
                    self.ts("dve", sr_[:, :, :n], sr_[:, :, :n], 1.0, None, ALU.add, None, [cx["srk"]], [cx["srk"]])
                    self.recip(sr_[:, :, :n], sr_[:, :, :n], [cx["srk"]], [cx["srk"]])
                    self.tt("dve", sr_[:, :, :n], sr_[:, :, :n], r3, ALU.mult, [cx["srk"], prrk], [cx["srk"]])

            def P2a(cx):
                i, dr, n = cx["i"], cx["dr"], cx["n"]
                ppre, pprek = self.psb("x")
                self.mm(ppre[:n, :256], cx["glT"][:, :n], wgu[:, dr, :], True, True, [cx["glk"], "b_wgu"], [pprek])
                xla, xlk = tb("xla", [64, 256], F32, i)
                self.tt("dve", xla[:n, :], ppre[:n, :256], gb[:n, dr * 256:(dr + 1) * 256], ALU.add, [pprek, "b_gb"], [xlk])
                self.act(xla[:n, :], xla[:n, :], AF.Exp, [xlk], [xlk], scale=-1.0)
                cx["sp"], cx["spk"] = tb("sp", [64, 256], F32, i)
                self.act(cx["sp"][:n, :], xla[:n, :], AF.Ln, [xlk], [cx["spk"]], bias=1.0)

            def P2b(cx):
                i, dr, t0, n = cx["i"], cx["dr"], cx["t0"], cx["n"]
                sp_, spk = cx["sp"], cx["spk"]
                ut = lambda kc: self.U(kc, t0, t0 + n)
                pqk, pqkk = self.psb("x")
                for qi in range(8):
                    for kc in range(8):
                        self.mm(pqk[:64, qi * 64:qi * 64 + n], wb[:, kc, qi * 64:(qi + 1) * 64], ut(kc), kc == 0, kc == 7, ["b_w"] + ukeys, [pqkk])
                pkt, pktk = self.psb("x")
                for kc in range(8):
                    self.mm(pkt[:n, :256], ut(kc), wb[:, kc, 256:512], kc == 0, kc == 7, ["b_w"] + ukeys, [pktk])
                pc, pck = self.psb("x")
                for h in range(4):
                    self.mm(pc[:64, h * 64:h * 64 + n], sp_[:n, h * 64:(h + 1) * 64], msk[:n, cx["mi_c"], :n], True, True, [spk, "b_msk"], [pck])
                pr_, prk = self.psb("x")
                self.mm(pr_[:n, :256], msk[:n, cx["mi_r"], :n], sp_[:n, :], True, True, [spk, "b_msk"], [prk])
                cx["eb"], cx["ebk"] = tb("eb", [64, 4, 64], F32, i)
                einv, eik = tb("einv", [64, 4, 64], F32, i)
                pc3 = pc[:64, 0:256].rearrange("p (h t) -> p h t", h=4)[:, :, :n]
                self.act(cx["eb"][:, :, :n], pc3, AF.Exp, [pck], [cx["ebk"]], scale=-1.0 / 16)
                self.act(einv[:, :, :n], pc3, AF.Exp, [pck], [eik], scale=1.0 / 16)
                eo, eok = tb("eo", [64, 256], F32, i)
                self.act(eo[:n, :], pr_[:n, :256], AF.Exp, [prk], [eok], scale=-1.0 / 16)
                cx["qd"], cx["qdk"] = tb("qd", [64, 4, 64], BF16, i)
                cx["ki"], cx["kik"] = tb("ki", [64, 4, 64], BF16, i)
                q3 = pqk[:64, 0:256].rearrange("p (h t) -> p h t", h=4)[:, :, :n]
                k3 = pqk[:64, 256:512].rearrange("p (h t) -> p h t", h=4)[:, :, :n]
                self.stt("dve", cx["qd"][:, :, :n], q3, 0.125, cx["eb"][:, :, :n], ALU.mult, ALU.mult, [pqkk, cx["ebk"]], [cx["qdk"]])
                self.tt("dve", cx["ki"][:, :, :n], k3, einv[:, :, :n], ALU.mult, [pqkk, eik], [cx["kik"]])
                cx["ko"], cx["kok"] = tb("ko", [64, 256], BF16, i)
                self.tt("dve", cx["ko"][:n, :], pkt[:n, :256], eo[:n, :], ALU.mult, [pktk, eok], [cx["kok"]])

            def P3a(cx):
                i, n = cx["i"], cx["n"]
                pat, patk = self.psb("x")
                for h in range(4):
                    self.mm(pat[:n, h * 64:h * 64 + n], cx["ki"][:, h, :n], cx["qd"][:, h, :n], True, True, [cx["kik"], cx["qdk"]], [patk])
                cx["att"], cx["atk"] = tb("att", [64, 4, 64], BF16, i)
                for h in range(4):
                    self.tt("dve", cx["att"][:n, h, :n], pat[:n, h * 64:h * 64 + n], msk[:n, cx["mi_c"], :n], ALU.mult, [patk, "b_msk"], [cx["atk"]])
                pds, pdsk = self.psb("x")
                for h in range(4):
                    self.mm(pds[:64, h * 128:(h + 1) * 128], cx["ko"][:n, h * 64:(h + 1) * 64], cx["vt"][:n, h * 128:(h + 1) * 128], True, True, [cx["kok"], cx["vtk"]], [pdsk])
                cx["pds"], cx["pdsk"] = pds, pdsk

            def P3b(cx):
                i, dr, t0, n = cx["i"], cx["dr"], cx["t0"], cx["n"]
                vt, vtk, att, atk, qd, qdk = cx["vt"], cx["vtk"], cx["att"], cx["atk"], cx["qd"], cx["qdk"]
                if i == 0 or seq[i][0] != seq[i - 1][0]:
                    self.memset("dve", S[:], 0.0, ["b_S"])
                    self.memset("dve", Sbf[:], 0.0, ["b_Sbf"])
                po, pok = self.psb("x")
                for h in range(4):
                    self.mm(po[:, h * 64:h * 64 + n], vt[:n, h * 128:(h + 1) * 128], att[:n, h, :n], True, False, [vtk, atk], [pok])
                    self.mm(po[:, h * 64:h * 64 + n], Sbf[:, h, :], qd[:, h, :n], False, True, ["b_Sbf", qdk], [pok])
                dcol = (n - 1) if dr == 0 else 0
                pds, pdsk = cx["pds"], cx["pdsk"]
                for h in range(4):
                    self.stt("dve", S[:, h, :], S[:, h, :], cx["eb"][:, h, dcol:dcol + 1], pds[:64, h * 128:(h + 1) * 128], ALU.mult, ALU.add, ["b_S", cx["ebk"], pdsk], ["b_S"])
                self.cp("act", Sbf[:], S[:], ["b_S"], ["b_Sbf"])
                if dr == 1:
                    self.cp("act", obw[:, :, t0:t0 + n], po[:, 0:256].rearrange("p (h t) -> p h t", h=4)[:, :, :n], [pok], ["b_obw"])
                else:
                    of, ofk = tb("of", [128, 4, 64], F32, i)
                    sq4, sqk = tb("sq4", [128, 4, 64], F32, i)
                    if ("of", i % NBUF) not in init_done:
                        init_done.add(("of", i % NBUF))
                        self.memset("dve", of[:], 0.0, [ofk])
                    self.tt("dve", of[:, :, :n], po[:, 0:256].rearrange("p (h t) -> p h t", h=4)[:, :, :n], obw[:, :, t0:t0 + n], ALU.add, [pok, "b_obw"], [ofk])
                    self.tt("pool", sq4[:], of[:], of[:], ALU.mult, [ofk], [sqk])
                    cx["of"], cx["ofk"], cx["sq4"], cx["sqk"] = of, ofk, sq4, sqk
                    pend3.append(cx)
                del cxs[i]

            def P3c(cx):
                i, t0, n = cx["i"], cx["t0"], cx["n"]
                of, ofk = cx["of"], cx["ofk"]
                pn, pnk = self.psb("x")
                self.mm(pn[:, 0:256], self.onesf[:], cx["sq4"][:].rearrange("p h t -> p (h t)"), True, True, [cx["sqk"], "onesf"], [pnk])
                rs4, rsk = tb("rs4", [128, 4, 64], F32, i)
                rs2 = rs4[:].rearrange("p h t -> p (h t)")
                self.ts("dve", rs2, pn[:, 0:256], 1.0 / 128, EPS, ALU.mult, ALU.add, [pnk], [rsk])
                self.act(rs2, rs2, AF.Ln, [rsk], [rsk])
                self.act(rs2, rs2, AF.Exp, [rsk], [rsk], scale=-0.5)
                self.stt("dve", of[:, :, :n], of[:, :, :n], self.smallc[:, l * 8 + 1:l * 8 + 2], rs4[:, :, :n], ALU.mult, ALU.mult, [ofk, rsk, "smallc"], [ofk])
                self.tt("dve", self.oall[:, 4:8, t0:t0 + n], of[:, :, :n], cx["sr"][:, :, :n], ALU.mult, [ofk, cx["srk"]], ["oall"])

            init_done = set()
            pend3 = []

            def tb3(name, shape, dt, i):
                k = (name, i % 3)
                if k not in bufs:
                    bufs[k] = self.sb(es, "b_%s%d" % (name, i % 3), shape, dt)
                return bufs[k], ("b_" + name, i % 3)

            P1(ctx(0))
            P1(ctx(1))
            P2a(ctx(0))
            P2b(ctx(0))
            for i in range(NS):
                if i + 2 < NS:
                    P1(ctx(i + 2))
                if i + 1 < NS:
                    P2a(ctx(i + 1))
                P3a(ctx(i))
                while pend3:
                    P3c(pend3.pop(0))
                if i + 1 < NS:
                    P2b(ctx(i + 1))
                P3b(ctx(i))
            while pend3:
                P3c(pend3.pop(0))

    def resid_block(self, y, ykeys, b, l, gw, hsrc, hdst, bufs, nextnorm=None, final=False):
        hb, hk = bufs
        hv_s = hsrc.rearrange("(c p) t -> p c t", p=128)
        hv_d = hdst.rearrange("(c p) t -> p c t", p=128)
        sl = slice(b * TB, (b + 1) * TB)
        self.dma("sp", hb[:], hv_s[:, :, sl], (), [hk])
        self.rstd_from([y[:, kc, :] for kc in range(8)], TB, D, ykeys, self.nrs[:, :], "nrs", self.nsq, "nsq")
        for kc in range(8):
            self.stt("dve", y[:, kc, :], y[:, kc, :], self.gcol(l, gw, kc), self.nrs[:, :], ALU.mult, ALU.mult,
                     ykeys + ["nrs", "gains"], ykeys)
        self.tt("dve", hb[:], hb[:], y[:], ALU.add, [hk] + ykeys, [hk])
        st = self.dma("sp", hv_d[:, :, sl], hb[:], [hk], [("hdram", b)])
        if nextnorm is not None:
            self.norm_block(hb, hk, b, nextnorm[0], nextnorm[1], "nn")
        return st

    def merge_out(self, l, hsrc, hT, les):
        I = self.I
        with ExitStack() as es:
            merged = self.sb(es, "m_merged", [128, 8, T], BF16)
            wo = self.sb(es, "o_w", [128, 8, D], BF16)
            with ExitStack() as es2:
                wbr = [self.sb(es2, "m_wbr%d" % i, [128, 16, 128], BF16) for i in range(2)]
                wg = [self.sb(es2, "m_wg%d" % i, [128, 8, 4, 128], BF16) for i in range(2)]
                sig = [self.sb(es2, "m_sig%d" % i, [128, TB], F32) for i in range(2)]
                acc = self.sb(es2, "m_acc", [128, TB], F32)
                prod = self.sb(es2, "m_prod", [128, TB], F32)
                ukeys = [("uT", kc) for kc in range(8)]
                def ldm(dc):
                    wb_, wbk = wbr[dc % 2], ("m_wbr", dc % 2)
                    wg_, wgk = wg[dc % 2], ("m_wg", dc % 2)
                    self.loadw(wb_[:], I["w_branch"][l].rearrange("n (ec p) d -> p (n ec) d", p=128)[:, :, dc * 128:(dc + 1) * 128], wbk)
                    for br in range(4):
                        self.loadw(wg_[:, :, br, :], I["w_in"][l].rearrange("(kc p) n -> p kc n", p=128)
                                   [:, :, O_GATE + br * 1024 + dc * 128:O_GATE + br * 1024 + (dc + 1) * 128], wgk)
                ldm(0)
                for dc in range(8):
                    wb_, wbk = wbr[dc % 2], ("m_wbr", dc % 2)
                    wg_, wgk = wg[dc % 2], ("m_wg", dc % 2)
                    if dc + 1 < 8:
                        ldm(dc + 1)
                    self.loadw(wo[:, dc, :], I["w_out"][l][dc * 128:(dc + 1) * 128, :], "o_w")
                    for b in range(NB):
                        sl = slice(b * TB, (b + 1) * TB)
                        for br in range(4):
                            pg, pgk = self.psb("x")
                            for kc in range(8):
                                self.mm(pg[:, :TB], wg_[:, kc, br, :], self.U(kc, b * TB, (b + 1) * TB), kc == 0, kc == 7, [wgk] + ukeys, [pgk])
                            pp, ppk = self.psb("x")
                            for ec in range(4):
                                self.mm(pp[:, :TB], wb_[:, br * 4 + ec, :], self.oall[:, br * 4 + ec, sl], ec == 0, ec == 3, [wbk, "oall"], [ppk])
                            sg, sgk = sig[br % 2], ("m_sig", br % 2)
                            self.act(sg[:, :], pg[:, :TB], AF.Sigmoid, [pgk], [sgk])
                            if br == 0:
                                self.tt("dve", acc[:, :], pp[:, :TB], sg[:, :], ALU.mult, [ppk, sgk], ["m_acc"])
                            else:
                                self.tt("dve", prod[:, :], pp[:, :TB], sg[:, :], ALU.mult, [ppk, sgk], ["m_prod"])
                                if br < 3:
                                    self.tt("dve", acc[:, :], acc[:, :], prod[:, :], ALU.add, ["m_acc", "m_prod"], ["m_acc"])
                                else:
                                    self.tt("dve", merged[:, dc, sl], acc[:, :], prod[:, :], ALU.add, ["m_acc", "m_prod"], ["m_merged"])
                self.P.barrier()
            if "merged" in self.dbg_out and l == 0:
                with ExitStack() as es3:
                    tmp = self.sb(es3, "dbgtmp2", [128, T], F32)
                    for c in range(8):
                        self.cp("dve", tmp[:], merged[:, c, :], ["m_merged"], ["dbgtmp2"])
                        self.dma("sp", self.dbg_out["merged"][c * 128:(c + 1) * 128, :], tmp[:], ["dbgtmp2"], ())
                    self.P.barrier()
            with ExitStack() as es2:
                y2 = [self.sb(es2, "o_y%d" % i, [128, 8, TB], F32) for i in range(2)]
                hb2 = [self.sb(es2, "o_h%d" % i, [128, 8, TB], F32) for i in range(2)]
                self.nsq = self.sb(es2, "o_sq", [128, 2, TB], F32)
                self.nrs = self.sb(es2, "o_rs", [128, TB], F32)
                for b in range(NB):
                    y, yk = y2[b % 2], ("o_y", b % 2)
                    sl = slice(b * TB, (b + 1) * TB)
                    for dc in range(8):
                        pst, pk = self.psb("x")
                        for kc in range(8):
                            self.mm(pst[:, :TB], wo[:, kc, dc * 128:(dc + 1) * 128], merged[:, kc, sl], kc == 0, kc == 7, ["o_w", "m_merged"], [pk])
                        self.cp("act", y[:, dc, :], pst[:, :TB], [pk], [yk])
                    self.resid_block(y, [yk], b, l, 1, hsrc, hT, (hb2[b % 2], ("o_h", b % 2)), nextnorm=(l, 2))
                self.P.barrier()

    def ffn(self, l, hT, hdst):
        I = self.I
        finals = []
        NJ = DFF // 128
        wd_es = ExitStack()
        wd = self.sb(wd_es, "f_wd", [128, NJ, D], BF16)
        wd_loaded = [False]
        for half in range(2):
            with ExitStack() as es:
                actT = self.sb(es, "f_act", [128, NJ, 3 * TB], BF16)
                with ExitStack() as es2:
                    wu = [self.sb(es2, "f_wu%d" % i, [128, 8, 2, 128], BF16) for i in range(2)]
                    cgs = [self.sb(es2, "f_cg%d" % i, [128, TB], F32) for i in range(2)]
                    cvs = [self.sb(es2, "f_cv%d" % i, [128, TB], F32) for i in range(2)]
                    t1s = [self.sb(es2, "f_t1%d" % i, [128, TB], F32) for i in range(2)]
                    wup = I["ffn_w_up"][l].rearrange("(kc p) n -> p kc n", p=128)
                    ukeys = [("uT", kc) for kc in range(8)]
                    cw = self.convw
                    it = 0
                    def ldw(j):
                        w_, wk = wu[j % 2], ("f_wu", j % 2)
                        self.loadw(w_[:, :, 0, :], wup[:, :, j * 128:(j + 1) * 128], wk)
                        self.loadw(w_[:, :, 1, :], wup[:, :, DFF + j * 128:DFF + (j + 1) * 128], wk)
                    ldw(0)
                    for j in range(NJ):
                        w_, wk = wu[j % 2], ("f_wu", j % 2)
                        if j + 1 < NJ:
                            ldw(j + 1)
                        if half == 0:
                            self.loadw(wd[:, j, :], I["ffn_w_down"][l][j * 128:(j + 1) * 128, :], "f_wd")
                        for b in range(3 * half, 3 * half + 3):
                            it += 1
                            cg, cv, t1 = cgs[it % 2], cvs[it % 2], t1s[it % 2]
                            cgk, cvk, t1k = ("f_cg", it % 2), ("f_cv", it % 2), ("f_t1", it % 2)
                            taps = []
                            for gv in range(2):
                                pst, pk = self.psb("x")
                                for kc in range(8):
                                    self.mm(pst[:, :TB + 2], w_[:, kc, gv, :], self.uT[:, kc, b * TB:b * TB + TB + 2], kc == 0, kc == 7, [wk] + ukeys, [pk])
                                ch = gv * NJ + j
                                base = (l * 4) * 44
                                c0 = cw[:, base + ch:base + ch + 1]
                                c1 = cw[:, base + 44 + ch:base + 44 + ch + 1]
                                c2 = cw[:, base + 88 + ch:base + 88 + ch + 1]
                                cb = cw[:, base + 132 + ch:base + 132 + ch + 1]
                                dst, dk = (cg, cgk) if gv == 0 else (cv, cvk)
                                self.act(dst[:, :], pst[:, 0:TB], AF.Identity, [pk, "convw"], [dk], bias=cb, scale=c0)
                                taps.append((dst, dk, pst, pk, c1, c2))
                            for (dst, dk, pst, pk, c1, c2) in taps:
                                self.stt("dve", dst[:, :], pst[:, 1:TB + 1], c1, dst[:, :], ALU.mult, ALU.add, [pk, dk, "convw"], [dk])
                            for (dst, dk, pst, pk, c1, c2) in taps:
                                self.stt("dve", dst[:, :], pst[:, 2:TB + 2], c2, dst[:, :], ALU.mult, ALU.add, [pk, dk, "convw"], [dk])
                            self.act(t1[:, :], cg[:, :], AF.Gelu_apprx_tanh, [cgk], [t1k])
                            self.tt("pool", actT[:, j, (b - 3 * half) * TB:(b - 3 * half + 1) * TB], t1[:, :], cv[:, :], ALU.mult, [t1k, cvk], ["f_act"])
                    self.P.barrier()
                with ExitStack() as es2:
                    y2 = [self.sb(es2, "f_y%d" % i, [128, 8, TB], F32) for i in range(2)]
                    hb2 = [self.sb(es2, "f_h%d" % i, [128, 8, TB], F32) for i in range(2)]
                    self.nsq = self.sb(es2, "f_sq", [128, 2, TB], F32)
                    self.nrs = self.sb(es2, "f_rs", [128, TB], F32)
                    for b in range(3 * half, 3 * half + 3):
                        y, yk = y2[b % 2], ("f_y", b % 2)
                        sl = slice(b * TB, (b + 1) * TB)
                        for dc in range(8):
                            pst, pk = self.psb("x")
                            for j in range(NJ):
                                self.mm(pst[:, :TB], wd[:, j, dc * 128:(dc + 1) * 128], actT[:, j, (b - 3 * half) * TB:(b - 3 * half + 1) * TB], j == 0, j == NJ - 1, ["f_wd", "f_act"], [pk])
                            self.cp("act", y[:, dc, :], pst[:, :TB], [pk], [yk])
                        st = self.resid_block(y, [yk], b, l, 3, hT, hdst, (hb2[b % 2], ("f_h", b % 2)))
                        finals.append(st)
                    self.P.barrier()
        wd_es.close()
        return finals


def host_consts(inp):
    f32 = np.float32
    c = {}
    tab = np.asarray(inp["rel_bias_table"], f32)
    kk = np.arange(128)[:, None]
    jj = np.arange(A_W)[None, :]
    bk = rel_bucket_jax(kk - jj + A_C)
    c["slabA"] = np.ascontiguousarray(np.transpose(tab[bk][:, :, 0:8], (2, 0, 1))).astype(f32)
    ac = np.concatenate([tab[15, 0:8], tab[31, 0:8]])
    c["aconst"] = np.ascontiguousarray(np.broadcast_to(ac[None, :], (128, 16))).astype(f32)
    tabd = tab[:, 8:16]
    jj = np.arange(D_W)[None, :]
    rel = kk - jj + D_C
    tz = np.where((np.abs(rel) <= 128)[:, :, None], tabd[rel_bucket_jax(rel)], f32(NEG))
    qq = np.arange(256)[None, :]
    rel0 = kk - qq
    t0 = np.where(((np.abs(rel0) <= 128) & (kk >= NMETA))[:, :, None], tabd[rel_bucket_jax(rel0)], f32(NEG))
    tm = np.where((kk < NMETA)[:, :, None], tabd[rel_bucket_jax(rel0)], f32(NEG))
    c["slabD"] = np.ascontiguousarray(np.transpose(np.concatenate([tz, t0, tm], axis=1), (2, 0, 1))).astype(f32)
    c["dconst"] = np.ascontiguousarray(np.broadcast_to(tabd[15][None, :], (128, 8))).astype(f32)
    half = 32
    inv = (10000.0 ** (-np.arange(half, dtype=np.float32) / half)).astype(f32)
    ang = np.arange(T, dtype=f32)[None, :] * inv[:, None]
    cos, sin = np.cos(ang).astype(f32), np.sin(ang).astype(f32)
    c["rope"] = np.ascontiguousarray(np.concatenate([np.concatenate([cos, cos], 0), np.concatenate([-sin, sin], 0)], 1)).astype(f32)
    s = np.arange(128)[:, None]
    t = np.arange(128)[None, :]
    same = (s // 64) == (t // 64)
    LT = (same & (s <= t)).astype(f32)
    L = (same & (s >= t)).astype(f32)
    SU = (same & (s > t)).astype(f32)
    SL = (same & (s < t)).astype(f32)
    c["glam"] = np.ascontiguousarray(np.concatenate([LT, L, SU, SL], 1))
    return c


def host_layout(inp):
    f32 = np.float32
    g = {}
    sw = np.concatenate([np.arange(32, 64), np.arange(0, 32)])
    wq = np.asarray(inp["mla_w_q_up"], f32).reshape(DEPTH, 256, 4, 192)
    g["mla_w_q_up_sw"] = np.ascontiguousarray(wq[:, :, :, 128:][:, :, :, sw].reshape(DEPTH, 256, 256))
    g["w_in_kr_sw"] = np.ascontiguousarray(np.asarray(inp["w_in"])[:, :, OC_KR:OC_KR + 64][:, :, sw])
    gains = np.stack([inp["norm_mix_pre"], inp["norm_mix_post"], inp["norm_ffn_pre"], inp["norm_ffn_post"]], 1)
    g["gains"] = np.ascontiguousarray(gains.reshape(DEPTH * 4 * 8, 128).T).astype(f32)
    cw = np.concatenate([np.asarray(inp["ffn_conv_w"], f32), np.asarray(inp["ffn_conv_b"], f32)[:, None, :]], 1)
    g["convw"] = np.ascontiguousarray(cw.reshape(DEPTH * 4 * 44, 128).T).astype(f32)
    sc = np.zeros((DEPTH, 8, 128), f32)
    sc[:, 0] = inp["diff_subln"]
    sc[:, 1] = inp["gla_norm"]
    sc[:, 2] = inp["mla_kv_norm"]
    sc[:, 3:5] = np.asarray(inp["mla_q_norm"]).reshape(DEPTH, 2, 128)
    g["smallc"] = np.ascontiguousarray(sc.reshape(DEPTH * 8, 128).T)
    g["lamrep"] = np.ascontiguousarray(np.broadcast_to(np.asarray(inp["diff_lambda"], f32).reshape(1, DEPTH * 256), (128, DEPTH * 256)))
    g["gbias"] = np.ascontiguousarray(np.broadcast_to(np.asarray(inp["gla_gate_bias"], f32).reshape(1, DEPTH * 512), (128, DEPTH * 512)))
    g["sinkrep"] = np.ascontiguousarray(np.broadcast_to(np.asarray(inp["swa_sinks"], f32).reshape(1, DEPTH * 8), (128, DEPTH * 8)))
    return g


_NC_CACHE = {}


def get_nc(layers=(0, 1), dbg=None):
    key = (tuple(layers), tuple(sorted((dbg or {}).items())))
    if key not in _NC_CACHE:
        nc = bass.Bass("TRN2", target_bir_lowering=False)
        KB(nc, dbg).build(layers)
        _NC_CACHE[key] = nc
    return _NC_CACHE[key]


def make_in_maps(inp, cores):
    shared = {}
    for k in ("w_in", "w_branch", "w_out", "ffn_w_up", "ffn_w_down", "mla_w_q_up", "mla_w_kv_up", "gla_gate_up"):
        shared[k] = np.ascontiguousarray(np.asarray(inp[k], np.float32))
    shared.update(host_layout(inp))
    shared.update(host_consts(inp))
    meta = np.asarray(inp["meta_tokens"], np.float32)
    x = np.asarray(inp["x"], np.float32)
    maps = []
    for b in cores:
        h0 = np.concatenate([meta, x[b]], axis=0)
        m = dict(shared)
        m["h0T"] = np.ascontiguousarray(h0.T)
        maps.append(m)
    return maps


def kernel(**inputs):
    nc = get_nc()
    maps = make_in_maps(inputs, list(range(8)))
    res = run_bass_kernel_spmd(nc, maps, core_ids=list(range(8)))
    out = np.stack([np.ascontiguousarray(r["outT"][:, NMETA:].T) for r in res.results], axis=0)
    return out.astype(np.float32)
```

```python
import math
import os
import numpy as np
from contextlib import ExitStack
import concourse.bass as bass
import concourse.mybir as mybir
from concourse.bass_utils import run_bass_kernel_spmd

F32 = mybir.dt.float32
BF16 = mybir.dt.bfloat16
AF = mybir.ActivationFunctionType
ALU = mybir.AluOpType

DEPTH = 2
D = 1024
SEQ = 2048
NMETA = 16
T = SEQ + NMETA
TB = 344
NB = 6
NT = 17
EPS = 1e-6
DFF = 2816
NIN = 8416
OA_Q, OA_K, OA_V = 0, 512, 1024
OB_Q, OB_K, OB_V, OB_R, OB_G = 1536, 1792, 2048, 2560, 3072
OC_QA, OC_KVA, OC_KR = 3104, 3360, 3488
OD_Q, OD_K, OD_V = 3552, 4064, 4192
O_GATE = 4320
NEG = -30000.0


def trows(j):
    return 128 if j < 16 else 16


class Dep:
    __slots__ = ("w", "r")

    def __init__(self):
        self.w = None
        self.r = []


class Op:
    __slots__ = ("eng", "fn", "deps", "ms", "val", "sem", "is_dma")

    def __init__(self, eng, fn, is_dma):
        self.eng = eng
        self.fn = fn
        self.deps = []
        self.ms = False
        self.val = 0
        self.sem = None
        self.is_dma = is_dma


ENGS = ("pe", "act", "dve", "pool", "sp")
NDMASEM = 8


class Prog:
    def __init__(self, nc, es):
        self.nc = nc
        self.es = es
        self.ops = {e: [] for e in ENGS}
        self.dma_hist = {e: [] for e in ENGS}
        self.dd = {}

    def D(self, key):
        d = self.dd.get(key)
        if d is None:
            d = self.dd[key] = Dep()
        return d

    def op(self, eng, fn, r=(), w=(), dma=False, extra=()):
        o = Op(eng, fn, dma)
        need = list(extra)
        for k in r:
            d = self.D(k)
            if d.w is not None:
                need.append(d.w)
        for k in w:
            d = self.D(k)
            if d.w is not None:
                need.append(d.w)
            for q in d.r:
                need.append(q)
        if dma:
            h = self.dma_hist[eng]
            if len(h) >= NDMASEM:
                need.append(h[-NDMASEM])
            h.append(o)
        seen = set()
        for p in need:
            if p is o or id(p) in seen:
                continue
            seen.add(id(p))
            if (not dma) and eng == "pe" and p.eng == "pe" and not p.is_dma:
                continue
            o.deps.append(p)
        for k in r:
            lst = self.D(k).r
            if not dma:
                lst[:] = [q for q in lst if q.is_dma or q.eng != eng]
            lst.append(o)
        for k in w:
            d = self.D(k)
            d.w = o
            d.r = []
        self.ops[eng].append(o)
        return o

    def barrier(self):
        lasts = []
        for e in ENGS:
            cl = [o for o in self.ops[e] if not o.is_dma]
            if cl:
                lasts.append(cl[-1])
            lasts.extend(self.dma_hist[e][-NDMASEM:])
        for e in ENGS:
            if self.ops[e]:
                self.op(e, lambda eng: eng.nop(), extra=lasts)

    def finalize(self, final_ops=()):
        nc, es = self.nc, self.es
        for e in ENGS:
            for o in self.ops[e]:
                for p in o.deps:
                    p.ms = True
        esem = {e: es.enter_context(nc.semaphore("s_" + e)) for e in ENGS}
        dsem = {e: [es.enter_context(nc.semaphore("d_%s%d" % (e, i))) for i in range(NDMASEM)]
                for e in ENGS if self.dma_hist[e]}
        for e in ENGS:
            cnt = 0
            dcnt = [0] * NDMASEM
            k = 0
            for o in self.ops[e]:
                if o.is_dma:
                    s = k % NDMASEM
                    k += 1
                    dcnt[s] += 16
                    o.sem = dsem[e][s]
                    o.val = dcnt[s]
                elif o.ms:
                    cnt += 1
                    o.sem = esem[e]
                    o.val = cnt
        engobj = {"pe": "tensor", "act": "scalar", "dve": "vector", "pool": "gpsimd", "sp": "sync"}
        block = es.enter_context(nc.Block())

        def emit(e):
            def body(eng):
                known = {}
                for o in self.ops[e]:
                    wl = {}
                    for p in o.deps:
                        key = id(p.sem)
                        if known.get(key, 0) >= p.val:
                            continue
                        if key not in wl or wl[key][1] < p.val:
                            wl[key] = (p.sem, p.val)
                    for key, (s, v) in wl.items():
                        eng.wait_ge(s, v)
                        known[key] = v
                    ins = o.fn(eng)
                    if o.is_dma:
                        ins.then_inc(o.sem, 16)
                    elif o.ms:
                        ins.then_inc(o.sem, 1)
                if e == "sp":
                    for o in final_ops:
                        eng.wait_ge(o.sem, o.val)
            return body

        for e in ENGS:
            if self.ops[e] or e == "sp":
                getattr(block, engobj[e])(emit(e))


def rel_bucket(rel):
    rel = np.asarray(rel, dtype=np.int64)
    half, max_exact = 16, 8
    ret = np.where(rel > 0, half, 0)
    n = np.abs(rel)
    nf = np.maximum(n, 1).astype(np.float32)
    large = max_exact + (np.log(nf / np.float32(max_exact)) / np.float32(math.log(128 / max_exact))
                         * (half - max_exact)).astype(np.int32)
    large = np.minimum(large, half - 1)
    return ret + np.where(n < max_exact, n, large)


def rel_bucket_jax(rel):
    import jax
    import jax.numpy as jnp
    with jax.default_device(jax.devices("cpu")[0]):
        rel = jnp.asarray(np.asarray(rel, dtype=np.int32))
        half, max_exact = 16, 8
        ret = jnp.where(rel > 0, half, 0)
        n = jnp.abs(rel)
        nf = jnp.maximum(n, 1).astype(jnp.float32)
        large = max_exact + (jnp.log(nf / max_exact) / math.log(128 / max_exact) * (half - max_exact)).astype(jnp.int32)
        large = jnp.minimum(large, half - 1)
        return np.asarray(ret + jnp.where(n < max_exact, n, large))


A_OS = sorted(set(TB * b - 128 * j for b in range(NB) for j in range(NT)))
A_NEAR = [o for o in A_OS if not (127 - o <= -91 or -o - (TB - 1) >= 91)]
A_C = -min(A_NEAR)
A_W = TB + max(A_NEAR) + A_C
D_QB = [(256 * i, 256) for i in range(8)] + [(2048, 16)]
D_C = 256
D_W = 256 + 384
D_SLABW = D_W + 256 + 256


class KB:
    def __init__(self, nc, dbg=None):
        self.nc = nc
        self.dbg = dbg or {}

    def mm(self, out, lhsT, rhs, start, stop, r, w):
        return self.P.op("pe", lambda e: e.matmul(out, lhsT=lhsT, rhs=rhs, start=start, stop=stop), r, w)

    def act(self, out, in_, func, r, w, bias=None, scale=1.0, accum=None):
        def f(e):
            kw = {}
            if bias is not None:
                kw["bias"] = bias
            if accum is not None:
                kw["accum_out"] = accum
            return e.activation(out=out, in_=in_, func=func, scale=scale, **kw)
        return self.P.op("act", f, r, w)

    def stt(self, eng, out, in0, scalar, in1, op0, op1, r, w):
        nm = {"dve": "vector", "pool": "gpsimd"}[eng]
        return self.P.op(eng, lambda e: e.scalar_tensor_tensor(out=out, in0=in0, scalar=scalar, in1=in1, op0=op0, op1=op1), r, w)

    def ts(self, eng, out, in0, s1, s2, op0, op1, r, w):
        if s2 is None:
            return self.P.op(eng, lambda e: e.tensor_scalar(out=out, in0=in0, scalar1=s1, scalar2=None, op0=op0), r, w)
        return self.P.op(eng, lambda e: e.tensor_scalar(out=out, in0=in0, scalar1=s1, scalar2=s2, op0=op0, op1=op1), r, w)

    def tt(self, eng, out, in0, in1, op, r, w):
        return self.P.op(eng, lambda e: e.tensor_tensor(out=out, in0=in0, in1=in1, op=op), r, w)

    def cp(self, eng, out, in_, r, w):
        if eng == "act":
            return self.P.op("act", lambda e: e.copy(out=out, in_=in_), r, w)
        return self.P.op(eng, lambda e: e.tensor_copy(out=out, in_=in_), r, w)

    def recip(self, out, in_, r, w):
        return self.P.op("dve", lambda e: e.reciprocal(out=out, in_=in_), r, w)

    def memset(self, eng, ap, val, w):
        return self.P.op(eng, lambda e: e.memset(ap, val), (), w)

    def dma(self, q, out, in_, r, w):
        return self.P.op(q, lambda e: e.dma_start(out=out, in_=in_), r, w, dma=True)

    def sb(self, es, name, shape, dt):
        self.sbcnt = getattr(self, "sbcnt", 0) + 1
        return es.enter_context(self.nc.sbuf_tensor("sb%d_%s" % (self.sbcnt, name), shape, dt))

    def U(self, kc, t0, t1):
        return self.uT[:, kc, t0 + 1:t1 + 1]

    def psb(self, group):
        lst = self.psgroups[group]
        i = self.psidx.get(group, 0)
        self.psidx[group] = i + 1
        b = lst[i % len(lst)]
        return self.ps[b], ("ps", b)

    def rstd_from(self, srcs, n, Dn, rkeys, out_ap, out_key, sq_ap, sq_key, sq_eng="pool"):
        pst, pk = self.ps[7], ("ps", 7)
        for i, s in enumerate(srcs):
            eng_ = sq_eng if (len(srcs) < 4 or i % 3 == 2) else "dve"
            self.tt(eng_, sq_ap[:, i % 2, :n], s, s, ALU.mult, rkeys, [(sq_key, i % 2)])
            self.mm(pst[:, :n], self.onesf[:], sq_ap[:, i % 2, :n], i == 0, i == len(srcs) - 1, [(sq_key, i % 2), "onesf"], [pk])
        self.ts("dve", out_ap, pst[:, :n], 1.0 / Dn, EPS, ALU.mult, ALU.add, [pk], [out_key])
        self.act(out_ap, out_ap, AF.Ln, [out_key], [out_key])
        self.act(out_ap, out_ap, AF.Exp, [out_key], [out_key], scale=-0.5)

    def loadw(self, dst, src, wkey):
        return self.dma("pool", dst, src, (), [wkey])

    def build(self, layers=(0, 1)):
        nc = self.nc
        I = {}

        def din(name, shape):
            I[name] = nc.dram_tensor(name, list(shape), F32, kind="ExternalInput").ap()
            return I[name]

        din("h0T", [D, T])
        din("w_in", [DEPTH, D, NIN])
        din("w_branch", [DEPTH, 4, 512, D])
        din("w_out", [DEPTH, D, D])
        din("ffn_w_up", [DEPTH, D, 2 * DFF])
        din("ffn_w_down", [DEPTH, DFF, D])
        din("mla_w_q_up", [DEPTH, 256, 768])
        din("mla_w_q_up_sw", [DEPTH, 256, 256])
        din("mla_w_kv_up", [DEPTH, 128, 1024])
        din("w_in_kr_sw", [DEPTH, D, 64])
        din("gla_gate_up", [DEPTH, 2, 16, 256])
        din("gains", [128, DEPTH * 4 * 8])
        din("convw", [128, DEPTH * 4 * 44])
        din("smallc", [128, DEPTH * 8])
        din("lamrep", [128, DEPTH * 256])
        din("gbias", [128, DEPTH * 512])
        din("sinkrep", [128, DEPTH * 8])
        din("aconst", [128, 16])
        din("dconst", [128, 8])
        din("slabA", [8, 128, A_W])
        din("slabD", [8, 128, D_SLABW])
        din("rope", [64, 2 * T])
        din("glam", [128, 4 * 128])
        outT = nc.dram_tensor("outT", [D, T], F32, kind="ExternalOutput").ap()
        hT = nc.dram_tensor("hT_scr", [D, T], F32, kind="Internal").ap()
        dbg_out = {}
        for k, shp in self.dbg.items():
            dbg_out[k] = nc.dram_tensor("dbg_" + k, list(shp), F32, kind="ExternalOutput").ap()
        self.dbg_out = dbg_out
        self.I = I

        with ExitStack() as es:
            P = self.P = Prog(nc, es)
            self.ps = [es.enter_context(nc.psum_tensor("ps%d" % i, [128, 512], F32)) for i in range(8)]
            self.psgroups = {"s": [0, 1, 2], "o": [3, 4], "d": [5, 6], "x": [0, 1, 2, 3, 4, 5, 6]}
            self.psidx = {}
            self.uT = self.sb(es, "uT", [128, 8, T + 2], BF16)
            self.onesf = self.sb(es, "onesf", [128, 128], F32)
            self.onesb = self.sb(es, "onesb", [128, 128], BF16)
            self.gains = self.sb(es, "gains", [128, DEPTH * 32], F32)
            self.convw = self.sb(es, "convw", [128, DEPTH * 4 * 44], F32)
            self.smallc = self.sb(es, "smallc", [128, DEPTH * 8], F32)
            self.aconst = self.sb(es, "aconst", [128, 16], F32)
            self.dconst = self.sb(es, "dconst", [128, 8], F32)
            self.memset("dve", self.onesf[:], 1.0, ["onesf"])
            self.memset("dve", self.onesb[:], 1.0, ["onesb"])
            self.ones16 = self.sb(es, "ones16", [128, 128], BF16)
            self.memset("dve", self.ones16[:], 0.0, ["onesb"])
            self.memset("dve", self.ones16[0:16, :], 1.0, ["onesb"])
            self.memset("dve", self.uT[:, :, 0:1], 0.0, ["uT"])
            self.memset("dve", self.uT[:, :, T + 1:T + 2], 0.0, ["uT"])
            self.dma("sp", self.gains[:], I["gains"], (), ["gains"])
            self.dma("sp", self.convw[:], I["convw"], (), ["convw"])
            self.dma("sp", self.smallc[:], I["smallc"], (), ["smallc"])
            self.dma("sp", self.aconst[:], I["aconst"], (), ["aconst"])
            self.dma("sp", self.dconst[:], I["dconst"], (), ["dconst"])

            finals = []
            nl = len(layers)
            for li, l in enumerate(layers):
                hsrc = I["h0T"] if li == 0 else hT
                last = (li == nl - 1)
                self.norm1(l, hsrc)
                with ExitStack() as les:
                    self.oall = self.sb(les, "oall", [128, 16, T], BF16)
                    self.mixers(l)
                    if not os.environ.get("ONLYMIX"):
                        self.merge_out(l, hsrc, hT, les)
                P.barrier()
                if not os.environ.get("ONLYMIX"):
                    finals += self.ffn(l, hT, outT if last else hT)
                P.barrier()
            P.finalize(finals)
        return nc

    def gcol(self, l, which, kc):
        i = (l * 4 + which) * 8 + kc
        return self.gains[:, i:i + 1]

    def norm_block(self, hb, hkey, b, gl, gw, tag):
        sq, rs = self.nsq, self.nrs
        self.rstd_from([hb[:, kc, :] for kc in range(8)], TB, D, [hkey], rs[:, :], "nrs", sq, "nsq")
        for kc in range(8):
            self.stt("dve", self.U(kc, b * TB, (b + 1) * TB), hb[:, kc, :], self.gcol(gl, gw, kc), rs[:, :],
                     ALU.mult, ALU.mult, [hkey, "nrs", "gains"], [("uT", kc)])

    def norm1(self, l, hsrc):
        with ExitStack() as es:
            hb2 = [self.sb(es, "n1h%d" % i, [128, 8, TB], F32) for i in range(2)]
            self.nsq = self.sb(es, "n1sq", [128, 2, TB], F32)
            self.nrs = self.sb(es, "n1rs", [128, TB], F32)
            hv = hsrc.rearrange("(c p) t -> p c t", p=128)
            for b in range(NB):
                hb = hb2[b % 2]
                hk = ("n1h", b % 2)
                self.dma("sp", hb[:], hv[:, :, b * TB:(b + 1) * TB], (), [hk])
                self.norm_block(hb, hk, b, l, 0, "n1")
            self.P.barrier()

    def mixers(self, l):
        import os
        sel = os.environ.get("MIX", "cadb")
        for nm, fn, c0 in (("c", self.mix_c, 8), ("a", self.mix_a, 0), ("d", self.mix_d, 12), ("b", self.mix_b, 4)):
            if nm in sel:
                fn(l)
            else:
                self.memset("dve", self.oall[:, c0:c0 + 4, :], 0.0, ["oall"])
            self.P.barrier()
        if "oall" in self.dbg_out and l == 0:
            self.dbgdump_oall()

    def dbgdump_oall(self):
        with ExitStack() as es:
            tmp = self.sb(es, "dbgtmp", [128, T], F32)
            for c in range(16):
                self.cp("dve", tmp[:], self.oall[:, c, :], ["oall"], ["dbgtmp"])
                self.dma("sp", self.dbg_out["oall"][c * 128:(c + 1) * 128, :], tmp[:], ["dbgtmp"], ())
            self.P.barrier()

    def proj_fm(self, w, wkey, ncontr, col0, M, rhs_fn, rkeys, evac):
        for b in range(NB):
            pst, pk = self.psb("x")
            for kc in range(ncontr):
                self.mm(pst[:M, :TB], w[:, kc, col0:col0 + M], rhs_fn(kc, b), kc == 0, kc == ncontr - 1,
                        [wkey] + rkeys, [pk])
            evac(b, pst[:M, :TB], pk)

    def proj_tm(self, w, wkey, ncontr, col0, N, lhs_fn, rkeys, tiles, evac):
        for j, (t0, n) in enumerate(tiles):
            pst, pk = self.psb("x")
            for kc in range(ncontr):
                self.mm(pst[:n, :N], lhs_fn(kc, t0, n), w[:, kc, col0:col0 + N], kc == 0, kc == ncontr - 1,
                        [wkey] + rkeys, [pk])
            evac(j, n, pst[:n, :N], pk)

    def attn_stream(self, tag, jobs, ptbuf, LOOK=2, DEFER=5):
        flat = []
        for ji, jb in enumerate(jobs):
            n = len(jb["tiles"])
            for i, tl in enumerate(jb["tiles"]):
                flat.append((ji, i, n, tl))
        pend = []
        state = {}
        pts = {}

        def issue_s(idx):
            ji, i, n, tl = flat[idx]
            jb = jobs[ji]
            if i == 0 and jb.get("pre") is not None:
                jb["pre"]()
            qn, scale = jb["qn"], jb["scale"]
            kr = tl["rows"]
            pst, pk = self.psb("s")
            nm = len(tl["mms"])
            for mi, (lt, rh) in enumerate(tl["mms"]):
                self.mm(pst[:kr, :qn], lt, rh, mi == 0, mi == nm - 1, jb["rkeys"], [pk])
            pi = self.ptidx
            self.ptidx += 1
            pt = ptbuf[pi % len(ptbuf)]
            ptk = (tag + "pt", pi % len(ptbuf))
            bias = tl["bias"]
            if bias is None:
                self.act(pt[:kr, :qn], pst[:kr, :qn], AF.Exp, [pk], [ptk], scale=scale)
            elif bias[0] == "c":
                self.act(pt[:kr, :qn], pst[:kr, :qn], AF.Exp, [pk] + bias[2], [ptk], bias=bias[1][:kr, :], scale=scale)
            else:
                tmp = self.sbias[pi % len(self.sbias)]
                tk = (tag + "sb", pi % len(self.sbias))
                self.stt("dve", tmp[:kr, :qn], pst[:kr, :qn], scale, bias[1], ALU.mult, ALU.add, [pk] + bias[2], [tk])
                self.act(pt[:kr, :qn], tmp[:kr, :qn], AF.Exp, [tk], [ptk])
            pts[idx] = (pt, ptk)

        def issue_pv(idx):
            ji, i, n, tl = flat[idx]
            jb = jobs[ji]
            qn, o_M = jb["qn"], jb["o_M"]
            kr = tl["rows"]
            if i == 0:
                state[ji] = self.psb("o") + self.psb("d")
            ops_, ok, dps, dk = state[ji]
            pt, ptk = pts.pop(idx)
            vl, vkeys = jb["v_fn"](tl)
            self.mm(ops_[:o_M, :qn], vl, pt[:kr, :qn], i == 0, i == n - 1, [ptk] + vkeys, [ok])
            self.mm(dps[:o_M, :qn], jb["ones_fn"](tl), pt[:kr, :qn], i == 0, i == n - 1, [ptk, "onesb"] + jb.get("okeys", []), [dk])
            if i == n - 1:
                jb["fin1"](ops_, ok, dps, dk)
                if jb.get("fin2") is not None:
                    pend.append((idx + DEFER, jb["fin2"]))
                del state[ji]

        N = len(flat)
        for idx in range(N + LOOK):
            if idx < N:
                issue_s(idx)
            if idx - LOOK >= 0:
                issue_pv(idx - LOOK)
            while pend and pend[0][0] <= idx - LOOK:
                pend.pop(0)[1]()
        for _, fn in pend:
            fn()

    def mix_c(self, l):
        I = self.I
        with ExitStack() as es:
            wc = self.sb(es, "c_w", [128, 8, 512], BF16)
            wq = self.sb(es, "c_wq", [128, 2, 768 + 256], BF16)
            wkv = self.sb(es, "c_wkv", [128, 1, 1024], BF16)
            wvv = self.sb(es, "c_wvv", [128, 1, 512], BF16)
            lat = self.sb(es, "c_lat", [128, 3, TB], F32)
            qn = self.sb(es, "c_qn", [128, 2, T], BF16)
            kvn = self.sb(es, "c_kvn", [128, T], BF16)
            kpe = self.sb(es, "c_kpe", [128, 17 * 128], BF16)
            rope = self.sb(es, "c_rope", [64, 2 * T], F32)
            vtok = self.sb(es, "c_v", [128, NT, 512], BF16)
            qno = self.sb(es, "c_qno", [128, 2, T], BF16)
            qpe = self.sb(es, "c_qpe", [128, 2, T], BF16)
            kno = self.sb(es, "c_kno", [128, 2, 17 * 128], BF16)
            ptbuf = [self.sb(es, "c_pt%d" % i, [128, TB], BF16) for i in range(4)]
            self.nsq = self.sb(es, "c_sq", [128, 2, TB], F32)
            self.nrs = self.sb(es, "c_rs", [128, TB], F32)
            t1 = self.sb(es, "c_t1", [64, TB], F32)
            t2 = self.sb(es, "c_t2", [64, TB], F32)
            rd = [self.sb(es, "c_rd%d" % i, [128, TB], F32) for i in range(2)]
            win = I["w_in"][l].rearrange("(kc p) n -> p kc n", p=128)
            self.loadw(wc[:, :, 0:448], win[:, :, OC_QA:OC_QA + 448], "c_w")
            self.loadw(wc[:, :, 448:512], I["w_in_kr_sw"][l].rearrange("(kc p) n -> p kc n", p=128), "c_w")
            self.loadw(wq[:, :, 0:768], I["mla_w_q_up"][l].rearrange("(kc p) n -> p kc n", p=128), "c_wq")
            self.loadw(wq[:, :, 768:1024], I["mla_w_q_up_sw"][l].rearrange("(kc p) n -> p kc n", p=128), "c_wq")
            self.loadw(wkv[:, 0, :], I["mla_w_kv_up"][l], "c_wkv")
            self.loadw(wvv[:, 0, :].rearrange("p (h e) -> p h e", h=4),
                       I["mla_w_kv_up"][l].rearrange("p (h e) -> p h e", h=4)[:, :, 128:256], "c_wvv")
            self.dma("sp", rope[:], I["rope"], (), ["c_rope"])
            self.memset("dve", kpe[:], 0.0, ["c_kpe"])
            for i_ in range(2):
                self.memset("dve", kno[:, i_, T:17 * 128], 0.0, [("c_kno", i_)])
            self.memset("dve", vtok[:, NT - 1, :], 0.0, ["c_v"])
            for i_ in range(2):
                self.memset("dve", qpe[64:128, i_, :], 0.0, [("c_qpe", i_)])
            ukeys = [("uT", kc) for kc in range(8)]
            urhs = lambda kc, b: self.U(kc, b * TB, (b + 1) * TB)
            for b in range(NB):
                sl = slice(b * TB, (b + 1) * TB)
                pst, pk = self.psb("x")
                pst2, pk2 = self.psb("x")
                for kc in range(8):
                    self.mm(pst[:64, :TB], wc[:, kc, 384:448], urhs(kc, b), kc == 0, kc == 7, ["c_w"] + ukeys, [pk])
                for kc in range(8):
                    self.mm(pst2[:64, :TB], wc[:, kc, 448:512], urhs(kc, b), kc == 0, kc == 7, ["c_w"] + ukeys, [pk2])
                self.tt("dve", t1[:, :], pst[:64, :TB], rope[:, sl], ALU.mult, [pk, "c_rope"], ["c_t1"])
                self.tt("dve", t2[:, :], pst2[:64, :TB], rope[:, T + b * TB:T + (b + 1) * TB], ALU.mult, [pk2, "c_rope"], ["c_t2"])
                self.tt("dve", kpe[0:64, sl], t1[:, :], t2[:, :], ALU.add, ["c_t1", "c_t2"], ["c_kpe"])
            sc = self.smallc
            for b in range(NB):
                sl = slice(b * TB, (b + 1) * TB)
                for ci in range(3):
                    pst, pk = self.psb("x")
                    for kc in range(8):
                        self.mm(pst[:, :TB], wc[:, kc, ci * 128:(ci + 1) * 128], urhs(kc, b), kc == 0, kc == 7, ["c_w"] + ukeys, [pk])
                    self.cp("act", lat[:, ci, :], pst[:, :TB], [pk], [("c_lat", ci)])
                self.rstd_from([lat[:, 0, :], lat[:, 1, :]], TB, 256, [("c_lat", 0), ("c_lat", 1)], self.nrs[:, :], "nrs", self.nsq, "nsq")
                for ci in range(2):
                    self.stt("dve", qn[:, ci, sl], lat[:, ci, :], sc[:, l * 8 + 3 + ci:l * 8 + 4 + ci], self.nrs[:, :],
                             ALU.mult, ALU.mult, [("c_lat", ci), "nrs", "smallc"], ["c_qn"])
                self.rstd_from([lat[:, 2, :]], TB, 128, [("c_lat", 2)], self.nrs[:, :], "nrs", self.nsq, "nsq")
                self.stt("dve", kvn[:, sl], lat[:, 2, :], sc[:, l * 8 + 2:l * 8 + 3], self.nrs[:, :],
                         ALU.mult, ALU.mult, [("c_lat", 2), "nrs", "smallc"], ["c_kvn"])
            tiles = [(128 * j, trows(j)) for j in range(NT)]
            self.proj_tm(wvv, "c_wvv", 1, 0, 512, lambda kc, t0, n: kvn[:, t0:t0 + n], ["c_kvn"], tiles,
                         lambda j, n, ps, pk: self.cp("act", vtok[:n, j, :], ps, [pk], ["c_v"]))
            scale = (128 + 64) ** -0.5
            self.ptidx = 0
            qrhs = lambda kc, b: qn[:, kc, b * TB:(b + 1) * TB]

            def cproj(h):
                hb_ = h % 2
                self.proj_fm(wq, "c_wq", 2, h * 192, 128, qrhs, ["c_qn"],
                             lambda b, ps, pk: self.cp("act", qno[:, hb_, b * TB:(b + 1) * TB], ps, [pk], [("c_qno", hb_)]))
                for b in range(NB):
                    sl = slice(b * TB, (b + 1) * TB)
                    pst, pk = self.psb("x")
                    pst2, pk2 = self.psb("x")
                    for kc in range(2):
                        self.mm(pst[:64, :TB], wq[:, kc, h * 192 + 128:h * 192 + 192], qrhs(kc, b), kc == 0, kc == 1, ["c_wq", "c_qn"], [pk])
                    for kc in range(2):
                        self.mm(pst2[:64, :TB], wq[:, kc, 768 + h * 64:768 + h * 64 + 64], qrhs(kc, b), kc == 0, kc == 1, ["c_wq", "c_qn"], [pk2])
                    self.tt("dve", t1[:, :], pst[:64, :TB], rope[:, sl], ALU.mult, [pk, "c_rope"], ["c_t1"])
                    self.tt("dve", t2[:, :], pst2[:64, :TB], rope[:, T + b * TB:T + (b + 1) * TB], ALU.mult, [pk2, "c_rope"], ["c_t2"])
                    self.tt("dve", qpe[0:64, hb_, sl], t1[:, :], t2[:, :], ALU.add, ["c_t1", "c_t2"], [("c_qpe", hb_)])
                self.proj_fm(wkv, "c_wkv", 1, h * 256, 128, lambda kc, b: kvn[:, b * TB:(b + 1) * TB], ["c_kvn"],
                             lambda b, ps, pk: self.cp("act", kno[:, hb_, b * TB:(b + 1) * TB], ps, [pk], [("c_kno", hb_)]))

            cproj(0)
            for h in range(4):
                if h + 1 < 4:
                    cproj(h + 1)
                hb_ = h % 2
                jobs = []
                for b in range(NB):
                    q0 = b * TB
                    tl = []
                    for j in range(NT):
                        kr = 128
                        tl.append(dict(rows=kr, j=j, bias=None,
                                       mms=[(kno[:, hb_, 128 * j:128 * j + kr], qno[:, hb_, q0:q0 + TB]),
                                            (kpe[:, 128 * j:128 * j + kr], qpe[:, hb_, q0:q0 + TB])]))

                    def fin1(ops_, ok, dps, dk, q0=q0, h=h, b=b):
                        rd_ = rd[b % 2]
                        self.recip(rd_[:, :], dps[:, :TB], [dk], [("c_rd", b % 2)])
                        self.tt("dve", self.oall[:, 8 + h, q0:q0 + TB], ops_[:, :TB], rd_[:, :], ALU.mult, [ok, ("c_rd", b % 2)], ["oall"])
                    jobs.append(dict(qn=TB, tiles=tl, o_M=128, scale=scale, fin1=fin1, fin2=None,
                                     v_fn=lambda t, h=h: (vtok[:t["rows"], t["j"], h * 128:(h + 1) * 128], ["c_v"]),
                                     ones_fn=lambda t: (self.ones16 if t["j"] == NT - 1 else self.onesb)[:, :],
                                     rkeys=[("c_kno", hb_), ("c_qno", hb_), "c_kpe", ("c_qpe", hb_)]))
                saved = dict(self.psgroups)
                self.psgroups.update({"s": [0, 1, 2, 3], "o": [4, 5], "d": [6, 7]})
                self.attn_stream("c", jobs, ptbuf, LOOK=3)
                self.psgroups = saved

    def mix_a(self, l):
        I = self.I
        lam_init = 0.8 - 0.6 * math.exp(-0.3 * l)
        with ExitStack() as es:
            vtok = self.sb(es, "a_v", [128, NT, 512], BF16)
            qT = self.sb(es, "a_q", [128, 8, T], BF16)
            kT = self.sb(es, "a_k", [128, 4, 17 * 128], BF16)
            lam = self.sb(es, "a_lam", [128, 256], F32)
            lt = self.sb(es, "a_lt", [128, 128], F32)
            lv = self.sb(es, "a_lv", [128, 4], F32)
            gsub = self.sb(es, "a_gs", [128, 1], F32)
            win = I["w_in"][l].rearrange("(kc p) n -> p kc n", p=128)
            self.dma("sp", lam[:], I["lamrep"][:, l * 256:(l + 1) * 256], (), ["a_lam"])
            self.tt("dve", lt[:, 0:64], lam[:, 0:64], lam[:, 64:128], ALU.mult, ["a_lam"], ["a_lt"])
            self.tt("dve", lt[:, 64:128], lam[:, 128:192], lam[:, 192:256], ALU.mult, ["a_lam"], ["a_lt"])
            self.P.op("dve", lambda e: e.reduce_sum(out=lv[:, 0:1], in_=lt[:, 0:64], axis=mybir.AxisListType.X), ["a_lt"], ["a_lv"])
            self.P.op("dve", lambda e: e.reduce_sum(out=lv[:, 1:2], in_=lt[:, 64:128], axis=mybir.AxisListType.X), ["a_lt"], ["a_lv"])
            self.act(lv[:, 0:2], lv[:, 0:2], AF.Exp, ["a_lv"], ["a_lv"])
            self.tt("dve", lv[:, 2:3], lv[:, 1:2], lv[:, 0:1], ALU.subtract, ["a_lv"], ["a_lv"])
            self.ts("dve", lv[:, 3:4], lv[:, 2:3], -lam_init, None, ALU.add, None, ["a_lv"], ["a_lv"])
            self.ts("dve", gsub[:, :], self.smallc[:, l * 8:l * 8 + 1], 1.0 - lam_init, None, ALU.mult, None, ["smallc"], ["a_gs"])
            self.memset("dve", qT[:], 0.0, ["a_q"])
            self.memset("dve", kT[:], 0.0, ["a_k"])
            self.memset("dve", vtok[:, NT - 1, :], 0.0, ["a_v"])
            ukeys = [("uT", kc) for kc in range(8)]
            urhs = lambda kc, b: self.U(kc, b * TB, (b + 1) * TB)
            tiles = [(128 * j, trows(j)) for j in range(NT)]
            with ExitStack() as es1:
                wqk = [self.sb(es1, "a_wqk%d" % i, [128, 8, 256], BF16) for i in range(2)]
                wv = self.sb(es1, "a_wv", [128, 8, 512], BF16)
                self.loadw(wv[:], win[:, :, OA_V:OA_V + 512], "a_wv")
                for h in range(4):
                    wq_, wqkk = wqk[h % 2], ("a_wqk", h % 2)
                    self.loadw(wq_[:, :, 0:128], win[:, :, OA_Q + h * 128:OA_Q + (h + 1) * 128], wqkk)
                    self.loadw(wq_[:, :, 128:256], win[:, :, OA_K + h * 128:OA_K + (h + 1) * 128], wqkk)
                    if h == 0:
                        self.proj_tm(wv, "a_wv", 8, 0, 512, lambda kc, t0, n: self.U(kc, t0, t0 + n), ukeys, tiles,
                                     lambda j, n, ps, pk: self.cp("act", vtok[:n, j, :], ps, [pk], ["a_v"]))

                    def qev(b, ps, pk, h=h):
                        self.cp("act", qT[0:64, 2 * h, b * TB:(b + 1) * TB], ps[0:64, :], [pk], ["a_q"])
                        self.cp("dve", qT[64:128, 2 * h + 1, b * TB:(b + 1) * TB], ps[64:128, :], [pk], ["a_q"])
                    self.proj_fm(wq_, wqkk, 8, 0, 128, urhs, ukeys, qev)
                    self.proj_fm(wq_, wqkk, 8, 128, 128, urhs, ukeys,
                                 lambda b, ps, pk, h=h: self.cp("act", kT[:, h, b * TB:(b + 1) * TB], ps, [pk], ["a_k"]))
                self.P.barrier()
            slab = [self.sb(es, "a_slab%d" % i, [128, 2, A_W], F32) for i in range(2)]
            ptbuf = [self.sb(es, "a_pt%d" % i, [128, TB], BF16) for i in range(3)]
            self.sbias = [self.sb(es, "a_sb%d" % i, [128, TB], F32) for i in range(2)]
            self.nsq = self.sb(es, "a_sq", [128, 2, TB], F32)
            self.nrs = self.sb(es, "a_rs", [128, TB], F32)
            rd = [self.sb(es, "a_rd%d" % i, [128, TB], F32) for i in range(2)]
            on = [self.sb(es, "a_on%d" % i, [128, TB], F32) for i in range(4)]
            self.ptidx = 0
            jobs = []
            for h in range(4):
                sl_ = slab[h % 2]
                slk = ("a_slab", h % 2)
                slab_loaded = [False]
                for b in range(NB):
                    q0 = b * TB
                    for m in range(2):
                        mh = m * 4 + h
                        tl = []
                        for j in range(NT):
                            kr = 128
                            o = TB * b - 128 * j
                            if 127 - o <= -91:
                                bias = ("c", self.aconst[:, mh:mh + 1], ["aconst"])
                            elif -o - (TB - 1) >= 91:
                                bias = ("c", self.aconst[:, 8 + mh:9 + mh], ["aconst"])
                            else:
                                bias = ("s", sl_[:kr, m, o + A_C:o + A_C + TB], [slk])
                            tl.append(dict(rows=kr, j=j, bias=bias,
                                           mms=[(kT[:, h, 128 * j:128 * j + kr], qT[:, 2 * h + m, q0:q0 + TB])]))
                        oi = (b % 2) * 2 + m

                        def fin1(ops_, ok, dps, dk, oi=oi):
                            rd_ = rd[oi % 2]
                            self.recip(rd_[:, :], dps[:, :TB], [dk], [("a_rd", oi % 2)])
                            self.tt("dve", on[oi][:, :], ops_[:, :TB], rd_[:, :], ALU.mult, [ok, ("a_rd", oi % 2)], [("a_on", oi)])

                        def fin2(b=b, h=h, q0=q0):
                            o0, o1 = (b % 2) * 2, (b % 2) * 2 + 1
                            self.stt("dve", on[o0][:, :], on[o1][:, :], lv[:, 3:4], on[o0][:, :], ALU.mult, ALU.add,
                                     [("a_on", o0), ("a_on", o1), "a_lv"], [("a_on", o0)])
                            self.rstd_from([on[o0][:, :]], TB, 128, [("a_on", o0)], self.nrs[:, :], "nrs", self.nsq, "nsq")
                            self.stt("dve", self.oall[:, h, q0:q0 + TB], on[o0][:, :], gsub[:, 0:1], self.nrs[:, :], ALU.mult, ALU.mult,
                                     [("a_on", o0), "nrs", "a_gs"], ["oall"])
                        jobs.append(dict(qn=TB, tiles=tl, o_M=128, scale=0.125, fin1=fin1, fin2=(fin2 if m == 1 else None),
                                         v_fn=lambda t, h=h: (vtok[:t["rows"], t["j"], h * 128:(h + 1) * 128], ["a_v"]),
                                         ones_fn=lambda t: (self.ones16 if t["j"] == NT - 1 else self.onesb)[:, :], rkeys=["a_q", "a_k"],
                                         pre=((lambda h=h: [self.dma("sp", slab[h % 2][:, mm_, :], self.I["slabA"][mm_ * 4 + h], (), [("a_slab", h % 2)]) for mm_ in range(2)])
                                              if (b == 0 and m == 0) else None)))
            self.attn_stream("a", jobs, ptbuf)

    def mix_d(self, l):
        I = self.I
        with ExitStack() as es:
            wq = self.sb(es, "d_wq", [128, 8, 512], BF16)
            wkk = self.sb(es, "d_wkk", [128, 8, 2, 128], BF16)
            wv = self.sb(es, "d_wv", [128, 8, 128], BF16)
            qT = self.sb(es, "d_q", [128, 8, T], BF16)
            kT2 = self.sb(es, "d_k", [128, 2, T], BF16)
            vpad = self.sb(es, "d_v", [128, NT * 4, 128], BF16)
            slab = [self.sb(es, "d_slab%d" % i, [128, D_SLABW], F32) for i in range(4)]
            ptbuf = [self.sb(es, "d_pt%d" % i, [128, 256], BF16) for i in range(4)]
            self.sbias = [self.sb(es, "d_sb%d" % i, [128, 256], F32) for i in range(3)]
            oh = self.sb(es, "d_oh", [128, 2, 128], BF16)
            es8 = self.sb(es, "d_es8", [128, 8], F32)
            es2 = self.sb(es, "d_es2", [128, 4], F32)
            rd = [self.sb(es, "d_rd%d" % i, [128, 256], F32) for i in range(2)]
            win = I["w_in"][l].rearrange("(kc p) n -> p kc n", p=128)
            self.loadw(wq[:], win[:, :, OD_Q:OD_Q + 512], "d_wq")
            for kv in range(2):
                for e in range(2):
                    self.loadw(wkk[:, :, kv, e * 64:(e + 1) * 64], win[:, :, OD_K + kv * 64:OD_K + (kv + 1) * 64], "d_wkk")
            self.loadw(wv[:], win[:, :, OD_V:OD_V + 128], "d_wv")
            self.memset("dve", vpad[:], 0.0, ["d_v"])
            self.memset("dve", oh[:], 0.0, ["d_oh"])
            self.memset("dve", oh[:, 0, 0:64], 1.0, ["d_oh"])
            self.memset("dve", oh[:, 1, 64:128], 1.0, ["d_oh"])
            self.dma("sp", es8[:], I["sinkrep"][:, l * 8:(l + 1) * 8], (), ["d_es8"])
            self.act(es8[:], es8[:], AF.Exp, ["d_es8"], ["d_es8"])
            for p in range(4):
                self.cp("dve", es2[0:64, p:p + 1], es8[0:64, 2 * p:2 * p + 1], ["d_es8"], ["d_es2"])
                self.cp("dve", es2[64:128, p:p + 1], es8[64:128, 2 * p + 1:2 * p + 2], ["d_es8"], ["d_es2"])
            ukeys = [("uT", kc) for kc in range(8)]
            urhs = lambda kc, b: self.U(kc, b * TB, (b + 1) * TB)
            self.memset("dve", qT[:], 0.0, ["d_q"])

            def qev(b, ps, pk, p):
                self.cp("act", qT[0:64, 2 * p, b * TB:(b + 1) * TB], ps[0:64, :], [pk], ["d_q"])
                self.cp("act", qT[64:128, 2 * p + 1, b * TB:(b + 1) * TB], ps[64:128, :], [pk], ["d_q"])
            for p in range(4):
                self.proj_fm(wq, "d_wq", 8, p * 128, 128, urhs, ukeys, lambda b, ps, pk, p=p: qev(b, ps, pk, p))
            for kv in range(2):
                self.proj_fm(wkk[:, :, kv, :], "d_wkk", 8, 0, 128, urhs, ukeys,
                             lambda b, ps, pk, kv=kv: self.cp("act", kT2[:, kv, b * TB:(b + 1) * TB], ps, [pk], ["d_k"]))
            tiles = [(128 * j, trows(j)) for j in range(NT)]

            def vev(j, n, ps, pk):
                for kv in range(2):
                    self.cp("act", vpad[:n, j * 4 + kv * 2, 0:64], ps[:, kv * 64:(kv + 1) * 64], [pk], ["d_v"])
                    self.cp("dve", vpad[:n, j * 4 + kv * 2 + 1, 64:128], ps[:, kv * 64:(kv + 1) * 64], [pk], ["d_v"])
            self.proj_tm(wv, "d_wv", 8, 0, 128, lambda kc, t0, n: self.U(kc, t0, t0 + n), ukeys, tiles, vev)
            self.ptidx = 0
            jobs = []
            for p in range(4):
                kv = p // 2
                sls = [slab[(p % 2) * 2 + e] for e in range(2)]
                slks = [("d_slab", (p % 2) * 2 + e) for e in range(2)]
                pre_p = (lambda p=p, sls=sls, slks=slks: [self.dma("sp", sls[e][:], I["slabD"][2 * p + e], (), [slks[e]]) for e in range(2)])
                for qb, (q0, qn) in enumerate(D_QB):
                    tl = []
                    for e in range(2):
                        h = 2 * p + e
                        pr = slice(64 * e, 64 * e + 64)
                        if qb == 0:
                            mb = ("s", sls[e][:16, D_W + 256:D_W + 256 + qn], [slks[e]])
                        else:
                            mb = ("c", self.dconst[:, h:h + 1], ["dconst"])
                        tl.append(dict(rows=16, j=0, e=e, bias=mb, mms=[(kT2[:, kv, 0:16], qT[:, h, q0:q0 + qn])]))
                        for j in range(max(0, q0 // 128 - 1), min(16, (q0 + qn + 127) // 128) + 1):
                            kr = trows(j)
                            o = q0 - 128 * j
                            if j == 0:
                                bs = sls[e][:kr, D_W:D_W + qn]
                            else:
                                bs = sls[e][:kr, o + D_C:o + D_C + qn]
                            tl.append(dict(rows=kr, j=j, e=e, bias=("s", bs, [slks[e]]),
                                           mms=[(kT2[:, kv, 128 * j:128 * j + kr], qT[:, h, q0:q0 + qn])]))

                    def fin1(ops_, ok, dps, dk, p=p, q0=q0, qn=qn, qb=qb):
                        rd_ = rd[qb % 2]
                        rk = ("d_rd", qb % 2)
                        self.ts("dve", rd_[:, :qn], dps[:, :qn], es2[:, p:p + 1], None, ALU.add, None, [dk, "d_es2"], [rk])
                        self.recip(rd_[:, :qn], rd_[:, :qn], [rk], [rk])
                        self.tt("dve", self.oall[:, 12 + p, q0:q0 + qn], ops_[:, :qn], rd_[:, :qn], ALU.mult, [ok, rk], ["oall"])
                    jobs.append(dict(qn=qn, tiles=tl, o_M=128, scale=0.125, fin1=fin1, fin2=None,
                                     v_fn=lambda t, kv=kv: (vpad[:t["rows"], t["j"] * 4 + kv * 2 + t["e"], :], ["d_v"]),
                                     ones_fn=lambda t: oh[:t["rows"], t["e"], :], okeys=["d_oh"], rkeys=["d_q", "d_k"],
                                     pre=(pre_p if qb == 0 else None)))
            saved = dict(self.psgroups)
            self.psgroups.update({"s": [0, 1, 2, 3], "o": [4, 5], "d": [6, 7]})
            self.attn_stream("d", jobs, ptbuf, LOOK=3)
            self.psgroups = saved

    def mix_b(self, l):
        I = self.I
        with ExitStack() as es:
            wb = self.sb(es, "b_w", [128, 8, 1568], BF16)
            wgu = self.sb(es, "b_wgu", [16, 2, 256], F32)
            gb = self.sb(es, "b_gb", [128, 512], F32)
            msk = self.sb(es, "b_msk", [128, 4, 128], F32)
            obw = self.sb(es, "b_obw", [128, 4, T], F32)
            S = self.sb(es, "b_S", [64, 4, 128], F32)
            Sbf = self.sb(es, "b_Sbf", [64, 4, 128], BF16)
            self.nsq = self.sb(es, "b_sq", [128, 2, 64], F32)
            self.nrs = self.sb(es, "b_rs", [128, 64], F32)
            NBUF = 2
            bufs = {}

            def tb(name, shape, dt, i):
                k = (name, i % NBUF)
                if k not in bufs:
                    bufs[k] = self.sb(es, "b_%s%d" % (name, i % NBUF), shape, dt)
                return bufs[k], ("b_" + name, i % NBUF)

            win = I["w_in"][l].rearrange("(kc p) n -> p kc n", p=128)
            self.loadw(wb[:], win[:, :, OB_Q:OB_Q + 1568], "b_w")
            self.dma("sp", wgu[:], I["gla_gate_up"][l].rearrange("g r c -> r g c"), (), ["b_wgu"])
            self.dma("sp", gb[:], I["gbias"][:, l * 512:(l + 1) * 512], (), ["b_gb"])
            self.dma("sp", msk[:], I["glam"].rearrange("p (m t) -> p m t", m=4), (), ["b_msk"])
            ukeys = [("uT", kc) for kc in range(8)]
            gch = [(0, 16)] + [(16 + 64 * (c - 1), 64) for c in range(1, 33)]
            seq = [(1, ci) for ci in range(32, -1, -1)] + [(0, ci) for ci in range(33)]
            NS = len(seq)
            cxs = {}

            def ctx(i):
                if i not in cxs:
                    dr, ci = seq[i]
                    t0, n = gch[ci]
                    cxs[i] = dict(i=i, dr=dr, ci=ci, t0=t0, n=n, mi_c=(0 if dr == 0 else 1), mi_r=(2 if dr == 0 else 3))
                return cxs[i]

            def P1(cx):
                i, dr, t0, n = cx["i"], cx["dr"], cx["t0"], cx["n"]
                ut = lambda kc: self.U(kc, t0, t0 + n)
                pgl, pglk = self.psb("x")
                for kc in range(8):
                    self.mm(pgl[:16, :n], wb[:, kc, 1536 + 16 * dr:1552 + 16 * dr], ut(kc), kc == 0, kc == 7, ["b_w"] + ukeys, [pglk])
                cx["glT"], cx["glk"] = tb3("glT", [16, 64], F32, i)
                self.cp("act", cx["glT"][:, :n], pgl[:16, :n], [pglk], [cx["glk"]])
                pvt, pvtk = self.psb("x")
                for kc in range(8):
                    self.mm(pvt[:n, :512], ut(kc), wb[:, kc, 512:1024], kc == 0, kc == 7, ["b_w"] + ukeys, [pvtk])
                cx["vt"], cx["vtk"] = tb3("vt", [64, 512], BF16, i)
                self.cp("act", cx["vt"][:n, :], pvt[:n, :512], [pvtk], [cx["vtk"]])
                if dr == 0:
                    prr, prrk = self.psb("x")
                    for h in range(4):
                        for kc in range(8):
                            self.mm(prr[:, h * 64:h * 64 + n], wb[:, kc, 1024 + h * 128:1024 + (h + 1) * 128], ut(kc), kc == 0, kc == 7, ["b_w"] + ukeys, [prrk])
                    k4 = ("sr", i % 4)
                    if k4 not in bufs:
                        bufs[k4] = self.sb(es, "b_sr%d" % (i % 4), [128, 4, 64], F32)
                    cx["sr"], cx["srk"] = bufs[k4], ("b_sr", i % 4)
                    r3 = prr[:, 0:256].rearrange("p (h t) -> p h t", h=4)[:, :, :n]
                    sr_ = cx["sr"]
                    self.act(sr_[:, :, :n], r3, AF.Exp, [prrk], [cx["srk"]], scale=-1.0)
                    self.ts("dve", sr_[:, :, :n], sr_[:, :, :n], 1.0, None, ALU.add, None, [cx["srk"]], [cx["srk"]])
                    self.recip(sr_[:, :, :n], sr_[:, :, :n], [cx["srk"]], [cx["srk"]])
                    self.tt("dve", sr_[:, :, :n], sr_[:, :, :n], r3, ALU.mult, [cx["srk"], prrk], [cx["srk"]])

            def P2a(cx):
                i, dr, n = cx["i"], cx["dr"], cx["n"]
                ppre, pprek = self.psb("x")
                self.mm(ppre[:n, :256], cx["glT"][:, :n], wgu[:, dr, :], True, True, [cx["glk"], "b_wgu"], [pprek])
                xla, xlk = tb("xla", [64, 256], F32, i)
                self.tt("dve", xla[:n, :], ppre[:n, :256], gb[:n, dr * 256:(dr + 1) * 256], ALU.add, [pprek, "b_gb"], [xlk])
                self.act(xla[:n, :], xla[:n, :], AF.Exp, [xlk], [xlk], scale=-1.0)
                cx["sp"], cx["spk"] = tb("sp", [64, 256], F32, i)
                self.act(cx["sp"][:n, :], xla[:n, :], AF.Ln, [xlk], [cx["spk"]], bias=1.0)

            def P2b(cx):
                i, dr, t0, n = cx["i"], cx["dr"], cx["t0"], cx["n"]
                sp_, spk = cx["sp"], cx["spk"]
                ut = lambda kc: self.U(kc, t0, t0 + n)
                pqk, pqkk = self.psb("x")
                for qi in range(8):
                    for kc in range(8):
                        self.mm(pqk[:64, qi * 64:qi * 64 + n], wb[:, kc, qi * 64:(qi + 1) * 64], ut(kc), kc == 0, kc == 7, ["b_w"] + ukeys, [pqkk])
                pkt, pktk = self.psb("x")
                for kc in range(8):
                    self.mm(pkt[:n, :256], ut(kc), wb[:, kc, 256:512], kc == 0, kc == 7, ["b_w"] + ukeys, [pktk])
                pc, pck = self.psb("x")
                for h in range(4):
                    self.mm(pc[:64, h * 64:h * 64 + n], sp_[:n, h * 64:(h + 1) * 64], msk[:n, cx["mi_c"], :n], True, True, [spk, "b_msk"], [pck])
                pr_, prk = self.psb("x")
                self.mm(pr_[:n, :256], msk[:n, cx["mi_r"], :n], sp_[:n, :], True, True, [spk, "b_msk"], [prk])
                cx["eb"], cx["ebk"] = tb("eb", [64, 4, 64], F32, i)
                einv, eik = tb("einv", [64, 4, 64], F32, i)
                pc3 = pc[:64, 0:256].rearrange("p (h t) -> p h t", h=4)[:, :, :n]
                self.act(cx["eb"][:, :, :n], pc3, AF.Exp, [pck], [cx["ebk"]], scale=-1.0 / 16)
                self.act(einv[:, :, :n], pc3, AF.Exp, [pck], [eik], scale=1.0 / 16)
                eo, eok = tb("eo", [64, 256], F32, i)
                self.act(eo[:n, :], pr_[:n, :256], AF.Exp, [prk], [eok], scale=-1.0 / 16)
                cx["qd"], cx["qdk"] = tb("qd", [64, 4, 64], BF16, i)
                cx["ki"], cx["kik"] = tb("ki", [64, 4, 64], BF16, i)
                q3 = pqk[:64, 0:256].rearrange("p (h t) -> p h t", h=4)[:, :, :n]
                k3 = pqk[:64, 256:512].rearrange("p (h t) -> p h t", h=4)[:, :, :n]
                self.stt("dve", cx["qd"][:, :, :n], q3, 0.125, cx["eb"][:, :, :n], ALU.mult, ALU.mult, [pqkk, cx["ebk"]], [cx["qdk"]])
                self.tt("dve", cx["ki"][:, :, :n], k3, einv[:, :, :n], ALU.mult, [pqkk, eik], [cx["kik"]])
                cx["ko"], cx["kok"] = tb("ko", [64, 256], BF16, i)
                self.tt("dve", cx["ko"][:n, :], pkt[:n, :256], eo[:n, :], ALU.mult, [pktk, eok], [cx["kok"]])

            def P3a(cx):
                i, n = cx["i"], cx["n"]
                pat, patk = self.psb("x")
                for h in range(4):
                    self.mm(pat[:n, h * 64:h * 64 + n], cx["ki"][:, h, :n], cx["qd"][:, h, :n], True, True, [cx["kik"], cx["qdk"]], [patk])
                cx["att"], cx["atk"] = tb("att", [64, 4, 64], BF16, i)
                for h in range(4):
                    self.tt("dve", cx["att"][:n, h, :n], pat[:n, h * 64:h * 64 + n], msk[:n, cx["mi_c"], :n], ALU.mult, [patk, "b_msk"], [cx["atk"]])
                pds, pdsk = self.psb("x")
                for h in range(4):
                    self.mm(pds[:64, h * 128:(h + 1) * 128], cx["ko"][:n, h * 64:(h + 1) * 64], cx["vt"][:n, h * 128:(h + 1) * 128], True, True, [cx["kok"], cx["vtk"]], [pdsk])
                cx["pds"], cx["pdsk"] = pds, pdsk

            def P3b(cx):
                i, dr, t0, n = cx["i"], cx["dr"], cx["t0"], cx["n"]
                vt, vtk, att, atk, qd, qdk = cx["vt"], cx["vtk"], cx["att"], cx["atk"], cx["qd"], cx["qdk"]
                if i == 0 or seq[i][0] != seq[i - 1][0]:
                    self.memset("dve", S[:], 0.0, ["b_S"])
                    self.memset("dve", Sbf[:], 0.0, ["b_Sbf"])
                po, pok = self.psb("x")
                for h in range(4):
                    self.mm(po[:, h * 64:h * 64 + n], vt[:n, h * 128:(h + 1) * 128], att[:n, h, :n], True, False, [vtk, atk], [pok])
                    self.mm(po[:, h * 64:h * 64 + n], Sbf[:, h, :], qd[:, h, :n], False, True, ["b_Sbf", qdk], [pok])
                dcol = (n - 1) if dr == 0 else 0
                pds, pdsk = cx["pds"], cx["pdsk"]
                for h in range(4):
                    self.stt("dve", S[:, h, :], S[:, h, :], cx["eb"][:, h, dcol:dcol + 1], pds[:64, h * 128:(h + 1) * 128], ALU.mult, ALU.add, ["b_S", cx["ebk"], pdsk], ["b_S"])
                self.cp("act", Sbf[:], S[:], ["b_S"], ["b_Sbf"])
                if dr == 1:
                    self.cp("act", obw[:, :, t0:t0 + n], po[:, 0:256].rearrange("p (h t) -> p h t", h=4)[:, :, :n], [pok], ["b_obw"])
                else:
                    of, ofk = tb("of", [128, 4, 64], F32, i)
                    sq4, sqk = tb("sq4", [128, 4, 64], F32, i)
                    if ("of", i % NBUF) not in init_done:
                        init_done.add(("of", i % NBUF))
                        self.memset("dve", of[:], 0.0, [ofk])
                    self.tt("dve", of[:, :, :n], po[:, 0:256].rearrange("p (h t) -> p h t", h=4)[:, :, :n], obw[:, :, t0:t0 + n], ALU.add, [pok, "b_obw"], [ofk])
                    self.tt("pool", sq4[:], of[:], of[:], ALU.mult, [ofk], [sqk])
                    cx["of"], cx["ofk"], cx["sq4"], cx["sqk"] = of, ofk, sq4, sqk
                    pend3.append(cx)
                del cxs[i]

            def P3c(cx):
                i, t0, n = cx["i"], cx["t0"], cx["n"]
                of, ofk = cx["of"], cx["ofk"]
                pn, pnk = self.psb("x")
                self.mm(pn[:, 0:256], self.onesf[:], cx["sq4"][:].rearrange("p h t -> p (h t)"), True, True, [cx["sqk"], "onesf"], [pnk])
                rs4, rsk = tb("rs4", [128, 4, 64], F32, i)
                rs2 = rs4[:].rearrange("p h t -> p (h t)")
                self.ts("dve", rs2, pn[:, 0:256], 1.0 / 128, EPS, ALU.mult, ALU.add, [pnk], [rsk])
                self.act(rs2, rs2, AF.Ln, [rsk], [rsk])
                self.act(rs2, rs2, AF.Exp, [rsk], [rsk], scale=-0.5)
                self.stt("dve", of[:, :, :n], of[:, :, :n], self.smallc[:, l * 8 + 1:l * 8 + 2], rs4[:, :, :n], ALU.mult, ALU.mult, [ofk, rsk, "smallc"], [ofk])
                self.tt("dve", self.oall[:, 4:8, t0:t0 + n], of[:, :, :n], cx["sr"][:, :, :n], ALU.mult, [ofk, cx["srk"]], ["oall"])

            init_done = set()
            pend3 = []

            def tb3(name, shape, dt, i):
                k = (name, i % 3)
                if k not in bufs:
                    bufs[k] = self.sb(es, "b_%s%d" % (name, i % 3), shape, dt)
                return bufs[k], ("b_" + name, i % 3)

            P1(ctx(0))
            P1(ctx(1))
            P2a(ctx(0))
            P2b(ctx(0))
            for i in range(NS):
                if i + 2 < NS:
                    P1(ctx(i + 2))
                if i + 1 < NS:
                    P2a(ctx(i + 1))
                P3a(ctx(i))
                while pend3:
                    P3c(pend3.pop(0))
                if i + 1 < NS:
                    P2b(ctx(i + 1))
                P3b(ctx(i))
            while pend3:
                P3c(pend3.pop(0))

    def resid_block(self, y, ykeys, b, l, gw, hsrc, hdst, bufs, nextnorm=None, final=False):
        hb, hk = bufs
        hv_s = hsrc.rearrange("(c p) t -> p c t", p=128)
        hv_d = hdst.rearrange("(c p) t -> p c t", p=128)
        sl = slice(b * TB, (b + 1) * TB)
        self.dma("sp", hb[:], hv_s[:, :, sl], (), [hk])
        self.rstd_from([y[:, kc, :] for kc in range(8)], TB, D, ykeys, self.nrs[:, :], "nrs", self.nsq, "nsq")
        for kc in range(8):
            self.stt("dve", y[:, kc, :], y[:, kc, :], self.gcol(l, gw, kc), self.nrs[:, :], ALU.mult, ALU.mult,
                     ykeys + ["nrs", "gains"], ykeys)
        self.tt("dve", hb[:], hb[:], y[:], ALU.add, [hk] + ykeys, [hk])
        st = self.dma("sp", hv_d[:, :, sl], hb[:], [hk], [("hdram", b)])
        if nextnorm is not None:
            self.norm_block(hb, hk, b, nextnorm[0], nextnorm[1], "nn")
        return st

    def merge_out(self, l, hsrc, hT, les):
        I = self.I
        with ExitStack() as es:
            merged = self.sb(es, "m_merged", [128, 8, T], BF16)
            wo = self.sb(es, "o_w", [128, 8, D], BF16)
            with ExitStack() as es2:
                wbr = [self.sb(es2, "m_wbr%d" % i, [128, 16, 128], BF16) for i in range(2)]
                wg = [self.sb(es2, "m_wg%d" % i, [128, 8, 4, 128], BF16) for i in range(2)]
                sig = [self.sb(es2, "m_sig%d" % i, [128, TB], F32) for i in range(2)]
                acc = self.sb(es2, "m_acc", [128, TB], F32)
                prod = self.sb(es2, "m_prod", [128, TB], F32)
                ukeys = [("uT", kc) for kc in range(8)]
                def ldm(dc):
                    wb_, wbk = wbr[dc % 2], ("m_wbr", dc % 2)
                    wg_, wgk = wg[dc % 2], ("m_wg", dc % 2)
                    self.loadw(wb_[:], I["w_branch"][l].rearrange("n (ec p) d -> p (n ec) d", p=128)[:, :, dc * 128:(dc + 1) * 128], wbk)
                    for br in range(4):
                        self.loadw(wg_[:, :, br, :], I["w_in"][l].rearrange("(kc p) n -> p kc n", p=128)
                                   [:, :, O_GATE + br * 1024 + dc * 128:O_GATE + br * 1024 + (dc + 1) * 128], wgk)
                ldm(0)
                for dc in range(8):
                    wb_, wbk = wbr[dc % 2], ("m_wbr", dc % 2)
                    wg_, wgk = wg[dc % 2], ("m_wg", dc % 2)
                    if dc + 1 < 8:
                        ldm(dc + 1)
                    self.loadw(wo[:, dc, :], I["w_out"][l][dc * 128:(dc + 1) * 128, :], "o_w")
                    for b in range(NB):
                        sl = slice(b * TB, (b + 1) * TB)
                        for br in range(4):
                            pg, pgk = self.psb("x")
                            for kc in range(8):
                                self.mm(pg[:, :TB], wg_[:, kc, br, :], self.U(kc, b * TB, (b + 1) * TB), kc == 0, kc == 7, [wgk] + ukeys, [pgk])
                            pp, ppk = self.psb("x")
                            for ec in range(4):
                                self.mm(pp[:, :TB], wb_[:, br * 4 + ec, :], self.oall[:, br * 4 + ec, sl], ec == 0, ec == 3, [wbk, "oall"], [ppk])
                            sg, sgk = sig[br % 2], ("m_sig", br % 2)
                            self.act(sg[:, :], pg[:, :TB], AF.Sigmoid, [pgk], [sgk])
                            if br == 0:
                                self.tt("dve", acc[:, :], pp[:, :TB], sg[:, :], ALU.mult, [ppk, sgk], ["m_acc"])
                            else:
                                self.tt("dve", prod[:, :], pp[:, :TB], sg[:, :], ALU.mult, [ppk, sgk], ["m_prod"])
                                if br < 3:
                                    self.tt("dve", acc[:, :], acc[:, :], prod[:, :], ALU.add, ["m_acc", "m_prod"], ["m_acc"])
                                else:
                                    self.tt("dve", merged[:, dc, sl], acc[:, :], prod[:, :], ALU.add, ["m_acc", "m_prod"], ["m_merged"])
                self.P.barrier()
            if "merged" in self.dbg_out and l == 0:
                with ExitStack() as es3:
                    tmp = self.sb(es3, "dbgtmp2", [128, T], F32)
                    for c in range(8):
                        self.cp("dve", tmp[:], merged[:, c, :], ["m_merged"], ["dbgtmp2"])
                        self.dma("sp", self.dbg_out["merged"][c * 128:(c + 1) * 128, :], tmp[:], ["dbgtmp2"], ())
                    self.P.barrier()
            with ExitStack() as es2:
                y2 = [self.sb(es2, "o_y%d" % i, [128, 8, TB], F32) for i in range(2)]
                hb2 = [self.sb(es2, "o_h%d" % i, [128, 8, TB], F32) for i in range(2)]
                self.nsq = self.sb(es2, "o_sq", [128, 2, TB], F32)
                self.nrs = self.sb(es2, "o_rs", [128, TB], F32)
                for b in range(NB):
                    y, yk = y2[b % 2], ("o_y", b % 2)
                    sl = slice(b * TB, (b + 1) * TB)
                    for dc in range(8):
                        pst, pk = self.psb("x")
                        for kc in range(8):
                            self.mm(pst[:, :TB], wo[:, kc, dc * 128:(dc + 1) * 128], merged[:, kc, sl], kc == 0, kc == 7, ["o_w", "m_merged"], [pk])
                        self.cp("act", y[:, dc, :], pst[:, :TB], [pk], [yk])
                    self.resid_block(y, [yk], b, l, 1, hsrc, hT, (hb2[b % 2], ("o_h", b % 2)), nextnorm=(l, 2))
                self.P.barrier()

    def ffn(self, l, hT, hdst):
        I = self.I
        finals = []
        NJ = DFF // 128
        wd_es = ExitStack()
        wd = self.sb(wd_es, "f_wd", [128, NJ, D], BF16)
        wd_loaded = [False]
        for half in range(2):
            with ExitStack() as es:
                actT = self.sb(es, "f_act", [128, NJ, 3 * TB], BF16)
                with ExitStack() as es2:
                    wu = [self.sb(es2, "f_wu%d" % i, [128, 8, 2, 128], BF16) for i in range(2)]
                    cgs = [self.sb(es2, "f_cg%d" % i, [128, TB], F32) for i in range(2)]
                    cvs = [self.sb(es2, "f_cv%d" % i, [128, TB], F32) for i in range(2)]
                    t1s = [self.sb(es2, "f_t1%d" % i, [128, TB], F32) for i in range(2)]
                    wup = I["ffn_w_up"][l].rearrange("(kc p) n -> p kc n", p=128)
                    ukeys = [("uT", kc) for kc in range(8)]
                    cw = self.convw
                    it = 0
                    def ldw(j):
                        w_, wk = wu[j % 2], ("f_wu", j % 2)
                        self.loadw(w_[:, :, 0, :], wup[:, :, j * 128:(j + 1) * 128], wk)
                        self.loadw(w_[:, :, 1, :], wup[:, :, DFF + j * 128:DFF + (j + 1) * 128], wk)
                    ldw(0)
                    for j in range(NJ):
                        w_, wk = wu[j % 2], ("f_wu", j % 2)
                        if j + 1 < NJ:
                            ldw(j + 1)
                        if half == 0:
                            self.loadw(wd[:, j, :], I["ffn_w_down"][l][j * 128:(j + 1) * 128, :], "f_wd")
                        for b in range(3 * half, 3 * half + 3):
                            it += 1
                            cg, cv, t1 = cgs[it % 2], cvs[it % 2], t1s[it % 2]
                            cgk, cvk, t1k = ("f_cg", it % 2), ("f_cv", it % 2), ("f_t1", it % 2)
                            taps = []
                            for gv in range(2):
                                pst, pk = self.psb("x")
                                for kc in range(8):
                                    self.mm(pst[:, :TB + 2], w_[:, kc, gv, :], self.uT[:, kc, b * TB:b * TB + TB + 2], kc == 0, kc == 7, [wk] + ukeys, [pk])
                                ch = gv * NJ + j
                                base = (l * 4) * 44
                                c0 = cw[:, base + ch:base + ch + 1]
                                c1 = cw[:, base + 44 + ch:base + 44 + ch + 1]
                                c2 = cw[:, base + 88 + ch:base + 88 + ch + 1]
                                cb = cw[:, base + 132 + ch:base + 132 + ch + 1]
                                dst, dk = (cg, cgk) if gv == 0 else (cv, cvk)
                                self.act(dst[:, :], pst[:, 0:TB], AF.Identity, [pk, "convw"], [dk], bias=cb, scale=c0)
                                taps.append((dst, dk, pst, pk, c1, c2))
                            for (dst, dk, pst, pk, c1, c2) in taps:
                                self.stt("dve", dst[:, :], pst[:, 1:TB + 1], c1, dst[:, :], ALU.mult, ALU.add, [pk, dk, "convw"], [dk])
                            for (dst, dk, pst, pk, c1, c2) in taps:
                                self.stt("dve", dst[:, :], pst[:, 2:TB + 2], c2, dst[:, :], ALU.mult, ALU.add, [pk, dk, "convw"], [dk])
                            self.act(t1[:, :], cg[:, :], AF.Gelu_apprx_tanh, [cgk], [t1k])
                            self.tt("pool", actT[:, j, (b - 3 * half) * TB:(b - 3 * half + 1) * TB], t1[:, :], cv[:, :], ALU.mult, [t1k, cvk], ["f_act"])
                    self.P.barrier()
                with ExitStack() as es2:
                    y2 = [self.sb(es2, "f_y%d" % i, [128, 8, TB], F32) for i in range(2)]
                    hb2 = [self.sb(es2, "f_h%d" % i, [128, 8, TB], F32) for i in range(2)]
                    self.nsq = self.sb(es2, "f_sq", [128, 2, TB], F32)
                    self.nrs = self.sb(es2, "f_rs", [128, TB], F32)
                    for b in range(3 * half, 3 * half + 3):
                        y, yk = y2[b % 2], ("f_y", b % 2)
                        sl = slice(b * TB, (b + 1) * TB)
                        for dc in range(8):
                            pst, pk = self.psb("x")
                            for j in range(NJ):
                                self.mm(pst[:, :TB], wd[:, j, dc * 128:(dc + 1) * 128], actT[:, j, (b - 3 * half) * TB:(b - 3 * half + 1) * TB], j == 0, j == NJ - 1, ["f_wd", "f_act"], [pk])
                            self.cp("act", y[:, dc, :], pst[:, :TB], [pk], [yk])
                        st = self.resid_block(y, [yk], b, l, 3, hT, hdst, (hb2[b % 2], ("f_h", b % 2)))
                        finals.append(st)
                    self.P.barrier()
        wd_es.close()
        return finals


def host_consts(inp):
    f32 = np.float32
    c = {}
    tab = np.asarray(inp["rel_bias_table"], f32)
    kk = np.arange(128)[:, None]
    jj = np.arange(A_W)[None, :]
    bk = rel_bucket_jax(kk - jj + A_C)
    c["slabA"] = np.ascontiguousarray(np.transpose(tab[bk][:, :, 0:8], (2, 0, 1))).astype(f32)
    ac = np.concatenate([tab[15, 0:8], tab[31, 0:8]])
    c["aconst"] = np.ascontiguousarray(np.broadcast_to(ac[None, :], (128, 16))).astype(f32)
    tabd = tab[:, 8:16]
    jj = np.arange(D_W)[None, :]
    rel = kk - jj + D_C
    tz = np.where((np.abs(rel) <= 128)[:, :, None], tabd[rel_bucket_jax(rel)], f32(NEG))
    qq = np.arange(256)[None, :]
    rel0 = kk - qq
    t0 = np.where(((np.abs(rel0) <= 128) & (kk >= NMETA))[:, :, None], tabd[rel_bucket_jax(rel0)], f32(NEG))
    tm = np.where((kk < NMETA)[:, :, None], tabd[rel_bucket_jax(rel0)], f32(NEG))
    c["slabD"] = np.ascontiguousarray(np.transpose(np.concatenate([tz, t0, tm], axis=1), (2, 0, 1))).astype(f32)
    c["dconst"] = np.ascontiguousarray(np.broadcast_to(tabd[15][None, :], (128, 8))).astype(f32)
    half = 32
    inv = (10000.0 ** (-np.arange(half, dtype=np.float32) / half)).astype(f32)
    ang = np.arange(T, dtype=f32)[None, :] * inv[:, None]
    cos, sin = np.cos(ang).astype(f32), np.sin(ang).astype(f32)
    c["rope"] = np.ascontiguousarray(np.concatenate([np.concatenate([cos, cos], 0), np.concatenate([-sin, sin], 0)], 1)).astype(f32)
    s = np.arange(128)[:, None]
    t = np.arange(128)[None, :]
    same = (s // 64) == (t // 64)
    LT = (same & (s <= t)).astype(f32)
    L = (same & (s >= t)).astype(f32)
    SU = (same & (s > t)).astype(f32)
    SL = (same & (s < t)).astype(f32)
    c["glam"] = np.ascontiguousarray(np.concatenate([LT, L, SU, SL], 1))
    return c


def host_layout(inp):
    f32 = np.float32
    g = {}
    sw = np.concatenate([np.arange(32, 64), np.arange(0, 32)])
    wq = np.asarray(inp["mla_w_q_up"], f32).reshape(DEPTH, 256, 4, 192)
    g["mla_w_q_up_sw"] = np.ascontiguousarray(wq[:, :, :, 128:][:, :, :, sw].reshape(DEPTH, 256, 256))
    g["w_in_kr_sw"] = np.ascontiguousarray(np.asarray(inp["w_in"])[:, :, OC_KR:OC_KR + 64][:, :, sw])
    gains = np.stack([inp["norm_mix_pre"], inp["norm_mix_post"], inp["norm_ffn_pre"], inp["norm_ffn_post"]], 1)
    g["gains"] = np.ascontiguousarray(gains.reshape(DEPTH * 4 * 8, 128).T).astype(f32)
    cw = np.concatenate([np.asarray(inp["ffn_conv_w"], f32), np.asarray(inp["ffn_conv_b"], f32)[:, None, :]], 1)
    g["convw"] = np.ascontiguousarray(cw.reshape(DEPTH * 4 * 44, 128).T).astype(f32)
    sc = np.zeros((DEPTH, 8, 128), f32)
    sc[:, 0] = inp["diff_subln"]
    sc[:, 1] = inp["gla_norm"]
    sc[:, 2] = inp["mla_kv_norm"]
    sc[:, 3:5] = np.asarray(inp["mla_q_norm"]).reshape(DEPTH, 2, 128)
    g["smallc"] = np.ascontiguousarray(sc.reshape(DEPTH * 8, 128).T)
    g["lamrep"] = np.ascontiguousarray(np.broadcast_to(np.asarray(inp["diff_lambda"], f32).reshape(1, DEPTH * 256), (128, DEPTH * 256)))
    g["gbias"] = np.ascontiguousarray(np.broadcast_to(np.asarray(inp["gla_gate_bias"], f32).reshape(1, DEPTH * 512), (128, DEPTH * 512)))
    g["sinkrep"] = np.ascontiguousarray(np.broadcast_to(np.asarray(inp["swa_sinks"], f32).reshape(1, DEPTH * 8), (128, DEPTH * 8)))
    return g


_NC_CACHE = {}


def get_nc(layers=(0, 1), dbg=None):
    key = (tuple(layers), tuple(sorted((dbg or {}).items())))
    if key not in _NC_CACHE:
        nc = bass.Bass("TRN2", target_bir_lowering=False)
        KB(nc, dbg).build(layers)
        _NC_CACHE[key] = nc
    return _NC_CACHE[key]


def make_in_maps(inp, cores):
    shared = {}
    for k in ("w_in", "w_branch", "w_out", "ffn_w_up", "ffn_w_down", "mla_w_q_up", "mla_w_kv_up", "gla_gate_up"):
        shared[k] = np.ascontiguousarray(np.asarray(inp[k], np.float32))
    shared.update(host_layout(inp))
    shared.update(host_consts(inp))
    meta = np.asarray(inp["meta_tokens"], np.float32)
    x = np.asarray(inp["x"], np.float32)
    maps = []
    for b in cores:
        h0 = np.concatenate([meta, x[b]], axis=0)
        m = dict(shared)
        m["h0T"] = np.ascontiguousarray(h0.T)
        maps.append(m)
    return maps


def kernel(**inputs):
    nc = get_nc()
    maps = make_in_maps(inputs, list(range(8)))
    res = run_bass_kernel_spmd(nc, maps, core_ids=list(range(8)))
    out = np.stack([np.ascontiguousarray(r["outT"][:, NMETA:].T) for r in res.results], axis=0)
    return out.astype(np.float32)
```
